# Optimizing a Trainium2 kernel written in Bass

```python
import math
import jax, jax.numpy as jnp
from jax import lax
import numpy as np

D_MODEL = 1024
BATCH = 8
SEQ = 4096
DEPTH = 1


GRID_W = 64
CTX_LEN = 256
N_MOD = 6
RMS_EPS = 1e-6

D_RNN = D_MODEL
RNN_HEADS = 16
RNN_HEAD_DIM = D_RNN // RNN_HEADS
RNN_CONV_W = 4
RNN_CONV_PAD = (2, 1)
RG_C = 8.0

D_HY = D_MODEL
HY_ORDER = 2
HY_CONV_W = 3
HY_CONV_PAD = (1, 1)
HY_EMB = 33
HY_BANDS = (HY_EMB - 1) // 2
HY_FILTER_ORDER = 64
HY_FAST_DECAY = 0.3
HY_SLOW_DECAY = 1.5
HY_TARGET = 1e-2

D_FF = ((8 * D_MODEL // 3 + 255) // 256) * 256

D_IN = 2 * D_RNN + 3 * D_HY + 2 * D_MODEL

kernel_name = 'hybrid_rglru_hyena_dit_block'


def rms_norm(x, g):
    xf = x.astype(jnp.float32)
    y = xf * lax.rsqrt(jnp.mean(xf * xf, axis=-1, keepdims=True) + RMS_EPS)
    return (y * g.astype(jnp.float32)).astype(x.dtype)


def modulate(h, shift, scale):
    return h * (1.0 + scale) + shift


def short_conv(u, w, b, pad):
    L = u.shape[-2]
    widths = [(0, 0)] * (u.ndim - 2) + [pad, (0, 0)]
    up = jnp.pad(u, widths)
    out = b
    for k in range(w.shape[0]):
        out = out + up[..., k:k + L, :] * w[k]
    return out


def seq_conv(u, w, b, pad, grid):
    if grid:
        B, T, C = u.shape
        rows = T // GRID_W
        return short_conv(u.reshape(B, rows, GRID_W, C), w, b, pad).reshape(B, T, C)
    return short_conv(u, w, b, pad)


def linear_scan(a, b, h0):
    def combine(e1, e2):
        a1, b1 = e1
        a2, b2 = e2
        return a1 * a2, a2 * b1 + b2
    A, H = lax.associative_scan(combine, (a, b), axis=1)
    return H + A * h0[:, None, :]


def rglru(u, wa, ba, wx, bx, lam, h0_f, h0_b):
    B, L, _ = u.shape
    uh = u.reshape(B, L, RNN_HEADS, RNN_HEAD_DIM)
    r = jax.nn.sigmoid(jnp.einsum('blhi,dhij->dblhj', uh, wa.astype(jnp.float32)).reshape(2, B, L, D_RNN)
                       + ba.astype(jnp.float32)[:, None, None, :])
    i = jax.nn.sigmoid(jnp.einsum('blhi,dhij->dblhj', uh, wx.astype(jnp.float32)).reshape(2, B, L, D_RNN)
                       + bx.astype(jnp.float32)[:, None, None, :])
    log_a = -RG_C * r * jax.nn.softplus(-lam.astype(jnp.float32))[:, None, None, :]
    a = jnp.exp(log_a)
    bb = jnp.sqrt(-jnp.expm1(2.0 * log_a)) * (i * u[None])
    h_f = linear_scan(a[0], bb[0], h0_f)
    h_b = jnp.flip(linear_scan(jnp.flip(a[1], 1), jnp.flip(bb[1], 1), h0_b), 1)
    return h_f, h_b


def hyena_filters(L, lp):
    f32 = jnp.float32
    t = jnp.linspace(0.0, 1.0, L, dtype=f32)[:, None]
    w = 2.0 * math.pi * jnp.arange(L, dtype=f32)[:, None] / L
    f = jnp.linspace(1e-4, HY_BANDS - 1, HY_BANDS, dtype=f32)[None, :]
    z = jnp.concatenate([t, jnp.cos(f * w), -jnp.sin(f * w)], axis=-1)
    freq = lp['hy_freq'].astype(f32)
    hdn = jnp.sin(freq * (z @ lp['hy_w1'].astype(f32) + lp['hy_b1'].astype(f32)))
    hdn = jnp.sin(freq * (hdn @ lp['hy_w2'].astype(f32) + lp['hy_b2'].astype(f32)))
    hdn = jnp.sin(freq * (hdn @ lp['hy_w3'].astype(f32) + lp['hy_b3'].astype(f32)))
    k = (hdn @ lp['hy_w4'].astype(f32)).reshape(L, HY_ORDER, 2, D_HY)
    max_decay = math.log(HY_TARGET) / HY_FAST_DECAY
    min_decay = math.log(HY_TARGET) / HY_SLOW_DECAY
    deltas = jnp.abs(jnp.linspace(min_decay, max_decay, D_HY, dtype=f32))
    k = k * jnp.exp(-t[:, :, None, None] * deltas)
    fwd, bwd = k[:, :, 0], k[:, :, 1]
    full = jnp.concatenate([fwd, jnp.zeros((1, HY_ORDER, D_HY), f32), jnp.flip(bwd[1:], 0)], axis=0)
    return full / jnp.sum(jnp.abs(full), axis=0, keepdims=True)


def hyena(phy, lp, grid):
    q = seq_conv(phy, lp['hy_conv_w'], lp['hy_conv_b'], HY_CONV_PAD, grid).astype(jnp.float32)
    v, x1, x2 = jnp.split(q, 3, axis=-1)
    L = q.shape[1]
    k_f = jnp.fft.rfft(hyena_filters(L, lp), axis=0)
    skip = lp['hy_skip'].astype(jnp.float32)
    z = v
    for o, gate in enumerate((x1, x2)):
        z_f = jnp.fft.rfft(z, n=2 * L, axis=1)
        conv = jnp.fft.irfft(z_f * k_f[None, :, o], n=2 * L, axis=1)[:, :L]
        z = gate * (conv + skip[o] * z)
    return z


def rnn_states(h, lp, grid, h0_f, h0_b):
    rx = h @ lp['w_in'][:, :D_RNN] + lp['b_in'][:D_RNN]
    u = seq_conv(rx, lp['rnn_conv_w'], lp['rnn_conv_b'], RNN_CONV_PAD, grid).astype(jnp.float32)
    return rglru(u, lp['rg_wa'], lp['rg_ba'], lp['rg_wx'], lp['rg_bx'], lp['rg_lambda'], h0_f, h0_b)


def mixer(h, lp, grid, h0_f, h0_b):
    h_f, h_b = rnn_states(h, lp, grid, h0_f, h0_b)
    rest = h @ lp['w_in'][:, D_RNN:] + lp['b_in'][D_RNN:]
    rg = rest[..., :D_RNN]
    phy = rest[..., D_RNN:D_RNN + 3 * D_HY]
    bg = rest[..., D_RNN + 3 * D_HY:]
    y_rnn = ((h_f + h_b) * jax.nn.gelu(rg.astype(jnp.float32))).astype(h.dtype)
    y_hy = hyena(phy, lp, grid).astype(h.dtype)
    g_a, g_b = jnp.split(jax.nn.sigmoid(bg), 2, axis=-1)
    merged = g_a * (y_rnn @ lp['w_a_out']) + g_b * (y_hy @ lp['w_b_out'])
    return merged @ lp['w_out'], h_f[:, -1], h_b[:, 0]


def swiglu(h, lp):
    g, u = jnp.split(h @ lp['w_ffn_in'], 2, axis=-1)
    return (jax.nn.silu(g) * u) @ lp['w_ffn_out']


def block(x, ctx, c, c_ctx, lp, update_ctx):
    mod = jax.nn.silu(c) @ lp['w_mod'] + lp['b_mod']
    mod_c = jax.nn.silu(c_ctx) @ lp['w_mod'] + lp['b_mod']
    sh1, sc1, g1, sh2, sc2, g2 = jnp.split(mod[:, None, :], N_MOD, axis=-1)
    csh1, csc1, cg1, csh2, csc2, cg2 = jnp.split(mod_c, N_MOD, axis=-1)
    B = x.shape[0]
    zeros = jnp.zeros((B, D_RNN), jnp.float32)
    hc = modulate(rms_norm(ctx, lp['norm1_g']), csh1, csc1)
    if update_ctx:
        yc, cf, cb = mixer(hc, lp, False, zeros, zeros)
        ctx = ctx + cg1 * yc
        ctx = ctx + cg2 * swiglu(modulate(rms_norm(ctx, lp['norm2_g']), csh2, csc2), lp)
    else:
        hcf, hcb = rnn_states(hc, lp, False, zeros, zeros)
        cf, cb = hcf[:, -1], hcb[:, 0]
    hx = modulate(rms_norm(x, lp['norm1_g']), sh1, sc1)
    yx, _, _ = mixer(hx, lp, True, cf, cb)
    x = x + g1 * yx
    x = x + g2 * swiglu(modulate(rms_norm(x, lp['norm2_g']), sh2, sc2), lp)
    return x, ctx


def setup_inputs(seed: int = 0) -> dict:
    key = jax.random.key(seed)
    ks = jax.random.split(key, 40)
    f32 = jnp.float32

    def nrm(k, shape, scale):
        return jax.random.normal(k, shape, f32) * scale

    u = jax.random.uniform(ks[16], (DEPTH, 2, D_RNN), f32, minval=0.9, maxval=0.999)
    a0 = u ** (1.0 / RG_C)
    return {
        'x': nrm(ks[0], (BATCH, SEQ, D_MODEL), 1.0),
        'c': nrm(ks[1], (BATCH, D_MODEL), 1.0),
        'ctx': nrm(ks[2], (BATCH, CTX_LEN, D_MODEL), 1.0),
        'c_ctx': nrm(ks[3], (D_MODEL,), 1.0),
        'w_mod': nrm(ks[4], (DEPTH, D_MODEL, N_MOD * D_MODEL), D_MODEL ** -0.5),
        'b_mod': nrm(ks[5], (DEPTH, N_MOD * D_MODEL), 0.02),
        'norm1_g': 1.0 + nrm(ks[6], (DEPTH, D_MODEL), 0.02),
        'norm2_g': 1.0 + nrm(ks[7], (DEPTH, D_MODEL), 0.02),
        'w_in': nrm(ks[8], (DEPTH, D_MODEL, D_IN), D_MODEL ** -0.5),
        'b_in': nrm(ks[9], (DEPTH, D_IN), 0.02),
        'rnn_conv_w': nrm(ks[10], (DEPTH, RNN_CONV_W, D_RNN), RNN_CONV_W ** -0.5),
        'rnn_conv_b': nrm(ks[11], (DEPTH, D_RNN), 0.02),
        'rg_wa': nrm(ks[12], (DEPTH, 2, RNN_HEADS, RNN_HEAD_DIM, RNN_HEAD_DIM), RNN_HEAD_DIM ** -0.5),
        'rg_ba': nrm(ks[13], (DEPTH, 2, D_RNN), 0.02),
        'rg_wx': nrm(ks[14], (DEPTH, 2, RNN_HEADS, RNN_HEAD_DIM, RNN_HEAD_DIM), RNN_HEAD_DIM ** -0.5),
        'rg_bx': nrm(ks[15], (DEPTH, 2, D_RNN), 0.02),
        'rg_lambda': jnp.log(a0) - jnp.log1p(-a0),
        'hy_conv_w': nrm(ks[17], (DEPTH, HY_CONV_W, 3 * D_HY), HY_CONV_W ** -0.5),
        'hy_conv_b': nrm(ks[18], (DEPTH, 3 * D_HY), 0.02),
        'hy_w1': nrm(ks[19], (DEPTH, HY_EMB, HY_FILTER_ORDER), HY_EMB ** -0.5),
        'hy_b1': nrm(ks[20], (DEPTH, HY_FILTER_ORDER), 0.1),
        'hy_w2': nrm(ks[21], (DEPTH, HY_FILTER_ORDER, HY_FILTER_ORDER), HY_FILTER_ORDER ** -0.5),
        'hy_b2': nrm(ks[22], (DEPTH, HY_FILTER_ORDER), 0.1),
        'hy_w3': nrm(ks[23], (DEPTH, HY_FILTER_ORDER, HY_FILTER_ORDER), HY_FILTER_ORDER ** -0.5),
        'hy_b3': nrm(ks[24], (DEPTH, HY_FILTER_ORDER), 0.1),
        'hy_freq': 1.0 + nrm(ks[25], (DEPTH, HY_FILTER_ORDER), 0.01),
        'hy_w4': nrm(ks[26], (DEPTH, HY_FILTER_ORDER, HY_ORDER * 2 * D_HY), HY_FILTER_ORDER ** -0.5),
        'hy_skip': nrm(ks[27], (DEPTH, HY_ORDER, D_HY), 1.0),
        'w_a_out': nrm(ks[28], (DEPTH, D_RNN, D_MODEL), D_RNN ** -0.5),
        'w_b_out': nrm(ks[29], (DEPTH, D_HY, D_MODEL), D_HY ** -0.5),
        'w_out': nrm(ks[30], (DEPTH, D_MODEL, D_MODEL), D_MODEL ** -0.5),
        'w_ffn_in': nrm(ks[31], (DEPTH, D_MODEL, 2 * D_FF), D_MODEL ** -0.5),
        'w_ffn_out': nrm(ks[32], (DEPTH, D_FF, D_MODEL), D_FF ** -0.5),
        'final_g': 1.0 + nrm(ks[33], (D_MODEL,), 0.02),
    }


def reference(x, c, ctx, c_ctx, w_mod, b_mod, norm1_g, norm2_g, w_in, b_in, rnn_conv_w, rnn_conv_b,
              rg_wa, rg_ba, rg_wx, rg_bx, rg_lambda, hy_conv_w, hy_conv_b, hy_w1, hy_b1, hy_w2, hy_b2,
              hy_w3, hy_b3, hy_freq, hy_w4, hy_skip, w_a_out, w_b_out, w_out, w_ffn_in, w_ffn_out, final_g):
    for l in range(DEPTH):
        lp = dict(w_mod=w_mod[l], b_mod=b_mod[l], norm1_g=norm1_g[l], norm2_g=norm2_g[l],
                  w_in=w_in[l], b_in=b_in[l], rnn_conv_w=rnn_conv_w[l], rnn_conv_b=rnn_conv_b[l],
                  rg_wa=rg_wa[l], rg_ba=rg_ba[l], rg_wx=rg_wx[l], rg_bx=rg_bx[l], rg_lambda=rg_lambda[l],
                  hy_conv_w=hy_conv_w[l], hy_conv_b=hy_conv_b[l], hy_w1=hy_w1[l], hy_b1=hy_b1[l],
                  hy_w2=hy_w2[l], hy_b2=hy_b2[l], hy_w3=hy_w3[l], hy_b3=hy_b3[l], hy_freq=hy_freq[l],
                  hy_w4=hy_w4[l], hy_skip=hy_skip[l], w_a_out=w_a_out[l], w_b_out=w_b_out[l],
                  w_out=w_out[l], w_ffn_in=w_ffn_in[l], w_ffn_out=w_ffn_out[l])
        x, ctx = block(x, ctx, c, c_ctx, lp, l < DEPTH - 1)
    return rms_norm(x, final_g)
```

```python
import numpy as np
import concourse.bass as bass
import concourse.mybir as mybir
from concourse.bass_utils import run_bass_kernel_spmd

F32 = mybir.dt.float32
BF16 = mybir.dt.bfloat16
AF = mybir.ActivationFunctionType
ALU = mybir.AluOpType

D = 1024
L = 4096
NFFT = 8192
NF1 = 33
DFF = 2816
EPS = 1e-6
NG = 8
NT = 8


class Tok:
    __slots__ = ("sem", "val")

    def __init__(self, sem, val):
        self.sem = sem
        self.val = val


class Fw:
    ENG = ("pe", "act", "dve", "pool", "sp")

    def __init__(self, nc):
        self.nc = nc
        self.ops = {e: [] for e in self.ENG}
        self.cnt = {e: 0 for e in self.ENG}
        self.esem = {e: nc.alloc_semaphore(name=f"cs_{e}") for e in self.ENG}
        self.waited = {e: {} for e in self.ENG}
        self.bufs = {}
        self.dsem = {}

    def op(self, eng, fn, reads=(), writes=(), dma=None, extra=()):
        deps = list(extra)
        for k in reads:
            b = self.bufs.get(k)
            if b is not None and b[0] is not None:
                deps.append(b[0])
        for k in writes:
            b = self.bufs.get(k)
            if b is not None:
                if b[1]:
                    deps.extend(b[1])
                elif b[0] is not None:
                    deps.append(b[0])
        best = {}
        for t in deps:
            k = id(t.sem)
            if k not in best or best[k].val < t.val:
                best[k] = t
        waits = []
        w = self.waited[eng]
        for k, t in best.items():
            if t.val > w.get(k, 0):
                w[k] = t.val
                waits.append((t.sem, t.val))
        if dma is None:
            self.cnt[eng] += 1
            tok = Tok(self.esem[eng], self.cnt[eng])
            inc = 1
        else:
            s = self.dsem.get(dma)
            if s is None:
                s = [self.nc.alloc_semaphore(name=f"ds_{len(self.dsem)}"), 0]
                self.dsem[dma] = s
            s[1] += 16
            tok = Tok(s[0], s[1])
            inc = 16
        self.ops[eng].append((waits, fn, tok.sem, inc))
        for k in reads:
            b = self.bufs.get(k)
            if b is None:
                b = [None, []]
                self.bufs[k] = b
            b[1].append(tok)
        for k in writes:
            self.bufs[k] = [tok, []]
        return tok

    def all_tokens(self):
        toks = []
        for e in self.ENG:
            if self.cnt[e] > 0:
                toks.append(Tok(self.esem[e], self.cnt[e]))
        for k, s in self.dsem.items():
            if s[1] > 0:
                toks.append(Tok(s[0], s[1]))
        return toks

    def barrier(self):
        toks = self.all_tokens()
        for e in self.ENG:
            self.op(e, None, extra=toks)

    def emit(self):
        nc = self.nc
        with nc.Block() as block:
            def mk(ename):
                def body(e):
                    for waits, fn, sem, inc in self.ops[ename]:
                        for s, v in waits:
                            e.wait_ge(s, v)
                        ins = e.nop() if fn is None else fn(e)
                        ins.then_inc(sem, inc)
                return body
            block.tensor(mk("pe"))
            block.scalar(mk("act"))
            block.vector(mk("dve"))
            block.gpsimd(mk("pool"))
            block.sync(mk("sp"))


def make_tables():
    tb = {}
    a = np.arange(32, dtype=np.float64)[:, None]
    f1 = np.arange(NF1, dtype=np.float64)[None, :]
    ang = 2 * np.pi * a * f1 / 64.0
    C, S = np.cos(ang), np.sin(ang)
    blk = np.concatenate([C, -S, S, C], axis=1)
    D1m = np.zeros((128, 2, 264), np.float64)
    for pair in range(2):
        for qq in range(2):
            q = 2 * pair + qq
            D1m[32 * q:32 * q + 32, pair, qq * 132:(qq + 1) * 132] = blk
    tb["D1m"] = D1m.reshape(128, 528).astype(np.float32)
    b = np.arange(128, dtype=np.float64)[:, None]
    f2 = np.arange(128, dtype=np.float64)[None, :]
    M = np.zeros((128, 2, NF1, 128))
    for k in range(NF1):
        ang = -2 * np.pi * b * (k + 64 * f2) / NFFT
        M[:, 0, k, :] = np.cos(ang)
        M[:, 1, k, :] = np.sin(ang)
    tb["M"] = M.reshape(128, 2 * NF1 * 128).astype(np.float32)
    ang = 2 * np.pi * f2.T * b.T / 128.0
    tb["CS"] = np.concatenate([np.cos(ang), np.sin(ang)], axis=1).astype(np.float32)
    c = np.full(NF1, 2.0)
    c[0] = 1.0
    c[NF1 - 1] = 1.0
    f1v = np.arange(NF1, dtype=np.float64)[:, None, None]
    bv = np.arange(128, dtype=np.float64)[None, :, None]
    av = np.arange(32, dtype=np.float64)[None, None, :]
    ang = 2 * np.pi * (av * f1v / 64.0 + bv * f1v / NFFT)
    Gr = c[:, None, None] / NFFT * np.cos(ang)
    Gi = c[:, None, None] / NFFT * np.sin(ang)
    G = np.zeros((128, 128 * 32), np.float64)
    G[:66] = np.concatenate([Gr, -Gi], axis=0).reshape(66, 128 * 32)
    tb["G"] = G.astype(np.float32)
    t = np.linspace(0.0, 1.0, L, dtype=np.float32)[:, None]
    w = (2.0 * np.pi * np.arange(L, dtype=np.float32)[:, None] / L).astype(np.float32)
    f = np.linspace(1e-4, 15, 16, dtype=np.float32)[None, :]
    fw_ = (f * w).astype(np.float32)
    z = np.concatenate([t, np.cos(fw_), -np.sin(fw_)], axis=-1).astype(np.float32)
    zp = np.zeros((128, L), np.float32)
    zp[:33] = z.T
    tb["zposT"] = zp
    max_decay = np.log(1e-2) / 0.3
    min_decay = np.log(1e-2) / 1.5
    tb["deltas"] = np.abs(np.linspace(min_decay, max_decay, D, dtype=np.float32)).astype(np.float32)
    tb["tbase"] = np.broadcast_to(t[:512, 0][None, :], (128, 512)).astype(np.float32).copy()
    tb["ident"] = np.eye(128, dtype=np.float32)
    return tb


def pv_layout():
    cols = {}
    n = [0]

    def add(name, k):
        cols[name] = n[0]
        n[0] += k
    for nm in ["norm1_g", "norm2_g", "final_g", "rnn_conv_b", "deltas"]:
        add(nm, 8)
    add("b_mod", 48)
    add("b_in", 56)
    add("rnn_conv_w", 32)
    add("rg_ba", 16)
    add("rg_bx", 16)
    add("rg_lambda", 16)
    add("hy_conv_w", 72)
    add("hy_conv_b", 24)
    add("hy_skip", 16)
    add("hy_small", 4)
    return cols, n[0]


def fm(v):
    v = np.asarray(v, np.float32).reshape(-1, 128)
    return np.ascontiguousarray(v.T)


def build_program(debug=False):
    nc = bass.Bass("TRN2", target_bir_lowering=False)
    fw = Fw(nc)
    PV, NPV = pv_layout()

    def din(name, shape, dt=F32):
        return nc.dram_tensor(name, list(shape), dt, kind="ExternalInput").ap()

    def dscr(name, shape, dt):
        return nc.dram_tensor(name, list(shape), dt, kind="Internal").ap()

    x_d = din("x", [L, D])
    ctx_d = din("ctx", [256, D])
    c2_d = din("c2", [128, 16])
    pv_d = din("pv", [128, NPV])
    w_mod_d = din("w_mod", [D, 6 * D])
    w_in_d = din("w_in", [D, 7 * D])
    rg_w_d = din("rg_w", [4, 16, 64, 64])
    hy_w1_d = din("hy_w1", [33, 64])
    hy_w2_d = din("hy_w2", [64, 64])
    hy_w3_d = din("hy_w3", [64, 64])
    hy_w4_d = din("hy_w4", [64, 4096])
    w_a_d = din("w_a_out", [D, D])
    w_b_d = din("w_b_out", [D, D])
    w_o_d = din("w_out", [D, D])
    w_fi_d = din("w_ffn_in", [D, 2 * DFF])
    w_fo_d = din("w_ffn_out", [DFF, D])
    t_D1m = din("t_D1m", [128, 528])
    t_M = din("t_M", [128, 2 * NF1 * 128])
    t_CS = din("t_CS", [128, 256])
    t_G = din("t_G", [128, 4096])
    t_zpos = din("t_zpos", [128, L])
    t_tbase = din("t_tbase", [128, 512])
    t_ident = din("t_ident", [128, 128])
    out_d = nc.dram_tensor("out", [L, D], F32, kind="ExternalOutput").ap()

    okind = "ExternalOutput" if debug else "Internal"
    hT_d = dscr("hT_s", [128, 8, L], BF16)
    yr_d = nc.dram_tensor("yr_s", [128, 8, L], BF16, kind=okind).ap()
    yh_d = nc.dram_tensor("yh_s", [128, 8, L], BF16, kind=okind).ap()
    gate_d = dscr("gate_s", [2, 128, L], F32)
    zb_d = dscr("zb_s", [128, L], BF16)
    kb_d = dscr("kb_s", [2, 128, L], BF16)
    WBLK = []
    for m in range(8):
        WBLK.append(("mix", 8, 512, [(w_a_d, m * 128, 128), (w_b_d, m * 128, 128),
                                      (w_in_d, 5 * D + m * 128, 128), (w_in_d, 6 * D + m * 128, 128)]))
    for blk in range(2):
        WBLK.append(("wo", 8, 512, [(w_o_d, blk * 512, 512)]))
    for blk in range(11):
        WBLK.append(("wfi", 8, 512, [(w_fi_d, blk * 256, 256), (w_fi_d, DFF + blk * 256, 256)]))
    for blk in range(8):
        WBLK.append(("wfo", 22, 128, [(w_fo_d, blk * 128, 128)]))
    NWB = len(WBLK)
    wblk_d = dscr("wblk_s", [NWB, 128, 4096], BF16)

    ARENA = 95000
    arena = nc.sbuf_tensor("arena", [128, ARENA], BF16).__enter__()
    psum = nc.psum_tensor("psum", [128, 8, 512], F32).__enter__()
    cur = [0]

    def alloc(shape, dt=F32):
        n = int(np.prod(shape))
        sz = n * (2 if dt == F32 else 1)
        a0 = cur[0]
        cur[0] += sz + (sz % 2)
        assert cur[0] <= ARENA, f"SBUF arena overflow {cur[0]}"
        v = arena[:, a0:a0 + sz]
        if dt == F32:
            v = v.bitcast(F32)
        if len(shape) == 2:
            v = v.rearrange("p (a b) -> p a b", b=shape[1])
        elif len(shape) == 3:
            v = v.rearrange("p (a b c) -> p a b c", b=shape[1], c=shape[2])
        elif len(shape) == 4:
            v = v.rearrange("p (a b c d) -> p a b c d", b=shape[1], c=shape[2], d=shape[3])
        return v

    psi = [0]

    def nextps():
        i = psi[0] % 8
        psi[0] += 1
        return i, f"ps{i}"

    def dma(eng, out, in_, reads, writes, key):
        return fw.op(eng, lambda e: e.dma_start(out=out, in_=in_), reads=reads, writes=writes, dma=key)

    def act(out, in_, func, reads, writes, bias=None, scale=None, accum=None):
        kw = {}
        if bias is not None:
            kw["bias"] = bias
        if scale is not None:
            kw["scale"] = scale
        if accum is not None:
            kw["accum_out"] = accum
        return fw.op("act", lambda e: e.activation(out=out, in_=in_, func=func, **kw), reads=reads, writes=writes)

    def tt(eng, out, in0, in1, op, reads, writes):
        return fw.op(eng, lambda e: e.tensor_tensor(out=out, in0=in0, in1=in1, op=op), reads=reads, writes=writes)

    def ts(eng, out, in0, s1, s2, op0, op1, reads, writes):
        if op1 is None:
            return fw.op(eng, lambda e: e.tensor_scalar(out=out, in0=in0, scalar1=s1, scalar2=None, op0=op0),
                         reads=reads, writes=writes)
        return fw.op(eng, lambda e: e.tensor_scalar(out=out, in0=in0, scalar1=s1, scalar2=s2, op0=op0, op1=op1),
                     reads=reads, writes=writes)

    def stt(out, in0, scalar, in1, op0, op1, reads, writes):
        return fw.op("dve", lambda e: e.scalar_tensor_tensor(out=out, in0=in0, scalar=scalar, in1=in1, op0=op0, op1=op1),
                     reads=reads, writes=writes)

    def mm(fn, reads, writes):
        return fw.op("pe", fn, reads=reads, writes=writes)

    pv = alloc([NPV])
    ident = alloc([128])
    ones_bf = alloc([128], BF16)
    modsb = alloc([48, 2])
    sc2 = alloc([8, 2])
    A1 = alloc([8, 2])
    A2 = alloc([8])
    cdec = alloc([2, 8])
    negdl = alloc([8])
    tmp_s = alloc([64])
    D1m = alloc([2, 264], BF16)
    Mtab = alloc([2, NF1, 128], BF16)
    CS = alloc([256], BF16)
    Gtab = alloc([128, 32], BF16)
    tbase = alloc([512])
    hdn3 = alloc([L])
    hcT = alloc([8, 256], BF16)
    halfc = alloc([512])
    hbias = alloc([32])
    chalf = alloc([2, 8])
    GLOBAL_END = cur[0]

    def pvc(name, j=0, n=1):
        c0 = PV[name] + j
        return pv[:, c0:c0 + n]

    dma("sp", pv, pv_d, [], ["pv"], "pv")
    dma("sp", ident, t_ident, [], ["ident"], "ident")
    dma("sp", tbase, t_tbase, [], ["tbase"], "tbase")
    dma("pool", D1m.rearrange("p a b -> p (a b)"), t_D1m, [], ["D1m"], "D1m")
    dma("pool", Mtab.rearrange("p a b c -> p (a b c)"), t_M, [], ["Mtab"], "Mtab")
    dma("pool", CS, t_CS, [], ["CS"], "CS")
    dma("pool", Gtab.rearrange("p a b -> p (a b)"), t_G, [], ["Gtab"], "Gtab")
    fw.op("dve", lambda e: e.memset(ones_bf, 1.0), writes=["ones"])

    p0_wm = [alloc([8, 256]) for _ in range(2)]
    p0_c2 = alloc([8, 2])
    p0_xt = [alloc([D]) for _ in range(2)]
    p0_xs = [alloc([D]) for _ in range(2)]
    p0_junk = alloc([D])
    p0_st = alloc([8])
    p0_hst = [alloc([8, 512], BF16) for _ in range(2)]
    p0_cf = [alloc([4096]) for _ in range(2)]
    p0_cb = [alloc([4096], BF16) for _ in range(2)]
    p0_zpos = [alloc([512]) for _ in range(2)]
    p0_hd = [alloc([512]) for _ in range(2)]
    p0_w = alloc([3, 64])
    p0_t1 = alloc([512])

    dma("sp", p0_c2.rearrange("p a b -> p (a b)"), c2_d, [], ["c2"], "c2")
    act(sc2, p0_c2, AF.Silu, ["c2"], ["sc2"])
    wmv = w_mod_d.rearrange("(kc p) n -> p kc n", p=128)
    pm, pmk = nextps()
    for piece in range(24):
        wt = p0_wm[piece % 2]
        dma("sp", wt, wmv[:, :, piece * 256:(piece + 1) * 256], [], [f"wm{piece % 2}"], f"wm{piece % 2}")

        def f(e, wt=wt, piece=piece):
            ins = None
            for jj in range(2):
                j = piece * 2 + jj
                for kc in range(8):
                    ins = e.matmul(psum[:, pm, 2 * j:2 * j + 2], lhsT=wt[:, kc, jj * 128:(jj + 1) * 128],
                                   rhs=sc2[:, kc, :], start=(kc == 0), stop=(kc == 7))
            return ins
        mm(f, [f"wm{piece % 2}", "sc2"], [pmk + f"_{piece}"] if piece else [pmk])
    bm = pv[:, PV["b_mod"]:PV["b_mod"] + 48]
    tt("dve", modsb, psum[:, pm, 0:96].rearrange("p (a b) -> p a b", b=2),
       bm.unsqueeze(2).broadcast_to([128, 48, 2]), ALU.add,
       ["pv", pmk] + [pmk + f"_{i}" for i in range(1, 24)], ["modsb"])
    ts("dve", A1, modsb[:, 8:16, :], 1.0, None, ALU.add, None, ["modsb"], ["A1"])
    tt("dve", A1, A1, pvc("norm1_g", 0, 8).unsqueeze(2).broadcast_to([128, 8, 2]), ALU.mult, ["A1", "pv"], ["A1"])
    ts("dve", A2, modsb[:, 32:40, 0], 1.0, None, ALU.add, None, ["modsb"], ["A2"])
    tt("dve", A2, A2, pvc("norm2_g", 0, 8), ALU.mult, ["A2", "pv"], ["A2"])
    lam = pv[:, PV["rg_lambda"]:PV["rg_lambda"] + 16]
    cdf = cdec.rearrange("p a b -> p (a b)")
    act(cdf, lam, AF.Exp, ["pv"], ["cdec"], scale=-1.0)
    act(cdf, cdf, AF.Ln, ["cdec"], ["cdec"], bias=1.0)
    ts("dve", cdf, cdf, -8.0, None, ALU.mult, None, ["cdec"], ["cdec"])
    ts("dve", negdl, pvc("deltas", 0, 8), -1.0, None, ALU.mult, None, ["pv"], ["negdl"])
    ts("dve", chalf.rearrange("p a b -> p (a b)"), cdf, 0.5, None, ALU.mult, None, ["cdec"], ["chalf"])
    ts("dve", hbias, pv[:, PV["rg_ba"]:PV["rg_ba"] + 32], 0.5, None, ALU.mult, None, ["pv"], ["hbias"])
    fw.op("pool", lambda e: e.memset(halfc, 0.5), writes=["halfc"])

    dma("sp", p0_w[0:33, 0, :], hy_w1_d, [], ["hw1"], "hw1")
    dma("sp", p0_w[0:64, 1, :], hy_w2_d, [], ["hw2"], "hw2")
    dma("sp", p0_w[0:64, 2, :], hy_w3_d, [], ["hw3"], "hw3")
    hs = PV["hy_small"]
    TWO_PI = 2.0 * np.pi
    fb = tmp_s[:, 0:3]
    tt("dve", fb[0:64], pv[0:64, hs:hs + 3], pv[0:64, hs + 3:hs + 4].broadcast_to([64, 3]), ALU.mult, ["pv"], ["fb"])
    fq = tmp_s[:, 4:5]
    ts("dve", fq[0:64], pv[0:64, hs + 3:hs + 4], 1.0 / TWO_PI, None, ALU.mult, None, ["pv"], ["fq"])
    ts("dve", fb[0:64], fb[0:64], 1.0 / TWO_PI, None, ALU.mult, None, ["fb"], ["fb"])
    for tt_ in range(NT):
        sl = slice(tt_ * 512, (tt_ + 1) * 512)
        zp = p0_zpos[tt_ % 2]
        zk = f"zpos{tt_ % 2}"
        dma("sp", zp, t_zpos[:, sl], [], [zk], zk)
        srcs = [zp, p0_hd[0], p0_hd[1]]
        srck = [zk, "hd0", "hd1"]
        dsts = [p0_hd[0], p0_hd[1], hdn3[:, sl]]
        dstk = ["hd0", "hd1", "hdn3"]
        for layer in range(3):
            kdim = 33 if layer == 0 else 64
            pi, pk = nextps()
            mm(lambda e, layer=layer, kdim=kdim, pi=pi, srcs=srcs: e.matmul(
                psum[0:64, pi, :], lhsT=p0_w[0:kdim, layer, :], rhs=srcs[layer][0:kdim, :], start=True, stop=True),
               [srck[layer], f"hw{layer + 1}"], [pk])
            ts("dve", p0_t1[0:64], psum[0:64, pi, :], fq[0:64], fb[0:64, layer:layer + 1], ALU.mult, ALU.add,
               [pk, "fq", "fb"], ["t1"])
            ts("dve", p0_junk[0:64, 0:512], p0_t1[0:64], 12582912.0, 12582912.0, ALU.add, ALU.subtract, ["t1"], ["junk"])
            tt("dve", p0_t1[0:64], p0_t1[0:64], p0_junk[0:64, 0:512], ALU.subtract, ["t1", "junk"], ["t1"])
            act(dsts[layer][0:64], p0_t1[0:64], AF.Sin, ["t1"], [dstk[layer]], scale=TWO_PI)

    def norm_tiles(src_d, ntile, which, store_fn):
        for ti in range(ntile):
            xt = p0_xt[ti % 2]
            xs = p0_xs[ti % 2]
            kx, ks = f"xt{ti % 2}", f"xs{ti % 2}"
            dma("sp", xt, src_d[ti * 128:(ti + 1) * 128, :], [], [kx], kx)
            act(p0_junk, xt, AF.Square, [kx], ["junk", "ss"], accum=p0_st[:, 0:1])
            act(p0_st[:, 1:2], p0_st[:, 0:1], AF.Sqrt, ["ss"], ["rs"], scale=1.0 / D, bias=EPS)
            fw.op("dve", lambda e: e.reciprocal(out=p0_st[:, 2:3], in_=p0_st[:, 1:2]), reads=["rs"], writes=["rstd"])
            ts("dve", xs, xt, p0_st[:, 2:3], None, ALU.mult, None, [kx, "rstd"], [ks])
            for half in range(2):
                pi, pk = nextps()

                def f(e, xs=xs, pi=pi, half=half):
                    ins = None
                    for gg in range(4):
                        g = half * 4 + gg
                        ins = e.transpose(psum[:, pi, gg * 128:(gg + 1) * 128], xs[:, g * 128:(g + 1) * 128], ident)
                    return ins
                mm(f, [ks, "ident"], [pk])
                for gg in range(4):
                    g = half * 4 + gg
                    store_fn(ti, g, psum[:, pi, gg * 128:(gg + 1) * 128], pk)

    def ctx_store(ti, g, src, pk):
        act(hcT[:, g, ti * 128:(ti + 1) * 128], src, AF.Identity, [pk, "A1", "modsb"], [f"hcT{g}"],
            scale=A1[:, g, 1:2], bias=modsb[:, g, 1:2])
    norm_tiles(ctx_d, 2, 1, ctx_store)

    def x_store(ti, g, src, pk):
        st = p0_hst[(ti // 4) % 2]
        kst = f"hst{(ti // 4) % 2}"
        act(st[:, g, (ti % 4) * 128:(ti % 4 + 1) * 128], src, AF.Identity, [pk, "A1", "modsb"], [kst + f"_{g}_{ti % 4}"],
            scale=A1[:, g, 0:1], bias=modsb[:, g, 0:1])
        if g == 7 and ti % 4 == 3:
            j = ti // 4
            dma("sp", hT_d[:, :, j * 512:(j + 1) * 512], st,
                [kst + f"_{gg}_{t4}" for gg in range(8) for t4 in range(4)], ["hT_d"], "hT_d")
    norm_tiles(x_d, 32, 0, x_store)

    for i, (nm, kc, cols, parts) in enumerate(WBLK):
        cf_ = p0_cf[i % 2]
        cb_ = p0_cb[i % 2]
        kf, kb = f"cf{i % 2}", f"cb{i % 2}"
        n = kc * cols
        cfv = cf_[:, 0:n].rearrange("p (a b) -> p a b", b=cols)
        c_at = 0
        for (src, col0, ncols) in parts:
            sv = src.rearrange("(kc p) n -> p kc n", p=128)[:, :, col0:col0 + ncols]
            dma("sp", cfv[:, :, c_at:c_at + ncols], sv, [], [kf], kf)
            c_at += ncols
        if i % 2 == 0:
            act(cb_[:, 0:n], cf_[:, 0:n], AF.Copy, [kf], [kb])
        else:
            fw.op("dve", lambda e, cb_=cb_, cf_=cf_, n=n: e.tensor_copy(out=cb_[:, 0:n], in_=cf_[:, 0:n]), reads=[kf], writes=[kb])
        dma("sp", wblk_d[i, :, 0:n], cb_[:, 0:n], [kb], [f"wblk{i}"], f"wblkst{i % 2}")

    fw.barrier()

    cur[0] = GLOBAL_END
    hTt = [alloc([8, 512], BF16) for _ in range(2)]
    wg = alloc([5, 8, 128], BF16)
    bd = alloc([4, 128])
    w4g = alloc([4, 128])
    sm = alloc([64])
    ptile = [alloc([512]) for _ in range(2)]
    B1 = alloc([L])
    mark = cur[0]
    B2 = alloc([L])
    B3 = alloc([L])
    B6 = alloc([L])
    rt = {nm: [alloc([512]) for _ in range(2)] for nm in ("ta", "ti", "t2", "tb", "hbt", "rgp", "hft")}
    ucx = alloc([256])
    endB = cur[0]
    cur[0] = mark
    kbf = alloc([L], BF16)
    ctile = [alloc([512]) for _ in range(2)]
    gtile = [alloc([512]) for _ in range(2)]
    dect = [alloc([512]) for _ in range(2)]
    zA4 = alloc([16, 128], BF16)
    kA4 = [alloc([16, 128], BF16) for _ in range(2)]
    AZ = alloc([8448], BF16)
    Kf = alloc([NF1, 2, 64])
    Y3 = alloc([3 * NF1, 64], BF16)
    xk = [alloc([4, 2, 64]) for _ in range(2)]
    cur[0] = max(cur[0], endB)
    A_sb = AZ.rearrange("p (c pl f) -> p c pl f", pl=4, f=NF1)
    Z_sb = AZ[:, 0:8192].rearrange("p (c b) -> p c b", b=128)
    BDK = [f"bd{m4}{hh}" for m4 in range(4) for hh in range(2)]
    fw.op("dve", lambda e: e.memset(bd.rearrange("p a b -> p (a b)"), 0.0), writes=BDK)

    w_in_v = w_in_d.rearrange("(kc p) n -> p kc n", p=128)
    hTloads = [0]

    def load_hT(j):
        i = hTloads[0] % 2
        hTloads[0] += 1
        dma("sp", hTt[i], hT_d[:, :, j * 512:(j + 1) * 512], ["hT_d"], [f"hTt{i}"], f"hTt{i}")
        return hTt[i], f"hTt{i}"

    def conv_ops(out, okey, P, pkey, taps, bias, rl):
        o3 = out.rearrange("p (r c) -> p r c", c=rl)
        p3 = P.rearrange("p (r c) -> p r c", c=rl)
        wc = [w for (o, w) in taps if o == 0][0]
        ts("dve", out, P, wc, bias, ALU.mult, ALU.add, [pkey, "pv"], [okey])
        for (o, w) in taps:
            if o == 0:
                continue
            if o < 0:
                stt(o3[:, :, -o:], p3[:, :, :rl + o], w, o3[:, :, -o:], ALU.mult, ALU.add, [pkey, okey, "pv"], [okey])
            else:
                stt(o3[:, :, :rl - o], p3[:, :, o:], w, o3[:, :, :rl - o], ALU.mult, ALU.add, [pkey, okey, "pv"], [okey])

    rtc = {nm: 0 for nm in rt}

    def rtile(nm):
        i = rtc[nm] % 2
        rtc[nm] += 1
        return rt[nm][i], f"{nm}{i}"

    def rglru(g, u, ukey, T, w, h0, h0k, hf, hfk, gel, gelk, ctx_mode):
        nt = T // w
        for d in range(2):
            order = range(nt) if d == 0 else range(nt - 1, -1, -1)
            prev = None
            for j in order:
                sl = slice(j * w, (j + 1) * w)
                uk = f"{ukey}{j}"
                ta, tak = rtile("ta")
                ti, tik = rtile("ti")
                t2, t2k = rtile("t2")
                tb, tbk = rtile("tb")
                pis = []
                for m in range(2):
                    pi, pk = nextps()
                    mm(lambda e, pi=pi, m=m, d=d, sl=sl: e.matmul(psum[:, pi, 0:w], lhsT=bd[:, 2 * m + d, :], rhs=u[:, sl],
                                                                  start=True, stop=True),
                       [f"bd{2 * m + d}0", f"bd{2 * m + d}1", uk], [pk])
                    pis.append((pi, pk))
                act(ta[:, 0:w], psum[:, pis[0][0], 0:w], AF.Tanh, [pis[0][1], "hbias"], [tak], scale=0.5,
                    bias=hbias[:, d * 8 + g:d * 8 + g + 1])
                act(ti[:, 0:w], psum[:, pis[1][0], 0:w], AF.Tanh, [pis[1][1], "hbias"], [tik], scale=0.5,
                    bias=hbias[:, 16 + d * 8 + g:16 + d * 8 + g + 1])
                act(t2[:, 0:w], ta[:, 0:w], AF.Exp, [tak, "cdec"], [t2k], scale=cdec[:, d, g:g + 1], bias=cdec[:, d, g:g + 1])
                act(ta[:, 0:w], ta[:, 0:w], AF.Exp, [tak, "chalf"], [tak], scale=chalf[:, d, g:g + 1], bias=chalf[:, d, g:g + 1])
                ts("pool", t2[:, 0:w], t2[:, 0:w], -1.0, 1.0, ALU.mult, ALU.add, [t2k], [t2k])
                tt("pool", t2[:, 0:w], t2[:, 0:w], halfc[:, 0:w], ALU.pow, [t2k, "halfc"], [t2k])
                stt(ti[:, 0:w], ti[:, 0:w], 1.0, u[:, sl], ALU.add, ALU.mult, [tik, uk], [tik])
                stt(tb[:, 0:w], t2[:, 0:w], 0.5, ti[:, 0:w], ALU.mult, ALU.mult, [t2k, tik], [tbk])
                if prev is None:
                    init = 0.0 if h0 is None else h0[d]
                    ik = [] if h0 is None else [h0k[d]]
                else:
                    init, ik = prev
                if d == 0:
                    if ctx_mode:
                        ho, hok = rtile("hft")
                        hov = ho[:, 0:w]
                    else:
                        hov, hok = hf[:, sl], f"{hfk}{j}"
                    fw.op("dve", lambda e, hov=hov, ta=ta, tb=tb, init=init: e.tensor_tensor_scan(
                        out=hov, data0=ta[:, 0:w], data1=tb[:, 0:w], initial=init, op0=ALU.mult, op1=ALU.add),
                        reads=[tak, tbk] + ik, writes=[hok])
                    prev = (hov[:, w - 1:w], [hok])
                    if ctx_mode and j == nt - 1:
                        fw.op("dve", lambda e, hov=hov: e.tensor_copy(out=sm[:, 0:1], in_=hov[:, w - 1:w]), reads=[hok], writes=["h0f"])
                else:
                    hb_, hbk_ = rtile("hbt")
                    hbv = hb_[:, 0:w]
                    fw.op("dve", lambda e, hbv=hbv, ta=ta, tb=tb, init=init: e.tensor_tensor_scan(
                        out=hbv[:, ::-1], data0=ta[:, 0:w][:, ::-1], data1=tb[:, 0:w][:, ::-1], initial=init,
                        op0=ALU.mult, op1=ALU.add), reads=[tak, tbk] + ik, writes=[hbk_])
                    prev = (hbv[:, 0:1], [hbk_])
                    if ctx_mode:
                        if j == 0:
                            fw.op("dve", lambda e, hbv=hbv: e.tensor_copy(out=sm[:, 1:2], in_=hbv[:, 0:1]), reads=[hbk_], writes=["h0b"])
                    else:
                        hk_ = f"{hfk}{j}"
                        tt("dve", hf[:, sl], hf[:, sl], hbv, ALU.add, [hk_, hbk_], [hk_])
                        stt(hf[:, sl], hf[:, sl], 0.5, gel[:, sl], ALU.mult, ALU.mult, [hk_, f"{gelk}{j}"], [hk_])

    AKEYS = [f"A_{ch}" for ch in range(64)]

    def fwd_transform(src4, skeys, consume):
        for c in range(16):
            for pair in range(2):
                pi, pk = nextps()
                mm(lambda e, c=c, pair=pair, pi=pi: e.matmul(psum[:, pi, 0:264], lhsT=src4[:, c, :], rhs=D1m[:, pair, :],
                                                            start=True, stop=True), skeys + ["D1m"], [pk])
                dstv = AZ[:, c * 528 + pair * 264: c * 528 + (pair + 1) * 264]
                wk_ = [f"A_{4 * c + 2 * pair}", f"A_{4 * c + 2 * pair + 1}"]
                if (2 * c + pair) % 2 == 0:
                    act(dstv, psum[:, pi, 0:264], AF.Copy, [pk], wk_)
                else:
                    fw.op("dve", lambda e, dstv=dstv, pi=pi: e.tensor_copy(out=dstv, in_=psum[:, pi, 0:264]), reads=[pk], writes=wk_)
        for bi in range(9):
            f1lo = bi * 4
            nf = min(4, NF1 - f1lo)
            pi, pk = nextps()

            def f(e, f1lo=f1lo, nf=nf, pi=pi):
                ins = None
                for k in range(nf):
                    f1 = f1lo + k
                    e.matmul(psum[:, pi, k * 128:(k + 1) * 128], lhsT=Mtab[:, 0, f1, :],
                             rhs=A_sb[:, :, 0:2, f1].rearrange("p c pl -> p pl c"), start=True, stop=False)
                    ins = e.matmul(psum[:, pi, k * 128:(k + 1) * 128], lhsT=Mtab[:, 1, f1, :],
                                   rhs=A_sb[:, :, 2:4, f1].rearrange("p c pl -> p pl c"), start=False, stop=True)
                return ins
            mm(f, AKEYS + ["Mtab"], [pk])
            consume(bi, pi, pk, f1lo, nf)

    for g in range(NG):
        for seg in range(5):
            dma("pool", wg[:, seg, :, :], w_in_v[:, :, seg * D + g * 128: seg * D + (g + 1) * 128], [], [f"wg{seg}"], f"wg{seg}")
        for m4 in range(4):
            for hh in range(2):
                dma("sp", bd[64 * hh:64 * hh + 64, m4, 64 * hh:64 * hh + 64], rg_w_d[m4, 2 * g + hh], [], [f"bd{m4}{hh}"], f"bdl{m4}{hh}")
        for cmb in range(4):
            o, dr = cmb // 2, cmb % 2
            c0 = o * 2048 + dr * 1024 + g * 128
            dma("sp", w4g[0:64, cmb, :], hy_w4_d[:, c0:c0 + 128], [], [f"w4g{cmb}"], f"w4g{cmb}")

        pi, pk = nextps()

        def f(e, pi=pi):
            ins = None
            for kc in range(8):
                ins = e.matmul(psum[:, pi, 0:256], lhsT=wg[:, 0, kc, :], rhs=hcT[:, kc, :], start=(kc == 0), stop=(kc == 7))
            return ins
        mm(f, ["wg0"] + [f"hcT{k}" for k in range(8)], [pk])
        pt, ptk = ptile[0], "pt0"
        act(pt[:, 0:256], psum[:, pi, 0:256], AF.Identity, [pk, "pv"], [ptk], bias=pvc("b_in", 0 * 8 + g))
        rtaps = [(k - 2, pvc("rnn_conv_w", k * 8 + g)) for k in range(4)]
        conv_ops(ucx, "uc0", pt[:, 0:256], ptk, rtaps, pvc("rnn_conv_b", g), 256)
        rglru(g, ucx, "uc", 256, 256, None, None, None, None, None, None, True)

        for j in range(NT):
            ht, hk = load_hT(j)
            sl = slice(j * 512, (j + 1) * 512)
            pis = []
            for seg in (0, 1):
                pi, pk = nextps()

                def f(e, pi=pi, seg=seg, ht=ht):
                    ins = None
                    for kc in range(8):
                        ins = e.matmul(psum[:, pi, :], lhsT=wg[:, seg, kc, :], rhs=ht[:, kc, :], start=(kc == 0), stop=(kc == 7))
                    return ins
                mm(f, [f"wg{seg}", hk], [pk])
                pis.append((pi, pk))
            pt, ptk = ptile[j % 2], f"pt{j % 2}"
            act(pt, psum[:, pis[0][0], :], AF.Identity, [pis[0][1], "pv"], [ptk], bias=pvc("b_in", 0 * 8 + g))
            conv_ops(B2[:, sl], f"um{j}", pt, ptk, rtaps, pvc("rnn_conv_b", g), 64)
            rg_, rgk = rtile("rgp")
            act(rg_, psum[:, pis[1][0], :], AF.Identity, [pis[1][1], "pv"], [rgk], bias=pvc("b_in", 1 * 8 + g))
            hb_, hbk_ = rtile("hbt")
            act(hb_, rg_, AF.Square, [rgk], [hbk_])
            ts("pool", hb_, hb_, 0.044715, 1.0, ALU.mult, ALU.add, [hbk_], [hbk_])
            tt("pool", hb_, hb_, rg_, ALU.mult, [hbk_, rgk], [hbk_])
            act(hb_, hb_, AF.Tanh, [hbk_], [hbk_], scale=0.7978845608028654)
            stt(B6[:, sl], hb_, 1.0, rg_, ALU.add, ALU.mult, [hbk_, rgk], [f"gel{j}"])
        rglru(g, B2, "um", L, 512, [sm[:, 0:1], sm[:, 1:2]], ["h0f", "h0b"], B3, "hf", B6, "gel", False)
        dma("pool", yr_d[:, g, :], B3, [f"hf{j}" for j in range(NT)], ["yr_d"], "yr_st")
        fw.barrier()

        for j in range(NT):
            ht, hk = load_hT(j)
            sl = slice(j * 512, (j + 1) * 512)
            for si, seg in enumerate((2, 3, 4)):
                pi, pk = nextps()

                def f(e, pi=pi, seg=seg, ht=ht):
                    ins = None
                    for kc in range(8):
                        ins = e.matmul(psum[:, pi, :], lhsT=wg[:, seg, kc, :], rhs=ht[:, kc, :], start=(kc == 0), stop=(kc == 7))
                    return ins
                mm(f, [f"wg{seg}", hk], [pk])
                pt = ptile[(j * 3 + si) % 2]
                ptk = f"pt{(j * 3 + si) % 2}"
                act(pt, psum[:, pi, :], AF.Identity, [pk, "pv"], [ptk], bias=pvc("b_in", seg * 8 + g))
                htaps = [(k - 1, pvc("hy_conv_w", k * 24 + si * 8 + g)) for k in range(3)]
                if si == 0:
                    conv_ops(B1[:, sl], "B1", pt, ptk, htaps, pvc("hy_conv_b", si * 8 + g), 64)
                else:
                    ct = ctile[(j * 2 + si) % 2]
                    ctk = f"ct{(j * 2 + si) % 2}"
                    conv_ops(ct, ctk, pt, ptk, htaps, pvc("hy_conv_b", si * 8 + g), 64)
                    dma("sp", gate_d[si - 1, :, sl], ct, [ctk], [f"gate_d{si - 1}"], f"gate_st{si - 1}")
        KBK = [f"kbf_{j}" for j in range(NT)]
        for o in range(2):
            dma("pool", zb_d, B1, ["B1"], ["zb_d"], "zb_st")
            for dr in range(2):
                cmb = o * 2 + dr
                for j in range(NT):
                    sl = slice(j * 512, (j + 1) * 512)
                    dt_ = dect[j % 2]
                    dk_ = f"dec{j % 2}"
                    ts("dve", sm[:, 8 + j % 2:9 + j % 2], negdl[:, g:g + 1], float(j * 512) / float(L - 1), None, ALU.mult, None,
                       ["negdl"], [f"decb{j % 2}"])
                    act(dt_, tbase, AF.Exp, ["tbase", "negdl", f"decb{j % 2}"], [dk_], scale=negdl[:, g:g + 1],
                        bias=sm[:, 8 + j % 2:9 + j % 2])
                    pi, pk = nextps()
                    mm(lambda e, pi=pi, cmb=cmb, sl=sl: e.matmul(psum[:, pi, :], lhsT=w4g[0:64, cmb, :], rhs=hdn3[0:64, sl],
                                                                 start=True, stop=True), [f"w4g{cmb}", "hdn3"], [pk])
                    tt("dve", kbf[:, sl], psum[:, pi, :], dt_, ALU.mult, [pk, dk_], [f"kbf_{j}"])
                    if dr == 1 and j == 0:
                        fw.op("dve", lambda e: e.memset(kbf[:, 0:1], 0.0), reads=["kbf_0"], writes=["kbf_0"])
                    act(ptile[j % 2], kbf[:, sl], AF.Abs, [f"kbf_{j}"], [f"pt{j % 2}", f"nrm{dr}_{j}"],
                        accum=sm[:, 16 + dr * 8 + j:17 + dr * 8 + j])
                dma("sp", kb_d[dr], kbf, KBK, [f"kb_d{dr}"], f"kb_st{dr}")
            fw.op("dve", lambda e: e.reduce_sum(out=sm[:, 32:33], in_=sm[:, 16:32], axis=mybir.AxisListType.X),
                  reads=[f"nrm{dr}_{j}" for dr in range(2) for j in range(NT)], writes=["nrmsum"])
            fw.op("dve", lambda e: e.reciprocal(out=sm[:, 33:34], in_=sm[:, 32:33]), reads=["nrmsum"], writes=["invn"])
            ts("dve", B1, B1, pvc("hy_skip", o * 8 + g), None, ALU.mult, None, ["B1", "pv"], ["B1"])
            for h in range(2):
                for dr in range(2):
                    KA = [f"kA4{dr}_{q}" for q in range(4)]
                    for q in range(4):
                        c0 = 64 * h + q
                        dma("sp", kA4[dr][32 * q:32 * q + 32, :, :],
                            kb_d[dr, c0:64 * (h + 1):4, :].rearrange("c (a b) -> a c b", b=128),
                            [f"kb_d{dr}"], [KA[q]], f"kA4{dr}")

                    def consume_k(bi, pi, pk, f1lo, nf, dr=dr):
                        src = psum[:, pi, 0:nf * 128].rearrange("p (f r c) -> p f r c", r=2, c=64)
                        if dr == 0:
                            act(Kf[:, f1lo:f1lo + nf, :, :], src, AF.Copy, [pk], [f"Kf{bi}"])
                        else:
                            tt("dve", Kf[:, f1lo:f1lo + nf, 0, :], Kf[:, f1lo:f1lo + nf, 0, :], src[:, :, 0, :], ALU.add,
                               [pk, f"Kf{bi}"], [f"Kf{bi}"])
                            tt("dve", Kf[:, f1lo:f1lo + nf, 1, :], Kf[:, f1lo:f1lo + nf, 1, :], src[:, :, 1, :], ALU.subtract,
                               [pk, f"Kf{bi}"], [f"Kf{bi}"])
                    fwd_transform(kA4[dr], KA, consume_k)
                ZA = [f"zA4_{q}" for q in range(4)]
                for q in range(4):
                    c0 = 64 * h + q
                    dma("sp", zA4[32 * q:32 * q + 32, :, :], zb_d[c0:64 * (h + 1):4, :].rearrange("c (a b) -> a c b", b=128),
                        ["zb_d"], [ZA[q]], "zA4")

                def consume_x(bi, pi, pk, f1lo, nf):
                    X = psum[:, pi, 0:nf * 128].rearrange("p (f r c) -> p f r c", r=2, c=64)
                    K = Kf[:, f1lo:f1lo + nf, :, :]
                    Ksw = Kf[:, f1lo:f1lo + nf, ::-1, :]
                    t1_ = xk[0][:, 0:nf]
                    t2_ = xk[1][:, 0:nf]
                    tt("dve", t1_, X, K, ALU.mult, [pk, f"Kf{bi}"], ["xk0"])
                    tt("dve", t2_, X, Ksw, ALU.mult, [pk, f"Kf{bi}"], ["xk1"])
                    yv = [Y3[:, pl * NF1 + f1lo: pl * NF1 + f1lo + nf, :] for pl in range(3)]
                    tt("dve", yv[1], t1_[:, :, 0, :], t1_[:, :, 1, :], ALU.subtract, ["xk0"], [f"Y3r_{bi}"])
                    tt("dve", yv[2], t2_[:, :, 0, :], t2_[:, :, 1, :], ALU.add, ["xk1"], [f"Y3i_{bi}"])
                    act(yv[0], yv[2], AF.Copy, [f"Y3i_{bi}"], [f"Y3n_{bi}"], scale=-1.0)
                fwd_transform(zA4, ZA, consume_x)
                ykeys = [f"Y3{x_}_{bi}" for bi in range(9) for x_ in "rin"]
                for c4 in range(16):
                    pi, pk = nextps()

                    def f(e, c4=c4, pi=pi):
                        ins = None
                        for cc in range(4):
                            ch = c4 * 4 + cc
                            e.matmul(psum[0:66, pi, cc * 128:(cc + 1) * 128],
                                     lhsT=Y3[:, NF1:3 * NF1, ch], rhs=CS[:, 0:128], start=True, stop=False)
                            ins = e.matmul(psum[0:66, pi, cc * 128:(cc + 1) * 128],
                                           lhsT=Y3[:, 0:2 * NF1, ch], rhs=CS[:, 128:256],
                                           start=False, stop=True)
                        return ins
                    mm(f, ykeys + ["CS"], [pk])
                    act(Z_sb[0:66, c4 * 4:(c4 + 1) * 4, :],
                        psum[0:66, pi, :].rearrange("p (c b) -> p c b", b=128), AF.Copy, [pk], [f"Z_{c4}"] + AKEYS)
                zkeys = [f"Z_{c4}" for c4 in range(16)]
                for bj in range(8):
                    pi, pk = nextps()

                    def f(e, bj=bj, pi=pi):
                        ins = None
                        for bb in range(16):
                            b_ = bj * 16 + bb
                            ins = e.matmul(psum[0:64, pi, bb * 32:(bb + 1) * 32], lhsT=Z_sb[0:66, :, b_], rhs=Gtab[0:66, b_, :],
                                           start=True, stop=True)
                        return ins
                    mm(f, zkeys + AKEYS + ["Gtab"], [pk])
                    ct = ctile[bj % 2]
                    ctk = f"ct{bj % 2}"
                    fw.op("dve", lambda e, ct=ct, pi=pi, h=h: e.tensor_copy(out=ct[64 * h:64 * h + 64, :], in_=psum[0:64, pi, :]),
                          reads=[pk], writes=[ctk])
                    zv = B1[64 * h:64 * h + 64, :].rearrange("p (a b) -> p a b", b=128)[:, :, bj * 16:(bj + 1) * 16]
                    stt(zv, ct[64 * h:64 * h + 64, :].rearrange("p (b a) -> p a b", a=32), sm[64 * h:64 * h + 64, 33:34], zv,
                        ALU.mult, ALU.add, [ctk, "invn", "B1"], ["B1"])
            for j in range(NT):
                sl = slice(j * 512, (j + 1) * 512)
                gt = gtile[j % 2]
                dma("sp", gt, gate_d[o, :, sl], [f"gate_d{o}"], [f"gt{j % 2}"], f"gt{j % 2}")
                tt("dve", B1[:, sl], B1[:, sl], gt, ALU.mult, ["B1", f"gt{j % 2}"], ["B1"])
        dma("pool", yh_d[:, g, :], B1, ["B1"], ["yh_d"], "yh_st")
        fw.barrier()

    cur[0] = GLOBAL_END
    ring = [alloc([4096], BF16) for _ in range(3)]
    yrt = alloc([8, 512], BF16)
    yht = alloc([8, 512], BF16)
    hTc = alloc([8, 512], BF16)
    xtok = alloc([4, D])
    xT = alloc([8, 512])
    merged = alloc([8, 512], BF16)
    gts = [alloc([512]) for _ in range(4)]
    sq = merged
    rstd = alloc([512])
    h2 = alloc([8, 512], BF16)
    hid = alloc([22, 512], BF16)
    wstate = {"issued": 0, "views": {}}
    TOTAL_W = NT * NWB

    def issue_w(upto):
        while wstate["issued"] < min(upto, TOTAL_W):
            sidx = wstate["issued"]
            i = sidx % NWB
            s_ = sidx % 3
            nm, kc, cols, parts = WBLK[i]
            n = kc * cols
            dma("sp", ring[s_][:, 0:n], wblk_d[i, :, 0:n], [f"wblk{i}"], [f"ring{s_}"], f"ring{s_}")
            wstate["views"][sidx] = (ring[s_][:, 0:n].rearrange("p (a b) -> p a b", b=cols), f"ring{s_}")
            wstate["issued"] += 1

    def get_w(j, i):
        sidx = j * NWB + i
        issue_w(sidx + 3)
        return wstate["views"][sidx]

    for j in range(NT):
        sl = slice(j * 512, (j + 1) * 512)
        dma("sp", yrt, yr_d[:, :, sl], ["yr_d"], ["yrt"], "yrt")
        dma("sp", yht, yh_d[:, :, sl], ["yh_d"], ["yht"], "yht")
        dma("sp", hTc, hT_d[:, :, sl], ["hT_d"], ["hTc"], "hTc")
        dma("sp", xtok, x_d[j * 512:(j + 1) * 512, :].rearrange("(a p) n -> p a n", p=128), [], ["xtok"], "xtok")
        for a4 in range(4):
            for half in range(2):
                pi, pk = nextps()

                def f(e, a4=a4, half=half, pi=pi):
                    ins = None
                    for gg in range(4):
                        g = half * 4 + gg
                        ins = e.transpose(psum[:, pi, gg * 128:(gg + 1) * 128], xtok[:, a4, g * 128:(g + 1) * 128], ident)
                    return ins
                mm(f, ["xtok", "ident"], [pk])
                act(xT[:, half * 4:(half + 1) * 4, a4 * 128:(a4 + 1) * 128],
                    psum[:, pi, :].rearrange("p (g t) -> p g t", t=128), AF.Copy, [pk], ["xT"])
        for m in range(8):
            wv, wk = get_w(j, m)
            pis = []
            for li, (mov, mk_) in enumerate(((yrt, "yrt"), (yht, "yht"), (hTc, "hTc"), (hTc, "hTc"))):
                pi, pk = nextps()

                def f(e, wv=wv, mov=mov, li=li, pi=pi):
                    ins = None
                    for kc in range(8):
                        ins = e.matmul(psum[:, pi, :], lhsT=wv[:, kc, li * 128:(li + 1) * 128], rhs=mov[:, kc, :],
                                       start=(kc == 0), stop=(kc == 7))
                    return ins
                mm(f, [wk, mk_], [pk])
                pis.append((pi, pk))
            act(gts[0], psum[:, pis[2][0], :], AF.Sigmoid, [pis[2][1], "pv"], ["gt0"], bias=pvc("b_in", 40 + m))
            act(gts[1], psum[:, pis[3][0], :], AF.Sigmoid, [pis[3][1], "pv"], ["gt1"], bias=pvc("b_in", 48 + m))
            tt("dve", gts[2], psum[:, pis[0][0], :], gts[0], ALU.mult, [pis[0][1], "gt0"], ["gt2"])
            tt("dve", gts[3], psum[:, pis[1][0], :], gts[1], ALU.mult, [pis[1][1], "gt1"], ["gt3"])
            tt("pool", merged[:, m, :], gts[2], gts[3], ALU.add, ["gt2", "gt3"], ["merged"])
        for m in range(8):
            wv, wk = get_w(j, 8 + m // 4)
            col = m % 4
            pi, pk = nextps()

            def f(e, wv=wv, col=col, pi=pi):
                ins = None
                for kc in range(8):
                    ins = e.matmul(psum[:, pi, :], lhsT=wv[:, kc, col * 128:(col + 1) * 128], rhs=merged[:, kc, :],
                                   start=(kc == 0), stop=(kc == 7))
                return ins
            mm(f, [wk, "merged"], [pk])
            stt(xT[:, m, :], psum[:, pi, :], modsb[:, 16 + m, 0:1], xT[:, m, :], ALU.mult, ALU.add, [pk, "modsb", "xT"], ["xT"])

        def rms_rstd():
            act(sq, xT, AF.Square, ["xT"], ["sq"])
            pi, pk = nextps()

            def f(e, pi=pi):
                ins = None
                for kc in range(8):
                    ins = e.matmul(psum[:, pi, :], lhsT=ones_bf, rhs=sq[:, kc, :], start=(kc == 0), stop=(kc == 7))
                return ins
            mm(f, ["ones", "sq"], [pk])
            act(rstd, psum[:, pi, :], AF.Sqrt, [pk], ["rstd_"], scale=1.0 / D, bias=EPS)
            fw.op("dve", lambda e: e.reciprocal(out=rstd, in_=rstd), reads=["rstd_"], writes=["rstd_"])
        rms_rstd()
        for m in range(8):
            tt("dve", gts[m % 2], xT[:, m, :], rstd, ALU.mult, ["xT", "rstd_"], [f"gt{m % 2}"])
            act(h2[:, m, :], gts[m % 2], AF.Identity, [f"gt{m % 2}", "A2", "modsb"], ["h2"], scale=A2[:, m:m + 1],
                bias=modsb[:, 24 + m, 0:1])
        for m in range(22):
            wv, wk = get_w(j, 10 + m // 2)
            pis = []
            for part in range(2):
                col = part * 2 + (m % 2)
                pi, pk = nextps()

                def f(e, wv=wv, col=col, pi=pi):
                    ins = None
                    for kc in range(8):
                        ins = e.matmul(psum[:, pi, :], lhsT=wv[:, kc, col * 128:(col + 1) * 128], rhs=h2[:, kc, :],
                                       start=(kc == 0), stop=(kc == 7))
                    return ins
                mm(f, [wk, "h2"], [pk])
                pis.append((pi, pk))
            act(gts[m % 2], psum[:, pis[0][0], :], AF.Silu, [pis[0][1]], [f"gt{m % 2}"])
            tt("dve", hid[:, m, :], psum[:, pis[1][0], :], gts[m % 2], ALU.mult, [pis[1][1], f"gt{m % 2}"], ["hid"])
        for m in range(8):
            wv, wk = get_w(j, 21 + m)
            pi, pk = nextps()

            def f(e, wv=wv, pi=pi):
                ins = None
                for kc in range(22):
                    ins = e.matmul(psum[:, pi, :], lhsT=wv[:, kc, :], rhs=hid[:, kc, :], start=(kc == 0), stop=(kc == 21))
                return ins
            mm(f, [wk, "hid"], [pk])
            stt(xT[:, m, :], psum[:, pi, :], modsb[:, 40 + m, 0:1], xT[:, m, :], ALU.mult, ALU.add, [pk, "modsb", "xT"], ["xT"])
        rms_rstd()
        for m in range(8):
            stt(xT[:, m, :], xT[:, m, :], pvc("final_g", m), rstd, ALU.mult, ALU.mult, ["xT", "rstd_", "pv"], ["xT"])
        for a4 in range(4):
            for half in range(2):
                pi, pk = nextps()

                def f(e, a4=a4, half=half, pi=pi):
                    ins = None
                    for gg in range(4):
                        g = half * 4 + gg
                        ins = e.transpose(psum[:, pi, gg * 128:(gg + 1) * 128], xT[:, g, a4 * 128:(a4 + 1) * 128], ident)
                    return ins
                mm(f, ["xT", "ident"], [pk])
                act(xtok[:, a4, half * 512:(half + 1) * 512], psum[:, pi, :], AF.Copy, [pk], ["xtok"])
        dma("sp", out_d[j * 512:(j + 1) * 512, :].rearrange("(a p) n -> p a n", p=128), xtok, ["xtok"], ["out_d"], "out_st")

    fw.op("sp", None, extra=fw.all_tokens())
    assert len(fw.dsem) + 5 <= 100, len(fw.dsem)
    fw.emit()
    return nc


_CACHE = {}


def prep_inputs(inp, tb, b):
    PV, NPV = pv_layout()
    pv = np.zeros((128, NPV), np.float32)

    def put(name, arr):
        a = fm(arr)
        pv[:, PV[name]:PV[name] + a.shape[1]] = a
    put("norm1_g", inp["norm1_g"][0])
    put("norm2_g", inp["norm2_g"][0])
    put("final_g", inp["final_g"])
    put("rnn_conv_b", inp["rnn_conv_b"][0])
    put("deltas", tb["deltas"])
    put("b_mod", inp["b_mod"][0])
    put("b_in", inp["b_in"][0])
    put("rnn_conv_w", inp["rnn_conv_w"][0].reshape(-1))
    put("rg_ba", inp["rg_ba"][0].reshape(-1))
    put("rg_bx", inp["rg_bx"][0].reshape(-1))
    put("rg_lambda", inp["rg_lambda"][0].reshape(-1))
    put("hy_conv_w", inp["hy_conv_w"][0].reshape(-1))
    put("hy_conv_b", inp["hy_conv_b"][0])
    put("hy_skip", inp["hy_skip"][0].reshape(-1))
    hs = PV["hy_small"]
    pv[:64, hs + 0] = inp["hy_b1"][0]
    pv[:64, hs + 1] = inp["hy_b2"][0]
    pv[:64, hs + 2] = inp["hy_b3"][0]
    pv[:64, hs + 3] = inp["hy_freq"][0]
    c2 = np.stack([fm(inp["c"][b]), fm(inp["c_ctx"])], axis=-1).reshape(128, 16)
    m = {
        "x": np.ascontiguousarray(inp["x"][b]),
        "ctx": np.ascontiguousarray(inp["ctx"][b]),
        "c2": np.ascontiguousarray(c2, dtype=np.float32),
        "pv": pv,
        "w_mod": inp["w_mod"][0], "w_in": inp["w_in"][0],
        "rg_w": np.ascontiguousarray(np.concatenate([inp["rg_wa"][0], inp["rg_wx"][0]], axis=0)),
        "hy_w1": inp["hy_w1"][0], "hy_w2": inp["hy_w2"][0], "hy_w3": inp["hy_w3"][0], "hy_w4": inp["hy_w4"][0],
        "w_a_out": inp["w_a_out"][0], "w_b_out": inp["w_b_out"][0], "w_out": inp["w_out"][0],
        "w_ffn_in": inp["w_ffn_in"][0], "w_ffn_out": inp["w_ffn_out"][0],
        "t_D1m": tb["D1m"], "t_M": tb["M"], "t_CS": tb["CS"], "t_G": tb["G"], "t_zpos": tb["zposT"],
        "t_tbase": tb["tbase"], "t_ident": tb["ident"],
    }
    return {k: np.ascontiguousarray(np.asarray(v, dtype=np.float32)) for k, v in m.items()}


def kernel(**inputs):
    inp = {k: np.asarray(v) for k, v in inputs.items()}
    tb = make_tables()
    if "nc" not in _CACHE:
        _CACHE["nc"] = build_program(False)
    nc = _CACHE["nc"]
    in_maps = [prep_inputs(inp, tb, b) for b in range(8)]
    res = run_bass_kernel_spmd(nc, in_maps, core_ids=list(range(8)))
    out = np.stack([np.asarray(r["out"]) for r in res.results], axis=0)
    return out.astype(np.float32)
```

```python
import numpy as np
import concourse.bass as bass
import concourse.mybir as mybir
from concourse.bass_utils import run_bass_kernel_spmd

F32 = mybir.dt.float32
BF16 = mybir.dt.bfloat16
AF = mybir.ActivationFunctionType
ALU = mybir.AluOpType

D = 1024
L = 4096
NFFT = 8192
NF1 = 33
DFF = 2816
EPS = 1e-6
NG = 8
NT = 8


class Tok:
    __slots__ = ("sem", "val")

    def __init__(self, sem, val):
        self.sem = sem
        self.val = val


class Fw:
    ENG = ("pe", "act", "dve", "pool", "sp")

    def __init__(self, nc):
        self.nc = nc
        self.ops = {e: [] for e in self.ENG}
        self.cnt = {e: 0 for e in self.ENG}
        self.esem = {e: nc.alloc_semaphore(name=f"cs_{e}") for e in self.ENG}
        self.waited = {e: {} for e in self.ENG}
        self.bufs = {}
        self.dsem = {}

    def op(self, eng, fn, reads=(), writes=(), dma=None, extra=()):
        deps = list(extra)
        for k in reads:
            b = self.bufs.get(k)
            if b is not None and b[0] is not None:
                deps.append(b[0])
        for k in writes:
            b = self.bufs.get(k)
            if b is not None:
                if b[1]:
                    deps.extend(b[1])
                elif b[0] is not None:
                    deps.append(b[0])
        best = {}
        for t in deps:
            k = id(t.sem)
            if k not in best or best[k].val < t.val:
                best[k] = t
        waits = []
        w = self.waited[eng]
        for k, t in best.items():
            if t.val > w.get(k, 0):
                w[k] = t.val
                waits.append((t.sem, t.val))
        if dma is None:
            self.cnt[eng] += 1
            tok = Tok(self.esem[eng], self.cnt[eng])
            inc = 1
        else:
            s = self.dsem.get(dma)
            if s is None:
                s = [self.nc.alloc_semaphore(name=f"ds_{len(self.dsem)}"), 0]
                self.dsem[dma] = s
            s[1] += 16
            tok = Tok(s[0], s[1])
            inc = 16
        self.ops[eng].append((waits, fn, tok.sem, inc))
        for k in reads:
            b = self.bufs.get(k)
            if b is None:
                b = [None, []]
                self.bufs[k] = b
            b[1].append(tok)
        for k in writes:
            self.bufs[k] = [tok, []]
        return tok

    def all_tokens(self):
        toks = []
        for e in self.ENG:
            if self.cnt[e] > 0:
                toks.append(Tok(self.esem[e], self.cnt[e]))
        for k, s in self.dsem.items():
            if s[1] > 0:
                toks.append(Tok(s[0], s[1]))
        return toks

    def barrier(self):
        toks = self.all_tokens()
        for e in self.ENG:
            self.op(e, None, extra=toks)

    def emit(self):
        nc = self.nc
        with nc.Block() as block:
            def mk(ename):
                def body(e):
                    for waits, fn, sem, inc in self.ops[ename]:
                        for s, v in waits:
                            e.wait_ge(s, v)
                        ins = e.nop() if fn is None else fn(e)
                        ins.then_inc(sem, inc)
                return body
            block.tensor(mk("pe"))
            block.scalar(mk("act"))
            block.vector(mk("dve"))
            block.gpsimd(mk("pool"))
            block.sync(mk("sp"))


def make_tables():
    tb = {}
    a = np.arange(32, dtype=np.float64)[:, None]
    f1 = np.arange(NF1, dtype=np.float64)[None, :]
    ang = 2 * np.pi * a * f1 / 64.0
    C, S = np.cos(ang), np.sin(ang)
    blk = np.concatenate([C, -S, S, C], axis=1)
    D1m = np.zeros((128, 2, 264), np.float64)
    for pair in range(2):
        for qq in range(2):
            q = 2 * pair + qq
            D1m[32 * q:32 * q + 32, pair, qq * 132:(qq + 1) * 132] = blk
    tb["D1m"] = D1m.reshape(128, 528).astype(np.float32)
    b = np.arange(128, dtype=np.float64)[:, None]
    f2 = np.arange(128, dtype=np.float64)[None, :]
    M = np.zeros((128, 2, NF1, 128))
    for k in range(NF1):
        ang = -2 * np.pi * b * (k + 64 * f2) / NFFT
        M[:, 0, k, :] = np.cos(ang)
        M[:, 1, k, :] = np.sin(ang)
    tb["M"] = M.reshape(128, 2 * NF1 * 128).astype(np.float32)
    ang = 2 * np.pi * f2.T * b.T / 128.0
    tb["CS"] = np.concatenate([np.cos(ang), np.sin(ang)], axis=1).astype(np.float32)
    c = np.full(NF1, 2.0)
    c[0] = 1.0
    c[NF1 - 1] = 1.0
    f1v = np.arange(NF1, dtype=np.float64)[:, None, None]
    bv = np.arange(128, dtype=np.float64)[None, :, None]
    av = np.arange(32, dtype=np.float64)[None, None, :]
    ang = 2 * np.pi * (av * f1v / 64.0 + bv * f1v / NFFT)
    Gr = c[:, None, None] / NFFT * np.cos(ang)
    Gi = c[:, None, None] / NFFT * np.sin(ang)
    G = np.zeros((128, 128 * 32), np.float64)
    G[:66] = np.concatenate([Gr, -Gi], axis=0).reshape(66, 128 * 32)
    tb["G"] = G.astype(np.float32)
    t = np.linspace(0.0, 1.0, L, dtype=np.float32)[:, None]
    w = (2.0 * np.pi * np.arange(L, dtype=np.float32)[:, None] / L).astype(np.float32)
    f = np.linspace(1e-4, 15, 16, dtype=np.float32)[None, :]
    fw_ = (f * w).astype(np.float32)
    z = np.concatenate([t, np.cos(fw_), -np.sin(fw_)], axis=-1).astype(np.float32)
    zp = np.zeros((128, L), np.float32)
    zp[:33] = z.T
    tb["zposT"] = zp
    max_decay = np.log(1e-2) / 0.3
    min_decay = np.log(1e-2) / 1.5
    tb["deltas"] = np.abs(np.linspace(min_decay, max_decay, D, dtype=np.float32)).astype(np.float32)
    tb["tbase"] = np.broadcast_to(t[:512, 0][None, :], (128, 512)).astype(np.float32).copy()
    tb["ident"] = np.eye(128, dtype=np.float32)
    return tb


def pv_layout():
    cols = {}
    n = [0]

    def add(name, k):
        cols[name] = n[0]
        n[0] += k
    for nm in ["norm1_g", "norm2_g", "final_g", "rnn_conv_b", "deltas"]:
        add(nm, 8)
    add("b_mod", 48)
    add("b_in", 56)
    add("rnn_conv_w", 32)
    add("rg_ba", 16)
    add("rg_bx", 16)
    add("rg_lambda", 16)
    add("hy_conv_w", 72)
    add("hy_conv_b", 24)
    add("hy_skip", 16)
    add("hy_small", 4)
    return cols, n[0]


def fm(v):
    v = np.asarray(v, np.float32).reshape(-1, 128)
    return np.ascontiguousarray(v.T)


def build_program(debug=False):
    nc = bass.Bass("TRN2", target_bir_lowering=False)
    fw = Fw(nc)
    PV, NPV = pv_layout()

    def din(name, shape, dt=F32):
        return nc.dram_tensor(name, list(shape), dt, kind="ExternalInput").ap()

    def dscr(name, shape, dt):
        return nc.dram_tensor(name, list(shape), dt, kind="Internal").ap()

    x_d = din("x", [L, D])
    ctx_d = din("ctx", [256, D])
    c2_d = din("c2", [128, 16])
    pv_d = din("pv", [128, NPV])
    w_mod_d = din("w_mod", [D, 6 * D])
    w_in_d = din("w_in", [D, 7 * D])
    rg_w_d = din("rg_w", [4, 16, 64, 64])
    hy_w1_d = din("hy_w1", [33, 64])
    hy_w2_d = din("hy_w2", [64, 64])
    hy_w3_d = din("hy_w3", [64, 64])
    hy_w4_d = din("hy_w4", [64, 4096])
    w_a_d = din("w_a_out", [D, D])
    w_b_d = din("w_b_out", [D, D])
    w_o_d = din("w_out", [D, D])
    w_fi_d = din("w_ffn_in", [D, 2 * DFF])
    w_fo_d = din("w_ffn_out", [DFF, D])
    t_D1m = din("t_D1m", [128, 528])
    t_M = din("t_M", [128, 2 * NF1 * 128])
    t_CS = din("t_CS", [128, 256])
    t_G = din("t_G", [128, 4096])
    t_zpos = din("t_zpos", [128, L])
    t_tbase = din("t_tbase", [128, 512])
    t_ident = din("t_ident", [128, 128])
    out_d = nc.dram_tensor("out", [L, D], F32, kind="ExternalOutput").ap()

    okind = "ExternalOutput" if debug else "Internal"
    hT_d = dscr("hT_s", [128, 8, L], BF16)
    yr_d = nc.dram_tensor("yr_s", [128, 8, L], BF16, kind=okind).ap()
    yh_d = nc.dram_tensor("yh_s", [128, 8, L], BF16, kind=okind).ap()
    gate_d = dscr("gate_s", [2, 128, L], F32)
    zb_d = dscr("zb_s", [128, L], BF16)
    kb_d = dscr("kb_s", [2, 128, L], BF16)
    WBLK = []
    for m in range(8):
        WBLK.append(("mix", 8, 512, [(w_a_d, m * 128, 128), (w_b_d, m * 128, 128),
                                      (w_in_d, 5 * D + m * 128, 128), (w_in_d, 6 * D + m * 128, 128)]))
    for blk in range(2):
        WBLK.append(("wo", 8, 512, [(w_o_d, blk * 512, 512)]))
    for blk in range(11):
        WBLK.append(("wfi", 8, 512, [(w_fi_d, blk * 256, 256), (w_fi_d, DFF + blk * 256, 256)]))
    for blk in range(8):
        WBLK.append(("wfo", 22, 128, [(w_fo_d, blk * 128, 128)]))
    NWB = len(WBLK)
    wblk_d = dscr("wblk_s", [NWB, 128, 4096], BF16)

    ARENA = 95000
    arena = nc.sbuf_tensor("arena", [128, ARENA], BF16).__enter__()
    psum = nc.psum_tensor("psum", [128, 8, 512], F32).__enter__()
    cur = [0]

    def alloc(shape, dt=F32):
        n = int(np.prod(shape))
        sz = n * (2 if dt == F32 else 1)
        a0 = cur[0]
        cur[0] += sz + (sz % 2)
        assert cur[0] <= ARENA, f"SBUF arena overflow {cur[0]}"
        v = arena[:, a0:a0 + sz]
        if dt == F32:
            v = v.bitcast(F32)
        if len(shape) == 2:
            v = v.rearrange("p (a b) -> p a b", b=shape[1])
        elif len(shape) == 3:
            v = v.rearrange("p (a b c) -> p a b c", b=shape[1], c=shape[2])
        elif len(shape) == 4:
            v = v.rearrange("p (a b c d) -> p a b c d", b=shape[1], c=shape[2], d=shape[3])
        return v

    psi = [0]

    def nextps():
        i = psi[0] % 8
        psi[0] += 1
        return i, f"ps{i}"

    def dma(eng, out, in_, reads, writes, key):
        return fw.op(eng, lambda e: e.dma_start(out=out, in_=in_), reads=reads, writes=writes, dma=key)

    def act(out, in_, func, reads, writes, bias=None, scale=None, accum=None):
        kw = {}
        if bias is not None:
            kw["bias"] = bias
        if scale is not None:
            kw["scale"] = scale
        if accum is not None:
            kw["accum_out"] = accum
        return fw.op("act", lambda e: e.activation(out=out, in_=in_, func=func, **kw), reads=reads, writes=writes)

    def tt(eng, out, in0, in1, op, reads, writes):
        return fw.op(eng, lambda e: e.tensor_tensor(out=out, in0=in0, in1=in1, op=op), reads=reads, writes=writes)

    def ts(eng, out, in0, s1, s2, op0, op1, reads, writes):
        if op1 is None:
            return fw.op(eng, lambda e: e.tensor_scalar(out=out, in0=in0, scalar1=s1, scalar2=None, op0=op0),
                         reads=reads, writes=writes)
        return fw.op(eng, lambda e: e.tensor_scalar(out=out, in0=in0, scalar1=s1, scalar2=s2, op0=op0, op1=op1),
                     reads=reads, writes=writes)

    def stt(out, in0, scalar, in1, op0, op1, reads, writes):
        return fw.op("dve", lambda e: e.scalar_tensor_tensor(out=out, in0=in0, scalar=scalar, in1=in1, op0=op0, op1=op1),
                     reads=reads, writes=writes)

    def mm(fn, reads, writes):
        return fw.op("pe", fn, reads=reads, writes=writes)

    pv = alloc([NPV])
    ident = alloc([128])
    ones_bf = alloc([128], BF16)
    modsb = alloc([48, 2])
    sc2 = alloc([8, 2])
    A1 = alloc([8, 2])
    A2 = alloc([8])
    cdec = alloc([2, 8])
    negdl = alloc([8])
    tmp_s = alloc([64])
    D1m = alloc([2, 264], BF16)
    Mtab = alloc([2, NF1, 128], BF16)
    CS = alloc([256], BF16)
    Gtab = alloc([128, 32], BF16)
    tbase = alloc([512])
    hdn3 = alloc([L], BF16)
    hcT = alloc([8, 256], BF16)
    GLOBAL_END = cur[0]

    def pvc(name, j=0, n=1):
        c0 = PV[name] + j
        return pv[:, c0:c0 + n]

    dma("sp", pv, pv_d, [], ["pv"], "pv")
    dma("sp", ident, t_ident, [], ["ident"], "ident")
    dma("sp", tbase, t_tbase, [], ["tbase"], "tbase")
    dma("pool", D1m.rearrange("p a b -> p (a b)"), t_D1m, [], ["D1m"], "D1m")
    dma("pool", Mtab.rearrange("p a b c -> p (a b c)"), t_M, [], ["Mtab"], "Mtab")
    dma("pool", CS, t_CS, [], ["CS"], "CS")
    dma("pool", Gtab.rearrange("p a b -> p (a b)"), t_G, [], ["Gtab"], "Gtab")
    fw.op("dve", lambda e: e.memset(ones_bf, 1.0), writes=["ones"])

    p0_wm = [alloc([8, 256]) for _ in range(2)]
    p0_c2 = alloc([8, 2])
    p0_xt = [alloc([D]) for _ in range(2)]
    p0_xs = [alloc([D]) for _ in range(2)]
    p0_junk = alloc([D])
    p0_st = alloc([8])
    p0_hst = [alloc([8, 512], BF16) for _ in range(2)]
    p0_cf = [alloc([4096]) for _ in range(2)]
    p0_cb = [alloc([4096], BF16) for _ in range(2)]
    p0_zpos = [alloc([512]) for _ in range(2)]
    p0_hd = [alloc([512]) for _ in range(2)]
    p0_w = alloc([3, 64])
    p0_t1 = alloc([512])

    dma("sp", p0_c2.rearrange("p a b -> p (a b)"), c2_d, [], ["c2"], "c2")
    act(sc2, p0_c2, AF.Silu, ["c2"], ["sc2"])
    wmv = w_mod_d.rearrange("(kc p) n -> p kc n", p=128)
    pm, pmk = nextps()
    for piece in range(24):
        wt = p0_wm[piece % 2]
        dma("sp", wt, wmv[:, :, piece * 256:(piece + 1) * 256], [], [f"wm{piece % 2}"], f"wm{piece % 2}")

        def f(e, wt=wt, piece=piece):
            ins = None
            for jj in range(2):
                j = piece * 2 + jj
                for kc in range(8):
                    ins = e.matmul(psum[:, pm, 2 * j:2 * j + 2], lhsT=wt[:, kc, jj * 128:(jj + 1) * 128],
                                   rhs=sc2[:, kc, :], start=(kc == 0), stop=(kc == 7))
            return ins
        mm(f, [f"wm{piece % 2}", "sc2"], [pmk + f"_{piece}"] if piece else [pmk])
    bm = pv[:, PV["b_mod"]:PV["b_mod"] + 48]
    tt("dve", modsb, psum[:, pm, 0:96].rearrange("p (a b) -> p a b", b=2),
       bm.unsqueeze(2).broadcast_to([128, 48, 2]), ALU.add,
       ["pv", pmk] + [pmk + f"_{i}" for i in range(1, 24)], ["modsb"])
    ts("dve", A1, modsb[:, 8:16, :], 1.0, None, ALU.add, None, ["modsb"], ["A1"])
    tt("dve", A1, A1, pvc("norm1_g", 0, 8).unsqueeze(2).broadcast_to([128, 8, 2]), ALU.mult, ["A1", "pv"], ["A1"])
    ts("dve", A2, modsb[:, 32:40, 0], 1.0, None, ALU.add, None, ["modsb"], ["A2"])
    tt("dve", A2, A2, pvc("norm2_g", 0, 8), ALU.mult, ["A2", "pv"], ["A2"])
    lam = pv[:, PV["rg_lambda"]:PV["rg_lambda"] + 16]
    cdf = cdec.rearrange("p a b -> p (a b)")
    act(cdf, lam, AF.Exp, ["pv"], ["cdec"], scale=-1.0)
    act(cdf, cdf, AF.Ln, ["cdec"], ["cdec"], bias=1.0)
    ts("dve", cdf, cdf, -8.0, None, ALU.mult, None, ["cdec"], ["cdec"])
    ts("dve", negdl, pvc("deltas", 0, 8), -1.0, None, ALU.mult, None, ["pv"], ["negdl"])

    dma("sp", p0_w[0:33, 0, :], hy_w1_d, [], ["hw1"], "hw1")
    dma("sp", p0_w[0:64, 1, :], hy_w2_d, [], ["hw2"], "hw2")
    dma("sp", p0_w[0:64, 2, :], hy_w3_d, [], ["hw3"], "hw3")
    hs = PV["hy_small"]
    TWO_PI = 2.0 * np.pi
    fb = tmp_s[:, 0:3]
    tt("dve", fb[0:64], pv[0:64, hs:hs + 3], pv[0:64, hs + 3:hs + 4].broadcast_to([64, 3]), ALU.mult, ["pv"], ["fb"])
    fq = tmp_s[:, 4:5]
    ts("dve", fq[0:64], pv[0:64, hs + 3:hs + 4], 1.0 / TWO_PI, None, ALU.mult, None, ["pv"], ["fq"])
    ts("dve", fb[0:64], fb[0:64], 1.0 / TWO_PI, None, ALU.mult, None, ["fb"], ["fb"])
    for tt_ in range(NT):
        sl = slice(tt_ * 512, (tt_ + 1) * 512)
        zp = p0_zpos[tt_ % 2]
        zk = f"zpos{tt_ % 2}"
        dma("sp", zp, t_zpos[:, sl], [], [zk], zk)
        srcs = [zp, p0_hd[0], p0_hd[1]]
        srck = [zk, "hd0", "hd1"]
        dsts = [p0_hd[0], p0_hd[1], hdn3[:, sl]]
        dstk = ["hd0", "hd1", "hdn3"]
        for layer in range(3):
            kdim = 33 if layer == 0 else 64
            pi, pk = nextps()
            mm(lambda e, layer=layer, kdim=kdim, pi=pi, srcs=srcs: e.matmul(
                psum[0:64, pi, :], lhsT=p0_w[0:kdim, layer, :], rhs=srcs[layer][0:kdim, :], start=True, stop=True),
               [srck[layer], f"hw{layer + 1}"], [pk])
            ts("dve", p0_t1[0:64], psum[0:64, pi, :], fq[0:64], fb[0:64, layer:layer + 1], ALU.mult, ALU.add,
               [pk, "fq", "fb"], ["t1"])
            ts("dve", p0_junk[0:64, 0:512], p0_t1[0:64], 12582912.0, 12582912.0, ALU.add, ALU.subtract, ["t1"], ["junk"])
            tt("dve", p0_t1[0:64], p0_t1[0:64], p0_junk[0:64, 0:512], ALU.subtract, ["t1", "junk"], ["t1"])
            act(dsts[layer][0:64], p0_t1[0:64], AF.Sin, ["t1"], [dstk[layer]], scale=TWO_PI)

    def norm_tiles(src_d, ntile, which, store_fn):
        for ti in range(ntile):
            xt = p0_xt[ti % 2]
            xs = p0_xs[ti % 2]
            kx, ks = f"xt{ti % 2}", f"xs{ti % 2}"
            dma("sp", xt, src_d[ti * 128:(ti + 1) * 128, :], [], [kx], kx)
            act(p0_junk, xt, AF.Square, [kx], ["junk", "ss"], accum=p0_st[:, 0:1])
            act(p0_st[:, 1:2], p0_st[:, 0:1], AF.Sqrt, ["ss"], ["rs"], scale=1.0 / D, bias=EPS)
            fw.op("dve", lambda e: e.reciprocal(out=p0_st[:, 2:3], in_=p0_st[:, 1:2]), reads=["rs"], writes=["rstd"])
            ts("dve", xs, xt, p0_st[:, 2:3], None, ALU.mult, None, [kx, "rstd"], [ks])
            for half in range(2):
                pi, pk = nextps()

                def f(e, xs=xs, pi=pi, half=half):
                    ins = None
                    for gg in range(4):
                        g = half * 4 + gg
                        ins = e.transpose(psum[:, pi, gg * 128:(gg + 1) * 128], xs[:, g * 128:(g + 1) * 128], ident)
                    return ins
                mm(f, [ks, "ident"], [pk])
                for gg in range(4):
                    g = half * 4 + gg
                    store_fn(ti, g, psum[:, pi, gg * 128:(gg + 1) * 128], pk)

    def ctx_store(ti, g, src, pk):
        act(hcT[:, g, ti * 128:(ti + 1) * 128], src, AF.Identity, [pk, "A1", "modsb"], [f"hcT{g}"],
            scale=A1[:, g, 1:2], bias=modsb[:, g, 1:2])
    norm_tiles(ctx_d, 2, 1, ctx_store)

    def x_store(ti, g, src, pk):
        st = p0_hst[(ti // 4) % 2]
        kst = f"hst{(ti // 4) % 2}"
        act(st[:, g, (ti % 4) * 128:(ti % 4 + 1) * 128], src, AF.Identity, [pk, "A1", "modsb"], [kst + f"_{g}_{ti % 4}"],
            scale=A1[:, g, 0:1], bias=modsb[:, g, 0:1])
        if g == 7 and ti % 4 == 3:
            j = ti // 4
            dma("sp", hT_d[:, :, j * 512:(j + 1) * 512], st,
                [kst + f"_{gg}_{t4}" for gg in range(8) for t4 in range(4)], ["hT_d"], "hT_d")
    norm_tiles(x_d, 32, 0, x_store)

    for i, (nm, kc, cols, parts) in enumerate(WBLK):
        cf_ = p0_cf[i % 2]
        cb_ = p0_cb[i % 2]
        kf, kb = f"cf{i % 2}", f"cb{i % 2}"
        n = kc * cols
        cfv = cf_[:, 0:n].rearrange("p (a b) -> p a b", b=cols)
        c_at = 0
        for (src, col0, ncols) in parts:
            sv = src.rearrange("(kc p) n -> p kc n", p=128)[:, :, col0:col0 + ncols]
            dma("sp", cfv[:, :, c_at:c_at + ncols], sv, [], [kf], kf)
            c_at += ncols
        if i % 2 == 0:
            act(cb_[:, 0:n], cf_[:, 0:n], AF.Copy, [kf], [kb])
        else:
            fw.op("dve", lambda e, cb_=cb_, cf_=cf_, n=n: e.tensor_copy(out=cb_[:, 0:n], in_=cf_[:, 0:n]), reads=[kf], writes=[kb])
        dma("sp", wblk_d[i, :, 0:n], cb_[:, 0:n], [kb], [f"wblk{i}"], f"wblkst{i % 2}")

    fw.barrier()

    cur[0] = GLOBAL_END
    hTt = [alloc([8, 512], BF16) for _ in range(2)]
    wg = alloc([5, 8, 128], BF16)
    bd = alloc([4, 128])
    w4g = alloc([4, 128], BF16)
    sm = alloc([64])
    ptile = [alloc([512]) for _ in range(2)]
    tq = [alloc([512]) for _ in range(6)]
    ucx = alloc([256])
    B1 = alloc([L])
    mark = cur[0]
    B2 = alloc([L])
    B3 = alloc([L])
    B4 = alloc([L])
    B5 = alloc([L])
    B6 = alloc([L])
    endB = cur[0]
    cur[0] = mark
    kbf = alloc([L], BF16)
    ctile = tq[0:2]
    gtile = tq[2:4]
    dect = [alloc([512]) for _ in range(2)]
    zA4 = alloc([16, 128], BF16)
    kA4 = [alloc([16, 128], BF16) for _ in range(2)]
    AZ = alloc([8448], BF16)
    Kf = alloc([NF1, 2, 64])
    Y3 = alloc([3 * NF1, 64], BF16)
    xk = [alloc([4, 2, 64]) for _ in range(2)]
    cur[0] = max(cur[0], endB)
    A_sb = AZ.rearrange("p (c pl f) -> p c pl f", pl=4, f=NF1)
    Z_sb = AZ[:, 0:8192].rearrange("p (c b) -> p c b", b=128)
    BDK = [f"bd{m4}{hh}" for m4 in range(4) for hh in range(2)]
    fw.op("dve", lambda e: e.memset(bd.rearrange("p a b -> p (a b)"), 0.0), writes=BDK)

    w_in_v = w_in_d.rearrange("(kc p) n -> p kc n", p=128)
    hTloads = [0]

    def load_hT(j):
        i = hTloads[0] % 2
        hTloads[0] += 1
        dma("sp", hTt[i], hT_d[:, :, j * 512:(j + 1) * 512], ["hT_d"], [f"hTt{i}"], f"hTt{i}")
        return hTt[i], f"hTt{i}"

    def conv_ops(out, okey, P, pkey, taps, bias, rl):
        o3 = out.rearrange("p (r c) -> p r c", c=rl)
        p3 = P.rearrange("p (r c) -> p r c", c=rl)
        wc = [w for (o, w) in taps if o == 0][0]
        ts("dve", out, P, wc, bias, ALU.mult, ALU.add, [pkey, "pv"], [okey])
        for (o, w) in taps:
            if o == 0:
                continue
            if o < 0:
                stt(o3[:, :, -o:], p3[:, :, :rl + o], w, o3[:, :, -o:], ALU.mult, ALU.add, [pkey, okey, "pv"], [okey])
            else:
                stt(o3[:, :, :rl - o], p3[:, :, o:], w, o3[:, :, :rl - o], ALU.mult, ALU.add, [pkey, okey, "pv"], [okey])

    def rglru(g, u, ukey, T, h0, h0k, hf, hfk, hb, hbk, t1, t1k, t3, t3k):
        nt = (T + 511) // 512
        w = min(T, 512)
        for d in range(2):
            hout, hk = (hf, hfk) if d == 0 else (hb, hbk)
            for m, dst, dk in ((0, t1, t1k), (1, t3, t3k)):
                bias = pvc("rg_ba" if m == 0 else "rg_bx", d * 8 + g)
                for j in range(nt):
                    pi, pk = nextps()
                    sl = slice(j * w, (j + 1) * w)
                    mm(lambda e, pi=pi, m=m, d=d, sl=sl: e.matmul(psum[:, pi, 0:w], lhsT=bd[:, 2 * m + d, :], rhs=u[:, sl],
                                                                  start=True, stop=True),
                       [f"bd{2 * m + d}0", f"bd{2 * m + d}1", ukey], [pk])
                    act(dst[:, sl], psum[:, pi, 0:w], AF.Sigmoid, [pk, "pv"], [dk], bias=bias)
            tt("pool", t3[:, 0:T], t3[:, 0:T], u[:, 0:T], ALU.mult, [t3k, ukey], [t3k])
            act(t1[:, 0:T], t1[:, 0:T], AF.Exp, [t1k, "cdec"], [t1k], scale=cdec[:, d, g:g + 1])
            tt("pool", hout[:, 0:T], t1[:, 0:T], t1[:, 0:T], ALU.mult, [t1k], [hk])
            act(hout[:, 0:T], hout[:, 0:T], AF.Sqrt, [hk], [hk], scale=-1.0, bias=1.0)
            tt("dve", t3[:, 0:T], t3[:, 0:T], hout[:, 0:T], ALU.mult, [t3k, hk], [t3k])
            init = 0.0 if h0 is None else h0[d]
            rk = [t1k, t3k] + ([] if h0 is None else [h0k[d]])
            if d == 0:
                fw.op("dve", lambda e, hout=hout, init=init: e.tensor_tensor_scan(
                    out=hout[:, 0:T], data0=t1[:, 0:T], data1=t3[:, 0:T], initial=init, op0=ALU.mult, op1=ALU.add),
                    reads=rk, writes=[hk])
            else:
                fw.op("dve", lambda e, hout=hout, init=init: e.tensor_tensor_scan(
                    out=hout[:, 0:T][:, ::-1], data0=t1[:, 0:T][:, ::-1], data1=t3[:, 0:T][:, ::-1], initial=init,
                    op0=ALU.mult, op1=ALU.add), reads=rk, writes=[hk])

    AKEYS = [f"A_{ch}" for ch in range(64)]

    def fwd_transform(src4, skeys, consume):
        for c in range(16):
            for pair in range(2):
                pi, pk = nextps()
                mm(lambda e, c=c, pair=pair, pi=pi: e.matmul(psum[:, pi, 0:264], lhsT=src4[:, c, :], rhs=D1m[:, pair, :],
                                                            start=True, stop=True), skeys + ["D1m"], [pk])
                dstv = AZ[:, c * 528 + pair * 264: c * 528 + (pair + 1) * 264]
                wk_ = [f"A_{4 * c + 2 * pair}", f"A_{4 * c + 2 * pair + 1}"]
                if (2 * c + pair) % 2 == 0:
                    act(dstv, psum[:, pi, 0:264], AF.Copy, [pk], wk_)
                else:
                    fw.op("dve", lambda e, dstv=dstv, pi=pi: e.tensor_copy(out=dstv, in_=psum[:, pi, 0:264]), reads=[pk], writes=wk_)
        for bi in range(9):
            f1lo = bi * 4
            nf = min(4, NF1 - f1lo)
            pi, pk = nextps()

            def f(e, f1lo=f1lo, nf=nf, pi=pi):
                ins = None
                for k in range(nf):
                    f1 = f1lo + k
                    e.matmul(psum[:, pi, k * 128:(k + 1) * 128], lhsT=Mtab[:, 0, f1, :],
                             rhs=A_sb[:, :, 0:2, f1].rearrange("p c pl -> p pl c"), start=True, stop=False)
                    ins = e.matmul(psum[:, pi, k * 128:(k + 1) * 128], lhsT=Mtab[:, 1, f1, :],
                                   rhs=A_sb[:, :, 2:4, f1].rearrange("p c pl -> p pl c"), start=False, stop=True)
                return ins
            mm(f, AKEYS + ["Mtab"], [pk])
            consume(bi, pi, pk, f1lo, nf)

    tqc = [0]

    def tqn():
        i = tqc[0] % 6
        tqc[0] += 1
        return tq[i], f"tq{i}"

    def load_wg(g):
        for seg in range(5):
            dma("pool", wg[:, seg, :, :], w_in_v[:, :, seg * D + g * 128: seg * D + (g + 1) * 128], [], [f"wg{seg}"], f"wg{seg}")

    def load_bd(g):
        for m4 in range(4):
            for hh in range(2):
                dma("sp", bd[64 * hh:64 * hh + 64, m4, 64 * hh:64 * hh + 64], rg_w_d[m4, 2 * g + hh], [], [f"bd{m4}{hh}"], f"bdl{m4}{hh}")

    def load_w4(g):
        for cmb in range(4):
            o, dr = cmb // 2, cmb % 2
            c0 = o * 2048 + dr * 1024 + g * 128
            dma("pool", w4g[0:64, cmb, :], hy_w4_d[:, c0:c0 + 128], [], [f"w4g{cmb}"], f"w4g{cmb}")

    load_wg(0)
    load_bd(0)
    load_w4(0)
    for g in range(NG):
        pi, pk = nextps()

        def f(e, pi=pi):
            ins = None
            for kc in range(8):
                ins = e.matmul(psum[:, pi, 0:256], lhsT=wg[:, 0, kc, :], rhs=hcT[:, kc, :], start=(kc == 0), stop=(kc == 7))
            return ins
        mm(f, ["wg0"] + [f"hcT{k}" for k in range(8)], [pk])
        pt, ptk = tqn()
        act(pt[:, 0:256], psum[:, pi, 0:256], AF.Identity, [pk, "pv"], [ptk], bias=pvc("b_in", 0 * 8 + g))
        rtaps = [(k - 2, pvc("rnn_conv_w", k * 8 + g)) for k in range(4)]
        conv_ops(ucx, "uc", pt[:, 0:256], ptk, rtaps, pvc("rnn_conv_b", g), 256)
        rglru(g, ucx, "uc", 256, None, None, B3, "B3", B4, "B4", B5, "B5", B6, "B6")
        fw.op("dve", lambda e: e.tensor_copy(out=sm[:, 0:1], in_=B3[:, 255:256]), reads=["B3"], writes=["h0f"])
        fw.op("dve", lambda e: e.tensor_copy(out=sm[:, 1:2], in_=B4[:, 0:1]), reads=["B4"], writes=["h0b"])

        for j in range(NT):
            ht, hk = load_hT(j)
            sl = slice(j * 512, (j + 1) * 512)
            for seg in range(5):
                pi, pk = nextps()

                def f(e, pi=pi, seg=seg, ht=ht):
                    ins = None
                    for kc in range(8):
                        ins = e.matmul(psum[:, pi, :], lhsT=wg[:, seg, kc, :], rhs=ht[:, kc, :], start=(kc == 0), stop=(kc == 7))
                    return ins
                mm(f, [f"wg{seg}", hk], [pk])
                pt, ptk = tqn()
                act(pt, psum[:, pi, :], AF.Identity, [pk, "pv"], [ptk], bias=pvc("b_in", seg * 8 + g))
                if seg == 0:
                    conv_ops(B2[:, sl], "B2", pt, ptk, rtaps, pvc("rnn_conv_b", g), 64)
                elif seg == 1:
                    gq, gqk = tqn()
                    act(gq, pt, AF.Square, [ptk], [gqk])
                    ts("pool", gq, gq, 0.044715, 1.0, ALU.mult, ALU.add, [gqk], [gqk])
                    tt("pool", gq, gq, pt, ALU.mult, [gqk, ptk], [gqk])
                    act(gq, gq, AF.Sigmoid, [gqk], [gqk], scale=1.5957691216057308)
                    tt("pool", B6[:, sl], gq, pt, ALU.mult, [gqk, ptk], ["B6"])
                else:
                    si = seg - 2
                    htaps = [(k - 1, pvc("hy_conv_w", k * 24 + si * 8 + g)) for k in range(3)]
                    if si == 0:
                        conv_ops(B1[:, sl], "B1", pt, ptk, htaps, pvc("hy_conv_b", si * 8 + g), 64)
                    else:
                        ct, ctk = tqn()
                        conv_ops(ct, ctk, pt, ptk, htaps, pvc("hy_conv_b", si * 8 + g), 64)
                        dma("sp", gate_d[si - 1, :, sl], ct, [ctk], [f"gate_d{si - 1}"], f"gate_st{si - 1}")
        if g + 1 < NG:
            load_wg(g + 1)
        rglru(g, B2, "B2", L, [sm[:, 0:1], sm[:, 1:2]], ["h0f", "h0b"], B3, "B3", B2, "B2", B5, "B5", B4, "B4")
        if g + 1 < NG:
            load_bd(g + 1)
        tt("dve", B3, B3, B2, ALU.add, ["B3", "B2"], ["B3"])
        tt("dve", B3, B3, B6, ALU.mult, ["B3", "B6"], ["B3"])
        dma("pool", yr_d[:, g, :], B3, ["B3"], ["yr_d"], "yr_st")
        fw.barrier()

        KBK = [f"kbf_{j}" for j in range(NT)]
        for o in range(2):
            dma("pool", zb_d, B1, ["B1"], ["zb_d"], "zb_st")
            for dr in range(2):
                cmb = o * 2 + dr
                for j in range(NT):
                    sl = slice(j * 512, (j + 1) * 512)
                    dt_ = dect[j % 2]
                    dk_ = f"dec{j % 2}"
                    ts("dve", sm[:, 8 + j % 2:9 + j % 2], negdl[:, g:g + 1], float(j * 512) / float(L - 1), None, ALU.mult, None,
                       ["negdl"], [f"decb{j % 2}"])
                    act(dt_, tbase, AF.Exp, ["tbase", "negdl", f"decb{j % 2}"], [dk_], scale=negdl[:, g:g + 1],
                        bias=sm[:, 8 + j % 2:9 + j % 2])
                    pi, pk = nextps()
                    mm(lambda e, pi=pi, cmb=cmb, sl=sl: e.matmul(psum[:, pi, :], lhsT=w4g[0:64, cmb, :], rhs=hdn3[0:64, sl],
                                                                 start=True, stop=True), [f"w4g{cmb}", "hdn3"], [pk])
                    tt("dve", kbf[:, sl], psum[:, pi, :], dt_, ALU.mult, [pk, dk_], [f"kbf_{j}"])
                    if dr == 1 and j == 0:
                        fw.op("dve", lambda e: e.memset(kbf[:, 0:1], 0.0), reads=["kbf_0"], writes=["kbf_0"])
                    act(ptile[j % 2], kbf[:, sl], AF.Abs, [f"kbf_{j}"], [f"pt{j % 2}", f"nrm{dr}_{j}"],
                        accum=sm[:, 16 + dr * 8 + j:17 + dr * 8 + j])
                dma("sp", kb_d[dr], kbf, KBK, [f"kb_d{dr}"], f"kb_st{dr}")
            fw.op("dve", lambda e: e.reduce_sum(out=sm[:, 32:33], in_=sm[:, 16:32], axis=mybir.AxisListType.X),
                  reads=[f"nrm{dr}_{j}" for dr in range(2) for j in range(NT)], writes=["nrmsum"])
            fw.op("dve", lambda e: e.reciprocal(out=sm[:, 33:34], in_=sm[:, 32:33]), reads=["nrmsum"], writes=["invn"])
            ts("dve", B1, B1, pvc("hy_skip", o * 8 + g), None, ALU.mult, None, ["B1", "pv"], ["B1"])
            for h in range(2):
                for dr in range(2):
                    KA = [f"kA4{dr}_{q}" for q in range(4)]
                    for q in range(4):
                        c0 = 64 * h + q
                        dma("sp", kA4[dr][32 * q:32 * q + 32, :, :],
                            kb_d[dr, c0:64 * (h + 1):4, :].rearrange("c (a b) -> a c b", b=128),
                            [f"kb_d{dr}"], [KA[q]], f"kA4{dr}")

                    def consume_k(bi, pi, pk, f1lo, nf, dr=dr):
                        src = psum[:, pi, 0:nf * 128].rearrange("p (f r c) -> p f r c", r=2, c=64)
                        if dr == 0:
                            act(Kf[:, f1lo:f1lo + nf, :, :], src, AF.Copy, [pk], [f"Kf{bi}"])
                        else:
                            tt("dve", Kf[:, f1lo:f1lo + nf, 0, :], Kf[:, f1lo:f1lo + nf, 0, :], src[:, :, 0, :], ALU.add,
                               [pk, f"Kf{bi}"], [f"Kf{bi}"])
                            tt("dve", Kf[:, f1lo:f1lo + nf, 1, :], Kf[:, f1lo:f1lo + nf, 1, :], src[:, :, 1, :], ALU.subtract,
                               [pk, f"Kf{bi}"], [f"Kf{bi}"])
                    fwd_transform(kA4[dr], KA, consume_k)
                ZA = [f"zA4_{q}" for q in range(4)]
                for q in range(4):
                    c0 = 64 * h + q
                    dma("sp", zA4[32 * q:32 * q + 32, :, :], zb_d[c0:64 * (h + 1):4, :].rearrange("c (a b) -> a c b", b=128),
                        ["zb_d"], [ZA[q]], "zA4")

                def consume_x(bi, pi, pk, f1lo, nf):
                    X = psum[:, pi, 0:nf * 128].rearrange("p (f r c) -> p f r c", r=2, c=64)
                    K = Kf[:, f1lo:f1lo + nf, :, :]
                    Ksw = Kf[:, f1lo:f1lo + nf, ::-1, :]
                    t1_ = xk[0][:, 0:nf]
                    t2_ = xk[1][:, 0:nf]
                    tt("dve", t1_, X, K, ALU.mult, [pk, f"Kf{bi}"], ["xk0"])
                    tt("dve", t2_, X, Ksw, ALU.mult, [pk, f"Kf{bi}"], ["xk1"])
                    yv = [Y3[:, pl * NF1 + f1lo: pl * NF1 + f1lo + nf, :] for pl in range(3)]
                    tt("dve", yv[1], t1_[:, :, 0, :], t1_[:, :, 1, :], ALU.subtract, ["xk0"], [f"Y3r_{bi}"])
                    tt("dve", yv[2], t2_[:, :, 0, :], t2_[:, :, 1, :], ALU.add, ["xk1"], [f"Y3i_{bi}"])
                    act(yv[0], yv[2], AF.Copy, [f"Y3i_{bi}"], [f"Y3n_{bi}"], scale=-1.0)
                fwd_transform(zA4, ZA, consume_x)
                ykeys = [f"Y3{x_}_{bi}" for bi in range(9) for x_ in "rin"]
                for c4 in range(16):
                    pi, pk = nextps()

                    def f(e, c4=c4, pi=pi):
                        ins = None
                        for cc in range(4):
                            ch = c4 * 4 + cc
                            e.matmul(psum[0:66, pi, cc * 128:(cc + 1) * 128],
                                     lhsT=Y3[:, NF1:3 * NF1, ch], rhs=CS[:, 0:128], start=True, stop=False)
                            ins = e.matmul(psum[0:66, pi, cc * 128:(cc + 1) * 128],
                                           lhsT=Y3[:, 0:2 * NF1, ch], rhs=CS[:, 128:256],
                                           start=False, stop=True)
                        return ins
                    mm(f, ykeys + ["CS"], [pk])
                    act(Z_sb[0:66, c4 * 4:(c4 + 1) * 4, :],
                        psum[0:66, pi, :].rearrange("p (c b) -> p c b", b=128), AF.Copy, [pk], [f"Z_{c4}"] + AKEYS)
                zkeys = [f"Z_{c4}" for c4 in range(16)]
                for bj in range(8):
                    pi, pk = nextps()

                    def f(e, bj=bj, pi=pi):
                        ins = None
                        for bb in range(16):
                            b_ = bj * 16 + bb
                            ins = e.matmul(psum[0:64, pi, bb * 32:(bb + 1) * 32], lhsT=Z_sb[0:66, :, b_], rhs=Gtab[0:66, b_, :],
                                           start=True, stop=True)
                        return ins
                    mm(f, zkeys + AKEYS + ["Gtab"], [pk])
                    ct = ctile[bj % 2]
                    ctk = f"ct{bj % 2}"
                    fw.op("dve", lambda e, ct=ct, pi=pi, h=h: e.tensor_copy(out=ct[64 * h:64 * h + 64, :], in_=psum[0:64, pi, :]),
                          reads=[pk], writes=[ctk])
                    zv = B1[64 * h:64 * h + 64, :].rearrange("p (a b) -> p a b", b=128)[:, :, bj * 16:(bj + 1) * 16]
                    stt(zv, ct[64 * h:64 * h + 64, :].rearrange("p (b a) -> p a b", a=32), sm[64 * h:64 * h + 64, 33:34], zv,
                        ALU.mult, ALU.add, [ctk, "invn", "B1"], ["B1"])
            for j in range(NT):
                sl = slice(j * 512, (j + 1) * 512)
                gt = gtile[j % 2]
                dma("sp", gt, gate_d[o, :, sl], [f"gate_d{o}"], [f"gt{j % 2}"], f"gt{j % 2}")
                tt("dve", B1[:, sl], B1[:, sl], gt, ALU.mult, ["B1", f"gt{j % 2}"], ["B1"])
        dma("pool", yh_d[:, g, :], B1, ["B1"], ["yh_d"], "yh_st")
        if g + 1 < NG:
            load_w4(g + 1)
        fw.barrier()

    cur[0] = GLOBAL_END
    ring = [alloc([4096], BF16) for _ in range(3)]
    yrt = alloc([8, 512], BF16)
    yht = alloc([8, 512], BF16)
    hTc = alloc([8, 512], BF16)
    xtok = alloc([4, D])
    xT = alloc([8, 512])
    merged = alloc([8, 512], BF16)
    gts = [alloc([512]) for _ in range(4)]
    sq = merged
    rstd = alloc([512])
    h2 = alloc([8, 512], BF16)
    hid = alloc([22, 512], BF16)
    wstate = {"issued": 0, "views": {}}
    TOTAL_W = NT * NWB

    def issue_w(upto):
        while wstate["issued"] < min(upto, TOTAL_W):
            sidx = wstate["issued"]
            i = sidx % NWB
            s_ = sidx % 3
            nm, kc, cols, parts = WBLK[i]
            n = kc * cols
            dma("sp", ring[s_][:, 0:n], wblk_d[i, :, 0:n], [f"wblk{i}"], [f"ring{s_}"], f"ring{s_}")
            wstate["views"][sidx] = (ring[s_][:, 0:n].rearrange("p (a b) -> p a b", b=cols), f"ring{s_}")
            wstate["issued"] += 1

    def get_w(j, i):
        sidx = j * NWB + i
        issue_w(sidx + 3)
        return wstate["views"][sidx]

    for j in range(NT):
        sl = slice(j * 512, (j + 1) * 512)
        dma("sp", yrt, yr_d[:, :, sl], ["yr_d"], ["yrt"], "yrt")
        dma("sp", yht, yh_d[:, :, sl], ["yh_d"], ["yht"], "yht")
        dma("sp", hTc, hT_d[:, :, sl], ["hT_d"], ["hTc"], "hTc")
        dma("sp", xtok, x_d[j * 512:(j + 1) * 512, :].rearrange("(a p) n -> p a n", p=128), [], ["xtok"], "xtok")
        for a4 in range(4):
            for half in range(2):
                pi, pk = nextps()

                def f(e, a4=a4, half=half, pi=pi):
                    ins = None
                    for gg in range(4):
                        g = half * 4 + gg
                        ins = e.transpose(psum[:, pi, gg * 128:(gg + 1) * 128], xtok[:, a4, g * 128:(g + 1) * 128], ident)
                    return ins
                mm(f, ["xtok", "ident"], [pk])
                act(xT[:, half * 4:(half + 1) * 4, a4 * 128:(a4 + 1) * 128],
                    psum[:, pi, :].rearrange("p (g t) -> p g t", t=128), AF.Copy, [pk], ["xT"])
        for m in range(8):
            wv, wk = get_w(j, m)
            pis = []
            for li, (mov, mk_) in enumerate(((yrt, "yrt"), (yht, "yht"), (hTc, "hTc"), (hTc, "hTc"))):
                pi, pk = nextps()

                def f(e, wv=wv, mov=mov, li=li, pi=pi):
                    ins = None
                    for kc in range(8):
                        ins = e.matmul(psum[:, pi, :], lhsT=wv[:, kc, li * 128:(li + 1) * 128], rhs=mov[:, kc, :],
                                       start=(kc == 0), stop=(kc == 7))
                    return ins
                mm(f, [wk, mk_], [pk])
                pis.append((pi, pk))
            act(gts[0], psum[:, pis[2][0], :], AF.Sigmoid, [pis[2][1], "pv"], ["gt0"], bias=pvc("b_in", 40 + m))
            act(gts[1], psum[:, pis[3][0], :], AF.Sigmoid, [pis[3][1], "pv"], ["gt1"], bias=pvc("b_in", 48 + m))
            tt("dve", gts[2], psum[:, pis[0][0], :], gts[0], ALU.mult, [pis[0][1], "gt0"], ["gt2"])
            tt("dve", gts[3], psum[:, pis[1][0], :], gts[1], ALU.mult, [pis[1][1], "gt1"], ["gt3"])
            tt("pool", merged[:, m, :], gts[2], gts[3], ALU.add, ["gt2", "gt3"], ["merged"])
        for m in range(8):
            wv, wk = get_w(j, 8 + m // 4)
            col = m % 4
            pi, pk = nextps()

            def f(e, wv=wv, col=col, pi=pi):
                ins = None
                for kc in range(8):
                    ins = e.matmul(psum[:, pi, :], lhsT=wv[:, kc, col * 128:(col + 1) * 128], rhs=merged[:, kc, :],
                                   start=(kc == 0), stop=(kc == 7))
                return ins
            mm(f, [wk, "merged"], [pk])
            stt(xT[:, m, :], psum[:, pi, :], modsb[:, 16 + m, 0:1], xT[:, m, :], ALU.mult, ALU.add, [pk, "modsb", "xT"], ["xT"])

        def rms_rstd():
            act(sq, xT, AF.Square, ["xT"], ["sq"])
            pi, pk = nextps()

            def f(e, pi=pi):
                ins = None
                for kc in range(8):
                    ins = e.matmul(psum[:, pi, :], lhsT=ones_bf, rhs=sq[:, kc, :], start=(kc == 0), stop=(kc == 7))
                return ins
            mm(f, ["ones", "sq"], [pk])
            act(rstd, psum[:, pi, :], AF.Sqrt, [pk], ["rstd_"], scale=1.0 / D, bias=EPS)
            fw.op("dve", lambda e: e.reciprocal(out=rstd, in_=rstd), reads=["rstd_"], writes=["rstd_"])
        rms_rstd()
        for m in range(8):
            tt("dve", gts[m % 2], xT[:, m, :], rstd, ALU.mult, ["xT", "rstd_"], [f"gt{m % 2}"])
            act(h2[:, m, :], gts[m % 2], AF.Identity, [f"gt{m % 2}", "A2", "modsb"], ["h2"], scale=A2[:, m:m + 1],
                bias=modsb[:, 24 + m, 0:1])
        for m in range(22):
            wv, wk = get_w(j, 10 + m // 2)
            pis = []
            for part in range(2):
                col = part * 2 + (m % 2)
                pi, pk = nextps()

                def f(e, wv=wv, col=col, pi=pi):
                    ins = None
                    for kc in range(8):
                        ins = e.matmul(psum[:, pi, :], lhsT=wv[:, kc, col * 128:(col + 1) * 128], rhs=h2[:, kc, :],
                                       start=(kc == 0), stop=(kc == 7))
                    return ins
                mm(f, [wk, "h2"], [pk])
                pis.append((pi, pk))
            act(gts[m % 2], psum[:, pis[0][0], :], AF.Silu, [pis[0][1]], [f"gt{m % 2}"])
            tt("dve", hid[:, m, :], psum[:, pis[1][0], :], gts[m % 2], ALU.mult, [pis[1][1], f"gt{m % 2}"], ["hid"])
        for m in range(8):
            wv, wk = get_w(j, 21 + m)
            pi, pk = nextps()

            def f(e, wv=wv, pi=pi):
                ins = None
                for kc in range(22):
                    ins = e.matmul(psum[:, pi, :], lhsT=wv[:, kc, :], rhs=hid[:, kc, :], start=(kc == 0), stop=(kc == 21))
                return ins
            mm(f, [wk, "hid"], [pk])
            stt(xT[:, m, :], psum[:, pi, :], modsb[:, 40 + m, 0:1], xT[:, m, :], ALU.mult, ALU.add, [pk, "modsb", "xT"], ["xT"])
        rms_rstd()
        for m in range(8):
            stt(xT[:, m, :], xT[:, m, :], pvc("final_g", m), rstd, ALU.mult, ALU.mult, ["xT", "rstd_", "pv"], ["xT"])
        for a4 in range(4):
            for half in range(2):
                pi, pk = nextps()

                def f(e, a4=a4, half=half, pi=pi):
                    ins = None
                    for gg in range(4):
                        g = half * 4 + gg
                        ins = e.transpose(psum[:, pi, gg * 128:(gg + 1) * 128], xT[:, g, a4 * 128:(a4 + 1) * 128], ident)
                    return ins
                mm(f, ["xT", "ident"], [pk])
                act(xtok[:, a4, half * 512:(half + 1) * 512], psum[:, pi, :], AF.Copy, [pk], ["xtok"])
        dma("sp", out_d[j * 512:(j + 1) * 512, :].rearrange("(a p) n -> p a n", p=128), xtok, ["xtok"], ["out_d"], "out_st")

    fw.op("sp", None, extra=fw.all_tokens())
    assert len(fw.dsem) + 5 <= 100, len(fw.dsem)
    fw.emit()
    return nc


_CACHE = {}


def prep_inputs(inp, tb, b):
    PV, NPV = pv_layout()
    pv = np.zeros((128, NPV), np.float32)

    def put(name, arr):
        a = fm(arr)
        pv[:, PV[name]:PV[name] + a.shape[1]] = a
    put("norm1_g", inp["norm1_g"][0])
    put("norm2_g", inp["norm2_g"][0])
    put("final_g", inp["final_g"])
    put("rnn_conv_b", inp["rnn_conv_b"][0])
    put("deltas", tb["deltas"])
    put("b_mod", inp["b_mod"][0])
    put("b_in", inp["b_in"][0])
    put("rnn_conv_w", inp["rnn_conv_w"][0].reshape(-1))
    put("rg_ba", inp["rg_ba"][0].reshape(-1))
    put("rg_bx", inp["rg_bx"][0].reshape(-1))
    put("rg_lambda", inp["rg_lambda"][0].reshape(-1))
    put("hy_conv_w", inp["hy_conv_w"][0].reshape(-1))
    put("hy_conv_b", inp["hy_conv_b"][0])
    put("hy_skip", inp["hy_skip"][0].reshape(-1))
    hs = PV["hy_small"]
    pv[:64, hs + 0] = inp["hy_b1"][0]
    pv[:64, hs + 1] = inp["hy_b2"][0]
    pv[:64, hs + 2] = inp["hy_b3"][0]
    pv[:64, hs + 3] = inp["hy_freq"][0]
    c2 = np.stack([fm(inp["c"][b]), fm(inp["c_ctx"])], axis=-1).reshape(128, 16)
    m = {
        "x": np.ascontiguousarray(inp["x"][b]),
        "ctx": np.ascontiguousarray(inp["ctx"][b]),
        "c2": np.ascontiguousarray(c2, dtype=np.float32),
        "pv": pv,
        "w_mod": inp["w_mod"][0], "w_in": inp["w_in"][0],
        "rg_w": np.ascontiguousarray(np.concatenate([inp["rg_wa"][0], inp["rg_wx"][0]], axis=0)),
        "hy_w1": inp["hy_w1"][0], "hy_w2": inp["hy_w2"][0], "hy_w3": inp["hy_w3"][0], "hy_w4": inp["hy_w4"][0],
        "w_a_out": inp["w_a_out"][0], "w_b_out": inp["w_b_out"][0], "w_out": inp["w_out"][0],
        "w_ffn_in": inp["w_ffn_in"][0], "w_ffn_out": inp["w_ffn_out"][0],
        "t_D1m": tb["D1m"], "t_M": tb["M"], "t_CS": tb["CS"], "t_G": tb["G"], "t_zpos": tb["zposT"],
        "t_tbase": tb["tbase"], "t_ident": tb["ident"],
    }
    return {k: np.ascontiguousarray(np.asarray(v, dtype=np.float32)) for k, v in m.items()}


def kernel(**inputs):
    inp = {k: np.asarray(v) for k, v in inputs.items()}
    tb = make_tables()
    if "nc" not in _CACHE:
        _CACHE["nc"] = build_program(False)
    nc = _CACHE["nc"]
    in_maps = [prep_inputs(inp, tb, b) for b in range(8)]
    res = run_bass_kernel_spmd(nc, in_maps, core_ids=list(range(8)))
    out = np.stack([np.asarray(r["out"]) for r in res.results], axis=0)
    return out.astype(np.float32)
```

```python
import numpy as np
import concourse.bass as bass
import concourse.mybir as mybir
from concourse.bass_utils import run_bass_kernel_spmd

F32 = mybir.dt.float32
BF16 = mybir.dt.bfloat16
AF = mybir.ActivationFunctionType
ALU = mybir.AluOpType

D = 1024
L = 4096
NFFT = 8192
NF1 = 33
DFF = 2816
EPS = 1e-6
NG = 8
NT = 8


class Tok:
    __slots__ = ("sem", "val")

    def __init__(self, sem, val):
        self.sem = sem
        self.val = val


class Fw:
    ENG = ("pe", "act", "dve", "pool", "sp")

    def __init__(self, nc):
        self.nc = nc
        self.ops = {e: [] for e in self.ENG}
        self.cnt = {e: 0 for e in self.ENG}
        self.esem = {e: nc.alloc_semaphore(name=f"cs_{e}") for e in self.ENG}
        self.waited = {e: {} for e in self.ENG}
        self.bufs = {}
        self.dsem = {}

    def op(self, eng, fn, reads=(), writes=(), dma=None, extra=()):
        deps = list(extra)
        for k in reads:
            b = self.bufs.get(k)
            if b is not None and b[0] is not None:
                deps.append(b[0])
        for k in writes:
            b = self.bufs.get(k)
            if b is not None:
                if b[1]:
                    deps.extend(b[1])
                elif b[0] is not None:
                    deps.append(b[0])
        best = {}
        for t in deps:
            k = id(t.sem)
            if k not in best or best[k].val < t.val:
                best[k] = t
        waits = []
        w = self.waited[eng]
        for k, t in best.items():
            if t.val > w.get(k, 0):
                w[k] = t.val
                waits.append((t.sem, t.val))
        if dma is None:
            self.cnt[eng] += 1
            tok = Tok(self.esem[eng], self.cnt[eng])
            inc = 1
        else:
            s = self.dsem.get(dma)
            if s is None:
                s = [self.nc.alloc_semaphore(name=f"ds_{len(self.dsem)}"), 0]
                self.dsem[dma] = s
            s[1] += 16
            tok = Tok(s[0], s[1])
            inc = 16
        self.ops[eng].append((waits, fn, tok.sem, inc))
        for k in reads:
            b = self.bufs.get(k)
            if b is None:
                b = [None, []]
                self.bufs[k] = b
            b[1].append(tok)
        for k in writes:
            self.bufs[k] = [tok, []]
        return tok

    def all_tokens(self):
        toks = []
        for e in self.ENG:
            if self.cnt[e] > 0:
                toks.append(Tok(self.esem[e], self.cnt[e]))
        for k, s in self.dsem.items():
            if s[1] > 0:
                toks.append(Tok(s[0], s[1]))
        return toks

    def barrier(self):
        toks = self.all_tokens()
        for e in self.ENG:
            self.op(e, None, extra=toks)

    def emit(self):
        nc = self.nc
        with nc.Block() as block:
            def mk(ename):
                def body(e):
                    for waits, fn, sem, inc in self.ops[ename]:
                        for s, v in waits:
                            e.wait_ge(s, v)
                        ins = e.nop() if fn is None else fn(e)
                        ins.then_inc(sem, inc)
                return body
            block.tensor(mk("pe"))
            block.scalar(mk("act"))
            block.vector(mk("dve"))
            block.gpsimd(mk("pool"))
            block.sync(mk("sp"))


def make_tables():
    tb = {}
    a = np.arange(32, dtype=np.float64)[:, None]
    f1 = np.arange(NF1, dtype=np.float64)[None, :]
    ang = 2 * np.pi * a * f1 / 64.0
    C, S = np.cos(ang), np.sin(ang)
    blk = np.concatenate([C, -S, S, C], axis=1)
    D1m = np.zeros((128, 2, 264), np.float64)
    for pair in range(2):
        for qq in range(2):
            q = 2 * pair + qq
            D1m[32 * q:32 * q + 32, pair, qq * 132:(qq + 1) * 132] = blk
    tb["D1m"] = D1m.reshape(128, 528).astype(np.float32)
    a64 = np.arange(64, dtype=np.float64)[:, None]
    ang = 2 * np.pi * a64 * f1 / 64.0
    C6, S6 = np.cos(ang), np.sin(ang)
    blk6 = np.concatenate([C6, -S6, S6, C6], axis=1)
    D1f = np.zeros((128, 264), np.float64)
    for q2 in range(2):
        D1f[64 * q2:64 * q2 + 64, q2 * 132:(q2 + 1) * 132] = blk6
    tb["D1f"] = D1f.astype(np.float32)
    b = np.arange(128, dtype=np.float64)[:, None]
    f2 = np.arange(128, dtype=np.float64)[None, :]
    M = np.zeros((128, 2, NF1, 128))
    for k in range(NF1):
        ang = -2 * np.pi * b * (k + 64 * f2) / NFFT
        M[:, 0, k, :] = np.cos(ang)
        M[:, 1, k, :] = np.sin(ang)
    tb["M"] = M.reshape(128, 2 * NF1 * 128).astype(np.float32)
    ang = 2 * np.pi * f2.T * b.T / 128.0
    tb["CS"] = np.concatenate([np.cos(ang), np.sin(ang)], axis=1).astype(np.float32)
    c = np.full(NF1, 2.0)
    c[0] = 1.0
    c[NF1 - 1] = 1.0
    f1v = np.arange(NF1, dtype=np.float64)[:, None, None]
    bv = np.arange(128, dtype=np.float64)[None, :, None]
    av = np.arange(32, dtype=np.float64)[None, None, :]
    ang = 2 * np.pi * (av * f1v / 64.0 + bv * f1v / NFFT)
    Gr = c[:, None, None] / NFFT * np.cos(ang)
    Gi = c[:, None, None] / NFFT * np.sin(ang)
    G = np.zeros((128, 128 * 32), np.float64)
    G[:66] = np.concatenate([Gr, -Gi], axis=0).reshape(66, 128 * 32)
    tb["G"] = G.astype(np.float32)
    t = np.linspace(0.0, 1.0, L, dtype=np.float32)[:, None]
    w = (2.0 * np.pi * np.arange(L, dtype=np.float32)[:, None] / L).astype(np.float32)
    f = np.linspace(1e-4, 15, 16, dtype=np.float32)[None, :]
    fw_ = (f * w).astype(np.float32)
    z = np.concatenate([t, np.cos(fw_), -np.sin(fw_)], axis=-1).astype(np.float32)
    zp = np.zeros((128, L), np.float32)
    zp[:33] = z.T
    tb["zposT"] = zp
    max_decay = np.log(1e-2) / 0.3
    min_decay = np.log(1e-2) / 1.5
    tb["deltas"] = np.abs(np.linspace(min_decay, max_decay, D, dtype=np.float32)).astype(np.float32)
    tb["tbase"] = np.broadcast_to(t[:512, 0][None, :], (128, 512)).astype(np.float32).copy()
    tb["ident"] = np.eye(128, dtype=np.float32)
    return tb


def pv_layout():
    cols = {}
    n = [0]

    def add(name, k):
        cols[name] = n[0]
        n[0] += k
    for nm in ["norm1_g", "norm2_g", "final_g", "rnn_conv_b", "deltas"]:
        add(nm, 8)
    add("b_mod", 48)
    add("b_in", 56)
    add("rnn_conv_w", 32)
    add("rg_ba", 16)
    add("rg_bx", 16)
    add("rg_lambda", 16)
    add("hy_conv_w", 72)
    add("hy_conv_b", 24)
    add("hy_skip", 16)
    add("hy_small", 4)
    return cols, n[0]


def fm(v):
    v = np.asarray(v, np.float32).reshape(-1, 128)
    return np.ascontiguousarray(v.T)


def build_program(debug=False):
    nc = bass.Bass("TRN2", target_bir_lowering=False)
    fw = Fw(nc)
    PV, NPV = pv_layout()

    def din(name, shape, dt=F32):
        return nc.dram_tensor(name, list(shape), dt, kind="ExternalInput").ap()

    def dscr(name, shape, dt):
        return nc.dram_tensor(name, list(shape), dt, kind="Internal").ap()

    x_d = din("x", [L, D])
    ctx_d = din("ctx", [256, D])
    c2_d = din("c2", [128, 16])
    pv_d = din("pv", [128, NPV])
    w_mod_d = din("w_mod", [D, 6 * D])
    w_in_d = din("w_in", [D, 7 * D])
    rg_w_d = din("rg_w", [4, 16, 64, 64])
    hy_w1_d = din("hy_w1", [33, 64])
    hy_w2_d = din("hy_w2", [64, 64])
    hy_w3_d = din("hy_w3", [64, 64])
    hy_w4_d = din("hy_w4", [64, 4096])
    w_a_d = din("w_a_out", [D, D])
    w_b_d = din("w_b_out", [D, D])
    w_o_d = din("w_out", [D, D])
    w_fi_d = din("w_ffn_in", [D, 2 * DFF])
    w_fo_d = din("w_ffn_out", [DFF, D])
    t_D1m = din("t_D1m", [128, 528])
    t_D1f = din("t_D1f", [128, 264])
    t_M = din("t_M", [128, 2 * NF1 * 128])
    t_CS = din("t_CS", [128, 256])
    t_G = din("t_G", [128, 4096])
    t_zpos = din("t_zpos", [128, L])
    t_tbase = din("t_tbase", [128, 512])
    t_ident = din("t_ident", [128, 128])
    out_d = nc.dram_tensor("out", [L, D], F32, kind="ExternalOutput").ap()

    okind = "ExternalOutput" if debug else "Internal"
    hT_d = dscr("hT_s", [128, 8, L], BF16)
    yr_d = nc.dram_tensor("yr_s", [128, 8, L], BF16, kind=okind).ap()
    yh_d = nc.dram_tensor("yh_s", [128, 8, L], BF16, kind=okind).ap()
    gate_d = dscr("gate_s", [2, 128, L], F32)
    zb_d = dscr("zb_s", [128, L], BF16)
    kb_d = dscr("kb_s", [2, 128, 2 * L], BF16)
    WBLK = []
    for m in range(8):
        WBLK.append(("mix", 8, 512, [(w_a_d, m * 128, 128), (w_b_d, m * 128, 128),
                                      (w_in_d, 5 * D + m * 128, 128), (w_in_d, 6 * D + m * 128, 128)]))
    for blk in range(2):
        WBLK.append(("wo", 8, 512, [(w_o_d, blk * 512, 512)]))
    for blk in range(11):
        WBLK.append(("wfi", 8, 512, [(w_fi_d, blk * 256, 256), (w_fi_d, DFF + blk * 256, 256)]))
    for blk in range(8):
        WBLK.append(("wfo", 22, 128, [(w_fo_d, blk * 128, 128)]))
    NWB = len(WBLK)
    wblk_d = dscr("wblk_s", [NWB, 128, 4096], BF16)

    ARENA = 95000
    arena = nc.sbuf_tensor("arena", [128, ARENA], BF16).__enter__()
    psum = nc.psum_tensor("psum", [128, 8, 512], F32).__enter__()
    cur = [0]

    def alloc(shape, dt=F32):
        n = int(np.prod(shape))
        sz = n * (2 if dt == F32 else 1)
        a0 = cur[0]
        cur[0] += sz + (sz % 2)
        assert cur[0] <= ARENA, f"SBUF arena overflow {cur[0]}"
        v = arena[:, a0:a0 + sz]
        if dt == F32:
            v = v.bitcast(F32)
        if len(shape) == 2:
            v = v.rearrange("p (a b) -> p a b", b=shape[1])
        elif len(shape) == 3:
            v = v.rearrange("p (a b c) -> p a b c", b=shape[1], c=shape[2])
        elif len(shape) == 4:
            v = v.rearrange("p (a b c d) -> p a b c d", b=shape[1], c=shape[2], d=shape[3])
        return v

    psi = [0]

    def nextps():
        i = psi[0] % 8
        psi[0] += 1
        return i, f"ps{i}"

    def dma(eng, out, in_, reads, writes, key):
        return fw.op(eng, lambda e: e.dma_start(out=out, in_=in_), reads=reads, writes=writes, dma=key)

    def act(out, in_, func, reads, writes, bias=None, scale=None, accum=None):
        kw = {}
        if bias is not None:
            kw["bias"] = bias
        if scale is not None:
            kw["scale"] = scale
        if accum is not None:
            kw["accum_out"] = accum
        return fw.op("act", lambda e: e.activation(out=out, in_=in_, func=func, **kw), reads=reads, writes=writes)

    def tt(eng, out, in0, in1, op, reads, writes):
        return fw.op(eng, lambda e: e.tensor_tensor(out=out, in0=in0, in1=in1, op=op), reads=reads, writes=writes)

    def ts(eng, out, in0, s1, s2, op0, op1, reads, writes):
        if op1 is None:
            return fw.op(eng, lambda e: e.tensor_scalar(out=out, in0=in0, scalar1=s1, scalar2=None, op0=op0),
                         reads=reads, writes=writes)
        return fw.op(eng, lambda e: e.tensor_scalar(out=out, in0=in0, scalar1=s1, scalar2=s2, op0=op0, op1=op1),
                     reads=reads, writes=writes)

    def stt(out, in0, scalar, in1, op0, op1, reads, writes):
        return fw.op("dve", lambda e: e.scalar_tensor_tensor(out=out, in0=in0, scalar=scalar, in1=in1, op0=op0, op1=op1),
                     reads=reads, writes=writes)

    def mm(fn, reads, writes):
        return fw.op("pe", fn, reads=reads, writes=writes)

    pv = alloc([NPV])
    ident = alloc([128])
    ones_bf = alloc([128], BF16)
    modsb = alloc([48, 2])
    sc2 = alloc([8, 2])
    A1 = alloc([8, 2])
    A2 = alloc([8])
    cdec = alloc([2, 8])
    negdl = alloc([8])
    tmp_s = alloc([64])
    D1m = alloc([2, 264], BF16)
    D1f = alloc([264], BF16)
    zcol = alloc([2], BF16)
    Mtab = alloc([2, NF1, 128], BF16)
    CS = alloc([256], BF16)
    Gtab = alloc([128, 32], BF16)
    tbase = alloc([512])
    hdn3 = alloc([L], BF16)
    hcT = alloc([8, 256], BF16)
    GLOBAL_END = cur[0]

    def pvc(name, j=0, n=1):
        c0 = PV[name] + j
        return pv[:, c0:c0 + n]

    dma("sp", pv, pv_d, [], ["pv"], "pv")
    dma("sp", ident, t_ident, [], ["ident"], "ident")
    dma("sp", tbase, t_tbase, [], ["tbase"], "tbase")
    dma("pool", D1m.rearrange("p a b -> p (a b)"), t_D1m, [], ["D1m"], "D1m")
    dma("pool", Mtab.rearrange("p a b c -> p (a b c)"), t_M, [], ["Mtab"], "Mtab")
    dma("pool", D1f, t_D1f, [], ["D1f"], "D1f")
    fw.op("dve", lambda e: e.memset(zcol, 0.0), writes=["zcol"])
    for o_ in range(2):
        dma("sp", kb_d[o_, :, L:L + 2], zcol, ["zcol"], [f"kb_d{o_}_z"], f"kb_st{o_}")
    dma("pool", CS, t_CS, [], ["CS"], "CS")
    dma("pool", Gtab.rearrange("p a b -> p (a b)"), t_G, [], ["Gtab"], "Gtab")
    fw.op("dve", lambda e: e.memset(ones_bf, 1.0), writes=["ones"])

    p0_wm = [alloc([8, 256]) for _ in range(2)]
    p0_c2 = alloc([8, 2])
    p0_xt = [alloc([D]) for _ in range(2)]
    p0_xs = [alloc([D]) for _ in range(2)]
    p0_junk = alloc([D])
    p0_st = alloc([8])
    p0_hst = [alloc([8, 512], BF16) for _ in range(2)]
    p0_cf = [alloc([4096]) for _ in range(2)]
    p0_cb = [alloc([4096], BF16) for _ in range(2)]
    p0_zpos = [alloc([512]) for _ in range(2)]
    p0_hd = [alloc([512]) for _ in range(2)]
    p0_w = alloc([3, 64])
    p0_t1 = alloc([512])

    dma("sp", p0_c2.rearrange("p a b -> p (a b)"), c2_d, [], ["c2"], "c2")
    act(sc2, p0_c2, AF.Silu, ["c2"], ["sc2"])
    wmv = w_mod_d.rearrange("(kc p) n -> p kc n", p=128)
    pm, pmk = nextps()
    for piece in range(24):
        wt = p0_wm[piece % 2]
        dma("sp", wt, wmv[:, :, piece * 256:(piece + 1) * 256], [], [f"wm{piece % 2}"], f"wm{piece % 2}")

        def f(e, wt=wt, piece=piece):
            ins = None
            for jj in range(2):
                j = piece * 2 + jj
                for kc in range(8):
                    ins = e.matmul(psum[:, pm, 2 * j:2 * j + 2], lhsT=wt[:, kc, jj * 128:(jj + 1) * 128],
                                   rhs=sc2[:, kc, :], start=(kc == 0), stop=(kc == 7))
            return ins
        mm(f, [f"wm{piece % 2}", "sc2"], [pmk + f"_{piece}"] if piece else [pmk])
    bm = pv[:, PV["b_mod"]:PV["b_mod"] + 48]
    tt("dve", modsb, psum[:, pm, 0:96].rearrange("p (a b) -> p a b", b=2),
       bm.unsqueeze(2).broadcast_to([128, 48, 2]), ALU.add,
       ["pv", pmk] + [pmk + f"_{i}" for i in range(1, 24)], ["modsb"])
    ts("dve", A1, modsb[:, 8:16, :], 1.0, None, ALU.add, None, ["modsb"], ["A1"])
    tt("dve", A1, A1, pvc("norm1_g", 0, 8).unsqueeze(2).broadcast_to([128, 8, 2]), ALU.mult, ["A1", "pv"], ["A1"])
    ts("dve", A2, modsb[:, 32:40, 0], 1.0, None, ALU.add, None, ["modsb"], ["A2"])
    tt("dve", A2, A2, pvc("norm2_g", 0, 8), ALU.mult, ["A2", "pv"], ["A2"])
    lam = pv[:, PV["rg_lambda"]:PV["rg_lambda"] + 16]
    cdf = cdec.rearrange("p a b -> p (a b)")
    act(cdf, lam, AF.Exp, ["pv"], ["cdec"], scale=-1.0)
    act(cdf, cdf, AF.Ln, ["cdec"], ["cdec"], bias=1.0)
    ts("dve", cdf, cdf, -8.0, None, ALU.mult, None, ["cdec"], ["cdec"])
    ts("dve", negdl, pvc("deltas", 0, 8), -1.0, None, ALU.mult, None, ["pv"], ["negdl"])

    dma("sp", p0_w[0:33, 0, :], hy_w1_d, [], ["hw1"], "hw1")
    dma("sp", p0_w[0:64, 1, :], hy_w2_d, [], ["hw2"], "hw2")
    dma("sp", p0_w[0:64, 2, :], hy_w3_d, [], ["hw3"], "hw3")
    hs = PV["hy_small"]
    TWO_PI = 2.0 * np.pi
    fb = tmp_s[:, 0:3]
    tt("dve", fb[0:64], pv[0:64, hs:hs + 3], pv[0:64, hs + 3:hs + 4].broadcast_to([64, 3]), ALU.mult, ["pv"], ["fb"])
    fq = tmp_s[:, 4:5]
    ts("dve", fq[0:64], pv[0:64, hs + 3:hs + 4], 1.0 / TWO_PI, None, ALU.mult, None, ["pv"], ["fq"])
    ts("dve", fb[0:64], fb[0:64], 1.0 / TWO_PI, None, ALU.mult, None, ["fb"], ["fb"])
    for tt_ in range(NT):
        sl = slice(tt_ * 512, (tt_ + 1) * 512)
        zp = p0_zpos[tt_ % 2]
        zk = f"zpos{tt_ % 2}"
        dma("sp", zp, t_zpos[:, sl], [], [zk], zk)
        srcs = [zp, p0_hd[0], p0_hd[1]]
        srck = [zk, "hd0", "hd1"]
        dsts = [p0_hd[0], p0_hd[1], hdn3[:, sl]]
        dstk = ["hd0", "hd1", "hdn3"]
        for layer in range(3):
            kdim = 33 if layer == 0 else 64
            pi, pk = nextps()
            mm(lambda e, layer=layer, kdim=kdim, pi=pi, srcs=srcs: e.matmul(
                psum[0:64, pi, :], lhsT=p0_w[0:kdim, layer, :], rhs=srcs[layer][0:kdim, :], start=True, stop=True),
               [srck[layer], f"hw{layer + 1}"], [pk])
            ts("dve", p0_t1[0:64], psum[0:64, pi, :], fq[0:64], fb[0:64, layer:layer + 1], ALU.mult, ALU.add,
               [pk, "fq", "fb"], ["t1"])
            ts("dve", p0_junk[0:64, 0:512], p0_t1[0:64], 12582912.0, 12582912.0, ALU.add, ALU.subtract, ["t1"], ["junk"])
            tt("dve", p0_t1[0:64], p0_t1[0:64], p0_junk[0:64, 0:512], ALU.subtract, ["t1", "junk"], ["t1"])
            act(dsts[layer][0:64], p0_t1[0:64], AF.Sin, ["t1"], [dstk[layer]], scale=TWO_PI)

    def norm_tiles(src_d, ntile, which, store_fn):
        for ti in range(ntile):
            xt = p0_xt[ti % 2]
            xs = p0_xs[ti % 2]
            kx, ks = f"xt{ti % 2}", f"xs{ti % 2}"
            dma("sp", xt, src_d[ti * 128:(ti + 1) * 128, :], [], [kx], kx)
            act(p0_junk, xt, AF.Square, [kx], ["junk", "ss"], accum=p0_st[:, 0:1])
            act(p0_st[:, 1:2], p0_st[:, 0:1], AF.Sqrt, ["ss"], ["rs"], scale=1.0 / D, bias=EPS)
            fw.op("dve", lambda e: e.reciprocal(out=p0_st[:, 2:3], in_=p0_st[:, 1:2]), reads=["rs"], writes=["rstd"])
            ts("dve", xs, xt, p0_st[:, 2:3], None, ALU.mult, None, [kx, "rstd"], [ks])
            for half in range(2):
                pi, pk = nextps()

                def f(e, xs=xs, pi=pi, half=half):
                    ins = None
                    for gg in range(4):
                        g = half * 4 + gg
                        ins = e.transpose(psum[:, pi, gg * 128:(gg + 1) * 128], xs[:, g * 128:(g + 1) * 128], ident)
                    return ins
                mm(f, [ks, "ident"], [pk])
                for gg in range(4):
                    g = half * 4 + gg
                    store_fn(ti, g, psum[:, pi, gg * 128:(gg + 1) * 128], pk)

    def ctx_store(ti, g, src, pk):
        act(hcT[:, g, ti * 128:(ti + 1) * 128], src, AF.Identity, [pk, "A1", "modsb"], [f"hcT{g}"],
            scale=A1[:, g, 1:2], bias=modsb[:, g, 1:2])
    norm_tiles(ctx_d, 2, 1, ctx_store)

    def x_store(ti, g, src, pk):
        st = p0_hst[(ti // 4) % 2]
        kst = f"hst{(ti // 4) % 2}"
        act(st[:, g, (ti % 4) * 128:(ti % 4 + 1) * 128], src, AF.Identity, [pk, "A1", "modsb"], [kst + f"_{g}_{ti % 4}"],
            scale=A1[:, g, 0:1], bias=modsb[:, g, 0:1])
        if g == 7 and ti % 4 == 3:
            j = ti // 4
            dma("sp", hT_d[:, :, j * 512:(j + 1) * 512], st,
                [kst + f"_{gg}_{t4}" for gg in range(8) for t4 in range(4)], ["hT_d"], "hT_d")
    norm_tiles(x_d, 32, 0, x_store)

    for i, (nm, kc, cols, parts) in enumerate(WBLK):
        cf_ = p0_cf[i % 2]
        cb_ = p0_cb[i % 2]
        kf, kb = f"cf{i % 2}", f"cb{i % 2}"
        n = kc * cols
        cfv = cf_[:, 0:n].rearrange("p (a b) -> p a b", b=cols)
        c_at = 0
        for (src, col0, ncols) in parts:
            sv = src.rearrange("(kc p) n -> p kc n", p=128)[:, :, col0:col0 + ncols]
            dma("sp", cfv[:, :, c_at:c_at + ncols], sv, [], [kf], kf)
            c_at += ncols
        if i % 2 == 0:
            act(cb_[:, 0:n], cf_[:, 0:n], AF.Copy, [kf], [kb])
        else:
            fw.op("dve", lambda e, cb_=cb_, cf_=cf_, n=n: e.tensor_copy(out=cb_[:, 0:n], in_=cf_[:, 0:n]), reads=[kf], writes=[kb])
        dma("sp", wblk_d[i, :, 0:n], cb_[:, 0:n], [kb], [f"wblk{i}"], f"wblkst{i % 2}")

    fw.barrier()

    cur[0] = GLOBAL_END
    hTt = [alloc([8, 512], BF16) for _ in range(2)]
    wg = alloc([5, 8, 128], BF16)
    bd = alloc([4, 128])
    w4g = alloc([4, 128], BF16)
    sm = alloc([64])
    kbt = [alloc([512], BF16) for _ in range(2)]
    tq = [alloc([512]) for _ in range(6)]
    ucx = alloc([256])
    B1 = alloc([L])
    mark = cur[0]
    B2 = alloc([L])
    B3 = alloc([L])
    B4 = alloc([L])
    B5 = alloc([L])
    B6 = alloc([L])
    endB = cur[0]
    cur[0] = mark
    ctile = tq[0:2]
    gtile = tq[2:4]
    zA4 = alloc([16, 128], BF16)
    kA = alloc([32, 128], BF16)
    AZ = alloc([8448], BF16)
    Kf = alloc([NF1, 2, 64])
    Y3 = alloc([3 * NF1, 64], BF16)
    xk = [alloc([4, 2, 64]) for _ in range(2)]
    cur[0] = max(cur[0], endB)
    A_sb = AZ.rearrange("p (c pl f) -> p c pl f", pl=4, f=NF1)
    Z_sb = AZ[:, 0:8192].rearrange("p (c b) -> p c b", b=128)
    BDK = [f"bd{m4}{hh}" for m4 in range(4) for hh in range(2)]
    fw.op("dve", lambda e: e.memset(bd.rearrange("p a b -> p (a b)"), 0.0), writes=BDK)

    w_in_v = w_in_d.rearrange("(kc p) n -> p kc n", p=128)
    hTloads = [0]

    def load_hT(j):
        i = hTloads[0] % 2
        hTloads[0] += 1
        dma("sp", hTt[i], hT_d[:, :, j * 512:(j + 1) * 512], ["hT_d"], [f"hTt{i}"], f"hTt{i}")
        return hTt[i], f"hTt{i}"

    def conv_ops(out, okey, P, pkey, taps, bias, rl):
        o3 = out.rearrange("p (r c) -> p r c", c=rl)
        p3 = P.rearrange("p (r c) -> p r c", c=rl)
        wc = [w for (o, w) in taps if o == 0][0]
        ts("dve", out, P, wc, bias, ALU.mult, ALU.add, [pkey, "pv"], [okey])
        for (o, w) in taps:
            if o == 0:
                continue
            if o < 0:
                stt(o3[:, :, -o:], p3[:, :, :rl + o], w, o3[:, :, -o:], ALU.mult, ALU.add, [pkey, okey, "pv"], [okey])
            else:
                stt(o3[:, :, :rl - o], p3[:, :, o:], w, o3[:, :, :rl - o], ALU.mult, ALU.add, [pkey, okey, "pv"], [okey])

    def rglru(g, u, ukey, T, h0, h0k, hf, hfk, hb, hbk, t1, t1k, t3, t3k):
        nt = (T + 511) // 512
        w = min(T, 512)
        for d in range(2):
            hout, hk = (hf, hfk) if d == 0 else (hb, hbk)
            for m, dst, dk in ((0, t1, t1k), (1, t3, t3k)):
                bias = pvc("rg_ba" if m == 0 else "rg_bx", d * 8 + g)
                for j in range(nt):
                    pi, pk = nextps()
                    sl = slice(j * w, (j + 1) * w)
                    mm(lambda e, pi=pi, m=m, d=d, sl=sl: e.matmul(psum[:, pi, 0:w], lhsT=bd[:, 2 * m + d, :], rhs=u[:, sl],
                                                                  start=True, stop=True),
                       [f"bd{2 * m + d}0", f"bd{2 * m + d}1", ukey], [pk])
                    act(dst[:, sl], psum[:, pi, 0:w], AF.Sigmoid, [pk, "pv"], [dk], bias=bias)
            tt("pool", t3[:, 0:T], t3[:, 0:T], u[:, 0:T], ALU.mult, [t3k, ukey], [t3k])
            act(t1[:, 0:T], t1[:, 0:T], AF.Exp, [t1k, "cdec"], [t1k], scale=cdec[:, d, g:g + 1])
            tt("pool", hout[:, 0:T], t1[:, 0:T], t1[:, 0:T], ALU.mult, [t1k], [hk])
            act(hout[:, 0:T], hout[:, 0:T], AF.Sqrt, [hk], [hk], scale=-1.0, bias=1.0)
            tt("dve", t3[:, 0:T], t3[:, 0:T], hout[:, 0:T], ALU.mult, [t3k, hk], [t3k])
            init = 0.0 if h0 is None else h0[d]
            rk = [t1k, t3k] + ([] if h0 is None else [h0k[d]])
            if d == 0:
                fw.op("dve", lambda e, hout=hout, init=init: e.tensor_tensor_scan(
                    out=hout[:, 0:T], data0=t1[:, 0:T], data1=t3[:, 0:T], initial=init, op0=ALU.mult, op1=ALU.add),
                    reads=rk, writes=[hk])
            else:
                fw.op("dve", lambda e, hout=hout, init=init: e.tensor_tensor_scan(
                    out=hout[:, 0:T][:, ::-1], data0=t1[:, 0:T][:, ::-1], data1=t3[:, 0:T][:, ::-1], initial=init,
                    op0=ALU.mult, op1=ALU.add), reads=rk, writes=[hk])

    AKEYS = [f"A_{ch}" for ch in range(64)]

    def fwd_transform(src4, skeys, consume):
        stage1(src4, skeys)
        stage2(consume)

    def stage1(src4, skeys):
        for c in range(16):
            for pair in range(2):
                pi, pk = nextps()
                mm(lambda e, c=c, pair=pair, pi=pi: e.matmul(psum[:, pi, 0:264], lhsT=src4[:, c, :], rhs=D1m[:, pair, :],
                                                            start=True, stop=True), skeys + ["D1m"], [pk])
                dstv = AZ[:, c * 528 + pair * 264: c * 528 + (pair + 1) * 264]
                wk_ = [f"A_{4 * c + 2 * pair}", f"A_{4 * c + 2 * pair + 1}"]
                if (2 * c + pair) % 2 == 0:
                    act(dstv, psum[:, pi, 0:264], AF.Copy, [pk], wk_)
                else:
                    fw.op("dve", lambda e, dstv=dstv, pi=pi: e.tensor_copy(out=dstv, in_=psum[:, pi, 0:264]), reads=[pk], writes=wk_)

    def stage2(consume):
        for bi in range(9):
            f1lo = bi * 4
            nf = min(4, NF1 - f1lo)
            pi, pk = nextps()

            def f(e, f1lo=f1lo, nf=nf, pi=pi):
                ins = None
                for k in range(nf):
                    f1 = f1lo + k
                    e.matmul(psum[:, pi, k * 128:(k + 1) * 128], lhsT=Mtab[:, 0, f1, :],
                             rhs=A_sb[:, :, 0:2, f1].rearrange("p c pl -> p pl c"), start=True, stop=False)
                    ins = e.matmul(psum[:, pi, k * 128:(k + 1) * 128], lhsT=Mtab[:, 1, f1, :],
                                   rhs=A_sb[:, :, 2:4, f1].rearrange("p c pl -> p pl c"), start=False, stop=True)
                return ins
            mm(f, AKEYS + ["Mtab"], [pk])
            consume(bi, pi, pk, f1lo, nf)

    tqc = [0]

    def tqn():
        i = tqc[0] % 6
        tqc[0] += 1
        return tq[i], f"tq{i}"

    def load_wg(g):
        for seg in range(5):
            dma("pool", wg[:, seg, :, :], w_in_v[:, :, seg * D + g * 128: seg * D + (g + 1) * 128], [], [f"wg{seg}"], f"wg{seg}")

    def load_bd(g):
        for m4 in range(4):
            for hh in range(2):
                dma("sp", bd[64 * hh:64 * hh + 64, m4, 64 * hh:64 * hh + 64], rg_w_d[m4, 2 * g + hh], [], [f"bd{m4}{hh}"], f"bdl{m4}{hh}")

    def load_w4(g):
        for cmb in range(4):
            o, dr = cmb // 2, cmb % 2
            c0 = o * 2048 + dr * 1024 + g * 128
            dma("pool", w4g[0:64, cmb, :], hy_w4_d[:, c0:c0 + 128], [], [f"w4g{cmb}"], f"w4g{cmb}")

    load_wg(0)
    load_bd(0)
    load_w4(0)
    for g in range(NG):
        pi, pk = nextps()

        def f(e, pi=pi):
            ins = None
            for kc in range(8):
                ins = e.matmul(psum[:, pi, 0:256], lhsT=wg[:, 0, kc, :], rhs=hcT[:, kc, :], start=(kc == 0), stop=(kc == 7))
            return ins
        mm(f, ["wg0"] + [f"hcT{k}" for k in range(8)], [pk])
        pt, ptk = tqn()
        act(pt[:, 0:256], psum[:, pi, 0:256], AF.Identity, [pk, "pv"], [ptk], bias=pvc("b_in", 0 * 8 + g))
        rtaps = [(k - 2, pvc("rnn_conv_w", k * 8 + g)) for k in range(4)]
        conv_ops(ucx, "uc", pt[:, 0:256], ptk, rtaps, pvc("rnn_conv_b", g), 256)
        rglru(g, ucx, "uc", 256, None, None, B3, "B3", B4, "B4", B5, "B5", B6, "B6")
        fw.op("dve", lambda e: e.tensor_copy(out=sm[:, 0:1], in_=B3[:, 255:256]), reads=["B3"], writes=["h0f"])
        fw.op("dve", lambda e: e.tensor_copy(out=sm[:, 1:2], in_=B4[:, 0:1]), reads=["B4"], writes=["h0b"])

        nxt = load_hT(0)
        for j in range(NT):
            ht, hk = nxt
            if j + 1 < NT:
                nxt = load_hT(j + 1)
            sl = slice(j * 512, (j + 1) * 512)
            for seg in range(5):
                pi, pk = nextps()

                def f(e, pi=pi, seg=seg, ht=ht):
                    ins = None
                    for kc in range(8):
                        ins = e.matmul(psum[:, pi, :], lhsT=wg[:, seg, kc, :], rhs=ht[:, kc, :], start=(kc == 0), stop=(kc == 7))
                    return ins
                mm(f, [f"wg{seg}", hk], [pk])
                pt, ptk = tqn()
                act(pt, psum[:, pi, :], AF.Identity, [pk, "pv"], [ptk], bias=pvc("b_in", seg * 8 + g))
                if seg == 0:
                    conv_ops(B2[:, sl], "B2", pt, ptk, rtaps, pvc("rnn_conv_b", g), 64)
                elif seg == 1:
                    gq, gqk = tqn()
                    act(gq, pt, AF.Square, [ptk], [gqk])
                    ts("pool", gq, gq, 0.044715, 1.0, ALU.mult, ALU.add, [gqk], [gqk])
                    tt("pool", gq, gq, pt, ALU.mult, [gqk, ptk], [gqk])
                    act(gq, gq, AF.Tanh, [gqk], [gqk], scale=0.7978845608028654)
                    stt(B6[:, sl], gq, 1.0, pt, ALU.add, ALU.mult, [gqk, ptk], ["B6"])
                else:
                    si = seg - 2
                    htaps = [(k - 1, pvc("hy_conv_w", k * 24 + si * 8 + g)) for k in range(3)]
                    if si == 0:
                        conv_ops(B1[:, sl], "B1", pt, ptk, htaps, pvc("hy_conv_b", si * 8 + g), 64)
                    else:
                        ct, ctk = tqn()
                        conv_ops(ct, ctk, pt, ptk, htaps, pvc("hy_conv_b", si * 8 + g), 64)
                        dma("sp", gate_d[si - 1, :, sl], ct, [ctk], [f"gate_d{si - 1}"], f"gate_st{si - 1}")
            dt_, dk_ = tqn()
            ts("dve", sm[:, 8 + j % 2:9 + j % 2], negdl[:, g:g + 1], float(j * 512) / float(L - 1), None, ALU.mult, None,
               ["negdl"], [f"decb{j % 2}"])
            act(dt_, tbase, AF.Exp, ["tbase", "negdl", f"decb{j % 2}"], [dk_], scale=negdl[:, g:g + 1],
                bias=sm[:, 8 + j % 2:9 + j % 2])
            for cmb in range(4):
                o_, dr = cmb // 2, cmb % 2
                pi, pk = nextps()
                mm(lambda e, pi=pi, cmb=cmb, sl=sl: e.matmul(psum[:, pi, :], lhsT=w4g[0:64, cmb, :], rhs=hdn3[0:64, sl],
                                                             start=True, stop=True), [f"w4g{cmb}", "hdn3"], [pk])
                kt = kbt[cmb % 2]
                ktk = f"kbt{cmb % 2}"
                if dr == 0:
                    tt("dve", kt, psum[:, pi, :], dt_, ALU.mult, [pk, dk_], [ktk])
                else:
                    tt("dve", kt[:, ::-1], psum[:, pi, :], dt_, ALU.mult, [pk, dk_], [ktk])
                    if j == 0:
                        fw.op("dve", lambda e, kt=kt: e.memset(kt[:, 511:512], 0.0), reads=[ktk], writes=[ktk])
                jq, jqk = tqn()
                ncol = 16 + o_ * 24 + dr * 8 + j
                act(jq, kt, AF.Abs, [ktk], [jqk, f"nrm{cmb}_{j}"], accum=sm[:, ncol:ncol + 1])
                if dr == 0:
                    dma("sp", kb_d[o_, :, sl], kt, [ktk], [f"kb_d{o_}_{dr}_{j}"], f"kb_st{o_}")
                else:
                    s0 = L + 3585 - 512 * j
                    n_ = 511 if j == 0 else 512
                    dma("sp", kb_d[o_, :, s0:s0 + n_], kt[:, 0:n_], [ktk], [f"kb_d{o_}_z", f"kb_d{o_}_{dr}_{j}"], f"kb_st{o_}")
        for o_ in range(2):
            base = 16 + o_ * 24
            fw.op("dve", lambda e, base=base: e.reduce_sum(out=sm[:, base + 16:base + 17], in_=sm[:, base:base + 16],
                                                          axis=mybir.AxisListType.X),
                  reads=[f"nrm{o_ * 2 + dr}_{j}" for dr in range(2) for j in range(NT)], writes=[f"nrmsum{o_}"])
            fw.op("dve", lambda e, base=base: e.reciprocal(out=sm[:, base + 17:base + 18], in_=sm[:, base + 16:base + 17]),
                  reads=[f"nrmsum{o_}"], writes=[f"invn{o_}"])
        if g + 1 < NG:
            load_wg(g + 1)
        rglru(g, B2, "B2", L, [sm[:, 0:1], sm[:, 1:2]], ["h0f", "h0b"], B3, "B3", B2, "B2", B5, "B5", B4, "B4")
        if g + 1 < NG:
            load_bd(g + 1)
        tt("dve", B3, B3, B2, ALU.add, ["B3", "B2"], ["B3"])
        stt(B3, B3, 0.5, B6, ALU.mult, ALU.mult, ["B3", "B6"], ["B3"])
        dma("pool", yr_d[:, g, :], B3, ["B3"], ["yr_d"], "yr_st")
        fw.barrier()

        def FT(o, h):
            KA = [f"kA_{q}" for q in range(2)]
            for q2 in range(2):
                c0 = 64 * h + q2
                dma("sp", kA[64 * q2:64 * q2 + 64, :, :],
                    kb_d[o, c0:64 * (h + 1):2, :].rearrange("c (a b) -> a c b", b=128),
                    [f"kb_d{o}_{dr_}_{j_}" for dr_ in range(2) for j_ in range(NT)], [KA[q2]], "kA")
            for c in range(32):
                pi, pk = nextps()
                mm(lambda e, c=c, pi=pi: e.matmul(psum[:, pi, 0:264], lhsT=kA[:, c, :], rhs=D1f, start=True, stop=True),
                   KA + ["D1f"], [pk])
                dstv = AZ[:, c * 264:(c + 1) * 264]
                wk_ = [f"A_{2 * c}", f"A_{2 * c + 1}"]
                if c % 2 == 0:
                    act(dstv, psum[:, pi, 0:264], AF.Copy, [pk], wk_)
                else:
                    fw.op("dve", lambda e, dstv=dstv, pi=pi: e.tensor_copy(out=dstv, in_=psum[:, pi, 0:264]), reads=[pk], writes=wk_)

            def consume_k(bi, pi, pk, f1lo, nf):
                src = psum[:, pi, 0:nf * 128].rearrange("p (f r c) -> p f r c", r=2, c=64)
                act(Kf[:, f1lo:f1lo + nf, :, :], src, AF.Copy, [pk], [f"Kf{bi}"])
            stage2(consume_k)

        def DT(o, h):
            ZA = [f"zA4_{q}" for q in range(4)]
            for q in range(4):
                c0 = 64 * h + q
                dma("sp", zA4[32 * q:32 * q + 32, :, :], zb_d[c0:64 * (h + 1):4, :].rearrange("c (a b) -> a c b", b=128),
                    ["zb_d"], [ZA[q]], "zA4")

            def consume_x(bi, pi, pk, f1lo, nf):
                X = psum[:, pi, 0:nf * 128].rearrange("p (f r c) -> p f r c", r=2, c=64)
                K = Kf[:, f1lo:f1lo + nf, :, :]
                Ksw = Kf[:, f1lo:f1lo + nf, ::-1, :]
                t1_ = xk[0][:, 0:nf]
                t2_ = xk[1][:, 0:nf]
                tt("dve", t1_, X, K, ALU.mult, [pk, f"Kf{bi}"], ["xk0"])
                tt("dve", t2_, X, Ksw, ALU.mult, [pk, f"Kf{bi}"], ["xk1"])
                yv = [Y3[:, pl * NF1 + f1lo: pl * NF1 + f1lo + nf, :] for pl in range(3)]
                tt("dve", yv[1], t1_[:, :, 0, :], t1_[:, :, 1, :], ALU.subtract, ["xk0"], [f"Y3r_{bi}"])
                tt("dve", yv[2], t2_[:, :, 0, :], t2_[:, :, 1, :], ALU.add, ["xk1"], [f"Y3i_{bi}"])
                act(yv[0], yv[2], AF.Copy, [f"Y3i_{bi}"], [f"Y3n_{bi}"], scale=-1.0)
            fwd_transform(zA4, ZA, consume_x)
            ykeys = [f"Y3{x_}_{bi}" for bi in range(9) for x_ in "rin"]
            for c4 in range(16):
                pi, pk = nextps()

                def f(e, c4=c4, pi=pi):
                    ins = None
                    for cc in range(4):
                        ch = c4 * 4 + cc
                        e.matmul(psum[0:66, pi, cc * 128:(cc + 1) * 128],
                                 lhsT=Y3[:, NF1:3 * NF1, ch], rhs=CS[:, 0:128], start=True, stop=False)
                        ins = e.matmul(psum[0:66, pi, cc * 128:(cc + 1) * 128],
                                       lhsT=Y3[:, 0:2 * NF1, ch], rhs=CS[:, 128:256],
                                       start=False, stop=True)
                    return ins
                mm(f, ykeys + ["CS"], [pk])
                act(Z_sb[0:66, c4 * 4:(c4 + 1) * 4, :],
                    psum[0:66, pi, :].rearrange("p (c b) -> p c b", b=128), AF.Copy, [pk], [f"Z_{c4}"] + AKEYS)
            zkeys = [f"Z_{c4}" for c4 in range(16)]
            icol = 16 + o * 24 + 17
            for bj in range(8):
                pi, pk = nextps()

                def f(e, bj=bj, pi=pi):
                    ins = None
                    for bb in range(16):
                        b_ = bj * 16 + bb
                        ins = e.matmul(psum[0:64, pi, bb * 32:(bb + 1) * 32], lhsT=Z_sb[0:66, :, b_], rhs=Gtab[0:66, b_, :],
                                       start=True, stop=True)
                    return ins
                mm(f, zkeys + AKEYS + ["Gtab"], [pk])
                ct = ctile[bj % 2]
                ctk = f"ct{bj % 2}"
                fw.op("dve", lambda e, ct=ct, pi=pi, h=h: e.tensor_copy(out=ct[64 * h:64 * h + 64, :], in_=psum[0:64, pi, :]),
                      reads=[pk], writes=[ctk])
                zv = B1[64 * h:64 * h + 64, :].rearrange("p (a b) -> p a b", b=128)[:, :, bj * 16:(bj + 1) * 16]
                stt(zv, ct[64 * h:64 * h + 64, :].rearrange("p (b a) -> p a b", a=32), sm[64 * h:64 * h + 64, icol:icol + 1], zv,
                    ALU.mult, ALU.add, [ctk, f"invn{o}", "B1"], ["B1"])

        passes = [(o, h) for o in range(2) for h in range(2)]
        dma("pool", zb_d, B1, ["B1"], ["zb_d"], "zb_st")
        FT(0, 0)
        ts("dve", B1, B1, pvc("hy_skip", 0 * 8 + g), None, ALU.mult, None, ["B1", "pv"], ["B1"])
        for pidx, (o, h) in enumerate(passes):
            DT(o, h)
            if pidx + 1 < len(passes):
                FT(*passes[pidx + 1])
            if h == 1:
                for j in range(NT):
                    sl = slice(j * 512, (j + 1) * 512)
                    gt = gtile[j % 2]
                    dma("sp", gt, gate_d[o, :, sl], [f"gate_d{o}"], [f"gt{j % 2}"], f"gt{j % 2}")
                    tt("dve", B1[:, sl], B1[:, sl], gt, ALU.mult, ["B1", f"gt{j % 2}"], ["B1"])
                if o == 0:
                    dma("pool", zb_d, B1, ["B1"], ["zb_d"], "zb_st")
                    ts("dve", B1, B1, pvc("hy_skip", 1 * 8 + g), None, ALU.mult, None, ["B1", "pv"], ["B1"])
        dma("pool", yh_d[:, g, :], B1, ["B1"], ["yh_d"], "yh_st")
        if g + 1 < NG:
            load_w4(g + 1)
        fw.barrier()

    cur[0] = GLOBAL_END
    ring = [alloc([4096], BF16) for _ in range(3)]
    yrt = alloc([8, 512], BF16)
    yht = alloc([8, 512], BF16)
    hTc = alloc([8, 512], BF16)
    xtok = alloc([4, D])
    xT = alloc([8, 512])
    merged = alloc([8, 512], BF16)
    gts = [alloc([512]) for _ in range(4)]
    sq = merged
    rstd = alloc([512])
    h2 = alloc([8, 512], BF16)
    hid = alloc([22, 512], BF16)
    wstate = {"issued": 0, "views": {}}
    TOTAL_W = NT * NWB

    def issue_w(upto):
        while wstate["issued"] < min(upto, TOTAL_W):
            sidx = wstate["issued"]
            i = sidx % NWB
            s_ = sidx % 3
            nm, kc, cols, parts = WBLK[i]
            n = kc * cols
            dma("sp", ring[s_][:, 0:n], wblk_d[i, :, 0:n], [f"wblk{i}"], [f"ring{s_}"], f"ring{s_}")
            wstate["views"][sidx] = (ring[s_][:, 0:n].rearrange("p (a b) -> p a b", b=cols), f"ring{s_}")
            wstate["issued"] += 1

    def get_w(j, i):
        sidx = j * NWB + i
        issue_w(sidx + 3)
        return wstate["views"][sidx]

    for j in range(NT):
        sl = slice(j * 512, (j + 1) * 512)
        dma("sp", yrt, yr_d[:, :, sl], ["yr_d"], ["yrt"], "yrt")
        dma("sp", yht, yh_d[:, :, sl], ["yh_d"], ["yht"], "yht")
        dma("sp", hTc, hT_d[:, :, sl], ["hT_d"], ["hTc"], "hTc")
        dma("sp", xtok, x_d[j * 512:(j + 1) * 512, :].rearrange("(a p) n -> p a n", p=128), [], ["xtok"], "xtok")
        for a4 in range(4):
            for half in range(2):
                pi, pk = nextps()

                def f(e, a4=a4, half=half, pi=pi):
                    ins = None
                    for gg in range(4):
                        g = half * 4 + gg
                        ins = e.transpose(psum[:, pi, gg * 128:(gg + 1) * 128], xtok[:, a4, g * 128:(g + 1) * 128], ident)
                    return ins
                mm(f, ["xtok", "ident"], [pk])
                act(xT[:, half * 4:(half + 1) * 4, a4 * 128:(a4 + 1) * 128],
                    psum[:, pi, :].rearrange("p (g t) -> p g t", t=128), AF.Copy, [pk], ["xT"])
        for m in range(8):
            wv, wk = get_w(j, m)
            pis = []
            for li, (mov, mk_) in enumerate(((yrt, "yrt"), (yht, "yht"), (hTc, "hTc"), (hTc, "hTc"))):
                pi, pk = nextps()

                def f(e, wv=wv, mov=mov, li=li, pi=pi):
                    ins = None
                    for kc in range(8):
                        ins = e.matmul(psum[:, pi, :], lhsT=wv[:, kc, li * 128:(li + 1) * 128], rhs=mov[:, kc, :],
                                       start=(kc == 0), stop=(kc == 7))
                    return ins
                mm(f, [wk, mk_], [pk])
                pis.append((pi, pk))
            act(gts[0], psum[:, pis[2][0], :], AF.Sigmoid, [pis[2][1], "pv"], ["gt0"], bias=pvc("b_in", 40 + m))
            act(gts[1], psum[:, pis[3][0], :], AF.Sigmoid, [pis[3][1], "pv"], ["gt1"], bias=pvc("b_in", 48 + m))
            tt("dve", gts[2], psum[:, pis[0][0], :], gts[0], ALU.mult, [pis[0][1], "gt0"], ["gt2"])
            tt("dve", gts[3], psum[:, pis[1][0], :], gts[1], ALU.mult, [pis[1][1], "gt1"], ["gt3"])
            tt("pool", merged[:, m, :], gts[2], gts[3], ALU.add, ["gt2", "gt3"], ["merged"])
        for m in range(8):
            wv, wk = get_w(j, 8 + m // 4)
            col = m % 4
            pi, pk = nextps()

            def f(e, wv=wv, col=col, pi=pi):
                ins = None
                for kc in range(8):
                    ins = e.matmul(psum[:, pi, :], lhsT=wv[:, kc, col * 128:(col + 1) * 128], rhs=merged[:, kc, :],
                                   start=(kc == 0), stop=(kc == 7))
                return ins
            mm(f, [wk, "merged"], [pk])
            stt(xT[:, m, :], psum[:, pi, :], modsb[:, 16 + m, 0:1], xT[:, m, :], ALU.mult, ALU.add, [pk, "modsb", "xT"], ["xT"])

        def rms_rstd():
            act(sq, xT, AF.Square, ["xT"], ["sq"])
            pi, pk = nextps()

            def f(e, pi=pi):
                ins = None
                for kc in range(8):
                    ins = e.matmul(psum[:, pi, :], lhsT=ones_bf, rhs=sq[:, kc, :], start=(kc == 0), stop=(kc == 7))
                return ins
            mm(f, ["ones", "sq"], [pk])
            act(rstd, psum[:, pi, :], AF.Sqrt, [pk], ["rstd_"], scale=1.0 / D, bias=EPS)
            fw.op("dve", lambda e: e.reciprocal(out=rstd, in_=rstd), reads=["rstd_"], writes=["rstd_"])
        rms_rstd()
        for m in range(8):
            tt("dve", gts[m % 2], xT[:, m, :], rstd, ALU.mult, ["xT", "rstd_"], [f"gt{m % 2}"])
            act(h2[:, m, :], gts[m % 2], AF.Identity, [f"gt{m % 2}", "A2", "modsb"], ["h2"], scale=A2[:, m:m + 1],
                bias=modsb[:, 24 + m, 0:1])
        for m in range(22):
            wv, wk = get_w(j, 10 + m // 2)
            pis = []
            for part in range(2):
                col = part * 2 + (m % 2)
                pi, pk = nextps()

                def f(e, wv=wv, col=col, pi=pi):
                    ins = None
                    for kc in range(8):
                        ins = e.matmul(psum[:, pi, :], lhsT=wv[:, kc, col * 128:(col + 1) * 128], rhs=h2[:, kc, :],
                                       start=(kc == 0), stop=(kc == 7))
                    return ins
                mm(f, [wk, "h2"], [pk])
                pis.append((pi, pk))
            act(gts[m % 2], psum[:, pis[0][0], :], AF.Silu, [pis[0][1]], [f"gt{m % 2}"])
            tt("dve", hid[:, m, :], psum[:, pis[1][0], :], gts[m % 2], ALU.mult, [pis[1][1], f"gt{m % 2}"], ["hid"])
        for m in range(8):
            wv, wk = get_w(j, 21 + m)
            pi, pk = nextps()

            def f(e, wv=wv, pi=pi):
                ins = None
                for kc in range(22):
                    ins = e.matmul(psum[:, pi, :], lhsT=wv[:, kc, :], rhs=hid[:, kc, :], start=(kc == 0), stop=(kc == 21))
                return ins
            mm(f, [wk, "hid"], [pk])
            stt(xT[:, m, :], psum[:, pi, :], modsb[:, 40 + m, 0:1], xT[:, m, :], ALU.mult, ALU.add, [pk, "modsb", "xT"], ["xT"])
        rms_rstd()
        for m in range(8):
            stt(xT[:, m, :], xT[:, m, :], pvc("final_g", m), rstd, ALU.mult, ALU.mult, ["xT", "rstd_", "pv"], ["xT"])
        for a4 in range(4):
            for half in range(2):
                pi, pk = nextps()

                def f(e, a4=a4, half=half, pi=pi):
                    ins = None
                    for gg in range(4):
                        g = half * 4 + gg
                        ins = e.transpose(psum[:, pi, gg * 128:(gg + 1) * 128], xT[:, g, a4 * 128:(a4 + 1) * 128], ident)
                    return ins
                mm(f, ["xT", "ident"], [pk])
                act(xtok[:, a4, half * 512:(half + 1) * 512], psum[:, pi, :], AF.Copy, [pk], ["xtok"])
        dma("sp", out_d[j * 512:(j + 1) * 512, :].rearrange("(a p) n -> p a n", p=128), xtok, ["xtok"], ["out_d"], "out_st")

    fw.op("sp", None, extra=fw.all_tokens())
    assert len(fw.dsem) + 5 <= 100, len(fw.dsem)
    fw.emit()
    return nc


_CACHE = {}


def prep_inputs(inp, tb, b):
    PV, NPV = pv_layout()
    pv = np.zeros((128, NPV), np.float32)

    def put(name, arr):
        a = fm(arr)
        pv[:, PV[name]:PV[name] + a.shape[1]] = a
    put("norm1_g", inp["norm1_g"][0])
    put("norm2_g", inp["norm2_g"][0])
    put("final_g", inp["final_g"])
    put("rnn_conv_b", inp["rnn_conv_b"][0])
    put("deltas", tb["deltas"])
    put("b_mod", inp["b_mod"][0])
    put("b_in", inp["b_in"][0])
    put("rnn_conv_w", inp["rnn_conv_w"][0].reshape(-1))
    put("rg_ba", inp["rg_ba"][0].reshape(-1))
    put("rg_bx", inp["rg_bx"][0].reshape(-1))
    put("rg_lambda", inp["rg_lambda"][0].reshape(-1))
    put("hy_conv_w", inp["hy_conv_w"][0].reshape(-1))
    put("hy_conv_b", inp["hy_conv_b"][0])
    put("hy_skip", inp["hy_skip"][0].reshape(-1))
    hs = PV["hy_small"]
    pv[:64, hs + 0] = inp["hy_b1"][0]
    pv[:64, hs + 1] = inp["hy_b2"][0]
    pv[:64, hs + 2] = inp["hy_b3"][0]
    pv[:64, hs + 3] = inp["hy_freq"][0]
    c2 = np.stack([fm(inp["c"][b]), fm(inp["c_ctx"])], axis=-1).reshape(128, 16)
    m = {
        "x": np.ascontiguousarray(inp["x"][b]),
        "ctx": np.ascontiguousarray(inp["ctx"][b]),
        "c2": np.ascontiguousarray(c2, dtype=np.float32),
        "pv": pv,
        "w_mod": inp["w_mod"][0], "w_in": inp["w_in"][0],
        "rg_w": np.ascontiguousarray(np.concatenate([inp["rg_wa"][0], inp["rg_wx"][0]], axis=0)),
        "hy_w1": inp["hy_w1"][0], "hy_w2": inp["hy_w2"][0], "hy_w3": inp["hy_w3"][0], "hy_w4": inp["hy_w4"][0],
        "w_a_out": inp["w_a_out"][0], "w_b_out": inp["w_b_out"][0], "w_out": inp["w_out"][0],
        "w_ffn_in": inp["w_ffn_in"][0], "w_ffn_out": inp["w_ffn_out"][0],
        "t_D1m": tb["D1m"], "t_D1f": tb["D1f"], "t_M": tb["M"], "t_CS": tb["CS"], "t_G": tb["G"], "t_zpos": tb["zposT"],
        "t_tbase": tb["tbase"], "t_ident": tb["ident"],
    }
    return {k: np.ascontiguousarray(np.asarray(v, dtype=np.float32)) for k, v in m.items()}


def kernel(**inputs):
    inp = {k: np.asarray(v) for k, v in inputs.items()}
    tb = make_tables()
    if "nc" not in _CACHE:
        _CACHE["nc"] = build_program(False)
    nc = _CACHE["nc"]
    in_maps = [prep_inputs(inp, tb, b) for b in range(8)]
    res = run_bass_kernel_spmd(nc, in_maps, core_ids=list(range(8)))
    out = np.stack([np.asarray(r["out"]) for r in res.results], axis=0)
    return out.astype(np.float32)
```

```python
import numpy as np
import concourse.bass as bass
import concourse.mybir as mybir
from concourse.bass_utils import run_bass_kernel_spmd

F32 = mybir.dt.float32
BF16 = mybir.dt.bfloat16
AF = mybir.ActivationFunctionType
ALU = mybir.AluOpType

D = 1024
L = 4096
NFFT = 8192
NF1 = 33
DFF = 2816
EPS = 1e-6
NG = 8
NT = 8


class Tok:
    __slots__ = ("sem", "val")

    def __init__(self, sem, val):
        self.sem = sem
        self.val = val


class Fw:
    ENG = ("pe", "act", "dve", "pool", "sp")

    def __init__(self, nc):
        self.nc = nc
        self.ops = {e: [] for e in self.ENG}
        self.cnt = {e: 0 for e in self.ENG}
        self.esem = {e: nc.alloc_semaphore(name=f"cs_{e}") for e in self.ENG}
        self.waited = {e: {} for e in self.ENG}
        self.bufs = {}
        self.dsem = {}

    def op(self, eng, fn, reads=(), writes=(), dma=None, extra=()):
        deps = list(extra)
        for k in reads:
            b = self.bufs.get(k)
            if b is not None and b[0] is not None:
                deps.append(b[0])
        for k in writes:
            b = self.bufs.get(k)
            if b is not None:
                if b[1]:
                    deps.extend(b[1])
                elif b[0] is not None:
                    deps.append(b[0])
        best = {}
        for t in deps:
            k = id(t.sem)
            if k not in best or best[k].val < t.val:
                best[k] = t
        waits = []
        w = self.waited[eng]
        for k, t in best.items():
            if t.val > w.get(k, 0):
                w[k] = t.val
                waits.append((t.sem, t.val))
        if dma is None:
            self.cnt[eng] += 1
            tok = Tok(self.esem[eng], self.cnt[eng])
            inc = 1
        else:
            s = self.dsem.get(dma)
            if s is None:
                s = [self.nc.alloc_semaphore(name=f"ds_{len(self.dsem)}"), 0]
                self.dsem[dma] = s
            s[1] += 16
            tok = Tok(s[0], s[1])
            inc = 16
        self.ops[eng].append((waits, fn, tok.sem, inc))
        for k in reads:
            b = self.bufs.get(k)
            if b is None:
                b = [None, []]
                self.bufs[k] = b
            b[1].append(tok)
        for k in writes:
            self.bufs[k] = [tok, []]
        return tok

    def all_tokens(self):
        toks = []
        for e in self.ENG:
            if self.cnt[e] > 0:
                toks.append(Tok(self.esem[e], self.cnt[e]))
        for k, s in self.dsem.items():
            if s[1] > 0:
                toks.append(Tok(s[0], s[1]))
        return toks

    def barrier(self):
        toks = self.all_tokens()
        for e in self.ENG:
            self.op(e, None, extra=toks)

    def emit(self):
        nc = self.nc
        with nc.Block() as block:
            def mk(ename):
                def body(e):
                    for waits, fn, sem, inc in self.ops[ename]:
                        for s, v in waits:
                            e.wait_ge(s, v)
                        ins = e.nop() if fn is None else fn(e)
                        ins.then_inc(sem, inc)
                return body
            block.tensor(mk("pe"))
            block.scalar(mk("act"))
            block.vector(mk("dve"))
            block.gpsimd(mk("pool"))
            block.sync(mk("sp"))


def make_tables():
    tb = {}
    a = np.arange(32, dtype=np.float64)[:, None]
    f1 = np.arange(NF1, dtype=np.float64)[None, :]
    ang = 2 * np.pi * a * f1 / 64.0
    C, S = np.cos(ang), np.sin(ang)
    blk = np.concatenate([C, -S, S, C], axis=1)
    D1m = np.zeros((128, 2, 264), np.float64)
    for pair in range(2):
        for qq in range(2):
            q = 2 * pair + qq
            D1m[32 * q:32 * q + 32, pair, qq * 132:(qq + 1) * 132] = blk
    tb["D1m"] = D1m.reshape(128, 528).astype(np.float32)
    a64 = np.arange(64, dtype=np.float64)[:, None]
    ang = 2 * np.pi * a64 * f1 / 64.0
    C6, S6 = np.cos(ang), np.sin(ang)
    blk6 = np.concatenate([C6, -S6, S6, C6], axis=1)
    D1f = np.zeros((128, 264), np.float64)
    for q2 in range(2):
        D1f[64 * q2:64 * q2 + 64, q2 * 132:(q2 + 1) * 132] = blk6
    tb["D1f"] = D1f.astype(np.float32)
    b = np.arange(128, dtype=np.float64)[:, None]
    f2 = np.arange(128, dtype=np.float64)[None, :]
    M = np.zeros((128, 2, NF1, 128))
    for k in range(NF1):
        ang = -2 * np.pi * b * (k + 64 * f2) / NFFT
        M[:, 0, k, :] = np.cos(ang)
        M[:, 1, k, :] = np.sin(ang)
    tb["M"] = M.reshape(128, 2 * NF1 * 128).astype(np.float32)
    ang = 2 * np.pi * f2.T * b.T / 128.0
    tb["CS"] = np.concatenate([np.cos(ang), np.sin(ang)], axis=1).astype(np.float32)
    c = np.full(NF1, 2.0)
    c[0] = 1.0
    c[NF1 - 1] = 1.0
    f1v = np.arange(NF1, dtype=np.float64)[:, None, None]
    bv = np.arange(128, dtype=np.float64)[None, :, None]
    av = np.arange(32, dtype=np.float64)[None, None, :]
    ang = 2 * np.pi * (av * f1v / 64.0 + bv * f1v / NFFT)
    Gr = c[:, None, None] / NFFT * np.cos(ang)
    Gi = c[:, None, None] / NFFT * np.sin(ang)
    G = np.zeros((128, 128 * 32), np.float64)
    G[:66] = np.concatenate([Gr, -Gi], axis=0).reshape(66, 128 * 32)
    tb["G"] = G.astype(np.float32)
    t = np.linspace(0.0, 1.0, L, dtype=np.float32)[:, None]
    w = (2.0 * np.pi * np.arange(L, dtype=np.float32)[:, None] / L).astype(np.float32)
    f = np.linspace(1e-4, 15, 16, dtype=np.float32)[None, :]
    fw_ = (f * w).astype(np.float32)
    z = np.concatenate([t, np.cos(fw_), -np.sin(fw_)], axis=-1).astype(np.float32)
    zp = np.zeros((128, L), np.float32)
    zp[:33] = z.T
    tb["zposT"] = zp
    max_decay = np.log(1e-2) / 0.3
    min_decay = np.log(1e-2) / 1.5
    tb["deltas"] = np.abs(np.linspace(min_decay, max_decay, D, dtype=np.float32)).astype(np.float32)
    tb["tbase"] = np.broadcast_to(t[:512, 0][None, :], (128, 512)).astype(np.float32).copy()
    tb["ident"] = np.eye(128, dtype=np.float32)
    return tb


def pv_layout():
    cols = {}
    n = [0]

    def add(name, k):
        cols[name] = n[0]
        n[0] += k
    for nm in ["norm1_g", "norm2_g", "final_g", "rnn_conv_b", "deltas"]:
        add(nm, 8)
    add("b_mod", 48)
    add("b_in", 56)
    add("rnn_conv_w", 32)
    add("rg_ba", 16)
    add("rg_bx", 16)
    add("rg_lambda", 16)
    add("hy_conv_w", 72)
    add("hy_conv_b", 24)
    add("hy_skip", 16)
    add("hy_small", 4)
    return cols, n[0]


def fm(v):
    v = np.asarray(v, np.float32).reshape(-1, 128)
    return np.ascontiguousarray(v.T)


def build_program(debug=False):
    nc = bass.Bass("TRN2", target_bir_lowering=False)
    fw = Fw(nc)
    PV, NPV = pv_layout()

    def din(name, shape, dt=F32):
        return nc.dram_tensor(name, list(shape), dt, kind="ExternalInput").ap()

    def dscr(name, shape, dt):
        return nc.dram_tensor(name, list(shape), dt, kind="Internal").ap()

    x_d = din("x", [L, D])
    ctx_d = din("ctx", [256, D])
    c2_d = din("c2", [128, 16])
    pv_d = din("pv", [128, NPV])
    w_mod_d = din("w_mod", [D, 6 * D])
    w_in_d = din("w_in", [D, 7 * D])
    rg_w_d = din("rg_w", [4, 16, 64, 64])
    hy_w1_d = din("hy_w1", [33, 64])
    hy_w2_d = din("hy_w2", [64, 64])
    hy_w3_d = din("hy_w3", [64, 64])
    hy_w4_d = din("hy_w4", [64, 4096])
    w_a_d = din("w_a_out", [D, D])
    w_b_d = din("w_b_out", [D, D])
    w_o_d = din("w_out", [D, D])
    w_fi_d = din("w_ffn_in", [D, 2 * DFF])
    w_fo_d = din("w_ffn_out", [DFF, D])
    t_D1m = din("t_D1m", [128, 528])
    t_D1f = din("t_D1f", [128, 264])
    t_M = din("t_M", [128, 2 * NF1 * 128])
    t_CS = din("t_CS", [128, 256])
    t_G = din("t_G", [128, 4096])
    t_zpos = din("t_zpos", [128, L])
    t_tbase = din("t_tbase", [128, 512])
    t_ident = din("t_ident", [128, 128])
    out_d = nc.dram_tensor("out", [L, D], F32, kind="ExternalOutput").ap()

    okind = "ExternalOutput" if debug else "Internal"
    hT_d = dscr("hT_s", [128, 8, L], BF16)
    yr_d = nc.dram_tensor("yr_s", [128, 8, L], BF16, kind=okind).ap()
    yh_d = nc.dram_tensor("yh_s", [128, 8, L], BF16, kind=okind).ap()
    gate_d = dscr("gate_s", [2, 128, L], F32)
    zb_d = dscr("zb_s", [128, L], BF16)
    kb_d = dscr("kb_s", [2, 128, 2 * L], BF16)
    WBLK = []
    for m in range(8):
        WBLK.append(("mix", 8, 512, [(w_a_d, m * 128, 128), (w_b_d, m * 128, 128),
                                      (w_in_d, 5 * D + m * 128, 128), (w_in_d, 6 * D + m * 128, 128)]))
    for blk in range(2):
        WBLK.append(("wo", 8, 512, [(w_o_d, blk * 512, 512)]))
    for blk in range(11):
        WBLK.append(("wfi", 8, 512, [(w_fi_d, blk * 256, 256), (w_fi_d, DFF + blk * 256, 256)]))
    for blk in range(8):
        WBLK.append(("wfo", 22, 128, [(w_fo_d, blk * 128, 128)]))
    NWB = len(WBLK)
    wblk_d = dscr("wblk_s", [NWB, 128, 4096], BF16)

    ARENA = 95000
    arena = nc.sbuf_tensor("arena", [128, ARENA], BF16).__enter__()
    psum = nc.psum_tensor("psum", [128, 8, 512], F32).__enter__()
    cur = [0]

    def alloc(shape, dt=F32):
        n = int(np.prod(shape))
        sz = n * (2 if dt == F32 else 1)
        a0 = cur[0]
        cur[0] += sz + (sz % 2)
        assert cur[0] <= ARENA, f"SBUF arena overflow {cur[0]}"
        v = arena[:, a0:a0 + sz]
        if dt == F32:
            v = v.bitcast(F32)
        if len(shape) == 2:
            v = v.rearrange("p (a b) -> p a b", b=shape[1])
        elif len(shape) == 3:
            v = v.rearrange("p (a b c) -> p a b c", b=shape[1], c=shape[2])
        elif len(shape) == 4:
            v = v.rearrange("p (a b c d) -> p a b c d", b=shape[1], c=shape[2], d=shape[3])
        return v

    psi = [0]

    def nextps():
        i = psi[0] % 8
        psi[0] += 1
        return i, f"ps{i}"

    def dma(eng, out, in_, reads, writes, key):
        return fw.op(eng, lambda e: e.dma_start(out=out, in_=in_), reads=reads, writes=writes, dma=key)

    def act(out, in_, func, reads, writes, bias=None, scale=None, accum=None):
        kw = {}
        if bias is not None:
            kw["bias"] = bias
        if scale is not None:
            kw["scale"] = scale
        if accum is not None:
            kw["accum_out"] = accum
        return fw.op("act", lambda e: e.activation(out=out, in_=in_, func=func, **kw), reads=reads, writes=writes)

    def tt(eng, out, in0, in1, op, reads, writes):
        return fw.op(eng, lambda e: e.tensor_tensor(out=out, in0=in0, in1=in1, op=op), reads=reads, writes=writes)

    def ts(eng, out, in0, s1, s2, op0, op1, reads, writes):
        if op1 is None:
            return fw.op(eng, lambda e: e.tensor_scalar(out=out, in0=in0, scalar1=s1, scalar2=None, op0=op0),
                         reads=reads, writes=writes)
        return fw.op(eng, lambda e: e.tensor_scalar(out=out, in0=in0, scalar1=s1, scalar2=s2, op0=op0, op1=op1),
                     reads=reads, writes=writes)

    def stt(out, in0, scalar, in1, op0, op1, reads, writes):
        return fw.op("dve", lambda e: e.scalar_tensor_tensor(out=out, in0=in0, scalar=scalar, in1=in1, op0=op0, op1=op1),
                     reads=reads, writes=writes)

    def mm(fn, reads, writes):
        return fw.op("pe", fn, reads=reads, writes=writes)

    pv = alloc([NPV])
    ident = alloc([128])
    ones_bf = alloc([128], BF16)
    modsb = alloc([48, 2])
    sc2 = alloc([8, 2])
    A1 = alloc([8, 2])
    A2 = alloc([8])
    cdec = alloc([2, 8])
    negdl = alloc([8])
    tmp_s = alloc([64])
    D1m = alloc([2, 264], BF16)
    D1f = alloc([264], BF16)
    zcol = alloc([2], BF16)
    Mtab = alloc([2, NF1, 128], BF16)
    CS = alloc([256], BF16)
    Gtab = alloc([128, 32], BF16)
    tbase = alloc([512])
    hdn3 = alloc([L], BF16)
    hcT = alloc([8, 256], BF16)
    GLOBAL_END = cur[0]

    def pvc(name, j=0, n=1):
        c0 = PV[name] + j
        return pv[:, c0:c0 + n]

    dma("sp", pv, pv_d, [], ["pv"], "pv")
    dma("sp", ident, t_ident, [], ["ident"], "ident")
    dma("sp", tbase, t_tbase, [], ["tbase"], "tbase")
    dma("pool", D1m.rearrange("p a b -> p (a b)"), t_D1m, [], ["D1m"], "D1m")
    dma("pool", Mtab.rearrange("p a b c -> p (a b c)"), t_M, [], ["Mtab"], "Mtab")
    dma("pool", D1f, t_D1f, [], ["D1f"], "D1f")
    fw.op("dve", lambda e: e.memset(zcol, 0.0), writes=["zcol"])
    for o_ in range(2):
        dma("sp", kb_d[o_, :, L:L + 2], zcol, ["zcol"], [f"kb_d{o_}_z"], f"kb_st{o_}")
    dma("pool", CS, t_CS, [], ["CS"], "CS")
    dma("pool", Gtab.rearrange("p a b -> p (a b)"), t_G, [], ["Gtab"], "Gtab")
    fw.op("dve", lambda e: e.memset(ones_bf, 1.0), writes=["ones"])

    p0_wm = [alloc([8, 256]) for _ in range(2)]
    p0_c2 = alloc([8, 2])
    p0_xt = [alloc([D]) for _ in range(2)]
    p0_xs = [alloc([D]) for _ in range(2)]
    p0_junk = alloc([D])
    p0_st = alloc([8])
    p0_hst = [alloc([8, 512], BF16) for _ in range(2)]
    p0_cf = [alloc([4096]) for _ in range(2)]
    p0_cb = [alloc([4096], BF16) for _ in range(2)]
    p0_zpos = [alloc([512]) for _ in range(2)]
    p0_hd = [alloc([512]) for _ in range(2)]
    p0_w = alloc([3, 64])
    p0_t1 = alloc([512])

    dma("sp", p0_c2.rearrange("p a b -> p (a b)"), c2_d, [], ["c2"], "c2")
    act(sc2, p0_c2, AF.Silu, ["c2"], ["sc2"])
    wmv = w_mod_d.rearrange("(kc p) n -> p kc n", p=128)
    pm, pmk = nextps()
    for piece in range(24):
        wt = p0_wm[piece % 2]
        dma("sp", wt, wmv[:, :, piece * 256:(piece + 1) * 256], [], [f"wm{piece % 2}"], f"wm{piece % 2}")

        def f(e, wt=wt, piece=piece):
            ins = None
            for jj in range(2):
                j = piece * 2 + jj
                for kc in range(8):
                    ins = e.matmul(psum[:, pm, 2 * j:2 * j + 2], lhsT=wt[:, kc, jj * 128:(jj + 1) * 128],
                                   rhs=sc2[:, kc, :], start=(kc == 0), stop=(kc == 7))
            return ins
        mm(f, [f"wm{piece % 2}", "sc2"], [pmk + f"_{piece}"] if piece else [pmk])
    bm = pv[:, PV["b_mod"]:PV["b_mod"] + 48]
    tt("dve", modsb, psum[:, pm, 0:96].rearrange("p (a b) -> p a b", b=2),
       bm.unsqueeze(2).broadcast_to([128, 48, 2]), ALU.add,
       ["pv", pmk] + [pmk + f"_{i}" for i in range(1, 24)], ["modsb"])
    ts("dve", A1, modsb[:, 8:16, :], 1.0, None, ALU.add, None, ["modsb"], ["A1"])
    tt("dve", A1, A1, pvc("norm1_g", 0, 8).unsqueeze(2).broadcast_to([128, 8, 2]), ALU.mult, ["A1", "pv"], ["A1"])
    ts("dve", A2, modsb[:, 32:40, 0], 1.0, None, ALU.add, None, ["modsb"], ["A2"])
    tt("dve", A2, A2, pvc("norm2_g", 0, 8), ALU.mult, ["A2", "pv"], ["A2"])
    lam = pv[:, PV["rg_lambda"]:PV["rg_lambda"] + 16]
    cdf = cdec.rearrange("p a b -> p (a b)")
    act(cdf, lam, AF.Exp, ["pv"], ["cdec"], scale=-1.0)
    act(cdf, cdf, AF.Ln, ["cdec"], ["cdec"], bias=1.0)
    ts("dve", cdf, cdf, -8.0, None, ALU.mult, None, ["cdec"], ["cdec"])
    ts("dve", negdl, pvc("deltas", 0, 8), -1.0, None, ALU.mult, None, ["pv"], ["negdl"])

    dma("sp", p0_w[0:33, 0, :], hy_w1_d, [], ["hw1"], "hw1")
    dma("sp", p0_w[0:64, 1, :], hy_w2_d, [], ["hw2"], "hw2")
    dma("sp", p0_w[0:64, 2, :], hy_w3_d, [], ["hw3"], "hw3")
    hs = PV["hy_small"]
    TWO_PI = 2.0 * np.pi
    fb = tmp_s[:, 0:3]
    tt("dve", fb[0:64], pv[0:64, hs:hs + 3], pv[0:64, hs + 3:hs + 4].broadcast_to([64, 3]), ALU.mult, ["pv"], ["fb"])
    fq = tmp_s[:, 4:5]
    ts("dve", fq[0:64], pv[0:64, hs + 3:hs + 4], 1.0 / TWO_PI, None, ALU.mult, None, ["pv"], ["fq"])
    ts("dve", fb[0:64], fb[0:64], 1.0 / TWO_PI, None, ALU.mult, None, ["fb"], ["fb"])
    for tt_ in range(NT):
        sl = slice(tt_ * 512, (tt_ + 1) * 512)
        zp = p0_zpos[tt_ % 2]
        zk = f"zpos{tt_ % 2}"
        dma("sp", zp, t_zpos[:, sl], [], [zk], zk)
        srcs = [zp, p0_hd[0], p0_hd[1]]
        srck = [zk, "hd0", "hd1"]
        dsts = [p0_hd[0], p0_hd[1], hdn3[:, sl]]
        dstk = ["hd0", "hd1", "hdn3"]
        for layer in range(3):
            kdim = 33 if layer == 0 else 64
            pi, pk = nextps()
            mm(lambda e, layer=layer, kdim=kdim, pi=pi, srcs=srcs: e.matmul(
                psum[0:64, pi, :], lhsT=p0_w[0:kdim, layer, :], rhs=srcs[layer][0:kdim, :], start=True, stop=True),
               [srck[layer], f"hw{layer + 1}"], [pk])
            ts("dve", p0_t1[0:64], psum[0:64, pi, :], fq[0:64], fb[0:64, layer:layer + 1], ALU.mult, ALU.add,
               [pk, "fq", "fb"], ["t1"])
            ts("dve", p0_junk[0:64, 0:512], p0_t1[0:64], 12582912.0, 12582912.0, ALU.add, ALU.subtract, ["t1"], ["junk"])
            tt("dve", p0_t1[0:64], p0_t1[0:64], p0_junk[0:64, 0:512], ALU.subtract, ["t1", "junk"], ["t1"])
            act(dsts[layer][0:64], p0_t1[0:64], AF.Sin, ["t1"], [dstk[layer]], scale=TWO_PI)

    def cast_load(i):
        nm, kc, cols, parts = WBLK[i]
        cf_ = p0_cf[i % 2]
        n = kc * cols
        cfv = cf_[:, 0:n].rearrange("p (a b) -> p a b", b=cols)
        c_at = 0
        for pi_, (src, col0, ncols) in enumerate(parts):
            sv = src.rearrange("(kc p) n -> p kc n", p=128)[:, :, col0:col0 + ncols]
            dma("sp", cfv[:, :, c_at:c_at + ncols], sv, [], [f"cf{i % 2}_{pi_}"], f"cf{i % 2}")
            c_at += ncols

    def cast_finish(i):
        nm, kc, cols, parts = WBLK[i]
        cf_ = p0_cf[i % 2]
        cb_ = p0_cb[i % 2]
        kb = f"cb{i % 2}"
        kfs = [f"cf{i % 2}_{pi_}" for pi_ in range(4)]
        n = kc * cols
        if i % 2 == 0:
            act(cb_[:, 0:n], cf_[:, 0:n], AF.Copy, kfs, [kb] + kfs[len(parts):])
        else:
            fw.op("dve", lambda e, cb_=cb_, cf_=cf_, n=n: e.tensor_copy(out=cb_[:, 0:n], in_=cf_[:, 0:n]), reads=kfs,
                  writes=[kb] + kfs[len(parts):])
        dma("pool", wblk_d[i, :, 0:n], cb_[:, 0:n], [kb], [f"wblk{i}"], f"wblkst{i % 2}")

    def norm_tiles(src_d, ntile, which, store_fn):
        for ti in range(ntile):
            if which == 0 and ti < NWB:
                cast_load(ti)
            xt = p0_xt[ti % 2]
            xs = p0_xs[ti % 2]
            kx, ks = f"xt{ti % 2}", f"xs{ti % 2}"
            dma("sp", xt, src_d[ti * 128:(ti + 1) * 128, :], [], [kx], kx)
            act(p0_junk, xt, AF.Square, [kx], ["junk", "ss"], accum=p0_st[:, 0:1])
            act(p0_st[:, 1:2], p0_st[:, 0:1], AF.Sqrt, ["ss"], ["rs"], scale=1.0 / D, bias=EPS)
            fw.op("dve", lambda e: e.reciprocal(out=p0_st[:, 2:3], in_=p0_st[:, 1:2]), reads=["rs"], writes=["rstd"])
            ts("dve", xs, xt, p0_st[:, 2:3], None, ALU.mult, None, [kx, "rstd"], [ks])
            for half in range(2):
                pi, pk = nextps()

                def f(e, xs=xs, pi=pi, half=half):
                    ins = None
                    for gg in range(4):
                        g = half * 4 + gg
                        ins = e.transpose(psum[:, pi, gg * 128:(gg + 1) * 128], xs[:, g * 128:(g + 1) * 128], ident)
                    return ins
                mm(f, [ks, "ident"], [pk])
                for gg in range(4):
                    g = half * 4 + gg
                    store_fn(ti, g, psum[:, pi, gg * 128:(gg + 1) * 128], pk)
            if which == 0 and ti < NWB:
                cast_finish(ti)

    def ctx_store(ti, g, src, pk):
        act(hcT[:, g, ti * 128:(ti + 1) * 128], src, AF.Identity, [pk, "A1", "modsb"], [f"hcT{g}"],
            scale=A1[:, g, 1:2], bias=modsb[:, g, 1:2])
    norm_tiles(ctx_d, 2, 1, ctx_store)

    def x_store(ti, g, src, pk):
        st = p0_hst[(ti // 4) % 2]
        kst = f"hst{(ti // 4) % 2}"
        act(st[:, g, (ti % 4) * 128:(ti % 4 + 1) * 128], src, AF.Identity, [pk, "A1", "modsb"], [kst + f"_{g}_{ti % 4}"],
            scale=A1[:, g, 0:1], bias=modsb[:, g, 0:1])
        if g == 7 and ti % 4 == 3:
            j = ti // 4
            dma("sp", hT_d[:, :, j * 512:(j + 1) * 512], st,
                [kst + f"_{gg}_{t4}" for gg in range(8) for t4 in range(4)], ["hT_d"], "hT_d")
    norm_tiles(x_d, 32, 0, x_store)

    fw.barrier()

    cur[0] = GLOBAL_END
    hTt = [alloc([8, 512], BF16) for _ in range(2)]
    wg = alloc([5, 8, 128], BF16)
    bd = alloc([4, 128])
    w4g = alloc([4, 128], BF16)
    sm = alloc([64])
    kbt = [alloc([512], BF16) for _ in range(2)]
    tq = [alloc([512]) for _ in range(6)]
    ucx = alloc([256])
    B1 = alloc([L])
    mark = cur[0]
    B2 = alloc([L])
    B3 = alloc([L])
    B4 = alloc([L])
    B5 = alloc([L])
    B6 = alloc([L])
    endB = cur[0]
    cur[0] = mark
    ctile = tq[0:2]
    gtile = tq[2:4]
    zA4 = alloc([16, 128], BF16)
    kA = alloc([32, 128], BF16)
    AZ = alloc([8448], BF16)
    Kf = alloc([NF1, 2, 64])
    Y3 = alloc([3 * NF1, 64], BF16)
    xk = [alloc([4, 2, 64]) for _ in range(2)]
    cur[0] = max(cur[0], endB)
    A_sb = AZ.rearrange("p (c pl f) -> p c pl f", pl=4, f=NF1)
    Z_sb = AZ[:, 0:8192].rearrange("p (c b) -> p c b", b=128)
    BDK = [f"bd{m4}{hh}" for m4 in range(4) for hh in range(2)]
    fw.op("dve", lambda e: e.memset(bd.rearrange("p a b -> p (a b)"), 0.0), writes=BDK)

    w_in_v = w_in_d.rearrange("(kc p) n -> p kc n", p=128)
    hTloads = [0]

    def load_hT(j):
        i = hTloads[0] % 2
        hTloads[0] += 1
        dma("sp", hTt[i], hT_d[:, :, j * 512:(j + 1) * 512], ["hT_d"], [f"hTt{i}"], f"hTt{i}")
        return hTt[i], f"hTt{i}"

    def conv_ops(out, okey, P, pkey, taps, bias, rl):
        o3 = out.rearrange("p (r c) -> p r c", c=rl)
        p3 = P.rearrange("p (r c) -> p r c", c=rl)
        wc = [w for (o, w) in taps if o == 0][0]
        ts("dve", out, P, wc, bias, ALU.mult, ALU.add, [pkey, "pv"], [okey])
        for (o, w) in taps:
            if o == 0:
                continue
            if o < 0:
                stt(o3[:, :, -o:], p3[:, :, :rl + o], w, o3[:, :, -o:], ALU.mult, ALU.add, [pkey, okey, "pv"], [okey])
            else:
                stt(o3[:, :, :rl - o], p3[:, :, o:], w, o3[:, :, :rl - o], ALU.mult, ALU.add, [pkey, okey, "pv"], [okey])

    def rglru(g, u, ukey, T, h0, h0k, hf, hfk, hb, hbk, t1, t1k, t3, t3k):
        nt = (T + 511) // 512
        w = min(T, 512)
        for d in range(2):
            hout, hk = (hf, hfk) if d == 0 else (hb, hbk)
            for m, dst, dk in ((0, t1, t1k), (1, t3, t3k)):
                bias = pvc("rg_ba" if m == 0 else "rg_bx", d * 8 + g)
                for j in range(nt):
                    pi, pk = nextps()
                    sl = slice(j * w, (j + 1) * w)
                    mm(lambda e, pi=pi, m=m, d=d, sl=sl: e.matmul(psum[:, pi, 0:w], lhsT=bd[:, 2 * m + d, :], rhs=u[:, sl],
                                                                  start=True, stop=True),
                       [f"bd{2 * m + d}0", f"bd{2 * m + d}1", ukey], [pk])
                    act(dst[:, sl], psum[:, pi, 0:w], AF.Sigmoid, [pk, "pv"], [dk], bias=bias)
            tt("pool", t3[:, 0:T], t3[:, 0:T], u[:, 0:T], ALU.mult, [t3k, ukey], [t3k])
            act(t1[:, 0:T], t1[:, 0:T], AF.Exp, [t1k, "cdec"], [t1k], scale=cdec[:, d, g:g + 1])
            tt("pool", hout[:, 0:T], t1[:, 0:T], t1[:, 0:T], ALU.mult, [t1k], [hk])
            act(hout[:, 0:T], hout[:, 0:T], AF.Sqrt, [hk], [hk], scale=-1.0, bias=1.0)
            tt("dve", t3[:, 0:T], t3[:, 0:T], hout[:, 0:T], ALU.mult, [t3k, hk], [t3k])
            init = 0.0 if h0 is None else h0[d]
            rk = [t1k, t3k] + ([] if h0 is None else [h0k[d]])
            if d == 0:
                fw.op("dve", lambda e, hout=hout, init=init: e.tensor_tensor_scan(
                    out=hout[:, 0:T], data0=t1[:, 0:T], data1=t3[:, 0:T], initial=init, op0=ALU.mult, op1=ALU.add),
                    reads=rk, writes=[hk])
            else:
                fw.op("dve", lambda e, hout=hout, init=init: e.tensor_tensor_scan(
                    out=hout[:, 0:T][:, ::-1], data0=t1[:, 0:T][:, ::-1], data1=t3[:, 0:T][:, ::-1], initial=init,
                    op0=ALU.mult, op1=ALU.add), reads=rk, writes=[hk])

    AKEYS = [f"A_{ch}" for ch in range(64)]

    def fwd_transform(src4, skeys, consume):
        stage1(src4, skeys)
        stage2(consume)

    def stage1(src4, skeys):
        for c in range(16):
            for pair in range(2):
                pi, pk = nextps()
                mm(lambda e, c=c, pair=pair, pi=pi: e.matmul(psum[:, pi, 0:264], lhsT=src4[:, c, :], rhs=D1m[:, pair, :],
                                                            start=True, stop=True), skeys + ["D1m"], [pk])
                dstv = AZ[:, c * 528 + pair * 264: c * 528 + (pair + 1) * 264]
                wk_ = [f"A_{4 * c + 2 * pair}", f"A_{4 * c + 2 * pair + 1}"]
                if (2 * c + pair) % 2 == 0:
                    act(dstv, psum[:, pi, 0:264], AF.Copy, [pk], wk_)
                else:
                    fw.op("dve", lambda e, dstv=dstv, pi=pi: e.tensor_copy(out=dstv, in_=psum[:, pi, 0:264]), reads=[pk], writes=wk_)

    def stage2(consume):
        for bi in range(9):
            f1lo = bi * 4
            nf = min(4, NF1 - f1lo)
            pi, pk = nextps()

            def f(e, f1lo=f1lo, nf=nf, pi=pi):
                ins = None
                for k in range(nf):
                    f1 = f1lo + k
                    e.matmul(psum[:, pi, k * 128:(k + 1) * 128], lhsT=Mtab[:, 0, f1, :],
                             rhs=A_sb[:, :, 0:2, f1].rearrange("p c pl -> p pl c"), start=True, stop=False)
                    ins = e.matmul(psum[:, pi, k * 128:(k + 1) * 128], lhsT=Mtab[:, 1, f1, :],
                                   rhs=A_sb[:, :, 2:4, f1].rearrange("p c pl -> p pl c"), start=False, stop=True)
                return ins
            mm(f, AKEYS + ["Mtab"], [pk])
            consume(bi, pi, pk, f1lo, nf)

    tqc = [0]

    def tqn():
        i = tqc[0] % 6
        tqc[0] += 1
        return tq[i], f"tq{i}"

    def load_wg(g):
        for seg in range(5):
            dma("pool", wg[:, seg, :, :], w_in_v[:, :, seg * D + g * 128: seg * D + (g + 1) * 128], [], [f"wg{seg}"], f"wg{seg}")

    def load_bd(g):
        for m4 in range(4):
            for hh in range(2):
                dma("sp", bd[64 * hh:64 * hh + 64, m4, 64 * hh:64 * hh + 64], rg_w_d[m4, 2 * g + hh], [], [f"bd{m4}{hh}"], f"bdl{m4}{hh}")

    def load_w4(g):
        for cmb in range(4):
            o, dr = cmb // 2, cmb % 2
            c0 = o * 2048 + dr * 1024 + g * 128
            dma("pool", w4g[0:64, cmb, :], hy_w4_d[:, c0:c0 + 128], [], [f"w4g{cmb}"], f"w4g{cmb}")

    load_wg(0)
    load_bd(0)
    load_w4(0)
    for g in range(NG):
        pi, pk = nextps()

        def f(e, pi=pi):
            ins = None
            for kc in range(8):
                ins = e.matmul(psum[:, pi, 0:256], lhsT=wg[:, 0, kc, :], rhs=hcT[:, kc, :], start=(kc == 0), stop=(kc == 7))
            return ins
        mm(f, ["wg0"] + [f"hcT{k}" for k in range(8)], [pk])
        pt, ptk = tqn()
        act(pt[:, 0:256], psum[:, pi, 0:256], AF.Identity, [pk, "pv"], [ptk], bias=pvc("b_in", 0 * 8 + g))
        rtaps = [(k - 2, pvc("rnn_conv_w", k * 8 + g)) for k in range(4)]
        conv_ops(ucx, "uc", pt[:, 0:256], ptk, rtaps, pvc("rnn_conv_b", g), 256)
        rglru(g, ucx, "uc", 256, None, None, B3, "B3", B4, "B4", B5, "B5", B6, "B6")
        fw.op("dve", lambda e: e.tensor_copy(out=sm[:, 0:1], in_=B3[:, 255:256]), reads=["B3"], writes=["h0f"])
        fw.op("dve", lambda e: e.tensor_copy(out=sm[:, 1:2], in_=B4[:, 0:1]), reads=["B4"], writes=["h0b"])

        nxt = load_hT(0)
        for j in range(NT):
            ht, hk = nxt
            if j + 1 < NT:
                nxt = load_hT(j + 1)
            sl = slice(j * 512, (j + 1) * 512)
            for seg in range(5):
                pi, pk = nextps()

                def f(e, pi=pi, seg=seg, ht=ht):
                    ins = None
                    for kc in range(8):
                        ins = e.matmul(psum[:, pi, :], lhsT=wg[:, seg, kc, :], rhs=ht[:, kc, :], start=(kc == 0), stop=(kc == 7))
                    return ins
                mm(f, [f"wg{seg}", hk], [pk])
                pt, ptk = tqn()
                act(pt, psum[:, pi, :], AF.Identity, [pk, "pv"], [ptk], bias=pvc("b_in", seg * 8 + g))
                if seg == 0:
                    conv_ops(B2[:, sl], "B2", pt, ptk, rtaps, pvc("rnn_conv_b", g), 64)
                elif seg == 1:
                    gq, gqk = tqn()
                    act(gq, pt, AF.Square, [ptk], [gqk])
                    ts("pool", gq, gq, 0.044715, 1.0, ALU.mult, ALU.add, [gqk], [gqk])
                    tt("pool", gq, gq, pt, ALU.mult, [gqk, ptk], [gqk])
                    act(gq, gq, AF.Tanh, [gqk], [gqk], scale=0.7978845608028654)
                    stt(B6[:, sl], gq, 1.0, pt, ALU.add, ALU.mult, [gqk, ptk], ["B6"])
                else:
                    si = seg - 2
                    htaps = [(k - 1, pvc("hy_conv_w", k * 24 + si * 8 + g)) for k in range(3)]
                    if si == 0:
                        conv_ops(B1[:, sl], "B1", pt, ptk, htaps, pvc("hy_conv_b", si * 8 + g), 64)
                    else:
                        ct, ctk = tqn()
                        conv_ops(ct, ctk, pt, ptk, htaps, pvc("hy_conv_b", si * 8 + g), 64)
                        dma("sp", gate_d[si - 1, :, sl], ct, [ctk], [f"gate_d{si - 1}"], f"gate_st{si - 1}")
            dt_, dk_ = tqn()
            ts("dve", sm[:, 8 + j % 2:9 + j % 2], negdl[:, g:g + 1], float(j * 512) / float(L - 1), None, ALU.mult, None,
               ["negdl"], [f"decb{j % 2}"])
            act(dt_, tbase, AF.Exp, ["tbase", "negdl", f"decb{j % 2}"], [dk_], scale=negdl[:, g:g + 1],
                bias=sm[:, 8 + j % 2:9 + j % 2])
            for cmb in range(4):
                o_, dr = cmb // 2, cmb % 2
                pi, pk = nextps()
                mm(lambda e, pi=pi, cmb=cmb, sl=sl: e.matmul(psum[:, pi, :], lhsT=w4g[0:64, cmb, :], rhs=hdn3[0:64, sl],
                                                             start=True, stop=True), [f"w4g{cmb}", "hdn3"], [pk])
                kt = kbt[cmb % 2]
                ktk = f"kbt{cmb % 2}"
                if dr == 0:
                    tt("dve", kt, psum[:, pi, :], dt_, ALU.mult, [pk, dk_], [ktk])
                else:
                    tt("dve", kt[:, ::-1], psum[:, pi, :], dt_, ALU.mult, [pk, dk_], [ktk])
                    if j == 0:
                        fw.op("dve", lambda e, kt=kt: e.memset(kt[:, 511:512], 0.0), reads=[ktk], writes=[ktk])
                jq, jqk = tqn()
                ncol = 16 + o_ * 24 + dr * 8 + j
                act(jq, kt, AF.Abs, [ktk], [jqk, f"nrm{cmb}_{j}"], accum=sm[:, ncol:ncol + 1])
                if dr == 0:
                    dma("sp", kb_d[o_, :, sl], kt, [ktk], [f"kb_d{o_}_{dr}_{j}"], f"kb_st{o_}")
                else:
                    s0 = L + 3585 - 512 * j
                    n_ = 511 if j == 0 else 512
                    dma("sp", kb_d[o_, :, s0:s0 + n_], kt[:, 0:n_], [ktk], ([f"kb_d{o_}_z"] if j == 7 else []) + [f"kb_d{o_}_{dr}_{j}"], f"kb_st{o_}")
        for o_ in range(2):
            base = 16 + o_ * 24
            fw.op("dve", lambda e, base=base: e.reduce_sum(out=sm[:, base + 16:base + 17], in_=sm[:, base:base + 16],
                                                          axis=mybir.AxisListType.X),
                  reads=[f"nrm{o_ * 2 + dr}_{j}" for dr in range(2) for j in range(NT)], writes=[f"nrmsum{o_}"])
            fw.op("dve", lambda e, base=base: e.reciprocal(out=sm[:, base + 17:base + 18], in_=sm[:, base + 16:base + 17]),
                  reads=[f"nrmsum{o_}"], writes=[f"invn{o_}"])
        if g + 1 < NG:
            load_wg(g + 1)
        rglru(g, B2, "B2", L, [sm[:, 0:1], sm[:, 1:2]], ["h0f", "h0b"], B3, "B3", B2, "B2", B5, "B5", B4, "B4")
        if g + 1 < NG:
            load_bd(g + 1)
        tt("dve", B3, B3, B2, ALU.add, ["B3", "B2"], ["B3"])
        stt(B3, B3, 0.5, B6, ALU.mult, ALU.mult, ["B3", "B6"], ["B3"])
        dma("pool", yr_d[:, g, :], B3, ["B3"], ["yr_d"], "yr_st")
        fw.barrier()

        def FT(o, h):
            KA = [f"kA_{q}" for q in range(2)]
            for q2 in range(2):
                c0 = 64 * h + q2
                dma("sp", kA[64 * q2:64 * q2 + 64, :, :],
                    kb_d[o, c0:64 * (h + 1):2, :].rearrange("c (a b) -> a c b", b=128),
                    [f"kb_d{o}_{dr_}_{j_}" for dr_ in range(2) for j_ in range(NT)], [KA[q2]], "kA")
            for c in range(32):
                pi, pk = nextps()
                mm(lambda e, c=c, pi=pi: e.matmul(psum[:, pi, 0:264], lhsT=kA[:, c, :], rhs=D1f, start=True, stop=True),
                   KA + ["D1f"], [pk])
                dstv = AZ[:, c * 264:(c + 1) * 264]
                wk_ = [f"A_{2 * c}", f"A_{2 * c + 1}"]
                if c % 2 == 0:
                    act(dstv, psum[:, pi, 0:264], AF.Copy, [pk], wk_)
                else:
                    fw.op("dve", lambda e, dstv=dstv, pi=pi: e.tensor_copy(out=dstv, in_=psum[:, pi, 0:264]), reads=[pk], writes=wk_)

            def consume_k(bi, pi, pk, f1lo, nf):
                src = psum[:, pi, 0:nf * 128].rearrange("p (f r c) -> p f r c", r=2, c=64)
                act(Kf[:, f1lo:f1lo + nf, :, :], src, AF.Copy, [pk], [f"Kf{bi}"])
            stage2(consume_k)

        def DT(o, h):
            ZA = [f"zA4_{q}" for q in range(4)]
            for q in range(4):
                c0 = 64 * h + q
                dma("sp", zA4[32 * q:32 * q + 32, :, :], zb_d[c0:64 * (h + 1):4, :].rearrange("c (a b) -> a c b", b=128),
                    ["zb_d"], [ZA[q]], "zA4")

            def consume_x(bi, pi, pk, f1lo, nf):
                X = psum[:, pi, 0:nf * 128].rearrange("p (f r c) -> p f r c", r=2, c=64)
                K = Kf[:, f1lo:f1lo + nf, :, :]
                Ksw = Kf[:, f1lo:f1lo + nf, ::-1, :]
                t1_ = xk[0][:, 0:nf]
                t2_ = xk[1][:, 0:nf]
                tt("dve", t1_, X, K, ALU.mult, [pk, f"Kf{bi}"], ["xk0"])
                tt("dve", t2_, X, Ksw, ALU.mult, [pk, f"Kf{bi}"], ["xk1"])
                yv = [Y3[:, pl * NF1 + f1lo: pl * NF1 + f1lo + nf, :] for pl in range(3)]
                tt("dve", yv[1], t1_[:, :, 0, :], t1_[:, :, 1, :], ALU.subtract, ["xk0"], [f"Y3r_{bi}"])
                tt("dve", yv[2], t2_[:, :, 0, :], t2_[:, :, 1, :], ALU.add, ["xk1"], [f"Y3i_{bi}"])
                act(yv[0], yv[2], AF.Copy, [f"Y3i_{bi}"], [f"Y3n_{bi}"], scale=-1.0)
            fwd_transform(zA4, ZA, consume_x)
            ykeys = [f"Y3{x_}_{bi}" for bi in range(9) for x_ in "rin"]
            for c4 in range(16):
                pi, pk = nextps()

                def f(e, c4=c4, pi=pi):
                    ins = None
                    for cc in range(4):
                        ch = c4 * 4 + cc
                        e.matmul(psum[0:66, pi, cc * 128:(cc + 1) * 128],
                                 lhsT=Y3[:, NF1:3 * NF1, ch], rhs=CS[:, 0:128], start=True, stop=False)
                        ins = e.matmul(psum[0:66, pi, cc * 128:(cc + 1) * 128],
                                       lhsT=Y3[:, 0:2 * NF1, ch], rhs=CS[:, 128:256],
                                       start=False, stop=True)
                    return ins
                mm(f, ykeys + ["CS"], [pk])
                act(Z_sb[0:66, c4 * 4:(c4 + 1) * 4, :],
                    psum[0:66, pi, :].rearrange("p (c b) -> p c b", b=128), AF.Copy, [pk], [f"Z_{c4}"] + AKEYS)
            zkeys = [f"Z_{c4}" for c4 in range(16)]
            icol = 16 + o * 24 + 17
            for bj in range(8):
                pi, pk = nextps()

                def f(e, bj=bj, pi=pi):
                    ins = None
                    for bb in range(16):
                        b_ = bj * 16 + bb
                        ins = e.matmul(psum[0:64, pi, bb * 32:(bb + 1) * 32], lhsT=Z_sb[0:66, :, b_], rhs=Gtab[0:66, b_, :],
                                       start=True, stop=True)
                    return ins
                mm(f, zkeys + AKEYS + ["Gtab"], [pk])
                ct = ctile[bj % 2]
                ctk = f"ct{bj % 2}"
                fw.op("dve", lambda e, ct=ct, pi=pi, h=h: e.tensor_copy(out=ct[64 * h:64 * h + 64, :], in_=psum[0:64, pi, :]),
                      reads=[pk], writes=[ctk])
                zv = B1[64 * h:64 * h + 64, :].rearrange("p (a b) -> p a b", b=128)[:, :, bj * 16:(bj + 1) * 16]
                stt(zv, ct[64 * h:64 * h + 64, :].rearrange("p (b a) -> p a b", a=32), sm[64 * h:64 * h + 64, icol:icol + 1], zv,
                    ALU.mult, ALU.add, [ctk, f"invn{o}", "B1"], ["B1"])

        passes = [(o, h) for o in range(2) for h in range(2)]
        dma("pool", zb_d, B1, ["B1"], ["zb_d"], "zb_st")
        FT(0, 0)
        ts("dve", B1, B1, pvc("hy_skip", 0 * 8 + g), None, ALU.mult, None, ["B1", "pv"], ["B1"])
        for pidx, (o, h) in enumerate(passes):
            DT(o, h)
            if pidx + 1 < len(passes):
                FT(*passes[pidx + 1])
            if h == 1:
                for j in range(NT):
                    sl = slice(j * 512, (j + 1) * 512)
                    gt = gtile[j % 2]
                    dma("sp", gt, gate_d[o, :, sl], [f"gate_d{o}"], [f"gt{j % 2}"], f"gt{j % 2}")
                    tt("dve", B1[:, sl], B1[:, sl], gt, ALU.mult, ["B1", f"gt{j % 2}"], ["B1"])
                if o == 0:
                    dma("pool", zb_d, B1, ["B1"], ["zb_d"], "zb_st")
                    ts("dve", B1, B1, pvc("hy_skip", 1 * 8 + g), None, ALU.mult, None, ["B1", "pv"], ["B1"])
        dma("pool", yh_d[:, g, :], B1, ["B1"], ["yh_d"], "yh_st")
        if g + 1 < NG:
            load_w4(g + 1)
        fw.barrier()

    cur[0] = GLOBAL_END
    ring = [alloc([4096], BF16) for _ in range(3)]
    yrt = alloc([8, 512], BF16)
    yht = alloc([8, 512], BF16)
    hTc = alloc([8, 512], BF16)
    xtok = alloc([4, D])
    xT = alloc([8, 512])
    merged = alloc([8, 512], BF16)
    gts = [alloc([512]) for _ in range(4)]
    sq = merged
    rstd = alloc([512])
    h2 = alloc([8, 512], BF16)
    hid = alloc([22, 512], BF16)
    wstate = {"issued": 0, "views": {}}
    TOTAL_W = NT * NWB

    def issue_w(upto):
        while wstate["issued"] < min(upto, TOTAL_W):
            sidx = wstate["issued"]
            i = sidx % NWB
            s_ = sidx % 3
            nm, kc, cols, parts = WBLK[i]
            n = kc * cols
            dma("sp", ring[s_][:, 0:n], wblk_d[i, :, 0:n], [f"wblk{i}"], [f"ring{s_}"], f"ring{s_}")
            wstate["views"][sidx] = (ring[s_][:, 0:n].rearrange("p (a b) -> p a b", b=cols), f"ring{s_}")
            wstate["issued"] += 1

    def get_w(j, i):
        sidx = j * NWB + i
        issue_w(sidx + 3)
        return wstate["views"][sidx]

    for j in range(NT):
        sl = slice(j * 512, (j + 1) * 512)
        dma("sp", yrt, yr_d[:, :, sl], ["yr_d"], ["yrt"], "yrt")
        dma("sp", yht, yh_d[:, :, sl], ["yh_d"], ["yht"], "yht")
        dma("sp", hTc, hT_d[:, :, sl], ["hT_d"], ["hTc"], "hTc")
        dma("sp", xtok, x_d[j * 512:(j + 1) * 512, :].rearrange("(a p) n -> p a n", p=128), [], ["xtok"], "xtok")
        for a4 in range(4):
            for half in range(2):
                pi, pk = nextps()

                def f(e, a4=a4, half=half, pi=pi):
                    ins = None
                    for gg in range(4):
                        g = half * 4 + gg
                        ins = e.transpose(psum[:, pi, gg * 128:(gg + 1) * 128], xtok[:, a4, g * 128:(g + 1) * 128], ident)
                    return ins
                mm(f, ["xtok", "ident"], [pk])
                act(xT[:, half * 4:(half + 1) * 4, a4 * 128:(a4 + 1) * 128],
                    psum[:, pi, :].rearrange("p (g t) -> p g t", t=128), AF.Copy, [pk], ["xT"])
        for m in range(8):
            wv, wk = get_w(j, m)
            pis = []
            for li, (mov, mk_) in enumerate(((yrt, "yrt"), (yht, "yht"), (hTc, "hTc"), (hTc, "hTc"))):
                pi, pk = nextps()

                def f(e, wv=wv, mov=mov, li=li, pi=pi):
                    ins = None
                    for kc in range(8):
                        ins = e.matmul(psum[:, pi, :], lhsT=wv[:, kc, li * 128:(li + 1) * 128], rhs=mov[:, kc, :],
                                       start=(kc == 0), stop=(kc == 7))
                    return ins
                mm(f, [wk, mk_], [pk])
                pis.append((pi, pk))
            act(gts[0], psum[:, pis[2][0], :], AF.Sigmoid, [pis[2][1], "pv"], ["gt0"], bias=pvc("b_in", 40 + m))
            act(gts[1], psum[:, pis[3][0], :], AF.Sigmoid, [pis[3][1], "pv"], ["gt1"], bias=pvc("b_in", 48 + m))
            tt("dve", gts[2], psum[:, pis[0][0], :], gts[0], ALU.mult, [pis[0][1], "gt0"], ["gt2"])
            tt("dve", gts[3], psum[:, pis[1][0], :], gts[1], ALU.mult, [pis[1][1], "gt1"], ["gt3"])
            tt("pool", merged[:, m, :], gts[2], gts[3], ALU.add, ["gt2", "gt3"], ["merged"])
        for m in range(8):
            wv, wk = get_w(j, 8 + m // 4)
            col = m % 4
            pi, pk = nextps()

            def f(e, wv=wv, col=col, pi=pi):
                ins = None
                for kc in range(8):
                    ins = e.matmul(psum[:, pi, :], lhsT=wv[:, kc, col * 128:(col + 1) * 128], rhs=merged[:, kc, :],
                                   start=(kc == 0), stop=(kc == 7))
                return ins
            mm(f, [wk, "merged"], [pk])
            stt(xT[:, m, :], psum[:, pi, :], modsb[:, 16 + m, 0:1], xT[:, m, :], ALU.mult, ALU.add, [pk, "modsb", "xT"], ["xT"])

        def rms_rstd():
            act(sq, xT, AF.Square, ["xT"], ["sq"])
            pi, pk = nextps()

            def f(e, pi=pi):
                ins = None
                for kc in range(8):
                    ins = e.matmul(psum[:, pi, :], lhsT=ones_bf, rhs=sq[:, kc, :], start=(kc == 0), stop=(kc == 7))
                return ins
            mm(f, ["ones", "sq"], [pk])
            act(rstd, psum[:, pi, :], AF.Sqrt, [pk], ["rstd_"], scale=1.0 / D, bias=EPS)
            fw.op("dve", lambda e: e.reciprocal(out=rstd, in_=rstd), reads=["rstd_"], writes=["rstd_"])
        rms_rstd()
        for m in range(8):
            tt("dve", gts[m % 2], xT[:, m, :], rstd, ALU.mult, ["xT", "rstd_"], [f"gt{m % 2}"])
            act(h2[:, m, :], gts[m % 2], AF.Identity, [f"gt{m % 2}", "A2", "modsb"], ["h2"], scale=A2[:, m:m + 1],
                bias=modsb[:, 24 + m, 0:1])
        for m in range(22):
            wv, wk = get_w(j, 10 + m // 2)
            pis = []
            for part in range(2):
                col = part * 2 + (m % 2)
                pi, pk = nextps()

                def f(e, wv=wv, col=col, pi=pi):
                    ins = None
                    for kc in range(8):
                        ins = e.matmul(psum[:, pi, :], lhsT=wv[:, kc, col * 128:(col + 1) * 128], rhs=h2[:, kc, :],
                                       start=(kc == 0), stop=(kc == 7))
                    return ins
                mm(f, [wk, "h2"], [pk])
                pis.append((pi, pk))
            act(gts[m % 2], psum[:, pis[0][0], :], AF.Silu, [pis[0][1]], [f"gt{m % 2}"])
            tt("dve", hid[:, m, :], psum[:, pis[1][0], :], gts[m % 2], ALU.mult, [pis[1][1], f"gt{m % 2}"], ["hid"])
        for m in range(8):
            wv, wk = get_w(j, 21 + m)
            pi, pk = nextps()

            def f(e, wv=wv, pi=pi):
                ins = None
                for kc in range(22):
                    ins = e.matmul(psum[:, pi, :], lhsT=wv[:, kc, :], rhs=hid[:, kc, :], start=(kc == 0), stop=(kc == 21))
                return ins
            mm(f, [wk, "hid"], [pk])
            stt(xT[:, m, :], psum[:, pi, :], modsb[:, 40 + m, 0:1], xT[:, m, :], ALU.mult, ALU.add, [pk, "modsb", "xT"], ["xT"])
        rms_rstd()
        for m in range(8):
            stt(xT[:, m, :], xT[:, m, :], pvc("final_g", m), rstd, ALU.mult, ALU.mult, ["xT", "rstd_", "pv"], ["xT"])
        for a4 in range(4):
            for half in range(2):
                pi, pk = nextps()

                def f(e, a4=a4, half=half, pi=pi):
                    ins = None
                    for gg in range(4):
                        g = half * 4 + gg
                        ins = e.transpose(psum[:, pi, gg * 128:(gg + 1) * 128], xT[:, g, a4 * 128:(a4 + 1) * 128], ident)
                    return ins
                mm(f, ["xT", "ident"], [pk])
                act(xtok[:, a4, half * 512:(half + 1) * 512], psum[:, pi, :], AF.Copy, [pk], ["xtok"])
        dma("sp", out_d[j * 512:(j + 1) * 512, :].rearrange("(a p) n -> p a n", p=128), xtok, ["xtok"], ["out_d"], "out_st")

    fw.op("sp", None, extra=fw.all_tokens())
    assert len(fw.dsem) + 5 <= 100, len(fw.dsem)
    fw.emit()
    return nc


_CACHE = {}


def prep_inputs(inp, tb, b):
    PV, NPV = pv_layout()
    pv = np.zeros((128, NPV), np.float32)

    def put(name, arr):
        a = fm(arr)
        pv[:, PV[name]:PV[name] + a.shape[1]] = a
    put("norm1_g", inp["norm1_g"][0])
    put("norm2_g", inp["norm2_g"][0])
    put("final_g", inp["final_g"])
    put("rnn_conv_b", inp["rnn_conv_b"][0])
    put("deltas", tb["deltas"])
    put("b_mod", inp["b_mod"][0])
    put("b_in", inp["b_in"][0])
    put("rnn_conv_w", inp["rnn_conv_w"][0].reshape(-1))
    put("rg_ba", inp["rg_ba"][0].reshape(-1))
    put("rg_bx", inp["rg_bx"][0].reshape(-1))
    put("rg_lambda", inp["rg_lambda"][0].reshape(-1))
    put("hy_conv_w", inp["hy_conv_w"][0].reshape(-1))
    put("hy_conv_b", inp["hy_conv_b"][0])
    put("hy_skip", inp["hy_skip"][0].reshape(-1))
    hs = PV["hy_small"]
    pv[:64, hs + 0] = inp["hy_b1"][0]
    pv[:64, hs + 1] = inp["hy_b2"][0]
    pv[:64, hs + 2] = inp["hy_b3"][0]
    pv[:64, hs + 3] = inp["hy_freq"][0]
    c2 = np.stack([fm(inp["c"][b]), fm(inp["c_ctx"])], axis=-1).reshape(128, 16)
    m = {
        "x": np.ascontiguousarray(inp["x"][b]),
        "ctx": np.ascontiguousarray(inp["ctx"][b]),
        "c2": np.ascontiguousarray(c2, dtype=np.float32),
        "pv": pv,
        "w_mod": inp["w_mod"][0], "w_in": inp["w_in"][0],
        "rg_w": np.ascontiguousarray(np.concatenate([inp["rg_wa"][0], inp["rg_wx"][0]], axis=0)),
        "hy_w1": inp["hy_w1"][0], "hy_w2": inp["hy_w2"][0], "hy_w3": inp["hy_w3"][0], "hy_w4": inp["hy_w4"][0],
        "w_a_out": inp["w_a_out"][0], "w_b_out": inp["w_b_out"][0], "w_out": inp["w_out"][0],
        "w_ffn_in": inp["w_ffn_in"][0], "w_ffn_out": inp["w_ffn_out"][0],
        "t_D1m": tb["D1m"], "t_D1f": tb["D1f"], "t_M": tb["M"], "t_CS": tb["CS"], "t_G": tb["G"], "t_zpos": tb["zposT"],
        "t_tbase": tb["tbase"], "t_ident": tb["ident"],
    }
    return {k: np.ascontiguousarray(np.asarray(v, dtype=np.float32)) for k, v in m.items()}


def kernel(**inputs):
    inp = {k: np.asarray(v) for k, v in inputs.items()}
    tb = make_tables()
    if "nc" not in _CACHE:
        _CACHE["nc"] = build_program(False)
    nc = _CACHE["nc"]
    in_maps = [prep_inputs(inp, tb, b) for b in range(8)]
    res = run_bass_kernel_spmd(nc, in_maps, core_ids=list(range(8)))
    out = np.stack([np.asarray(r["out"]) for r in res.results], axis=0)
    return out.astype(np.float32)
```

```python
import numpy as np
import concourse.bass as bass
import concourse.mybir as mybir
from concourse.bass_utils import run_bass_kernel_spmd

F32 = mybir.dt.float32
BF16 = mybir.dt.bfloat16
AF = mybir.ActivationFunctionType
ALU = mybir.AluOpType

D = 1024
L = 4096
NFFT = 8192
NF1 = 33
DFF = 2816
EPS = 1e-6
NG = 8
NT = 8


class Tok:
    __slots__ = ("sem", "val")

    def __init__(self, sem, val):
        self.sem = sem
        self.val = val


class Fw:
    ENG = ("pe", "act", "dve", "pool", "sp")

    def __init__(self, nc):
        self.nc = nc
        self.ops = {e: [] for e in self.ENG}
        self.cnt = {e: 0 for e in self.ENG}
        self.esem = {e: nc.alloc_semaphore(name=f"cs_{e}") for e in self.ENG}
        self.waited = {e: {} for e in self.ENG}
        self.bufs = {}
        self.dsem = {}

    def op(self, eng, fn, reads=(), writes=(), dma=None, extra=()):
        deps = list(extra)
        for k in reads:
            b = self.bufs.get(k)
            if b is not None and b[0] is not None:
                deps.append(b[0])
        for k in writes:
            b = self.bufs.get(k)
            if b is not None:
                if b[1]:
                    deps.extend(b[1])
                elif b[0] is not None:
                    deps.append(b[0])
        best = {}
        for t in deps:
            k = id(t.sem)
            if k not in best or best[k].val < t.val:
                best[k] = t
        waits = []
        w = self.waited[eng]
        for k, t in best.items():
            if t.val > w.get(k, 0):
                w[k] = t.val
                waits.append((t.sem, t.val))
        if dma is None:
            self.cnt[eng] += 1
            tok = Tok(self.esem[eng], self.cnt[eng])
            inc = 1
        else:
            s = self.dsem.get(dma)
            if s is None:
                s = [self.nc.alloc_semaphore(name=f"ds_{len(self.dsem)}"), 0]
                self.dsem[dma] = s
            s[1] += 16
            tok = Tok(s[0], s[1])
            inc = 16
        self.ops[eng].append((waits, fn, tok.sem, inc))
        for k in reads:
            b = self.bufs.get(k)
            if b is None:
                b = [None, []]
                self.bufs[k] = b
            b[1].append(tok)
        for k in writes:
            self.bufs[k] = [tok, []]
        return tok

    def all_tokens(self):
        toks = []
        for e in self.ENG:
            if self.cnt[e] > 0:
                toks.append(Tok(self.esem[e], self.cnt[e]))
        for k, s in self.dsem.items():
            if s[1] > 0:
                toks.append(Tok(s[0], s[1]))
        return toks

    def barrier(self):
        toks = self.all_tokens()
        for e in self.ENG:
            self.op(e, None, extra=toks)

    def emit(self):
        nc = self.nc
        with nc.Block() as block:
            def mk(ename):
                def body(e):
                    for waits, fn, sem, inc in self.ops[ename]:
                        for s, v in waits:
                            e.wait_ge(s, v)
                        ins = e.nop() if fn is None else fn(e)
                        ins.then_inc(sem, inc)
                return body
            block.tensor(mk("pe"))
            block.scalar(mk("act"))
            block.vector(mk("dve"))
            block.gpsimd(mk("pool"))
            block.sync(mk("sp"))


def make_tables():
    tb = {}
    a = np.arange(32, dtype=np.float64)[:, None]
    f1 = np.arange(NF1, dtype=np.float64)[None, :]
    ang = 2 * np.pi * a * f1 / 64.0
    C, S = np.cos(ang), np.sin(ang)
    blk = np.concatenate([C, -S, S, C], axis=1)
    D1m = np.zeros((128, 2, 264), np.float64)
    for pair in range(2):
        for qq in range(2):
            q = 2 * pair + qq
            D1m[32 * q:32 * q + 32, pair, qq * 132:(qq + 1) * 132] = blk
    tb["D1m"] = D1m.reshape(128, 528).astype(np.float32)
    a64 = np.arange(64, dtype=np.float64)[:, None]
    ang = 2 * np.pi * a64 * f1 / 64.0
    C6, S6 = np.cos(ang), np.sin(ang)
    blk6 = np.concatenate([C6, -S6, S6, C6], axis=1)
    D1f = np.zeros((128, 264), np.float64)
    for q2 in range(2):
        D1f[64 * q2:64 * q2 + 64, q2 * 132:(q2 + 1) * 132] = blk6
    tb["D1f"] = D1f.astype(np.float32)
    b = np.arange(128, dtype=np.float64)[:, None]
    f2 = np.arange(128, dtype=np.float64)[None, :]
    M = np.zeros((128, 2, NF1, 128))
    for k in range(NF1):
        ang = -2 * np.pi * b * (k + 64 * f2) / NFFT
        M[:, 0, k, :] = np.cos(ang)
        M[:, 1, k, :] = np.sin(ang)
    tb["M"] = M.reshape(128, 2 * NF1 * 128).astype(np.float32)
    ang = 2 * np.pi * f2.T * b.T / 128.0
    tb["CS"] = np.concatenate([np.cos(ang), np.sin(ang)], axis=1).astype(np.float32)
    c = np.full(NF1, 2.0)
    c[0] = 1.0
    c[NF1 - 1] = 1.0
    f1v = np.arange(NF1, dtype=np.float64)[:, None, None]
    bv = np.arange(128, dtype=np.float64)[None, :, None]
    av = np.arange(32, dtype=np.float64)[None, None, :]
    ang = 2 * np.pi * (av * f1v / 64.0 + bv * f1v / NFFT)
    Gr = c[:, None, None] / NFFT * np.cos(ang)
    Gi = c[:, None, None] / NFFT * np.sin(ang)
    G = np.zeros((128, 128 * 32), np.float64)
    G[:66] = np.concatenate([Gr, -Gi], axis=0).reshape(66, 128 * 32)
    tb["G"] = G.astype(np.float32)
    t = np.linspace(0.0, 1.0, L, dtype=np.float32)[:, None]
    w = (2.0 * np.pi * np.arange(L, dtype=np.float32)[:, None] / L).astype(np.float32)
    f = np.linspace(1e-4, 15, 16, dtype=np.float32)[None, :]
    fw_ = (f * w).astype(np.float32)
    z = np.concatenate([t, np.cos(fw_), -np.sin(fw_)], axis=-1).astype(np.float32)
    zp = np.zeros((128, L), np.float32)
    zp[:33] = z.T
    tb["zposT"] = zp
    max_decay = np.log(1e-2) / 0.3
    min_decay = np.log(1e-2) / 1.5
    tb["deltas"] = np.abs(np.linspace(min_decay, max_decay, D, dtype=np.float32)).astype(np.float32)
    tb["tbase"] = np.broadcast_to(t[:512, 0][None, :], (128, 512)).astype(np.float32).copy()
    tb["ident"] = np.eye(128, dtype=np.float32)
    return tb


def pv_layout():
    cols = {}
    n = [0]

    def add(name, k):
        cols[name] = n[0]
        n[0] += k
    for nm in ["norm1_g", "norm2_g", "final_g", "rnn_conv_b", "deltas"]:
        add(nm, 8)
    add("b_mod", 48)
    add("b_in", 56)
    add("rnn_conv_w", 32)
    add("rg_ba", 16)
    add("rg_bx", 16)
    add("rg_lambda", 16)
    add("hy_conv_w", 72)
    add("hy_conv_b", 24)
    add("hy_skip", 16)
    add("hy_small", 4)
    return cols, n[0]


def fm(v):
    v = np.asarray(v, np.float32).reshape(-1, 128)
    return np.ascontiguousarray(v.T)


def build_program(debug=False):
    nc = bass.Bass("TRN2", target_bir_lowering=False)
    fw = Fw(nc)
    PV, NPV = pv_layout()

    def din(name, shape, dt=F32):
        return nc.dram_tensor(name, list(shape), dt, kind="ExternalInput").ap()

    def dscr(name, shape, dt):
        return nc.dram_tensor(name, list(shape), dt, kind="Internal").ap()

    x_d = din("x", [L, D])
    ctx_d = din("ctx", [256, D])
    c2_d = din("c2", [128, 16])
    pv_d = din("pv", [128, NPV])
    w_mod_d = din("w_mod", [D, 6 * D])
    w_in_d = din("w_in", [D, 7 * D])
    rg_w_d = din("rg_w", [4, 16, 64, 64])
    hy_w1_d = din("hy_w1", [33, 64])
    hy_w2_d = din("hy_w2", [64, 64])
    hy_w3_d = din("hy_w3", [64, 64])
    hy_w4_d = din("hy_w4", [64, 4096])
    w_a_d = din("w_a_out", [D, D])
    w_b_d = din("w_b_out", [D, D])
    w_o_d = din("w_out", [D, D])
    w_fi_d = din("w_ffn_in", [D, 2 * DFF])
    w_fo_d = din("w_ffn_out", [DFF, D])
    t_D1m = din("t_D1m", [128, 528])
    t_D1f = din("t_D1f", [128, 264])
    t_M = din("t_M", [128, 2 * NF1 * 128])
    t_CS = din("t_CS", [128, 256])
    t_G = din("t_G", [128, 4096])
    t_zpos = din("t_zpos", [128, L])
    t_tbase = din("t_tbase", [128, 512])
    t_ident = din("t_ident", [128, 128])
    out_d = nc.dram_tensor("out", [L, D], F32, kind="ExternalOutput").ap()

    okind = "ExternalOutput" if debug else "Internal"
    hT_d = dscr("hT_s", [128, 8, L], BF16)
    yr_d = nc.dram_tensor("yr_s", [128, 8, L], BF16, kind=okind).ap()
    yh_d = nc.dram_tensor("yh_s", [128, 8, L], BF16, kind=okind).ap()
    gate_d = dscr("gate_s", [2, 128, L], F32)
    zb_d = dscr("zb_s", [128, L], BF16)
    kb_d = dscr("kb_s", [2, 128, 2 * L], BF16)
    WBLK = []
    for m in range(8):
        WBLK.append(("mix", 8, 512, [(w_a_d, m * 128, 128), (w_b_d, m * 128, 128),
                                      (w_in_d, 5 * D + m * 128, 128), (w_in_d, 6 * D + m * 128, 128)]))
    for blk in range(2):
        WBLK.append(("wo", 8, 512, [(w_o_d, blk * 512, 512)]))
    for blk in range(11):
        WBLK.append(("wfi", 8, 512, [(w_fi_d, blk * 256, 256), (w_fi_d, DFF + blk * 256, 256)]))
    for blk in range(8):
        WBLK.append(("wfo", 22, 128, [(w_fo_d, blk * 128, 128)]))
    NWB = len(WBLK)
    wblk_d = dscr("wblk_s", [NWB, 128, 4096], BF16)

    ARENA = 95000
    arena = nc.sbuf_tensor("arena", [128, ARENA], BF16).__enter__()
    psum = nc.psum_tensor("psum", [128, 8, 512], F32).__enter__()
    cur = [0]

    def alloc(shape, dt=F32):
        n = int(np.prod(shape))
        sz = n * (2 if dt == F32 else 1)
        a0 = cur[0]
        cur[0] += sz + (sz % 2)
        assert cur[0] <= ARENA, f"SBUF arena overflow {cur[0]}"
        v = arena[:, a0:a0 + sz]
        if dt == F32:
            v = v.bitcast(F32)
        if len(shape) == 2:
            v = v.rearrange("p (a b) -> p a b", b=shape[1])
        elif len(shape) == 3:
            v = v.rearrange("p (a b c) -> p a b c", b=shape[1], c=shape[2])
        elif len(shape) == 4:
            v = v.rearrange("p (a b c d) -> p a b c d", b=shape[1], c=shape[2], d=shape[3])
        return v

    psi = [0]

    def nextps():
        i = psi[0] % 8
        psi[0] += 1
        return i, f"ps{i}"

    def dma(eng, out, in_, reads, writes, key):
        return fw.op(eng, lambda e: e.dma_start(out=out, in_=in_), reads=reads, writes=writes, dma=key)

    def act(out, in_, func, reads, writes, bias=None, scale=None, accum=None):
        kw = {}
        if bias is not None:
            kw["bias"] = bias
        if scale is not None:
            kw["scale"] = scale
        if accum is not None:
            kw["accum_out"] = accum
        return fw.op("act", lambda e: e.activation(out=out, in_=in_, func=func, **kw), reads=reads, writes=writes)

    def tt(eng, out, in0, in1, op, reads, writes):
        return fw.op(eng, lambda e: e.tensor_tensor(out=out, in0=in0, in1=in1, op=op), reads=reads, writes=writes)

    def ts(eng, out, in0, s1, s2, op0, op1, reads, writes):
        if op1 is None:
            return fw.op(eng, lambda e: e.tensor_scalar(out=out, in0=in0, scalar1=s1, scalar2=None, op0=op0),
                         reads=reads, writes=writes)
        return fw.op(eng, lambda e: e.tensor_scalar(out=out, in0=in0, scalar1=s1, scalar2=s2, op0=op0, op1=op1),
                     reads=reads, writes=writes)

    def stt(out, in0, scalar, in1, op0, op1, reads, writes):
        return fw.op("dve", lambda e: e.scalar_tensor_tensor(out=out, in0=in0, scalar=scalar, in1=in1, op0=op0, op1=op1),
                     reads=reads, writes=writes)

    def mm(fn, reads, writes):
        return fw.op("pe", fn, reads=reads, writes=writes)

    pv = alloc([NPV])
    ident = alloc([128])
    ones_bf = alloc([128], BF16)
    modsb = alloc([48, 2])
    sc2 = alloc([8, 2])
    A1 = alloc([8, 2])
    A2 = alloc([8])
    cdec = alloc([2, 8])
    negdl = alloc([8])
    tmp_s = alloc([64])
    D1m = alloc([2, 264], BF16)
    D1f = alloc([264], BF16)
    zcol = alloc([2], BF16)
    Mtab = alloc([2, NF1, 128], BF16)
    CS = alloc([256], BF16)
    Gtab = alloc([128, 32], BF16)
    tbase = alloc([512])
    hdn3 = alloc([L], BF16)
    hcT = alloc([8, 256], BF16)
    GLOBAL_END = cur[0]

    def pvc(name, j=0, n=1):
        c0 = PV[name] + j
        return pv[:, c0:c0 + n]

    dma("sp", pv, pv_d, [], ["pv"], "pv")
    dma("sp", ident, t_ident, [], ["ident"], "ident")
    dma("sp", tbase, t_tbase, [], ["tbase"], "tbase")
    dma("pool", D1m.rearrange("p a b -> p (a b)"), t_D1m, [], ["D1m"], "D1m")
    dma("pool", Mtab.rearrange("p a b c -> p (a b c)"), t_M, [], ["Mtab"], "Mtab")
    dma("pool", D1f, t_D1f, [], ["D1f"], "D1f")
    fw.op("dve", lambda e: e.memset(zcol, 0.0), writes=["zcol"])
    for o_ in range(2):
        dma("sp", kb_d[o_, :, L:L + 2], zcol, ["zcol"], [f"kb_d{o_}_z"], f"kb_st{o_}")
    dma("pool", CS, t_CS, [], ["CS"], "CS")
    dma("pool", Gtab.rearrange("p a b -> p (a b)"), t_G, [], ["Gtab"], "Gtab")
    fw.op("dve", lambda e: e.memset(ones_bf, 1.0), writes=["ones"])

    p0_wm = [alloc([8, 256]) for _ in range(2)]
    p0_c2 = alloc([8, 2])
    p0_xt = [alloc([D]) for _ in range(2)]
    p0_xs = [alloc([D]) for _ in range(2)]
    p0_junk = alloc([D])
    p0_st = alloc([8])
    p0_hst = [alloc([8, 512], BF16) for _ in range(2)]
    p0_cf = [alloc([4096]) for _ in range(2)]
    p0_cb = [alloc([4096], BF16) for _ in range(2)]
    p0_zpos = [alloc([512]) for _ in range(2)]
    p0_hd = [alloc([512]) for _ in range(2)]
    p0_w = alloc([3, 64])
    p0_t1 = alloc([512])

    dma("sp", p0_c2.rearrange("p a b -> p (a b)"), c2_d, [], ["c2"], "c2")
    act(sc2, p0_c2, AF.Silu, ["c2"], ["sc2"])
    wmv = w_mod_d.rearrange("(kc p) n -> p kc n", p=128)
    pm, pmk = nextps()
    for piece in range(24):
        wt = p0_wm[piece % 2]
        dma("sp", wt, wmv[:, :, piece * 256:(piece + 1) * 256], [], [f"wm{piece % 2}"], f"wm{piece % 2}")

        def f(e, wt=wt, piece=piece):
            ins = None
            for jj in range(2):
                j = piece * 2 + jj
                for kc in range(8):
                    ins = e.matmul(psum[:, pm, 2 * j:2 * j + 2], lhsT=wt[:, kc, jj * 128:(jj + 1) * 128],
                                   rhs=sc2[:, kc, :], start=(kc == 0), stop=(kc == 7))
            return ins
        mm(f, [f"wm{piece % 2}", "sc2"], [pmk + f"_{piece}"] if piece else [pmk])
    bm = pv[:, PV["b_mod"]:PV["b_mod"] + 48]
    tt("dve", modsb, psum[:, pm, 0:96].rearrange("p (a b) -> p a b", b=2),
       bm.unsqueeze(2).broadcast_to([128, 48, 2]), ALU.add,
       ["pv", pmk] + [pmk + f"_{i}" for i in range(1, 24)], ["modsb"])
    ts("dve", A1, modsb[:, 8:16, :], 1.0, None, ALU.add, None, ["modsb"], ["A1"])
    tt("dve", A1, A1, pvc("norm1_g", 0, 8).unsqueeze(2).broadcast_to([128, 8, 2]), ALU.mult, ["A1", "pv"], ["A1"])
    ts("dve", A2, modsb[:, 32:40, 0], 1.0, None, ALU.add, None, ["modsb"], ["A2"])
    tt("dve", A2, A2, pvc("norm2_g", 0, 8), ALU.mult, ["A2", "pv"], ["A2"])
    lam = pv[:, PV["rg_lambda"]:PV["rg_lambda"] + 16]
    cdf = cdec.rearrange("p a b -> p (a b)")
    act(cdf, lam, AF.Exp, ["pv"], ["cdec"], scale=-1.0)
    act(cdf, cdf, AF.Ln, ["cdec"], ["cdec"], bias=1.0)
    ts("dve", cdf, cdf, -8.0, None, ALU.mult, None, ["cdec"], ["cdec"])
    ts("dve", negdl, pvc("deltas", 0, 8), -1.0, None, ALU.mult, None, ["pv"], ["negdl"])

    dma("sp", p0_w[0:33, 0, :], hy_w1_d, [], ["hw1"], "hw1")
    dma("sp", p0_w[0:64, 1, :], hy_w2_d, [], ["hw2"], "hw2")
    dma("sp", p0_w[0:64, 2, :], hy_w3_d, [], ["hw3"], "hw3")
    hs = PV["hy_small"]
    TWO_PI = 2.0 * np.pi
    fb = tmp_s[:, 0:3]
    tt("dve", fb[0:64], pv[0:64, hs:hs + 3], pv[0:64, hs + 3:hs + 4].broadcast_to([64, 3]), ALU.mult, ["pv"], ["fb"])
    fq = tmp_s[:, 4:5]
    ts("dve", fq[0:64], pv[0:64, hs + 3:hs + 4], 1.0 / TWO_PI, None, ALU.mult, None, ["pv"], ["fq"])
    ts("dve", fb[0:64], fb[0:64], 1.0 / TWO_PI, None, ALU.mult, None, ["fb"], ["fb"])
    for tt_ in range(NT):
        sl = slice(tt_ * 512, (tt_ + 1) * 512)
        zp = p0_zpos[tt_ % 2]
        zk = f"zpos{tt_ % 2}"
        dma("sp", zp, t_zpos[:, sl], [], [zk], zk)
        srcs = [zp, p0_hd[0], p0_hd[1]]
        srck = [zk, "hd0", "hd1"]
        dsts = [p0_hd[0], p0_hd[1], hdn3[:, sl]]
        dstk = ["hd0", "hd1", "hdn3"]
        for layer in range(3):
            kdim = 33 if layer == 0 else 64
            pi, pk = nextps()
            mm(lambda e, layer=layer, kdim=kdim, pi=pi, srcs=srcs: e.matmul(
                psum[0:64, pi, :], lhsT=p0_w[0:kdim, layer, :], rhs=srcs[layer][0:kdim, :], start=True, stop=True),
               [srck[layer], f"hw{layer + 1}"], [pk])
            ts("dve", p0_t1[0:64], psum[0:64, pi, :], fq[0:64], fb[0:64, layer:layer + 1], ALU.mult, ALU.add,
               [pk, "fq", "fb"], ["t1"])
            ts("dve", p0_junk[0:64, 0:512], p0_t1[0:64], 12582912.0, 12582912.0, ALU.add, ALU.subtract, ["t1"], ["junk"])
            tt("dve", p0_t1[0:64], p0_t1[0:64], p0_junk[0:64, 0:512], ALU.subtract, ["t1", "junk"], ["t1"])
            act(dsts[layer][0:64], p0_t1[0:64], AF.Sin, ["t1"], [dstk[layer]], scale=TWO_PI)

    def cast_load(i):
        nm, kc, cols, parts = WBLK[i]
        cf_ = p0_cf[i % 2]
        n = kc * cols
        cfv = cf_[:, 0:n].rearrange("p (a b) -> p a b", b=cols)
        c_at = 0
        for pi_, (src, col0, ncols) in enumerate(parts):
            sv = src.rearrange("(kc p) n -> p kc n", p=128)[:, :, col0:col0 + ncols]
            dma("sp", cfv[:, :, c_at:c_at + ncols], sv, [], [f"cf{i % 2}_{pi_}"], f"cf{i % 2}")
            c_at += ncols

    def cast_finish(i):
        nm, kc, cols, parts = WBLK[i]
        cf_ = p0_cf[i % 2]
        cb_ = p0_cb[i % 2]
        kb = f"cb{i % 2}"
        kfs = [f"cf{i % 2}_{pi_}" for pi_ in range(4)]
        n = kc * cols
        if i % 2 == 0:
            act(cb_[:, 0:n], cf_[:, 0:n], AF.Copy, kfs, [kb] + kfs[len(parts):])
        else:
            fw.op("dve", lambda e, cb_=cb_, cf_=cf_, n=n: e.tensor_copy(out=cb_[:, 0:n], in_=cf_[:, 0:n]), reads=kfs,
                  writes=[kb] + kfs[len(parts):])
        dma("pool", wblk_d[i, :, 0:n], cb_[:, 0:n], [kb], [f"wblk{i}"], f"wblkst{i % 2}")

    def norm_tiles(src_d, ntile, which, store_fn):
        for ti in range(ntile):
            if which == 0 and ti < NWB:
                cast_load(ti)
            xt = p0_xt[ti % 2]
            xs = p0_xs[ti % 2]
            kx, ks = f"xt{ti % 2}", f"xs{ti % 2}"
            dma("sp", xt, src_d[ti * 128:(ti + 1) * 128, :], [], [kx], kx)
            act(p0_junk, xt, AF.Square, [kx], ["junk", "ss"], accum=p0_st[:, 0:1])
            act(p0_st[:, 1:2], p0_st[:, 0:1], AF.Sqrt, ["ss"], ["rs"], scale=1.0 / D, bias=EPS)
            fw.op("dve", lambda e: e.reciprocal(out=p0_st[:, 2:3], in_=p0_st[:, 1:2]), reads=["rs"], writes=["rstd"])
            ts("dve", xs, xt, p0_st[:, 2:3], None, ALU.mult, None, [kx, "rstd"], [ks])
            for half in range(2):
                pi, pk = nextps()

                def f(e, xs=xs, pi=pi, half=half):
                    ins = None
                    for gg in range(4):
                        g = half * 4 + gg
                        ins = e.transpose(psum[:, pi, gg * 128:(gg + 1) * 128], xs[:, g * 128:(g + 1) * 128], ident)
                    return ins
                mm(f, [ks, "ident"], [pk])
                for gg in range(4):
                    g = half * 4 + gg
                    store_fn(ti, g, psum[:, pi, gg * 128:(gg + 1) * 128], pk)
            if which == 0 and ti < NWB:
                cast_finish(ti)

    def ctx_store(ti, g, src, pk):
        act(hcT[:, g, ti * 128:(ti + 1) * 128], src, AF.Identity, [pk, "A1", "modsb"], [f"hcT{g}"],
            scale=A1[:, g, 1:2], bias=modsb[:, g, 1:2])
    norm_tiles(ctx_d, 2, 1, ctx_store)

    def x_store(ti, g, src, pk):
        st = p0_hst[(ti // 4) % 2]
        kst = f"hst{(ti // 4) % 2}"
        act(st[:, g, (ti % 4) * 128:(ti % 4 + 1) * 128], src, AF.Identity, [pk, "A1", "modsb"], [kst + f"_{g}_{ti % 4}"],
            scale=A1[:, g, 0:1], bias=modsb[:, g, 0:1])
        if g == 7 and ti % 4 == 3:
            j = ti // 4
            dma("sp", hT_d[:, :, j * 512:(j + 1) * 512], st,
                [kst + f"_{gg}_{t4}" for gg in range(8) for t4 in range(4)], ["hT_d"], "hT_d")
    norm_tiles(x_d, 32, 0, x_store)

    fw.barrier()

    cur[0] = GLOBAL_END
    hTt = [alloc([8, 512], BF16) for _ in range(2)]
    wg = alloc([5, 8, 128], BF16)
    bd = alloc([4, 128])
    w4g = alloc([4, 128], BF16)
    sm = alloc([64])
    kbt = [alloc([512], BF16) for _ in range(2)]
    tq = [alloc([512]) for _ in range(6)]
    ucx = alloc([256])
    B1 = alloc([L])
    mark = cur[0]
    B2 = alloc([L])
    B3 = alloc([L])
    B4 = alloc([L])
    B5 = alloc([L])
    B6 = alloc([L])
    endB = cur[0]
    cur[0] = mark
    ctile = tq[0:2]
    gtile = tq[2:4]
    zA4 = alloc([16, 128], BF16)
    kA = hTt[0].rearrange("p a b -> p (a b)").rearrange("p (c b) -> p c b", b=128)
    AZ = alloc([8448], BF16)
    Kf = alloc([NF1, 2, 64])
    Y3 = alloc([3 * NF1, 64], BF16)
    xk = [alloc([4, 2, 64]) for _ in range(2)]
    cur[0] = max(cur[0], endB)
    A_sb = AZ.rearrange("p (c pl f) -> p c pl f", pl=4, f=NF1)
    Z_sb = AZ[:, 0:8192].rearrange("p (c b) -> p c b", b=128)
    BDK = [f"bd{m4}{hh}" for m4 in range(4) for hh in range(2)]
    fw.op("dve", lambda e: e.memset(bd.rearrange("p a b -> p (a b)"), 0.0), writes=BDK)

    w_in_v = w_in_d.rearrange("(kc p) n -> p kc n", p=128)
    hTloads = [0]

    def load_hT(j):
        i = hTloads[0] % 2
        hTloads[0] += 1
        dma("sp", hTt[i], hT_d[:, :, j * 512:(j + 1) * 512], ["hT_d"], [f"hTt{i}"], f"hTt{i}")
        return hTt[i], f"hTt{i}"

    def conv_ops(out, okey, P, pkey, taps, bias, rl):
        o3 = out.rearrange("p (r c) -> p r c", c=rl)
        p3 = P.rearrange("p (r c) -> p r c", c=rl)
        wc = [w for (o, w) in taps if o == 0][0]
        ts("dve", out, P, wc, bias, ALU.mult, ALU.add, [pkey, "pv"], [okey])
        for (o, w) in taps:
            if o == 0:
                continue
            if o < 0:
                stt(o3[:, :, -o:], p3[:, :, :rl + o], w, o3[:, :, -o:], ALU.mult, ALU.add, [pkey, okey, "pv"], [okey])
            else:
                stt(o3[:, :, :rl - o], p3[:, :, o:], w, o3[:, :, :rl - o], ALU.mult, ALU.add, [pkey, okey, "pv"], [okey])

    def rglru(g, u, ukey, T, h0, h0k, hf, hfk, hb, hbk, t1, t1k, t3, t3k):
        nt = (T + 511) // 512
        w = min(T, 512)
        for d in range(2):
            hout, hk = (hf, hfk) if d == 0 else (hb, hbk)
            for m, dst, dk in ((0, t1, t1k), (1, t3, t3k)):
                bias = pvc("rg_ba" if m == 0 else "rg_bx", d * 8 + g)
                for j in range(nt):
                    pi, pk = nextps()
                    sl = slice(j * w, (j + 1) * w)
                    mm(lambda e, pi=pi, m=m, d=d, sl=sl: e.matmul(psum[:, pi, 0:w], lhsT=bd[:, 2 * m + d, :], rhs=u[:, sl],
                                                                  start=True, stop=True),
                       [f"bd{2 * m + d}0", f"bd{2 * m + d}1", ukey], [pk])
                    act(dst[:, sl], psum[:, pi, 0:w], AF.Sigmoid, [pk, "pv"], [dk], bias=bias)
            tt("pool", t3[:, 0:T], t3[:, 0:T], u[:, 0:T], ALU.mult, [t3k, ukey], [t3k])
            act(t1[:, 0:T], t1[:, 0:T], AF.Exp, [t1k, "cdec"], [t1k], scale=cdec[:, d, g:g + 1])
            tt("pool", hout[:, 0:T], t1[:, 0:T], t1[:, 0:T], ALU.mult, [t1k], [hk])
            act(hout[:, 0:T], hout[:, 0:T], AF.Sqrt, [hk], [hk], scale=-1.0, bias=1.0)
            tt("dve", t3[:, 0:T], t3[:, 0:T], hout[:, 0:T], ALU.mult, [t3k, hk], [t3k])
            init = 0.0 if h0 is None else h0[d]
            rk = [t1k, t3k] + ([] if h0 is None else [h0k[d]])
            if d == 0:
                fw.op("dve", lambda e, hout=hout, init=init: e.tensor_tensor_scan(
                    out=hout[:, 0:T], data0=t1[:, 0:T], data1=t3[:, 0:T], initial=init, op0=ALU.mult, op1=ALU.add),
                    reads=rk, writes=[hk])
            else:
                fw.op("dve", lambda e, hout=hout, init=init: e.tensor_tensor_scan(
                    out=hout[:, 0:T][:, ::-1], data0=t1[:, 0:T][:, ::-1], data1=t3[:, 0:T][:, ::-1], initial=init,
                    op0=ALU.mult, op1=ALU.add), reads=rk, writes=[hk])

    AKEYS = [f"A_{ch}" for ch in range(64)]

    def fwd_transform(src4, skeys, consume):
        stage1(src4, skeys)
        stage2(consume)

    def stage1(src4, skeys):
        for c in range(16):
            for pair in range(2):
                pi, pk = nextps()
                mm(lambda e, c=c, pair=pair, pi=pi: e.matmul(psum[:, pi, 0:264], lhsT=src4[:, c, :], rhs=D1m[:, pair, :],
                                                            start=True, stop=True), skeys + ["D1m"], [pk])
                dstv = AZ[:, c * 528 + pair * 264: c * 528 + (pair + 1) * 264]
                wk_ = [f"A_{4 * c + 2 * pair}", f"A_{4 * c + 2 * pair + 1}"]
                if (2 * c + pair) % 2 == 0:
                    act(dstv, psum[:, pi, 0:264], AF.Copy, [pk], wk_)
                else:
                    fw.op("dve", lambda e, dstv=dstv, pi=pi: e.tensor_copy(out=dstv, in_=psum[:, pi, 0:264]), reads=[pk], writes=wk_)

    def stage2(consume):
        for bi in range(9):
            f1lo = bi * 4
            nf = min(4, NF1 - f1lo)
            pi, pk = nextps()

            def f(e, f1lo=f1lo, nf=nf, pi=pi):
                ins = None
                for k in range(nf):
                    f1 = f1lo + k
                    e.matmul(psum[:, pi, k * 128:(k + 1) * 128], lhsT=Mtab[:, 0, f1, :],
                             rhs=A_sb[:, :, 0:2, f1].rearrange("p c pl -> p pl c"), start=True, stop=False)
                    ins = e.matmul(psum[:, pi, k * 128:(k + 1) * 128], lhsT=Mtab[:, 1, f1, :],
                                   rhs=A_sb[:, :, 2:4, f1].rearrange("p c pl -> p pl c"), start=False, stop=True)
                return ins
            mm(f, AKEYS + ["Mtab"], [pk])
            consume(bi, pi, pk, f1lo, nf)

    tqc = [0]

    def tqn():
        i = tqc[0] % 6
        tqc[0] += 1
        return tq[i], f"tq{i}"

    def load_wg(g):
        for seg in range(5):
            dma("pool", wg[:, seg, :, :], w_in_v[:, :, seg * D + g * 128: seg * D + (g + 1) * 128], [], [f"wg{seg}"], f"wg{seg}")

    def load_bd(g):
        for m4 in range(4):
            for hh in range(2):
                dma("sp", bd[64 * hh:64 * hh + 64, m4, 64 * hh:64 * hh + 64], rg_w_d[m4, 2 * g + hh], [], [f"bd{m4}{hh}"], f"bdl{m4}{hh}")

    def load_w4(g):
        for cmb in range(4):
            o, dr = cmb // 2, cmb % 2
            c0 = o * 2048 + dr * 1024 + g * 128
            dma("pool", w4g[0:64, cmb, :], hy_w4_d[:, c0:c0 + 128], [], [f"w4g{cmb}"], f"w4g{cmb}")

    load_wg(0)
    load_bd(0)
    load_w4(0)
    for g in range(NG):
        pi, pk = nextps()

        def f(e, pi=pi):
            ins = None
            for kc in range(8):
                ins = e.matmul(psum[:, pi, 0:256], lhsT=wg[:, 0, kc, :], rhs=hcT[:, kc, :], start=(kc == 0), stop=(kc == 7))
            return ins
        mm(f, ["wg0"] + [f"hcT{k}" for k in range(8)], [pk])
        pt, ptk = tqn()
        act(pt[:, 0:256], psum[:, pi, 0:256], AF.Identity, [pk, "pv"], [ptk], bias=pvc("b_in", 0 * 8 + g))
        rtaps = [(k - 2, pvc("rnn_conv_w", k * 8 + g)) for k in range(4)]
        conv_ops(ucx, "uc", pt[:, 0:256], ptk, rtaps, pvc("rnn_conv_b", g), 256)
        rglru(g, ucx, "uc", 256, None, None, B3, "B3", B4, "B4", B5, "B5", B6, "B6")
        fw.op("dve", lambda e: e.tensor_copy(out=sm[:, 0:1], in_=B3[:, 255:256]), reads=["B3"], writes=["h0f"])
        fw.op("dve", lambda e: e.tensor_copy(out=sm[:, 1:2], in_=B4[:, 0:1]), reads=["B4"], writes=["h0b"])

        nxt = load_hT(0)
        for j in range(NT):
            ht, hk = nxt
            if j + 1 < NT:
                nxt = load_hT(j + 1)
            sl = slice(j * 512, (j + 1) * 512)
            for seg in range(5):
                pi, pk = seg, f"ps{seg}"

                def f(e, pi=pi, seg=seg, ht=ht):
                    ins = None
                    for kc in range(8):
                        ins = e.matmul(psum[:, pi, :], lhsT=wg[:, seg, kc, :], rhs=ht[:, kc, :], start=(kc == 0), stop=(kc == 7))
                    return ins
                mm(f, [f"wg{seg}", hk], [pk])
                pt, ptk = tqn()
                act(pt, psum[:, pi, :], AF.Identity, [pk, "pv"], [ptk], bias=pvc("b_in", seg * 8 + g))
                if seg == 0:
                    conv_ops(B2[:, sl], "B2", pt, ptk, rtaps, pvc("rnn_conv_b", g), 64)
                elif seg == 1:
                    gq, gqk = tqn()
                    act(gq, pt, AF.Square, [ptk], [gqk])
                    ts("pool", gq, gq, 0.044715, 1.0, ALU.mult, ALU.add, [gqk], [gqk])
                    tt("pool", gq, gq, pt, ALU.mult, [gqk, ptk], [gqk])
                    act(gq, gq, AF.Tanh, [gqk], [gqk], scale=0.7978845608028654)
                    stt(B6[:, sl], gq, 1.0, pt, ALU.add, ALU.mult, [gqk, ptk], ["B6"])
                else:
                    si = seg - 2
                    htaps = [(k - 1, pvc("hy_conv_w", k * 24 + si * 8 + g)) for k in range(3)]
                    if si == 0:
                        conv_ops(B1[:, sl], "B1", pt, ptk, htaps, pvc("hy_conv_b", si * 8 + g), 64)
                    else:
                        ct, ctk = tqn()
                        conv_ops(ct, ctk, pt, ptk, htaps, pvc("hy_conv_b", si * 8 + g), 64)
                        dma("sp", gate_d[si - 1, :, sl], ct, [ctk], [f"gate_d{si - 1}"], f"gate_st{si - 1}")
            dt_, dk_ = tqn()
            ts("dve", sm[:, 8 + j % 2:9 + j % 2], negdl[:, g:g + 1], float(j * 512) / float(L - 1), None, ALU.mult, None,
               ["negdl"], [f"decb{j % 2}"])
            act(dt_, tbase, AF.Exp, ["tbase", "negdl", f"decb{j % 2}"], [dk_], scale=negdl[:, g:g + 1],
                bias=sm[:, 8 + j % 2:9 + j % 2])
            for cmb in range(4):
                o_, dr = cmb // 2, cmb % 2
                pi = 5 + (j * 4 + cmb) % 3
                pk = f"ps{pi}"
                mm(lambda e, pi=pi, cmb=cmb, sl=sl: e.matmul(psum[:, pi, :], lhsT=w4g[0:64, cmb, :], rhs=hdn3[0:64, sl],
                                                             start=True, stop=True), [f"w4g{cmb}", "hdn3"], [pk])
                kt = kbt[cmb % 2]
                ktk = f"kbt{cmb % 2}"
                if dr == 0:
                    tt("dve", kt, psum[:, pi, :], dt_, ALU.mult, [pk, dk_], [ktk])
                else:
                    tt("dve", kt[:, ::-1], psum[:, pi, :], dt_, ALU.mult, [pk, dk_], [ktk])
                    if j == 0:
                        fw.op("dve", lambda e, kt=kt: e.memset(kt[:, 511:512], 0.0), reads=[ktk], writes=[ktk])
                jq, jqk = tqn()
                ncol = 16 + o_ * 24 + dr * 8 + j
                act(jq, kt, AF.Abs, [ktk], [jqk, f"nrm{cmb}_{j}"], accum=sm[:, ncol:ncol + 1])
                if dr == 0:
                    dma("sp", kb_d[o_, :, sl], kt, [ktk], [f"kb_d{o_}_{dr}_{j}"], f"kb_st{o_}")
                else:
                    s0 = L + 3585 - 512 * j
                    n_ = 511 if j == 0 else 512
                    dma("sp", kb_d[o_, :, s0:s0 + n_], kt[:, 0:n_], [ktk], ([f"kb_d{o_}_z"] if j == 7 else []) + [f"kb_d{o_}_{dr}_{j}"], f"kb_st{o_}")
        for o_ in range(2):
            base = 16 + o_ * 24
            fw.op("dve", lambda e, base=base: e.reduce_sum(out=sm[:, base + 16:base + 17], in_=sm[:, base:base + 16],
                                                          axis=mybir.AxisListType.X),
                  reads=[f"nrm{o_ * 2 + dr}_{j}" for dr in range(2) for j in range(NT)], writes=[f"nrmsum{o_}"])
            fw.op("dve", lambda e, base=base: e.reciprocal(out=sm[:, base + 17:base + 18], in_=sm[:, base + 16:base + 17]),
                  reads=[f"nrmsum{o_}"], writes=[f"invn{o_}"])
        if g + 1 < NG:
            load_wg(g + 1)
        KA = [f"kA_{q}_{c8}" for q in range(2) for c8 in range(2)]

        def FT_load(o, h):
            b_ = fw.bufs.get("hTt0")
            ex = ([] if b_ is None else ([b_[0]] if b_[0] is not None else []) + list(b_[1]))
            for q2 in range(2):
                for c8 in range(2):
                    c0 = 64 * h + q2 + 32 * c8
                    fw.op("sp", lambda e, q2=q2, c8=c8, c0=c0, o=o: e.dma_start(
                        out=kA[64 * q2:64 * q2 + 64, 16 * c8:16 * c8 + 16, :],
                        in_=kb_d[o, c0:c0 + 31:2, :].rearrange("c (a b) -> a c b", b=128)),
                        reads=[f"kb_d{o}_{dr_}_{j_}" for dr_ in range(2) for j_ in range(NT)], writes=[f"kA_{q2}_{c8}"],
                        dma="kA", extra=ex)

        dma("pool", zb_d, B1, ["B1"], ["zb_d"], "zb_st")
        FT_load(0, 0)
        rglru(g, B2, "B2", L, [sm[:, 0:1], sm[:, 1:2]], ["h0f", "h0b"], B3, "B3", B2, "B2", B5, "B5", B4, "B4")
        if g + 1 < NG:
            load_bd(g + 1)
        tt("dve", B3, B3, B2, ALU.add, ["B3", "B2"], ["B3"])
        stt(B3, B3, 0.5, B6, ALU.mult, ALU.mult, ["B3", "B6"], ["B3"])
        dma("pool", yr_d[:, g, :], B3, ["B3"], ["yr_d"], "yr_st")
        fw.barrier()

        def FT(o, h, load=True):
            if load:
                FT_load(o, h)
            for c in range(32):
                pi, pk = nextps()
                mm(lambda e, c=c, pi=pi: e.matmul(psum[:, pi, 0:264], lhsT=kA[:, c, :], rhs=D1f, start=True, stop=True),
                   KA + ["D1f"], [pk])
                dstv = AZ[:, c * 264:(c + 1) * 264]
                wk_ = [f"A_{2 * c}", f"A_{2 * c + 1}"]
                if c % 2 == 0:
                    act(dstv, psum[:, pi, 0:264], AF.Copy, [pk], wk_)
                else:
                    fw.op("dve", lambda e, dstv=dstv, pi=pi: e.tensor_copy(out=dstv, in_=psum[:, pi, 0:264]), reads=[pk], writes=wk_)

            def consume_k(bi, pi, pk, f1lo, nf):
                src = psum[:, pi, 0:nf * 128].rearrange("p (f r c) -> p f r c", r=2, c=64)
                act(Kf[:, f1lo:f1lo + nf, :, :], src, AF.Copy, [pk], [f"Kf{bi}"])
            stage2(consume_k)

        def DT(o, h):
            ZA = [f"zA4_{q}" for q in range(4)]
            for q in range(4):
                c0 = 64 * h + q
                dma("sp", zA4[32 * q:32 * q + 32, :, :], zb_d[c0:64 * (h + 1):4, :].rearrange("c (a b) -> a c b", b=128),
                    ["zb_d"], [ZA[q]], "zA4")

            def consume_x(bi, pi, pk, f1lo, nf):
                X = psum[:, pi, 0:nf * 128].rearrange("p (f r c) -> p f r c", r=2, c=64)
                K = Kf[:, f1lo:f1lo + nf, :, :]
                Ksw = Kf[:, f1lo:f1lo + nf, ::-1, :]
                t1_ = xk[0][:, 0:nf]
                t2_ = xk[1][:, 0:nf]
                tt("dve", t1_, X, K, ALU.mult, [pk, f"Kf{bi}"], ["xk0"])
                tt("dve", t2_, X, Ksw, ALU.mult, [pk, f"Kf{bi}"], ["xk1"])
                yv = [Y3[:, pl * NF1 + f1lo: pl * NF1 + f1lo + nf, :] for pl in range(3)]
                tt("dve", yv[1], t1_[:, :, 0, :], t1_[:, :, 1, :], ALU.subtract, ["xk0"], [f"Y3r_{bi}"])
                tt("dve", yv[2], t2_[:, :, 0, :], t2_[:, :, 1, :], ALU.add, ["xk1"], [f"Y3i_{bi}"])
                act(yv[0], yv[2], AF.Copy, [f"Y3i_{bi}"], [f"Y3n_{bi}"], scale=-1.0)
            fwd_transform(zA4, ZA, consume_x)
            ykeys = [f"Y3{x_}_{bi}" for bi in range(9) for x_ in "rin"]
            for c4 in range(16):
                pi, pk = nextps()

                def f(e, c4=c4, pi=pi):
                    ins = None
                    for cc in range(4):
                        ch = c4 * 4 + cc
                        e.matmul(psum[0:66, pi, cc * 128:(cc + 1) * 128],
                                 lhsT=Y3[:, NF1:3 * NF1, ch], rhs=CS[:, 0:128], start=True, stop=False)
                        ins = e.matmul(psum[0:66, pi, cc * 128:(cc + 1) * 128],
                                       lhsT=Y3[:, 0:2 * NF1, ch], rhs=CS[:, 128:256],
                                       start=False, stop=True)
                    return ins
                mm(f, ykeys + ["CS"], [pk])
                act(Z_sb[0:66, c4 * 4:(c4 + 1) * 4, :],
                    psum[0:66, pi, :].rearrange("p (c b) -> p c b", b=128), AF.Copy, [pk], [f"Z_{c4}"] + AKEYS)
            zkeys = [f"Z_{c4}" for c4 in range(16)]
            icol = 16 + o * 24 + 17
            for bj in range(8):
                pi, pk = nextps()

                def f(e, bj=bj, pi=pi):
                    ins = None
                    for bb in range(16):
                        b_ = bj * 16 + bb
                        ins = e.matmul(psum[0:64, pi, bb * 32:(bb + 1) * 32], lhsT=Z_sb[0:66, :, b_], rhs=Gtab[0:66, b_, :],
                                       start=True, stop=True)
                    return ins
                mm(f, zkeys + AKEYS + ["Gtab"], [pk])
                ct = ctile[bj % 2]
                ctk = f"ct{bj % 2}"
                fw.op("dve", lambda e, ct=ct, pi=pi, h=h: e.tensor_copy(out=ct[64 * h:64 * h + 64, :], in_=psum[0:64, pi, :]),
                      reads=[pk], writes=[ctk])
                zv = B1[64 * h:64 * h + 64, :].rearrange("p (a b) -> p a b", b=128)[:, :, bj * 16:(bj + 1) * 16]
                stt(zv, ct[64 * h:64 * h + 64, :].rearrange("p (b a) -> p a b", a=32), sm[64 * h:64 * h + 64, icol:icol + 1], zv,
                    ALU.mult, ALU.add, [ctk, f"invn{o}", "B1"], ["B1"])

        passes = [(o, h) for o in range(2) for h in range(2)]
        FT(0, 0, load=False)
        ts("dve", B1, B1, pvc("hy_skip", 0 * 8 + g), None, ALU.mult, None, ["B1", "pv"], ["B1"])
        for pidx, (o, h) in enumerate(passes):
            DT(o, h)
            if pidx + 1 < len(passes):
                FT(*passes[pidx + 1])
            if h == 1:
                for j in range(NT):
                    sl = slice(j * 512, (j + 1) * 512)
                    gt = gtile[j % 2]
                    dma("sp", gt, gate_d[o, :, sl], [f"gate_d{o}"], [f"gt{j % 2}"], f"gt{j % 2}")
                    tt("dve", B1[:, sl], B1[:, sl], gt, ALU.mult, ["B1", f"gt{j % 2}"], ["B1"])
                if o == 0:
                    dma("pool", zb_d, B1, ["B1"], ["zb_d"], "zb_st")
                    ts("dve", B1, B1, pvc("hy_skip", 1 * 8 + g), None, ALU.mult, None, ["B1", "pv"], ["B1"])
        dma("pool", yh_d[:, g, :], B1, ["B1"], ["yh_d"], "yh_st")
        if g + 1 < NG:
            load_w4(g + 1)
        fw.barrier()

    cur[0] = GLOBAL_END
    ring = [alloc([4096], BF16) for _ in range(3)]
    yrt = alloc([8, 512], BF16)
    yht = alloc([8, 512], BF16)
    hTc = alloc([8, 512], BF16)
    xtok = alloc([4, D])
    xT = alloc([8, 512])
    merged = alloc([8, 512], BF16)
    gts = [alloc([512]) for _ in range(4)]
    sq = merged
    rstd = alloc([512])
    h2 = alloc([8, 512], BF16)
    hid = alloc([22, 512], BF16)
    wstate = {"issued": 0, "views": {}}
    TOTAL_W = NT * NWB

    def issue_w(upto):
        while wstate["issued"] < min(upto, TOTAL_W):
            sidx = wstate["issued"]
            i = sidx % NWB
            s_ = sidx % 3
            nm, kc, cols, parts = WBLK[i]
            n = kc * cols
            dma("sp", ring[s_][:, 0:n], wblk_d[i, :, 0:n], [f"wblk{i}"], [f"ring{s_}"], f"ring{s_}")
            wstate["views"][sidx] = (ring[s_][:, 0:n].rearrange("p (a b) -> p a b", b=cols), f"ring{s_}")
            wstate["issued"] += 1

    def get_w(j, i):
        sidx = j * NWB + i
        issue_w(sidx + 3)
        return wstate["views"][sidx]

    ostage = hid.rearrange("p a b -> p (a b)")[:, 0:8192].bitcast(F32).rearrange("p (a n) -> p a n", n=D)

    def load_acts(j):
        sl_ = slice(j * 512, (j + 1) * 512)
        dma("sp", yrt, yr_d[:, :, sl_], ["yr_d"], ["yrt"], "yrt")
        dma("sp", yht, yh_d[:, :, sl_], ["yh_d"], ["yht"], "yht")
        dma("sp", hTc, hT_d[:, :, sl_], ["hT_d"], ["hTc"], "hTc")

    def load_x(j):
        dma("sp", xtok, x_d[j * 512:(j + 1) * 512, :].rearrange("(a p) n -> p a n", p=128), [], ["xtok"], "xtok")

    load_x(0)
    load_acts(0)
    for j in range(NT):
        sl = slice(j * 512, (j + 1) * 512)
        for a4 in range(4):
            for half in range(2):
                pi, pk = nextps()

                def f(e, a4=a4, half=half, pi=pi):
                    ins = None
                    for gg in range(4):
                        g = half * 4 + gg
                        ins = e.transpose(psum[:, pi, gg * 128:(gg + 1) * 128], xtok[:, a4, g * 128:(g + 1) * 128], ident)
                    return ins
                mm(f, ["xtok", "ident"], [pk])
                act(xT[:, half * 4:(half + 1) * 4, a4 * 128:(a4 + 1) * 128],
                    psum[:, pi, :].rearrange("p (g t) -> p g t", t=128), AF.Copy, [pk], ["xT"])
        if j + 1 < NT:
            load_x(j + 1)
        for m in range(8):
            wv, wk = get_w(j, m)
            pis = []
            for li, (mov, mk_) in enumerate(((yrt, "yrt"), (yht, "yht"), (hTc, "hTc"), (hTc, "hTc"))):
                pi, pk = nextps()

                def f(e, wv=wv, mov=mov, li=li, pi=pi):
                    ins = None
                    for kc in range(8):
                        ins = e.matmul(psum[:, pi, :], lhsT=wv[:, kc, li * 128:(li + 1) * 128], rhs=mov[:, kc, :],
                                       start=(kc == 0), stop=(kc == 7))
                    return ins
                mm(f, [wk, mk_], [pk])
                pis.append((pi, pk))
            act(gts[0], psum[:, pis[2][0], :], AF.Sigmoid, [pis[2][1], "pv"], ["gt0"], bias=pvc("b_in", 40 + m))
            act(gts[1], psum[:, pis[3][0], :], AF.Sigmoid, [pis[3][1], "pv"], ["gt1"], bias=pvc("b_in", 48 + m))
            tt("dve", gts[2], psum[:, pis[0][0], :], gts[0], ALU.mult, [pis[0][1], "gt0"], ["gt2"])
            tt("dve", gts[3], psum[:, pis[1][0], :], gts[1], ALU.mult, [pis[1][1], "gt1"], ["gt3"])
            tt("pool", merged[:, m, :], gts[2], gts[3], ALU.add, ["gt2", "gt3"], ["merged"])
        if j + 1 < NT:
            load_acts(j + 1)
        for m in range(8):
            wv, wk = get_w(j, 8 + m // 4)
            col = m % 4
            pi, pk = nextps()

            def f(e, wv=wv, col=col, pi=pi):
                ins = None
                for kc in range(8):
                    ins = e.matmul(psum[:, pi, :], lhsT=wv[:, kc, col * 128:(col + 1) * 128], rhs=merged[:, kc, :],
                                   start=(kc == 0), stop=(kc == 7))
                return ins
            mm(f, [wk, "merged"], [pk])
            stt(xT[:, m, :], psum[:, pi, :], modsb[:, 16 + m, 0:1], xT[:, m, :], ALU.mult, ALU.add, [pk, "modsb", "xT"], ["xT"])

        def rms_rstd():
            act(sq, xT, AF.Square, ["xT"], ["sq"])
            pi, pk = nextps()

            def f(e, pi=pi):
                ins = None
                for kc in range(8):
                    ins = e.matmul(psum[:, pi, :], lhsT=ones_bf, rhs=sq[:, kc, :], start=(kc == 0), stop=(kc == 7))
                return ins
            mm(f, ["ones", "sq"], [pk])
            act(rstd, psum[:, pi, :], AF.Sqrt, [pk], ["rstd_"], scale=1.0 / D, bias=EPS)
            fw.op("dve", lambda e: e.reciprocal(out=rstd, in_=rstd), reads=["rstd_"], writes=["rstd_"])
        rms_rstd()
        for m in range(8):
            tt("dve", gts[m % 2], xT[:, m, :], rstd, ALU.mult, ["xT", "rstd_"], [f"gt{m % 2}"])
            act(h2[:, m, :], gts[m % 2], AF.Identity, [f"gt{m % 2}", "A2", "modsb"], ["h2"], scale=A2[:, m:m + 1],
                bias=modsb[:, 24 + m, 0:1])
        for m in range(22):
            wv, wk = get_w(j, 10 + m // 2)
            pis = []
            for part in range(2):
                col = part * 2 + (m % 2)
                pi, pk = nextps()

                def f(e, wv=wv, col=col, pi=pi):
                    ins = None
                    for kc in range(8):
                        ins = e.matmul(psum[:, pi, :], lhsT=wv[:, kc, col * 128:(col + 1) * 128], rhs=h2[:, kc, :],
                                       start=(kc == 0), stop=(kc == 7))
                    return ins
                mm(f, [wk, "h2"], [pk])
                pis.append((pi, pk))
            act(gts[m % 2], psum[:, pis[0][0], :], AF.Silu, [pis[0][1]], [f"gt{m % 2}"])
            tt("dve", hid[:, m, :], psum[:, pis[1][0], :], gts[m % 2], ALU.mult, [pis[1][1], f"gt{m % 2}"], ["hid"])
        for m in range(8):
            wv, wk = get_w(j, 21 + m)
            pi, pk = nextps()

            def f(e, wv=wv, pi=pi):
                ins = None
                for kc in range(22):
                    ins = e.matmul(psum[:, pi, :], lhsT=wv[:, kc, :], rhs=hid[:, kc, :], start=(kc == 0), stop=(kc == 21))
                return ins
            mm(f, [wk, "hid"], [pk])
            stt(xT[:, m, :], psum[:, pi, :], modsb[:, 40 + m, 0:1], xT[:, m, :], ALU.mult, ALU.add, [pk, "modsb", "xT"], ["xT"])
        rms_rstd()
        for m in range(8):
            stt(xT[:, m, :], xT[:, m, :], pvc("final_g", m), rstd, ALU.mult, ALU.mult, ["xT", "rstd_", "pv"], ["xT"])
        for a4 in range(4):
            for half in range(2):
                pi, pk = nextps()

                def f(e, a4=a4, half=half, pi=pi):
                    ins = None
                    for gg in range(4):
                        g = half * 4 + gg
                        ins = e.transpose(psum[:, pi, gg * 128:(gg + 1) * 128], xT[:, g, a4 * 128:(a4 + 1) * 128], ident)
                    return ins
                mm(f, ["xT", "ident"], [pk])
                act(ostage[:, a4, half * 512:(half + 1) * 512], psum[:, pi, :], AF.Copy, [pk], ["hid"])
        dma("sp", out_d[j * 512:(j + 1) * 512, :].rearrange("(a p) n -> p a n", p=128), ostage, ["hid"], ["out_d"], "out_st")

    fw.op("sp", None, extra=fw.all_tokens())
    assert len(fw.dsem) + 5 <= 100, len(fw.dsem)
    fw.emit()
    return nc


_CACHE = {}


def prep_inputs(inp, tb, b):
    PV, NPV = pv_layout()
    pv = np.zeros((128, NPV), np.float32)

    def put(name, arr):
        a = fm(arr)
        pv[:, PV[name]:PV[name] + a.shape[1]] = a
    put("norm1_g", inp["norm1_g"][0])
    put("norm2_g", inp["norm2_g"][0])
    put("final_g", inp["final_g"])
    put("rnn_conv_b", inp["rnn_conv_b"][0])
    put("deltas", tb["deltas"])
    put("b_mod", inp["b_mod"][0])
    put("b_in", inp["b_in"][0])
    put("rnn_conv_w", inp["rnn_conv_w"][0].reshape(-1))
    put("rg_ba", inp["rg_ba"][0].reshape(-1))
    put("rg_bx", inp["rg_bx"][0].reshape(-1))
    put("rg_lambda", inp["rg_lambda"][0].reshape(-1))
    put("hy_conv_w", inp["hy_conv_w"][0].reshape(-1))
    put("hy_conv_b", inp["hy_conv_b"][0])
    put("hy_skip", inp["hy_skip"][0].reshape(-1))
    hs = PV["hy_small"]
    pv[:64, hs + 0] = inp["hy_b1"][0]
    pv[:64, hs + 1] = inp["hy_b2"][0]
    pv[:64, hs + 2] = inp["hy_b3"][0]
    pv[:64, hs + 3] = inp["hy_freq"][0]
    c2 = np.stack([fm(inp["c"][b]), fm(inp["c_ctx"])], axis=-1).reshape(128, 16)
    m = {
        "x": np.ascontiguousarray(inp["x"][b]),
        "ctx": np.ascontiguousarray(inp["ctx"][b]),
        "c2": np.ascontiguousarray(c2, dtype=np.float32),
        "pv": pv,
        "w_mod": inp["w_mod"][0], "w_in": inp["w_in"][0],
        "rg_w": np.ascontiguousarray(np.concatenate([inp["rg_wa"][0], inp["rg_wx"][0]], axis=0)),
        "hy_w1": inp["hy_w1"][0], "hy_w2": inp["hy_w2"][0], "hy_w3": inp["hy_w3"][0], "hy_w4": inp["hy_w4"][0],
        "w_a_out": inp["w_a_out"][0], "w_b_out": inp["w_b_out"][0], "w_out": inp["w_out"][0],
        "w_ffn_in": inp["w_ffn_in"][0], "w_ffn_out": inp["w_ffn_out"][0],
        "t_D1m": tb["D1m"], "t_D1f": tb["D1f"], "t_M": tb["M"], "t_CS": tb["CS"], "t_G": tb["G"], "t_zpos": tb["zposT"],
        "t_tbase": tb["tbase"], "t_ident": tb["ident"],
    }
    return {k: np.ascontiguousarray(np.asarray(v, dtype=np.float32)) for k, v in m.items()}


def kernel(**inputs):
    inp = {k: np.asarray(v) for k, v in inputs.items()}
    tb = make_tables()
    if "nc" not in _CACHE:
        _CACHE["nc"] = build_program(False)
    nc = _CACHE["nc"]
    in_maps = [prep_inputs(inp, tb, b) for b in range(8)]
    res = run_bass_kernel_spmd(nc, in_maps, core_ids=list(range(8)))
    out = np.stack([np.asarray(r["out"]) for r in res.results], axis=0)
    return out.astype(np.float32)
```

```python
import numpy as np
import concourse.bass as bass
import concourse.mybir as mybir
from concourse.bass_utils import run_bass_kernel_spmd

F32 = mybir.dt.float32
BF16 = mybir.dt.bfloat16
AF = mybir.ActivationFunctionType
ALU = mybir.AluOpType

D = 1024
L = 4096
NFFT = 8192
NF1 = 33
DFF = 2816
EPS = 1e-6
NG = 8
NT = 8


class Tok:
    __slots__ = ("sem", "val")

    def __init__(self, sem, val):
        self.sem = sem
        self.val = val


class Fw:
    ENG = ("pe", "act", "dve", "pool", "sp")

    def __init__(self, nc):
        self.nc = nc
        self.ops = {e: [] for e in self.ENG}
        self.cnt = {e: 0 for e in self.ENG}
        self.esem = {e: nc.alloc_semaphore(name=f"cs_{e}") for e in self.ENG}
        self.waited = {e: {} for e in self.ENG}
        self.bufs = {}
        self.dsem = {}
        self.dtotal = {}

    def op(self, eng, fn, reads=(), writes=(), dma=None, extra=()):
        deps = list(extra)
        for k in reads:
            b = self.bufs.get(k)
            if b is not None and b[0] is not None:
                deps.append(b[0])
        for k in writes:
            b = self.bufs.get(k)
            if b is not None:
                if b[1]:
                    deps.extend(b[1])
                elif b[0] is not None:
                    deps.append(b[0])
        best = {}
        for t in deps:
            k = id(t.sem)
            cur_total = self.dtotal.get(k)
            if cur_total is not None and cur_total[1] > t.val:
                t = Tok(t.sem, cur_total[1])
            if k not in best or best[k].val < t.val:
                best[k] = t
        waits = []
        w = self.waited[eng]
        for k, t in best.items():
            if t.val > w.get(k, 0):
                w[k] = t.val
                waits.append((t.sem, t.val))
        if dma is None:
            self.cnt[eng] += 1
            tok = Tok(self.esem[eng], self.cnt[eng])
            inc = 1
        else:
            s = self.dsem.get(dma)
            if s is None:
                s = [self.nc.alloc_semaphore(name=f"ds_{len(self.dsem)}"), 0]
                self.dsem[dma] = s
                self.dtotal[id(s[0])] = s
            s[1] += 16
            tok = Tok(s[0], s[1])
            inc = 16
        self.ops[eng].append((waits, fn, tok.sem, inc))
        for k in reads:
            b = self.bufs.get(k)
            if b is None:
                b = [None, []]
                self.bufs[k] = b
            b[1].append(tok)
        for k in writes:
            self.bufs[k] = [tok, []]
        return tok

    def all_tokens(self):
        toks = []
        for e in self.ENG:
            if self.cnt[e] > 0:
                toks.append(Tok(self.esem[e], self.cnt[e]))
        for k, s in self.dsem.items():
            if s[1] > 0:
                toks.append(Tok(s[0], s[1]))
        return toks

    def barrier(self, dma_keys=None):
        if dma_keys is None:
            toks = self.all_tokens()
        else:
            toks = [Tok(self.esem[e], self.cnt[e]) for e in self.ENG if self.cnt[e] > 0]
            toks += [Tok(self.dsem[k][0], self.dsem[k][1]) for k in dma_keys if k in self.dsem]
        for e in self.ENG:
            self.op(e, None, extra=toks)

    def emit(self):
        nc = self.nc
        with nc.Block() as block:
            def mk(ename):
                def body(e):
                    for waits, fn, sem, inc in self.ops[ename]:
                        for s, v in waits:
                            e.wait_ge(s, v)
                        ins = e.nop() if fn is None else fn(e)
                        ins.then_inc(sem, inc)
                return body
            block.tensor(mk("pe"))
            block.scalar(mk("act"))
            block.vector(mk("dve"))
            block.gpsimd(mk("pool"))
            block.sync(mk("sp"))


def make_tables():
    tb = {}
    a = np.arange(32, dtype=np.float64)[:, None]
    f1 = np.arange(NF1, dtype=np.float64)[None, :]
    ang = 2 * np.pi * a * f1 / 64.0
    C, S = np.cos(ang), np.sin(ang)
    blk = np.concatenate([C, -S, S, C], axis=1)
    D1m = np.zeros((128, 2, 264), np.float64)
    for pair in range(2):
        for qq in range(2):
            q = 2 * pair + qq
            D1m[32 * q:32 * q + 32, pair, qq * 132:(qq + 1) * 132] = blk
    tb["D1m"] = D1m.reshape(128, 528).astype(np.float32)
    a64 = np.arange(64, dtype=np.float64)[:, None]
    ang = 2 * np.pi * a64 * f1 / 64.0
    C6, S6 = np.cos(ang), np.sin(ang)
    blk6 = np.concatenate([C6, -S6, S6, C6], axis=1)
    D1f = np.zeros((128, 264), np.float64)
    for q2 in range(2):
        D1f[64 * q2:64 * q2 + 64, q2 * 132:(q2 + 1) * 132] = blk6
    tb["D1f"] = D1f.astype(np.float32)
    b = np.arange(128, dtype=np.float64)[:, None]
    f2 = np.arange(128, dtype=np.float64)[None, :]
    M = np.zeros((128, 2, NF1, 128))
    for k in range(NF1):
        ang = -2 * np.pi * b * (k + 64 * f2) / NFFT
        M[:, 0, k, :] = np.cos(ang)
        M[:, 1, k, :] = np.sin(ang)
    tb["M"] = M.reshape(128, 2 * NF1 * 128).astype(np.float32)
    ang = 2 * np.pi * f2.T * b.T / 128.0
    tb["CS"] = np.concatenate([np.cos(ang), np.sin(ang)], axis=1).astype(np.float32)
    c = np.full(NF1, 2.0)
    c[0] = 1.0
    c[NF1 - 1] = 1.0
    f1v = np.arange(NF1, dtype=np.float64)[:, None, None]
    bv = np.arange(128, dtype=np.float64)[None, :, None]
    av = np.arange(32, dtype=np.float64)[None, None, :]
    ang = 2 * np.pi * (av * f1v / 64.0 + bv * f1v / NFFT)
    Gr = c[:, None, None] / NFFT * np.cos(ang)
    Gi = c[:, None, None] / NFFT * np.sin(ang)
    G = np.zeros((128, 128 * 32), np.float64)
    G[:66] = np.concatenate([Gr, -Gi], axis=0).reshape(66, 128 * 32)
    tb["G"] = G.astype(np.float32)
    t = np.linspace(0.0, 1.0, L, dtype=np.float32)[:, None]
    w = (2.0 * np.pi * np.arange(L, dtype=np.float32)[:, None] / L).astype(np.float32)
    f = np.linspace(1e-4, 15, 16, dtype=np.float32)[None, :]
    fw_ = (f * w).astype(np.float32)
    z = np.concatenate([t, np.cos(fw_), -np.sin(fw_)], axis=-1).astype(np.float32)
    zp = np.zeros((128, L), np.float32)
    zp[:33] = z.T
    tb["zposT"] = zp
    max_decay = np.log(1e-2) / 0.3
    min_decay = np.log(1e-2) / 1.5
    tb["deltas"] = np.abs(np.linspace(min_decay, max_decay, D, dtype=np.float32)).astype(np.float32)
    tb["tbase"] = np.broadcast_to(t[:512, 0][None, :], (128, 512)).astype(np.float32).copy()
    tb["ident"] = np.eye(128, dtype=np.float32)
    return tb


def pv_layout():
    cols = {}
    n = [0]

    def add(name, k):
        cols[name] = n[0]
        n[0] += k
    for nm in ["norm1_g", "norm2_g", "final_g", "rnn_conv_b", "deltas"]:
        add(nm, 8)
    add("b_mod", 48)
    add("b_in", 56)
    add("rnn_conv_w", 32)
    add("rg_ba", 16)
    add("rg_bx", 16)
    add("rg_lambda", 16)
    add("hy_conv_w", 72)
    add("hy_conv_b", 24)
    add("hy_skip", 16)
    add("hy_small", 4)
    return cols, n[0]


def fm(v):
    v = np.asarray(v, np.float32).reshape(-1, 128)
    return np.ascontiguousarray(v.T)


def build_program(debug=False):
    nc = bass.Bass("TRN2", target_bir_lowering=False)
    fw = Fw(nc)
    PV, NPV = pv_layout()

    def din(name, shape, dt=F32):
        return nc.dram_tensor(name, list(shape), dt, kind="ExternalInput").ap()

    def dscr(name, shape, dt):
        return nc.dram_tensor(name, list(shape), dt, kind="Internal").ap()

    x_d = din("x", [L, D])
    ctx_d = din("ctx", [256, D])
    c2_d = din("c2", [128, 16])
    pv_d = din("pv", [128, NPV])
    w_mod_d = din("w_mod", [D, 6 * D])
    w_in_d = din("w_in", [D, 7 * D])
    rg_w_d = din("rg_w", [4, 16, 64, 64])
    hy_w1_d = din("hy_w1", [33, 64])
    hy_w2_d = din("hy_w2", [64, 64])
    hy_w3_d = din("hy_w3", [64, 64])
    hy_w4_d = din("hy_w4", [64, 4096])
    w_a_d = din("w_a_out", [D, D])
    w_b_d = din("w_b_out", [D, D])
    w_o_d = din("w_out", [D, D])
    w_fi_d = din("w_ffn_in", [D, 2 * DFF])
    w_fo_d = din("w_ffn_out", [DFF, D])
    t_D1m = din("t_D1m", [128, 528])
    t_D1f = din("t_D1f", [128, 264])
    t_M = din("t_M", [128, 2 * NF1 * 128])
    t_CS = din("t_CS", [128, 256])
    t_G = din("t_G", [128, 4096])
    t_zpos = din("t_zpos", [128, L])
    t_tbase = din("t_tbase", [128, 512])
    t_ident = din("t_ident", [128, 128])
    out_d = nc.dram_tensor("out", [L, D], F32, kind="ExternalOutput").ap()

    okind = "ExternalOutput" if debug else "Internal"
    hT_d = dscr("hT_s", [128, 8, L], BF16)
    yr_d = nc.dram_tensor("yr_s", [128, 8, L], BF16, kind=okind).ap()
    yh_d = nc.dram_tensor("yh_s", [128, 8, L], BF16, kind=okind).ap()
    gate_d = dscr("gate_s", [2, 128, L], F32)
    zb_d = dscr("zb_s", [128, L], BF16)
    kb_d = dscr("kb_s", [2, 128, 2 * L], BF16)
    WBLK = []
    for m in range(8):
        WBLK.append(("mix", 8, 512, [(w_a_d, m * 128, 128), (w_b_d, m * 128, 128),
                                      (w_in_d, 5 * D + m * 128, 128), (w_in_d, 6 * D + m * 128, 128)]))
    for blk in range(2):
        WBLK.append(("wo", 8, 512, [(w_o_d, blk * 512, 512)]))
    for blk in range(11):
        WBLK.append(("wfi", 8, 512, [(w_fi_d, blk * 256, 256), (w_fi_d, DFF + blk * 256, 256)]))
    for blk in range(8):
        WBLK.append(("wfo", 22, 128, [(w_fo_d, blk * 128, 128)]))
    NWB = len(WBLK)
    wblk_d = dscr("wblk_s", [NWB, 128, 4096], BF16)

    ARENA = 95000
    arena = nc.sbuf_tensor("arena", [128, ARENA], BF16).__enter__()
    psum = nc.psum_tensor("psum", [128, 8, 512], F32).__enter__()
    cur = [0]

    def alloc(shape, dt=F32):
        n = int(np.prod(shape))
        sz = n * (2 if dt == F32 else 1)
        a0 = cur[0]
        cur[0] += sz + (sz % 2)
        assert cur[0] <= ARENA, f"SBUF arena overflow {cur[0]}"
        v = arena[:, a0:a0 + sz]
        if dt == F32:
            v = v.bitcast(F32)
        if len(shape) == 2:
            v = v.rearrange("p (a b) -> p a b", b=shape[1])
        elif len(shape) == 3:
            v = v.rearrange("p (a b c) -> p a b c", b=shape[1], c=shape[2])
        elif len(shape) == 4:
            v = v.rearrange("p (a b c d) -> p a b c d", b=shape[1], c=shape[2], d=shape[3])
        return v

    psi = [0]

    def nextps():
        i = psi[0] % 8
        psi[0] += 1
        return i, f"ps{i}"

    def dma(eng, out, in_, reads, writes, key):
        return fw.op(eng, lambda e: e.dma_start(out=out, in_=in_), reads=reads, writes=writes, dma=key)

    def act(out, in_, func, reads, writes, bias=None, scale=None, accum=None):
        kw = {}
        if bias is not None:
            kw["bias"] = bias
        if scale is not None:
            kw["scale"] = scale
        if accum is not None:
            kw["accum_out"] = accum
        return fw.op("act", lambda e: e.activation(out=out, in_=in_, func=func, **kw), reads=reads, writes=writes)

    def tt(eng, out, in0, in1, op, reads, writes):
        return fw.op(eng, lambda e: e.tensor_tensor(out=out, in0=in0, in1=in1, op=op), reads=reads, writes=writes)

    def ts(eng, out, in0, s1, s2, op0, op1, reads, writes):
        if op1 is None:
            return fw.op(eng, lambda e: e.tensor_scalar(out=out, in0=in0, scalar1=s1, scalar2=None, op0=op0),
                         reads=reads, writes=writes)
        return fw.op(eng, lambda e: e.tensor_scalar(out=out, in0=in0, scalar1=s1, scalar2=s2, op0=op0, op1=op1),
                     reads=reads, writes=writes)

    def stt(out, in0, scalar, in1, op0, op1, reads, writes):
        return fw.op("dve", lambda e: e.scalar_tensor_tensor(out=out, in0=in0, scalar=scalar, in1=in1, op0=op0, op1=op1),
                     reads=reads, writes=writes)

    def mm(fn, reads, writes):
        return fw.op("pe", fn, reads=reads, writes=writes)

    pv = alloc([NPV])
    ident = alloc([128])
    ones_bf = alloc([128], BF16)
    modsb = alloc([48, 2])
    sc2 = alloc([8, 2])
    A1 = alloc([8, 2])
    A2 = alloc([8])
    cdec = alloc([2, 8])
    negdl = alloc([8])
    tmp_s = alloc([64])
    D1m = alloc([2, 264], BF16)
    D1f = alloc([264], BF16)
    zcol = alloc([2], BF16)
    Mtab = alloc([2, NF1, 128], BF16)
    CS = alloc([256], BF16)
    Gtab = alloc([128, 32], BF16)
    tbase = alloc([512])
    hdn3 = alloc([L], BF16)
    hcT = alloc([8, 256], BF16)
    GLOBAL_END = cur[0]

    def pvc(name, j=0, n=1):
        c0 = PV[name] + j
        return pv[:, c0:c0 + n]

    dma("sp", pv, pv_d, [], ["pv"], "pv")
    dma("sp", ident, t_ident, [], ["ident"], "ident")
    dma("sp", tbase, t_tbase, [], ["tbase"], "tbase")
    dma("pool", D1m.rearrange("p a b -> p (a b)"), t_D1m, [], ["D1m"], "D1m")
    dma("pool", Mtab.rearrange("p a b c -> p (a b c)"), t_M, [], ["Mtab"], "Mtab")
    dma("pool", D1f, t_D1f, [], ["D1f"], "D1f")
    fw.op("dve", lambda e: e.memset(zcol, 0.0), writes=["zcol"])
    for o_ in range(2):
        dma("sp", kb_d[o_, :, L:L + 2], zcol, ["zcol"], [f"kb_d{o_}_z"], f"kb_st{o_}")
    dma("pool", CS, t_CS, [], ["CS"], "CS")
    dma("pool", Gtab.rearrange("p a b -> p (a b)"), t_G, [], ["Gtab"], "Gtab")
    fw.op("dve", lambda e: e.memset(ones_bf, 1.0), writes=["ones"])

    p0_wm = [alloc([8, 256]) for _ in range(2)]
    p0_c2 = alloc([8, 2])
    p0_xt = [alloc([D]) for _ in range(2)]
    p0_xs = [alloc([D]) for _ in range(2)]
    p0_junk = alloc([D])
    p0_st = alloc([8])
    p0_hst = [alloc([8, 512], BF16) for _ in range(2)]
    p0_cf = [alloc([4096]) for _ in range(2)]
    p0_cb = [alloc([4096], BF16) for _ in range(2)]
    p0_zpos = [alloc([512]) for _ in range(2)]
    p0_hd = [alloc([512]) for _ in range(2)]
    p0_w = alloc([3, 64])
    p0_t1 = alloc([512])

    dma("sp", p0_c2.rearrange("p a b -> p (a b)"), c2_d, [], ["c2"], "c2")
    act(sc2, p0_c2, AF.Silu, ["c2"], ["sc2"])
    wmv = w_mod_d.rearrange("(kc p) n -> p kc n", p=128)
    pm, pmk = nextps()
    for piece in range(24):
        wt = p0_wm[piece % 2]
        dma("sp", wt, wmv[:, :, piece * 256:(piece + 1) * 256], [], [f"wm{piece % 2}"], f"wm{piece % 2}")

        def f(e, wt=wt, piece=piece):
            ins = None
            for jj in range(2):
                j = piece * 2 + jj
                for kc in range(8):
                    ins = e.matmul(psum[:, pm, 2 * j:2 * j + 2], lhsT=wt[:, kc, jj * 128:(jj + 1) * 128],
                                   rhs=sc2[:, kc, :], start=(kc == 0), stop=(kc == 7))
            return ins
        mm(f, [f"wm{piece % 2}", "sc2"], [pmk + f"_{piece}"] if piece else [pmk])
    bm = pv[:, PV["b_mod"]:PV["b_mod"] + 48]
    tt("dve", modsb, psum[:, pm, 0:96].rearrange("p (a b) -> p a b", b=2),
       bm.unsqueeze(2).broadcast_to([128, 48, 2]), ALU.add,
       ["pv", pmk] + [pmk + f"_{i}" for i in range(1, 24)], ["modsb"])
    ts("dve", A1, modsb[:, 8:16, :], 1.0, None, ALU.add, None, ["modsb"], ["A1"])
    tt("dve", A1, A1, pvc("norm1_g", 0, 8).unsqueeze(2).broadcast_to([128, 8, 2]), ALU.mult, ["A1", "pv"], ["A1"])
    ts("dve", A2, modsb[:, 32:40, 0], 1.0, None, ALU.add, None, ["modsb"], ["A2"])
    tt("dve", A2, A2, pvc("norm2_g", 0, 8), ALU.mult, ["A2", "pv"], ["A2"])
    lam = pv[:, PV["rg_lambda"]:PV["rg_lambda"] + 16]
    cdf = cdec.rearrange("p a b -> p (a b)")
    act(cdf, lam, AF.Exp, ["pv"], ["cdec"], scale=-1.0)
    act(cdf, cdf, AF.Ln, ["cdec"], ["cdec"], bias=1.0)
    ts("dve", cdf, cdf, -8.0, None, ALU.mult, None, ["cdec"], ["cdec"])
    ts("dve", negdl, pvc("deltas", 0, 8), -1.0, None, ALU.mult, None, ["pv"], ["negdl"])

    dma("sp", p0_w[0:33, 0, :], hy_w1_d, [], ["hw1"], "hw1")
    dma("sp", p0_w[0:64, 1, :], hy_w2_d, [], ["hw2"], "hw2")
    dma("sp", p0_w[0:64, 2, :], hy_w3_d, [], ["hw3"], "hw3")
    hs = PV["hy_small"]
    TWO_PI = 2.0 * np.pi
    fb = tmp_s[:, 0:3]
    tt("dve", fb[0:64], pv[0:64, hs:hs + 3], pv[0:64, hs + 3:hs + 4].broadcast_to([64, 3]), ALU.mult, ["pv"], ["fb"])
    fq = tmp_s[:, 4:5]
    ts("dve", fq[0:64], pv[0:64, hs + 3:hs + 4], 1.0 / TWO_PI, None, ALU.mult, None, ["pv"], ["fq"])
    ts("dve", fb[0:64], fb[0:64], 1.0 / TWO_PI, None, ALU.mult, None, ["fb"], ["fb"])
    for tt_ in range(NT):
        sl = slice(tt_ * 512, (tt_ + 1) * 512)
        zp = p0_zpos[tt_ % 2]
        zk = f"zpos{tt_ % 2}"
        dma("sp", zp, t_zpos[:, sl], [], [zk], zk)
        srcs = [zp, p0_hd[0], p0_hd[1]]
        srck = [zk, "hd0", "hd1"]
        dsts = [p0_hd[0], p0_hd[1], hdn3[:, sl]]
        dstk = ["hd0", "hd1", "hdn3"]
        for layer in range(3):
            kdim = 33 if layer == 0 else 64
            pi, pk = nextps()
            mm(lambda e, layer=layer, kdim=kdim, pi=pi, srcs=srcs: e.matmul(
                psum[0:64, pi, :], lhsT=p0_w[0:kdim, layer, :], rhs=srcs[layer][0:kdim, :], start=True, stop=True),
               [srck[layer], f"hw{layer + 1}"], [pk])
            ts("dve", p0_t1[0:64], psum[0:64, pi, :], fq[0:64], fb[0:64, layer:layer + 1], ALU.mult, ALU.add,
               [pk, "fq", "fb"], ["t1"])
            ts("dve", p0_junk[0:64, 0:512], p0_t1[0:64], 12582912.0, 12582912.0, ALU.add, ALU.subtract, ["t1"], ["junk"])
            tt("dve", p0_t1[0:64], p0_t1[0:64], p0_junk[0:64, 0:512], ALU.subtract, ["t1", "junk"], ["t1"])
            act(dsts[layer][0:64], p0_t1[0:64], AF.Sin, ["t1"], [dstk[layer]], scale=TWO_PI)

    def cast_load(i):
        nm, kc, cols, parts = WBLK[i]
        cf_ = p0_cf[i % 2]
        n = kc * cols
        cfv = cf_[:, 0:n].rearrange("p (a b) -> p a b", b=cols)
        c_at = 0
        for pi_, (src, col0, ncols) in enumerate(parts):
            sv = src.rearrange("(kc p) n -> p kc n", p=128)[:, :, col0:col0 + ncols]
            dma("sp", cfv[:, :, c_at:c_at + ncols], sv, [], [f"cf{i % 2}_{pi_}"], f"cf{i % 2}")
            c_at += ncols

    def cast_finish(i):
        nm, kc, cols, parts = WBLK[i]
        cf_ = p0_cf[i % 2]
        cb_ = p0_cb[i % 2]
        kb = f"cb{i % 2}"
        kfs = [f"cf{i % 2}_{pi_}" for pi_ in range(4)]
        n = kc * cols
        if i % 2 == 0:
            act(cb_[:, 0:n], cf_[:, 0:n], AF.Copy, kfs, [kb] + kfs[len(parts):])
        else:
            fw.op("dve", lambda e, cb_=cb_, cf_=cf_, n=n: e.tensor_copy(out=cb_[:, 0:n], in_=cf_[:, 0:n]), reads=kfs,
                  writes=[kb] + kfs[len(parts):])
        dma("pool", wblk_d[i, :, 0:n], cb_[:, 0:n], [kb], [f"wblk{i}"], f"wblkst{i % 2}")

    def norm_tiles(src_d, ntile, which, store_fn):
        for ti in range(ntile):
            if which == 0 and ti < NWB:
                cast_load(ti)
            xt = p0_xt[ti % 2]
            xs = p0_xs[ti % 2]
            kx, ks = f"xt{ti % 2}", f"xs{ti % 2}"
            dma("sp", xt, src_d[ti * 128:(ti + 1) * 128, :], [], [kx], kx)
            act(p0_junk, xt, AF.Square, [kx], ["junk", "ss"], accum=p0_st[:, 0:1])
            act(p0_st[:, 1:2], p0_st[:, 0:1], AF.Sqrt, ["ss"], ["rs"], scale=1.0 / D, bias=EPS)
            fw.op("dve", lambda e: e.reciprocal(out=p0_st[:, 2:3], in_=p0_st[:, 1:2]), reads=["rs"], writes=["rstd"])
            ts("dve", xs, xt, p0_st[:, 2:3], None, ALU.mult, None, [kx, "rstd"], [ks])
            for half in range(2):
                pi, pk = nextps()

                def f(e, xs=xs, pi=pi, half=half):
                    ins = None
                    for gg in range(4):
                        g = half * 4 + gg
                        ins = e.transpose(psum[:, pi, gg * 128:(gg + 1) * 128], xs[:, g * 128:(g + 1) * 128], ident)
                    return ins
                mm(f, [ks, "ident"], [pk])
                for gg in range(4):
                    g = half * 4 + gg
                    store_fn(ti, g, psum[:, pi, gg * 128:(gg + 1) * 128], pk)
            if which == 0 and ti < NWB:
                cast_finish(ti)

    def ctx_store(ti, g, src, pk):
        act(hcT[:, g, ti * 128:(ti + 1) * 128], src, AF.Identity, [pk, "A1", "modsb"], [f"hcT{g}"],
            scale=A1[:, g, 1:2], bias=modsb[:, g, 1:2])
    norm_tiles(ctx_d, 2, 1, ctx_store)

    def x_store(ti, g, src, pk):
        st = p0_hst[(ti // 4) % 2]
        kst = f"hst{(ti // 4) % 2}"
        act(st[:, g, (ti % 4) * 128:(ti % 4 + 1) * 128], src, AF.Identity, [pk, "A1", "modsb"], [kst + f"_{g}_{ti % 4}"],
            scale=A1[:, g, 0:1], bias=modsb[:, g, 0:1])
        if g == 7 and ti % 4 == 3:
            j = ti // 4
            dma("sp", hT_d[:, :, j * 512:(j + 1) * 512], st,
                [kst + f"_{gg}_{t4}" for gg in range(8) for t4 in range(4)], ["hT_d"], "hT_d")
    norm_tiles(x_d, 32, 0, x_store)

    fw.barrier()

    cur[0] = GLOBAL_END
    hTt = [alloc([8, 512], BF16) for _ in range(2)]
    wg = alloc([5, 8, 128], BF16)
    bd = alloc([4, 128])
    w4g = alloc([4, 128], BF16)
    sm = alloc([64])
    kbt = [alloc([512], BF16) for _ in range(2)]
    tq = [alloc([512]) for _ in range(6)]
    ucx = alloc([256])
    B1 = alloc([L])
    mark = cur[0]
    B2 = alloc([L])
    B3 = alloc([L])
    B4 = alloc([L])
    B5 = alloc([L])
    B6 = alloc([L])
    endB = cur[0]
    cur[0] = mark
    ctile = tq[0:2]
    gtile = tq[2:4]
    zA4 = alloc([16, 128], BF16)
    kA = hTt[0].rearrange("p a b -> p (a b)").rearrange("p (c b) -> p c b", b=128)
    AZ = alloc([8448], BF16)
    Kf = alloc([NF1, 2, 64])
    Y3 = alloc([3 * NF1, 64], BF16)
    xk = [alloc([4, 2, 64]) for _ in range(2)]
    cur[0] = max(cur[0], endB)
    A_sb = AZ.rearrange("p (c pl f) -> p c pl f", pl=4, f=NF1)
    Z_sb = AZ[:, 0:8192].rearrange("p (c b) -> p c b", b=128)
    BDK = [f"bd{m4}{hh}" for m4 in range(4) for hh in range(2)]
    fw.op("dve", lambda e: e.memset(bd.rearrange("p a b -> p (a b)"), 0.0), writes=BDK)

    w_in_v = w_in_d.rearrange("(kc p) n -> p kc n", p=128)
    hTloads = [0]

    def load_hT(j):
        i = hTloads[0] % 2
        hTloads[0] += 1
        dma("sp", hTt[i], hT_d[:, :, j * 512:(j + 1) * 512], ["hT_d"], [f"hTt{i}"], f"hTt{i}")
        return hTt[i], f"hTt{i}"

    def conv_ops(out, okey, P, pkey, taps, bias, rl):
        o3 = out.rearrange("p (r c) -> p r c", c=rl)
        p3 = P.rearrange("p (r c) -> p r c", c=rl)
        wc = [w for (o, w) in taps if o == 0][0]
        act(out, P, AF.Identity, [pkey, "pv"], [okey], scale=wc, bias=bias)
        for (o, w) in taps:
            if o == 0:
                continue
            if o < 0:
                stt(o3[:, :, -o:], p3[:, :, :rl + o], w, o3[:, :, -o:], ALU.mult, ALU.add, [pkey, okey, "pv"], [okey])
            else:
                stt(o3[:, :, :rl - o], p3[:, :, o:], w, o3[:, :, :rl - o], ALU.mult, ALU.add, [pkey, okey, "pv"], [okey])

    def rglru(g, u, ukey, T, h0, h0k, hf, hfk, hb, hbk, t1, t1k, t3, t3k):
        nt = (T + 511) // 512
        w = min(T, 512)
        for d in range(2):
            hout, hk = (hf, hfk) if d == 0 else (hb, hbk)
            for m, dst, dk in ((0, t1, t1k), (1, t3, t3k)):
                bias = pvc("rg_ba" if m == 0 else "rg_bx", d * 8 + g)
                for j in range(nt):
                    pi, pk = nextps()
                    sl = slice(j * w, (j + 1) * w)
                    mm(lambda e, pi=pi, m=m, d=d, sl=sl: e.matmul(psum[:, pi, 0:w], lhsT=bd[:, 2 * m + d, :], rhs=u[:, sl],
                                                                  start=True, stop=True),
                       [f"bd{2 * m + d}0", f"bd{2 * m + d}1", ukey], [pk])
                    act(dst[:, sl], psum[:, pi, 0:w], AF.Sigmoid, [pk, "pv"], [dk], bias=bias)
            tt("pool", t3[:, 0:T], t3[:, 0:T], u[:, 0:T], ALU.mult, [t3k, ukey], [t3k])
            act(t1[:, 0:T], t1[:, 0:T], AF.Exp, [t1k, "cdec"], [t1k], scale=cdec[:, d, g:g + 1])
            tt("pool", hout[:, 0:T], t1[:, 0:T], t1[:, 0:T], ALU.mult, [t1k], [hk])
            act(hout[:, 0:T], hout[:, 0:T], AF.Sqrt, [hk], [hk], scale=-1.0, bias=1.0)
            tt("dve", t3[:, 0:T], t3[:, 0:T], hout[:, 0:T], ALU.mult, [t3k, hk], [t3k])
            init = 0.0 if h0 is None else h0[d]
            rk = [t1k, t3k] + ([] if h0 is None else [h0k[d]])
            if d == 0:
                fw.op("dve", lambda e, hout=hout, init=init: e.tensor_tensor_scan(
                    out=hout[:, 0:T], data0=t1[:, 0:T], data1=t3[:, 0:T], initial=init, op0=ALU.mult, op1=ALU.add),
                    reads=rk, writes=[hk])
            else:
                fw.op("dve", lambda e, hout=hout, init=init: e.tensor_tensor_scan(
                    out=hout[:, 0:T][:, ::-1], data0=t1[:, 0:T][:, ::-1], data1=t3[:, 0:T][:, ::-1], initial=init,
                    op0=ALU.mult, op1=ALU.add), reads=rk, writes=[hk])

    AKEYS = [f"A_{ch}" for ch in range(64)]

    def fwd_transform(src4, skeys, consume):
        stage1(src4, skeys)
        stage2(consume)

    def stage1(src4, skeys):
        for c in range(16):
            for pair in range(2):
                pi, pk = nextps()
                mm(lambda e, c=c, pair=pair, pi=pi: e.matmul(psum[:, pi, 0:264], lhsT=src4[:, c, :], rhs=D1m[:, pair, :],
                                                            start=True, stop=True), skeys + ["D1m"], [pk])
                dstv = AZ[:, c * 528 + pair * 264: c * 528 + (pair + 1) * 264]
                wk_ = [f"A_{4 * c + 2 * pair}", f"A_{4 * c + 2 * pair + 1}"]
                if (2 * c + pair) % 2 == 0:
                    act(dstv, psum[:, pi, 0:264], AF.Copy, [pk], wk_)
                else:
                    fw.op("dve", lambda e, dstv=dstv, pi=pi: e.tensor_copy(out=dstv, in_=psum[:, pi, 0:264]), reads=[pk], writes=wk_)

    def stage2(consume):
        for bi in range(9):
            f1lo = bi * 4
            nf = min(4, NF1 - f1lo)
            pi, pk = nextps()

            def f(e, f1lo=f1lo, nf=nf, pi=pi):
                ins = None
                for k in range(nf):
                    f1 = f1lo + k
                    e.matmul(psum[:, pi, k * 128:(k + 1) * 128], lhsT=Mtab[:, 0, f1, :],
                             rhs=A_sb[:, :, 0:2, f1].rearrange("p c pl -> p pl c"), start=True, stop=False)
                    ins = e.matmul(psum[:, pi, k * 128:(k + 1) * 128], lhsT=Mtab[:, 1, f1, :],
                                   rhs=A_sb[:, :, 2:4, f1].rearrange("p c pl -> p pl c"), start=False, stop=True)
                return ins
            mm(f, AKEYS + ["Mtab"], [pk])
            consume(bi, pi, pk, f1lo, nf)

    tqc = [0]

    def tqn():
        i = tqc[0] % 6
        tqc[0] += 1
        return tq[i], f"tq{i}"

    def load_wg(g):
        for seg in range(5):
            dma("pool", wg[:, seg, :, :], w_in_v[:, :, seg * D + g * 128: seg * D + (g + 1) * 128], [], [f"wg{seg}"], f"wg{seg}")

    def load_bd(g):
        for m4 in range(4):
            for hh in range(2):
                dma("sp", bd[64 * hh:64 * hh + 64, m4, 64 * hh:64 * hh + 64], rg_w_d[m4, 2 * g + hh], [], [f"bd{m4}{hh}"], f"bdl{m4}{hh}")

    def load_w4(g):
        for cmb in range(4):
            o, dr = cmb // 2, cmb % 2
            c0 = o * 2048 + dr * 1024 + g * 128
            dma("pool", w4g[0:64, cmb, :], hy_w4_d[:, c0:c0 + 128], [], [f"w4g{cmb}"], f"w4g{cmb}")

    load_wg(0)
    load_bd(0)
    load_w4(0)
    for g in range(NG):
        pi, pk = nextps()

        def f(e, pi=pi):
            ins = None
            for kc in range(8):
                ins = e.matmul(psum[:, pi, 0:256], lhsT=wg[:, 0, kc, :], rhs=hcT[:, kc, :], start=(kc == 0), stop=(kc == 7))
            return ins
        mm(f, ["wg0"] + [f"hcT{k}" for k in range(8)], [pk])
        pt, ptk = tqn()
        act(pt[:, 0:256], psum[:, pi, 0:256], AF.Identity, [pk, "pv"], [ptk], bias=pvc("b_in", 0 * 8 + g))
        rtaps = [(k - 2, pvc("rnn_conv_w", k * 8 + g)) for k in range(4)]
        conv_ops(ucx, "uc", pt[:, 0:256], ptk, rtaps, pvc("rnn_conv_b", g), 256)
        rglru(g, ucx, "uc", 256, None, None, B3, "B3", B4, "B4", B5, "B5", B6, "B6")
        fw.op("dve", lambda e: e.tensor_copy(out=sm[:, 0:1], in_=B3[:, 255:256]), reads=["B3"], writes=["h0f"])
        fw.op("dve", lambda e: e.tensor_copy(out=sm[:, 1:2], in_=B4[:, 0:1]), reads=["B4"], writes=["h0b"])

        nxt = load_hT(0)
        for j in range(NT):
            ht, hk = nxt
            if j + 1 < NT:
                nxt = load_hT(j + 1)
            sl = slice(j * 512, (j + 1) * 512)
            for seg in (0, 2, 3, 4, 1):
                pi, pk = seg, f"ps{seg}"

                def f(e, pi=pi, seg=seg, ht=ht):
                    ins = None
                    for kc in range(8):
                        ins = e.matmul(psum[:, pi, :], lhsT=wg[:, seg, kc, :], rhs=ht[:, kc, :], start=(kc == 0), stop=(kc == 7))
                    return ins
                mm(f, [f"wg{seg}", hk], [pk])
                pt, ptk = tqn()
                act(pt, psum[:, pi, :], AF.Identity, [pk, "pv"], [ptk], bias=pvc("b_in", seg * 8 + g))
                if seg == 0:
                    conv_ops(B2[:, sl], "B2", pt, ptk, rtaps, pvc("rnn_conv_b", g), 64)
                elif seg == 1:
                    gq, gqk = tqn()
                    act(gq, pt, AF.Square, [ptk], [gqk])
                    ts("pool", gq, gq, 0.044715, 1.0, ALU.mult, ALU.add, [gqk], [gqk])
                    tt("pool", gq, gq, pt, ALU.mult, [gqk, ptk], [gqk])
                    act(gq, gq, AF.Tanh, [gqk], [gqk], scale=0.7978845608028654)
                    stt(B6[:, sl], gq, 1.0, pt, ALU.add, ALU.mult, [gqk, ptk], ["B6"])
                else:
                    si = seg - 2
                    htaps = [(k - 1, pvc("hy_conv_w", k * 24 + si * 8 + g)) for k in range(3)]
                    if si == 0:
                        conv_ops(B1[:, sl], "B1", pt, ptk, htaps, pvc("hy_conv_b", si * 8 + g), 64)
                    else:
                        ct, ctk = tqn()
                        conv_ops(ct, ctk, pt, ptk, htaps, pvc("hy_conv_b", si * 8 + g), 64)
                        dma("sp", gate_d[si - 1, :, sl], ct, [ctk], [f"gate_d{si - 1}"], f"gate_st{si - 1}")
            dt_, dk_ = tqn()
            ts("dve", sm[:, 8 + j % 2:9 + j % 2], negdl[:, g:g + 1], float(j * 512) / float(L - 1), None, ALU.mult, None,
               ["negdl"], [f"decb{j % 2}"])
            act(dt_, tbase, AF.Exp, ["tbase", "negdl", f"decb{j % 2}"], [dk_], scale=negdl[:, g:g + 1],
                bias=sm[:, 8 + j % 2:9 + j % 2])
            for cmb in range(4):
                o_, dr = cmb // 2, cmb % 2
                pi = 5 + (j * 4 + cmb) % 3
                pk = f"ps{pi}"
                mm(lambda e, pi=pi, cmb=cmb, sl=sl: e.matmul(psum[:, pi, :], lhsT=w4g[0:64, cmb, :], rhs=hdn3[0:64, sl],
                                                             start=True, stop=True), [f"w4g{cmb}", "hdn3"], [pk])
                kt = kbt[cmb % 2]
                ktk = f"kbt{cmb % 2}"
                if dr == 0:
                    tt("dve", kt, psum[:, pi, :], dt_, ALU.mult, [pk, dk_], [ktk])
                else:
                    tt("dve", kt[:, ::-1], psum[:, pi, :], dt_, ALU.mult, [pk, dk_], [ktk])
                    if j == 0:
                        fw.op("dve", lambda e, kt=kt: e.memset(kt[:, 511:512], 0.0), reads=[ktk], writes=[ktk])
                jq, jqk = tqn()
                ncol = 16 + o_ * 24 + dr * 8 + j
                act(jq, kt, AF.Abs, [ktk], [jqk, f"nrm{cmb}_{j}"], accum=sm[:, ncol:ncol + 1])
                if dr == 0:
                    dma("sp", kb_d[o_, :, sl], kt, [ktk], [f"kb_d{o_}_{dr}_{j}"], f"kb_st{o_}")
                else:
                    s0 = L + 3585 - 512 * j
                    n_ = 511 if j == 0 else 512
                    dma("sp", kb_d[o_, :, s0:s0 + n_], kt[:, 0:n_], [ktk], ([f"kb_d{o_}_z"] if j == 7 else []) + [f"kb_d{o_}_{dr}_{j}"], f"kb_st{o_}")
        for o_ in range(2):
            base = 16 + o_ * 24
            fw.op("dve", lambda e, base=base: e.reduce_sum(out=sm[:, base + 16:base + 17], in_=sm[:, base:base + 16],
                                                          axis=mybir.AxisListType.X),
                  reads=[f"nrm{o_ * 2 + dr}_{j}" for dr in range(2) for j in range(NT)], writes=[f"nrmsum{o_}"])
            fw.op("dve", lambda e, base=base: e.reciprocal(out=sm[:, base + 17:base + 18], in_=sm[:, base + 16:base + 17]),
                  reads=[f"nrmsum{o_}"], writes=[f"invn{o_}"])
        if g + 1 < NG:
            load_wg(g + 1)
        KA = [f"kA_{q}_{c8}" for q in range(2) for c8 in range(2)]

        def FT_load(o, h):
            b_ = fw.bufs.get("hTt0")
            ex = ([] if b_ is None else ([b_[0]] if b_[0] is not None else []) + list(b_[1]))
            for q2 in range(2):
                for c8 in range(2):
                    c0 = 64 * h + q2 + 32 * c8
                    fw.op("sp", lambda e, q2=q2, c8=c8, c0=c0, o=o: e.dma_start(
                        out=kA[64 * q2:64 * q2 + 64, 16 * c8:16 * c8 + 16, :],
                        in_=kb_d[o, c0:c0 + 31:2, :].rearrange("c (a b) -> a c b", b=128)),
                        reads=[f"kb_d{o}_{dr_}_{j_}" for dr_ in range(2) for j_ in range(NT)], writes=[f"kA_{q2}_{c8}"],
                        dma="kA", extra=ex)

        dma("pool", zb_d, B1, ["B1"], ["zb_d"], "zb_st")
        FT_load(0, 0)
        rglru(g, B2, "B2", L, [sm[:, 0:1], sm[:, 1:2]], ["h0f", "h0b"], B3, "B3", B2, "B2", B5, "B5", B4, "B4")
        if g + 1 < NG:
            load_bd(g + 1)
        tt("dve", B3, B3, B2, ALU.add, ["B3", "B2"], ["B3"])
        stt(B3, B3, 0.5, B6, ALU.mult, ALU.mult, ["B3", "B6"], ["B3"])
        dma("pool", yr_d[:, g, :], B3, ["B3"], ["yr_d"], "yr_st")
        fw.barrier(dma_keys=["yr_st"] + [f"gate_st{i}" for i in range(2)])

        def FT(o, h, load=True):
            if load:
                FT_load(o, h)
            for c in range(32):
                pi, pk = nextps()
                mm(lambda e, c=c, pi=pi: e.matmul(psum[:, pi, 0:264], lhsT=kA[:, c, :], rhs=D1f, start=True, stop=True),
                   KA + ["D1f"], [pk])
                dstv = AZ[:, c * 264:(c + 1) * 264]
                wk_ = [f"A_{2 * c}", f"A_{2 * c + 1}"]
                if c % 2 == 0:
                    act(dstv, psum[:, pi, 0:264], AF.Copy, [pk], wk_)
                else:
                    fw.op("dve", lambda e, dstv=dstv, pi=pi: e.tensor_copy(out=dstv, in_=psum[:, pi, 0:264]), reads=[pk], writes=wk_)

            def consume_k(bi, pi, pk, f1lo, nf):
                src = psum[:, pi, 0:nf * 128].rearrange("p (f r c) -> p f r c", r=2, c=64)
                act(Kf[:, f1lo:f1lo + nf, :, :], src, AF.Copy, [pk], [f"Kf{bi}"])
            stage2(consume_k)

        def DT(o, h):
            ZA = [f"zA4_{q}" for q in range(4)]
            for q in range(4):
                c0 = 64 * h + q
                dma("sp", zA4[32 * q:32 * q + 32, :, :], zb_d[c0:64 * (h + 1):4, :].rearrange("c (a b) -> a c b", b=128),
                    ["zb_d"], [ZA[q]], "zA4")

            def consume_x(bi, pi, pk, f1lo, nf):
                X = psum[:, pi, 0:nf * 128].rearrange("p (f r c) -> p f r c", r=2, c=64)
                K = Kf[:, f1lo:f1lo + nf, :, :]
                Ksw = Kf[:, f1lo:f1lo + nf, ::-1, :]
                t1_ = xk[0][:, 0:nf]
                t2_ = xk[1][:, 0:nf]
                tt("dve", t1_, X, K, ALU.mult, [pk, f"Kf{bi}"], ["xk0"])
                tt("dve", t2_, X, Ksw, ALU.mult, [pk, f"Kf{bi}"], ["xk1"])
                yv = [Y3[:, pl * NF1 + f1lo: pl * NF1 + f1lo + nf, :] for pl in range(3)]
                tt("dve", yv[1], t1_[:, :, 0, :], t1_[:, :, 1, :], ALU.subtract, ["xk0"], [f"Y3r_{bi}"])
                tt("dve", yv[2], t2_[:, :, 0, :], t2_[:, :, 1, :], ALU.add, ["xk1"], [f"Y3i_{bi}"])
                act(yv[0], yv[2], AF.Copy, [f"Y3i_{bi}"], [f"Y3n_{bi}"], scale=-1.0)
            fwd_transform(zA4, ZA, consume_x)
            ykeys = [f"Y3{x_}_{bi}" for bi in range(9) for x_ in "rin"]
            for c4 in range(16):
                pi, pk = nextps()

                def f(e, c4=c4, pi=pi):
                    ins = None
                    for cc in range(4):
                        ch = c4 * 4 + cc
                        e.matmul(psum[0:66, pi, cc * 128:(cc + 1) * 128],
                                 lhsT=Y3[:, NF1:3 * NF1, ch], rhs=CS[:, 0:128], start=True, stop=False)
                        ins = e.matmul(psum[0:66, pi, cc * 128:(cc + 1) * 128],
                                       lhsT=Y3[:, 0:2 * NF1, ch], rhs=CS[:, 128:256],
                                       start=False, stop=True)
                    return ins
                mm(f, ykeys + ["CS"], [pk])
                act(Z_sb[0:66, c4 * 4:(c4 + 1) * 4, :],
                    psum[0:66, pi, :].rearrange("p (c b) -> p c b", b=128), AF.Copy, [pk], [f"Z_{c4}"] + AKEYS)
            zkeys = [f"Z_{c4}" for c4 in range(16)]
            icol = 16 + o * 24 + 17
            for bj in range(8):
                pi, pk = nextps()

                def f(e, bj=bj, pi=pi):
                    ins = None
                    for bb in range(16):
                        b_ = bj * 16 + bb
                        ins = e.matmul(psum[0:64, pi, bb * 32:(bb + 1) * 32], lhsT=Z_sb[0:66, :, b_], rhs=Gtab[0:66, b_, :],
                                       start=True, stop=True)
                    return ins
                mm(f, zkeys + AKEYS + ["Gtab"], [pk])
                ct = ctile[bj % 2]
                ctk = f"ct{bj % 2}"
                fw.op("dve", lambda e, ct=ct, pi=pi, h=h: e.tensor_copy(out=ct[64 * h:64 * h + 64, :], in_=psum[0:64, pi, :]),
                      reads=[pk], writes=[ctk])
                zv = B1[64 * h:64 * h + 64, :].rearrange("p (a b) -> p a b", b=128)[:, :, bj * 16:(bj + 1) * 16]
                stt(zv, ct[64 * h:64 * h + 64, :].rearrange("p (b a) -> p a b", a=32), sm[64 * h:64 * h + 64, icol:icol + 1], zv,
                    ALU.mult, ALU.add, [ctk, f"invn{o}", "B1"], ["B1"])

        passes = [(o, h) for o in range(2) for h in range(2)]
        FT(0, 0, load=False)
        ts("dve", B1, B1, pvc("hy_skip", 0 * 8 + g), None, ALU.mult, None, ["B1", "pv"], ["B1"])
        for pidx, (o, h) in enumerate(passes):
            DT(o, h)
            if pidx + 1 < len(passes):
                FT(*passes[pidx + 1])
            if h == 1:
                for j in range(NT):
                    sl = slice(j * 512, (j + 1) * 512)
                    gt = gtile[j % 2]
                    dma("sp", gt, gate_d[o, :, sl], [f"gate_d{o}"], [f"gt{j % 2}"], f"gt{j % 2}")
                    tt("dve", B1[:, sl], B1[:, sl], gt, ALU.mult, ["B1", f"gt{j % 2}"], ["B1"])
                if o == 0:
                    dma("pool", zb_d, B1, ["B1"], ["zb_d"], "zb_st")
                    ts("dve", B1, B1, pvc("hy_skip", 1 * 8 + g), None, ALU.mult, None, ["B1", "pv"], ["B1"])
        dma("pool", yh_d[:, g, :], B1, ["B1"], ["yh_d"], "yh_st")
        if g + 1 < NG:
            load_w4(g + 1)
        fw.barrier(dma_keys=["zA4", "kA"] if g + 1 < NG else None)

    cur[0] = GLOBAL_END
    ring = [alloc([4096], BF16) for _ in range(3)]
    yrt = alloc([8, 512], BF16)
    yht = alloc([8, 512], BF16)
    hTc = alloc([8, 512], BF16)
    xtok = alloc([4, D])
    xT = alloc([8, 512])
    merged = alloc([8, 512], BF16)
    gts = [alloc([512]) for _ in range(4)]
    sq = merged
    rstd = alloc([512])
    h2 = alloc([8, 512], BF16)
    hid = alloc([22, 512], BF16)
    wstate = {"issued": 0, "views": {}}
    TOTAL_W = NT * NWB

    def issue_w(upto):
        while wstate["issued"] < min(upto, TOTAL_W):
            sidx = wstate["issued"]
            i = sidx % NWB
            s_ = sidx % 3
            nm, kc, cols, parts = WBLK[i]
            n = kc * cols
            dma("sp", ring[s_][:, 0:n], wblk_d[i, :, 0:n], [f"wblk{i}"], [f"ring{s_}"], f"ring{s_}")
            wstate["views"][sidx] = (ring[s_][:, 0:n].rearrange("p (a b) -> p a b", b=cols), f"ring{s_}")
            wstate["issued"] += 1

    def get_w(j, i):
        sidx = j * NWB + i
        issue_w(sidx + 3)
        return wstate["views"][sidx]

    ostage = hid.rearrange("p a b -> p (a b)")[:, 0:8192].bitcast(F32).rearrange("p (a n) -> p a n", n=D)

    def load_acts(j):
        sl_ = slice(j * 512, (j + 1) * 512)
        dma("sp", yrt, yr_d[:, :, sl_], ["yr_d"], ["yrt"], "yrt")
        dma("sp", yht, yh_d[:, :, sl_], ["yh_d"], ["yht"], "yht")
        dma("sp", hTc, hT_d[:, :, sl_], ["hT_d"], ["hTc"], "hTc")

    def load_x(j):
        dma("sp", xtok, x_d[j * 512:(j + 1) * 512, :].rearrange("(a p) n -> p a n", p=128), [], ["xtok"], "xtok")

    load_x(0)
    load_acts(0)
    for j in range(NT):
        sl = slice(j * 512, (j + 1) * 512)
        for a4 in range(4):
            for half in range(2):
                pi, pk = nextps()

                def f(e, a4=a4, half=half, pi=pi):
                    ins = None
                    for gg in range(4):
                        g = half * 4 + gg
                        ins = e.transpose(psum[:, pi, gg * 128:(gg + 1) * 128], xtok[:, a4, g * 128:(g + 1) * 128], ident)
                    return ins
                mm(f, ["xtok", "ident"], [pk])
                act(xT[:, half * 4:(half + 1) * 4, a4 * 128:(a4 + 1) * 128],
                    psum[:, pi, :].rearrange("p (g t) -> p g t", t=128), AF.Copy, [pk], ["xT"])
        if j + 1 < NT:
            load_x(j + 1)
        for m in range(8):
            wv, wk = get_w(j, m)
            pis = []
            for li, (mov, mk_) in enumerate(((yrt, "yrt"), (yht, "yht"), (hTc, "hTc"), (hTc, "hTc"))):
                pi, pk = nextps()

                def f(e, wv=wv, mov=mov, li=li, pi=pi):
                    ins = None
                    for kc in range(8):
                        ins = e.matmul(psum[:, pi, :], lhsT=wv[:, kc, li * 128:(li + 1) * 128], rhs=mov[:, kc, :],
                                       start=(kc == 0), stop=(kc == 7))
                    return ins
                mm(f, [wk, mk_], [pk])
                pis.append((pi, pk))
            act(gts[0], psum[:, pis[2][0], :], AF.Sigmoid, [pis[2][1], "pv"], ["gt0"], bias=pvc("b_in", 40 + m))
            act(gts[1], psum[:, pis[3][0], :], AF.Sigmoid, [pis[3][1], "pv"], ["gt1"], bias=pvc("b_in", 48 + m))
            tt("dve", gts[2], psum[:, pis[0][0], :], gts[0], ALU.mult, [pis[0][1], "gt0"], ["gt2"])
            tt("dve", gts[3], psum[:, pis[1][0], :], gts[1], ALU.mult, [pis[1][1], "gt1"], ["gt3"])
            tt("pool", merged[:, m, :], gts[2], gts[3], ALU.add, ["gt2", "gt3"], ["merged"])
        if j + 1 < NT:
            load_acts(j + 1)
        for m in range(8):
            wv, wk = get_w(j, 8 + m // 4)
            col = m % 4
            pi, pk = nextps()

            def f(e, wv=wv, col=col, pi=pi):
                ins = None
                for kc in range(8):
                    ins = e.matmul(psum[:, pi, :], lhsT=wv[:, kc, col * 128:(col + 1) * 128], rhs=merged[:, kc, :],
                                   start=(kc == 0), stop=(kc == 7))
                return ins
            mm(f, [wk, "merged"], [pk])
            stt(xT[:, m, :], psum[:, pi, :], modsb[:, 16 + m, 0:1], xT[:, m, :], ALU.mult, ALU.add, [pk, "modsb", "xT"], ["xT"])

        def rms_rstd():
            act(sq, xT, AF.Square, ["xT"], ["sq"])
            pi, pk = nextps()

            def f(e, pi=pi):
                ins = None
                for kc in range(8):
                    ins = e.matmul(psum[:, pi, :], lhsT=ones_bf, rhs=sq[:, kc, :], start=(kc == 0), stop=(kc == 7))
                return ins
            mm(f, ["ones", "sq"], [pk])
            act(rstd, psum[:, pi, :], AF.Sqrt, [pk], ["rstd_"], scale=1.0 / D, bias=EPS)
            fw.op("dve", lambda e: e.reciprocal(out=rstd, in_=rstd), reads=["rstd_"], writes=["rstd_"])
        rms_rstd()
        for m in range(8):
            tt("dve", gts[m % 2], xT[:, m, :], rstd, ALU.mult, ["xT", "rstd_"], [f"gt{m % 2}"])
            act(h2[:, m, :], gts[m % 2], AF.Identity, [f"gt{m % 2}", "A2", "modsb"], ["h2"], scale=A2[:, m:m + 1],
                bias=modsb[:, 24 + m, 0:1])
        for m in range(22):
            wv, wk = get_w(j, 10 + m // 2)
            pis = []
            for part in range(2):
                col = part * 2 + (m % 2)
                pi, pk = nextps()

                def f(e, wv=wv, col=col, pi=pi):
                    ins = None
                    for kc in range(8):
                        ins = e.matmul(psum[:, pi, :], lhsT=wv[:, kc, col * 128:(col + 1) * 128], rhs=h2[:, kc, :],
                                       start=(kc == 0), stop=(kc == 7))
                    return ins
                mm(f, [wk, "h2"], [pk])
                pis.append((pi, pk))
            act(gts[m % 2], psum[:, pis[0][0], :], AF.Silu, [pis[0][1]], [f"gt{m % 2}"])
            tt("dve", hid[:, m, :], psum[:, pis[1][0], :], gts[m % 2], ALU.mult, [pis[1][1], f"gt{m % 2}"], ["hid"])
        for m in range(8):
            wv, wk = get_w(j, 21 + m)
            pi, pk = nextps()

            def f(e, wv=wv, pi=pi):
                ins = None
                for kc in range(22):
                    ins = e.matmul(psum[:, pi, :], lhsT=wv[:, kc, :], rhs=hid[:, kc, :], start=(kc == 0), stop=(kc == 21))
                return ins
            mm(f, [wk, "hid"], [pk])
            stt(xT[:, m, :], psum[:, pi, :], modsb[:, 40 + m, 0:1], xT[:, m, :], ALU.mult, ALU.add, [pk, "modsb", "xT"], ["xT"])
        rms_rstd()
        for m in range(8):
            stt(xT[:, m, :], xT[:, m, :], pvc("final_g", m), rstd, ALU.mult, ALU.mult, ["xT", "rstd_", "pv"], ["xT"])
        for a4 in range(4):
            for half in range(2):
                pi, pk = nextps()

                def f(e, a4=a4, half=half, pi=pi):
                    ins = None
                    for gg in range(4):
                        g = half * 4 + gg
                        ins = e.transpose(psum[:, pi, gg * 128:(gg + 1) * 128], xT[:, g, a4 * 128:(a4 + 1) * 128], ident)
                    return ins
                mm(f, ["xT", "ident"], [pk])
                act(ostage[:, a4, half * 512:(half + 1) * 512], psum[:, pi, :], AF.Copy, [pk], ["hid"])
        dma("sp", out_d[j * 512:(j + 1) * 512, :].rearrange("(a p) n -> p a n", p=128), ostage, ["hid"], ["out_d"], "out_st")

    fw.op("sp", None, extra=fw.all_tokens())
    assert len(fw.dsem) + 5 <= 100, len(fw.dsem)
    fw.emit()
    return nc


_CACHE = {}


def prep_inputs(inp, tb, b):
    PV, NPV = pv_layout()
    pv = np.zeros((128, NPV), np.float32)

    def put(name, arr):
        a = fm(arr)
        pv[:, PV[name]:PV[name] + a.shape[1]] = a
    put("norm1_g", inp["norm1_g"][0])
    put("norm2_g", inp["norm2_g"][0])
    put("final_g", inp["final_g"])
    put("rnn_conv_b", inp["rnn_conv_b"][0])
    put("deltas", tb["deltas"])
    put("b_mod", inp["b_mod"][0])
    put("b_in", inp["b_in"][0])
    put("rnn_conv_w", inp["rnn_conv_w"][0].reshape(-1))
    put("rg_ba", inp["rg_ba"][0].reshape(-1))
    put("rg_bx", inp["rg_bx"][0].reshape(-1))
    put("rg_lambda", inp["rg_lambda"][0].reshape(-1))
    put("hy_conv_w", inp["hy_conv_w"][0].reshape(-1))
    put("hy_conv_b", inp["hy_conv_b"][0])
    put("hy_skip", inp["hy_skip"][0].reshape(-1))
    hs = PV["hy_small"]
    pv[:64, hs + 0] = inp["hy_b1"][0]
    pv[:64, hs + 1] = inp["hy_b2"][0]
    pv[:64, hs + 2] = inp["hy_b3"][0]
    pv[:64, hs + 3] = inp["hy_freq"][0]
    c2 = np.stack([fm(inp["c"][b]), fm(inp["c_ctx"])], axis=-1).reshape(128, 16)
    m = {
        "x": np.ascontiguousarray(inp["x"][b]),
        "ctx": np.ascontiguousarray(inp["ctx"][b]),
        "c2": np.ascontiguousarray(c2, dtype=np.float32),
        "pv": pv,
        "w_mod": inp["w_mod"][0], "w_in": inp["w_in"][0],
        "rg_w": np.ascontiguousarray(np.concatenate([inp["rg_wa"][0], inp["rg_wx"][0]], axis=0)),
        "hy_w1": inp["hy_w1"][0], "hy_w2": inp["hy_w2"][0], "hy_w3": inp["hy_w3"][0], "hy_w4": inp["hy_w4"][0],
        "w_a_out": inp["w_a_out"][0], "w_b_out": inp["w_b_out"][0], "w_out": inp["w_out"][0],
        "w_ffn_in": inp["w_ffn_in"][0], "w_ffn_out": inp["w_ffn_out"][0],
        "t_D1m": tb["D1m"], "t_D1f": tb["D1f"], "t_M": tb["M"], "t_CS": tb["CS"], "t_G": tb["G"], "t_zpos": tb["zposT"],
        "t_tbase": tb["tbase"], "t_ident": tb["ident"],
    }
    return {k: np.ascontiguousarray(np.asarray(v, dtype=np.float32)) for k, v in m.items()}


def kernel(**inputs):
    inp = {k: np.asarray(v) for k, v in inputs.items()}
    tb = make_tables()
    if "nc" not in _CACHE:
        _CACHE["nc"] = build_program(False)
    nc = _CACHE["nc"]
    in_maps = [prep_inputs(inp, tb, b) for b in range(8)]
    res = run_bass_kernel_spmd(nc, in_maps, core_ids=list(range(8)))
    out = np.stack([np.asarray(r["out"]) for r in res.results], axis=0)
    return out.astype(np.float32)
```

```python
import numpy as np
import concourse.bass as bass
import concourse.mybir as mybir
from concourse.bass_utils import run_bass_kernel_spmd

F32 = mybir.dt.float32
BF16 = mybir.dt.bfloat16
AF = mybir.ActivationFunctionType
ALU = mybir.AluOpType

D = 1024
L = 4096
NFFT = 8192
NF1 = 33
DFF = 2816
EPS = 1e-6
NG = 8
NT = 8


class Tok:
    __slots__ = ("sem", "val")

    def __init__(self, sem, val):
        self.sem = sem
        self.val = val


class Fw:
    ENG = ("pe", "act", "dve", "pool", "sp")

    def __init__(self, nc):
        self.nc = nc
        self.ops = {e: [] for e in self.ENG}
        self.cnt = {e: 0 for e in self.ENG}
        self.esem = {e: nc.alloc_semaphore(name=f"cs_{e}") for e in self.ENG}
        self.waited = {e: {} for e in self.ENG}
        self.bufs = {}
        self.dsem = {}
        self.dtotal = {}

    def op(self, eng, fn, reads=(), writes=(), dma=None, extra=()):
        deps = list(extra)
        for k in reads:
            b = self.bufs.get(k)
            if b is not None and b[0] is not None:
                deps.append(b[0])
        for k in writes:
            b = self.bufs.get(k)
            if b is not None:
                if b[1]:
                    deps.extend(b[1])
                elif b[0] is not None:
                    deps.append(b[0])
        best = {}
        for t in deps:
            k = id(t.sem)
            cur_total = self.dtotal.get(k)
            if cur_total is not None and cur_total[1] > t.val:
                t = Tok(t.sem, cur_total[1])
            if k not in best or best[k].val < t.val:
                best[k] = t
        waits = []
        w = self.waited[eng]
        for k, t in best.items():
            if t.val > w.get(k, 0):
                w[k] = t.val
                waits.append((t.sem, t.val))
        if dma is None:
            self.cnt[eng] += 1
            tok = Tok(self.esem[eng], self.cnt[eng])
            inc = 1
        else:
            s = self.dsem.get(dma)
            if s is None:
                s = [self.nc.alloc_semaphore(name=f"ds_{len(self.dsem)}"), 0]
                self.dsem[dma] = s
                self.dtotal[id(s[0])] = s
            s[1] += 16
            tok = Tok(s[0], s[1])
            inc = 16
        self.ops[eng].append((waits, fn, tok.sem, inc))
        for k in reads:
            b = self.bufs.get(k)
            if b is None:
                b = [None, []]
                self.bufs[k] = b
            b[1].append(tok)
        for k in writes:
            self.bufs[k] = [tok, []]
        return tok

    def all_tokens(self):
        toks = []
        for e in self.ENG:
            if self.cnt[e] > 0:
                toks.append(Tok(self.esem[e], self.cnt[e]))
        for k, s in self.dsem.items():
            if s[1] > 0:
                toks.append(Tok(s[0], s[1]))
        return toks

    def barrier(self, dma_keys=None):
        if dma_keys is None:
            toks = self.all_tokens()
        else:
            toks = [Tok(self.esem[e], self.cnt[e]) for e in self.ENG if self.cnt[e] > 0]
            toks += [Tok(self.dsem[k][0], self.dsem[k][1]) for k in dma_keys if k in self.dsem]
        for e in self.ENG:
            self.op(e, None, extra=toks)

    def emit(self):
        nc = self.nc
        with nc.Block() as block:
            def mk(ename):
                def body(e):
                    for waits, fn, sem, inc in self.ops[ename]:
                        for s, v in waits:
                            e.wait_ge(s, v)
                        ins = e.nop() if fn is None else fn(e)
                        ins.then_inc(sem, inc)
                return body
            block.tensor(mk("pe"))
            block.scalar(mk("act"))
            block.vector(mk("dve"))
            block.gpsimd(mk("pool"))
            block.sync(mk("sp"))


def make_tables():
    tb = {}
    a = np.arange(32, dtype=np.float64)[:, None]
    f1 = np.arange(NF1, dtype=np.float64)[None, :]
    ang = 2 * np.pi * a * f1 / 64.0
    C, S = np.cos(ang), np.sin(ang)
    blk = np.concatenate([C, -S, S, C], axis=1)
    D1m = np.zeros((128, 2, 264), np.float64)
    for pair in range(2):
        for qq in range(2):
            q = 2 * pair + qq
            D1m[32 * q:32 * q + 32, pair, qq * 132:(qq + 1) * 132] = blk
    tb["D1m"] = D1m.reshape(128, 528).astype(np.float32)
    a64 = np.arange(64, dtype=np.float64)[:, None]
    ang = 2 * np.pi * a64 * f1 / 64.0
    C6, S6 = np.cos(ang), np.sin(ang)
    blk6 = np.concatenate([C6, -S6, S6, C6], axis=1)
    D1f = np.zeros((128, 264), np.float64)
    for q2 in range(2):
        D1f[64 * q2:64 * q2 + 64, q2 * 132:(q2 + 1) * 132] = blk6
    tb["D1f"] = D1f.astype(np.float32)
    b = np.arange(128, dtype=np.float64)[:, None]
    f2 = np.arange(128, dtype=np.float64)[None, :]
    M = np.zeros((128, 2, NF1, 128))
    for k in range(NF1):
        ang = -2 * np.pi * b * (k + 64 * f2) / NFFT
        M[:, 0, k, :] = np.cos(ang)
        M[:, 1, k, :] = np.sin(ang)
    tb["M"] = M.reshape(128, 2 * NF1 * 128).astype(np.float32)
    ang = 2 * np.pi * f2.T * b.T / 128.0
    tb["CS"] = np.concatenate([np.cos(ang), np.sin(ang)], axis=1).astype(np.float32)
    c = np.full(NF1, 2.0)
    c[0] = 1.0
    c[NF1 - 1] = 1.0
    f1v = np.arange(NF1, dtype=np.float64)[:, None, None]
    bv = np.arange(128, dtype=np.float64)[None, :, None]
    av = np.arange(32, dtype=np.float64)[None, None, :]
    ang = 2 * np.pi * (av * f1v / 64.0 + bv * f1v / NFFT)
    Gr = c[:, None, None] / NFFT * np.cos(ang)
    Gi = c[:, None, None] / NFFT * np.sin(ang)
    G = np.zeros((128, 128 * 32), np.float64)
    G[:66] = np.concatenate([Gr, -Gi], axis=0).reshape(66, 128 * 32)
    tb["G"] = G.astype(np.float32)
    t = np.linspace(0.0, 1.0, L, dtype=np.float32)[:, None]
    w = (2.0 * np.pi * np.arange(L, dtype=np.float32)[:, None] / L).astype(np.float32)
    f = np.linspace(1e-4, 15, 16, dtype=np.float32)[None, :]
    fw_ = (f * w).astype(np.float32)
    z = np.concatenate([t, np.cos(fw_), -np.sin(fw_)], axis=-1).astype(np.float32)
    zp = np.zeros((128, L), np.float32)
    zp[:33] = z.T
    tb["zposT"] = zp
    max_decay = np.log(1e-2) / 0.3
    min_decay = np.log(1e-2) / 1.5
    tb["deltas"] = np.abs(np.linspace(min_decay, max_decay, D, dtype=np.float32)).astype(np.float32)
    tb["tbase"] = np.broadcast_to(t[:512, 0][None, :], (128, 512)).astype(np.float32).copy()
    tb["ident"] = np.eye(128, dtype=np.float32)
    return tb


def pv_layout():
    cols = {}
    n = [0]

    def add(name, k):
        cols[name] = n[0]
        n[0] += k
    for nm in ["norm1_g", "norm2_g", "final_g", "rnn_conv_b", "deltas"]:
        add(nm, 8)
    add("b_mod", 48)
    add("b_in", 56)
    add("rnn_conv_w", 32)
    add("rg_ba", 16)
    add("rg_bx", 16)
    add("rg_lambda", 16)
    add("hy_conv_w", 72)
    add("hy_conv_b", 24)
    add("hy_skip", 16)
    add("hy_small", 4)
    return cols, n[0]


def fm(v):
    v = np.asarray(v, np.float32).reshape(-1, 128)
    return np.ascontiguousarray(v.T)


def build_program(debug=False):
    nc = bass.Bass("TRN2", target_bir_lowering=False)
    fw = Fw(nc)
    PV, NPV = pv_layout()

    def din(name, shape, dt=F32):
        return nc.dram_tensor(name, list(shape), dt, kind="ExternalInput").ap()

    def dscr(name, shape, dt):
        return nc.dram_tensor(name, list(shape), dt, kind="Internal").ap()

    x_d = din("x", [L, D])
    ctx_d = din("ctx", [256, D])
    c2_d = din("c2", [128, 16])
    pv_d = din("pv", [128, NPV])
    w_mod_d = din("w_mod", [D, 6 * D])
    w_in_d = din("w_in", [D, 7 * D])
    rg_w_d = din("rg_w", [4, 16, 64, 64])
    hy_w1_d = din("hy_w1", [33, 64])
    hy_w2_d = din("hy_w2", [64, 64])
    hy_w3_d = din("hy_w3", [64, 64])
    hy_w4_d = din("hy_w4", [64, 4096])
    w_a_d = din("w_a_out", [D, D])
    w_b_d = din("w_b_out", [D, D])
    w_o_d = din("w_out", [D, D])
    w_fi_d = din("w_ffn_in", [D, 2 * DFF])
    w_fo_d = din("w_ffn_out", [DFF, D])
    t_D1m = din("t_D1m", [128, 528])
    t_D1f = din("t_D1f", [128, 264])
    t_M = din("t_M", [128, 2 * NF1 * 128])
    t_CS = din("t_CS", [128, 256])
    t_G = din("t_G", [128, 4096])
    t_zpos = din("t_zpos", [128, L])
    t_tbase = din("t_tbase", [128, 512])
    t_ident = din("t_ident", [128, 128])
    out_d = nc.dram_tensor("out", [L, D], F32, kind="ExternalOutput").ap()

    okind = "ExternalOutput" if debug else "Internal"
    hT_d = dscr("hT_s", [128, 8, L], BF16)
    yr_d = nc.dram_tensor("yr_s", [128, 8, L], BF16, kind=okind).ap()
    yh_d = nc.dram_tensor("yh_s", [128, 8, L], BF16, kind=okind).ap()
    gate_d = dscr("gate_s", [2, 128, L], F32)
    zb_d = dscr("zb_s", [128, L], BF16)
    kb_d = dscr("kb_s", [2, 128, 2 * L], BF16)
    WBLK = []
    for m in range(8):
        WBLK.append(("mix", 8, 512, [(w_a_d, m * 128, 128), (w_b_d, m * 128, 128),
                                      (w_in_d, 5 * D + m * 128, 128), (w_in_d, 6 * D + m * 128, 128)]))
    for blk in range(2):
        WBLK.append(("wo", 8, 512, [(w_o_d, blk * 512, 512)]))
    for blk in range(11):
        WBLK.append(("wfi", 8, 512, [(w_fi_d, blk * 256, 256), (w_fi_d, DFF + blk * 256, 256)]))
    for blk in range(8):
        WBLK.append(("wfo", 22, 128, [(w_fo_d, blk * 128, 128)]))
    NWB = len(WBLK)
    wblk_d = dscr("wblk_s", [NWB, 128, 4096], BF16)

    ARENA = 95000
    arena = nc.sbuf_tensor("arena", [128, ARENA], BF16).__enter__()
    psum = nc.psum_tensor("psum", [128, 8, 512], F32).__enter__()
    cur = [0]

    def alloc(shape, dt=F32):
        n = int(np.prod(shape))
        sz = n * (2 if dt == F32 else 1)
        a0 = cur[0]
        cur[0] += sz + (sz % 2)
        assert cur[0] <= ARENA, f"SBUF arena overflow {cur[0]}"
        v = arena[:, a0:a0 + sz]
        if dt == F32:
            v = v.bitcast(F32)
        if len(shape) == 2:
            v = v.rearrange("p (a b) -> p a b", b=shape[1])
        elif len(shape) == 3:
            v = v.rearrange("p (a b c) -> p a b c", b=shape[1], c=shape[2])
        elif len(shape) == 4:
            v = v.rearrange("p (a b c d) -> p a b c d", b=shape[1], c=shape[2], d=shape[3])
        return v

    psi = [0]

    def nextps():
        i = psi[0] % 8
        psi[0] += 1
        return i, f"ps{i}"

    def dma(eng, out, in_, reads, writes, key):
        return fw.op(eng, lambda e: e.dma_start(out=out, in_=in_), reads=reads, writes=writes, dma=key)

    def act(out, in_, func, reads, writes, bias=None, scale=None, accum=None):
        kw = {}
        if bias is not None:
            kw["bias"] = bias
        if scale is not None:
            kw["scale"] = scale
        if accum is not None:
            kw["accum_out"] = accum
        return fw.op("act", lambda e: e.activation(out=out, in_=in_, func=func, **kw), reads=reads, writes=writes)

    def tt(eng, out, in0, in1, op, reads, writes):
        return fw.op(eng, lambda e: e.tensor_tensor(out=out, in0=in0, in1=in1, op=op), reads=reads, writes=writes)

    def ts(eng, out, in0, s1, s2, op0, op1, reads, writes):
        if op1 is None:
            return fw.op(eng, lambda e: e.tensor_scalar(out=out, in0=in0, scalar1=s1, scalar2=None, op0=op0),
                         reads=reads, writes=writes)
        return fw.op(eng, lambda e: e.tensor_scalar(out=out, in0=in0, scalar1=s1, scalar2=s2, op0=op0, op1=op1),
                     reads=reads, writes=writes)

    def stt(out, in0, scalar, in1, op0, op1, reads, writes):
        return fw.op("dve", lambda e: e.scalar_tensor_tensor(out=out, in0=in0, scalar=scalar, in1=in1, op0=op0, op1=op1),
                     reads=reads, writes=writes)

    def mm(fn, reads, writes):
        return fw.op("pe", fn, reads=reads, writes=writes)

    pv = alloc([NPV])
    ident = alloc([128])
    ones_bf = alloc([128], BF16)
    modsb = alloc([48, 2])
    sc2 = alloc([8, 2])
    A1 = alloc([8, 2])
    A2 = alloc([8])
    cdec = alloc([2, 8])
    negdl = alloc([8])
    tmp_s = alloc([64])
    D1m = alloc([2, 264], BF16)
    D1f = alloc([264], BF16)
    zcol = alloc([2], BF16)
    Mtab = alloc([2, NF1, 128], BF16)
    CS = alloc([256], BF16)
    Gtab = alloc([128, 32], BF16)
    tbase = alloc([512])
    hdn3 = alloc([L], BF16)
    hcT = alloc([8, 256], BF16)
    GLOBAL_END = cur[0]

    def pvc(name, j=0, n=1):
        c0 = PV[name] + j
        return pv[:, c0:c0 + n]

    dma("sp", pv, pv_d, [], ["pv"], "pv")
    dma("sp", ident, t_ident, [], ["ident"], "ident")
    dma("sp", tbase, t_tbase, [], ["tbase"], "tbase")
    dma("pool", D1m.rearrange("p a b -> p (a b)"), t_D1m, [], ["D1m"], "D1m")
    dma("pool", Mtab.rearrange("p a b c -> p (a b c)"), t_M, [], ["Mtab"], "Mtab")
    dma("pool", D1f, t_D1f, [], ["D1f"], "D1f")
    fw.op("dve", lambda e: e.memset(zcol, 0.0), writes=["zcol"])
    for o_ in range(2):
        dma("sp", kb_d[o_, :, L:L + 2], zcol, ["zcol"], [f"kb_d{o_}_z"], f"kb_st{o_}")
    dma("pool", CS, t_CS, [], ["CS"], "CS")
    dma("pool", Gtab.rearrange("p a b -> p (a b)"), t_G, [], ["Gtab"], "Gtab")
    fw.op("dve", lambda e: e.memset(ones_bf, 1.0), writes=["ones"])

    p0_wm = [alloc([8, 256]) for _ in range(2)]
    p0_c2 = alloc([8, 2])
    p0_xt = [alloc([D]) for _ in range(2)]
    p0_xs = [alloc([D]) for _ in range(2)]
    p0_junk = alloc([D])
    p0_st = alloc([8])
    p0_hst = [alloc([8, 512], BF16) for _ in range(2)]
    p0_cf = [alloc([4096]) for _ in range(2)]
    p0_cb = [alloc([4096], BF16) for _ in range(2)]
    p0_zpos = [alloc([512]) for _ in range(2)]
    p0_hd = [alloc([512]) for _ in range(2)]
    p0_w = alloc([3, 64])
    p0_t1 = alloc([512])

    dma("sp", p0_c2.rearrange("p a b -> p (a b)"), c2_d, [], ["c2"], "c2")
    act(sc2, p0_c2, AF.Silu, ["c2"], ["sc2"])
    wmv = w_mod_d.rearrange("(kc p) n -> p kc n", p=128)
    pm, pmk = nextps()
    for piece in range(24):
        wt = p0_wm[piece % 2]
        dma("sp", wt, wmv[:, :, piece * 256:(piece + 1) * 256], [], [f"wm{piece % 2}"], f"wm{piece % 2}")

        def f(e, wt=wt, piece=piece):
            ins = None
            for jj in range(2):
                j = piece * 2 + jj
                for kc in range(8):
                    ins = e.matmul(psum[:, pm, 2 * j:2 * j + 2], lhsT=wt[:, kc, jj * 128:(jj + 1) * 128],
                                   rhs=sc2[:, kc, :], start=(kc == 0), stop=(kc == 7))
            return ins
        mm(f, [f"wm{piece % 2}", "sc2"], [pmk + f"_{piece}"] if piece else [pmk])
    bm = pv[:, PV["b_mod"]:PV["b_mod"] + 48]
    tt("dve", modsb, psum[:, pm, 0:96].rearrange("p (a b) -> p a b", b=2),
       bm.unsqueeze(2).broadcast_to([128, 48, 2]), ALU.add,
       ["pv", pmk] + [pmk + f"_{i}" for i in range(1, 24)], ["modsb"])
    ts("dve", A1, modsb[:, 8:16, :], 1.0, None, ALU.add, None, ["modsb"], ["A1"])
    tt("dve", A1, A1, pvc("norm1_g", 0, 8).unsqueeze(2).broadcast_to([128, 8, 2]), ALU.mult, ["A1", "pv"], ["A1"])
    ts("dve", A2, modsb[:, 32:40, 0], 1.0, None, ALU.add, None, ["modsb"], ["A2"])
    tt("dve", A2, A2, pvc("norm2_g", 0, 8), ALU.mult, ["A2", "pv"], ["A2"])
    lam = pv[:, PV["rg_lambda"]:PV["rg_lambda"] + 16]
    cdf = cdec.rearrange("p a b -> p (a b)")
    act(cdf, lam, AF.Exp, ["pv"], ["cdec"], scale=-1.0)
    act(cdf, cdf, AF.Ln, ["cdec"], ["cdec"], bias=1.0)
    ts("dve", cdf, cdf, -8.0, None, ALU.mult, None, ["cdec"], ["cdec"])
    ts("dve", negdl, pvc("deltas", 0, 8), -1.0, None, ALU.mult, None, ["pv"], ["negdl"])

    dma("sp", p0_w[0:33, 0, :], hy_w1_d, [], ["hw1"], "hw1")
    dma("sp", p0_w[0:64, 1, :], hy_w2_d, [], ["hw2"], "hw2")
    dma("sp", p0_w[0:64, 2, :], hy_w3_d, [], ["hw3"], "hw3")
    hs = PV["hy_small"]
    TWO_PI = 2.0 * np.pi
    fb = tmp_s[:, 0:3]
    tt("dve", fb[0:64], pv[0:64, hs:hs + 3], pv[0:64, hs + 3:hs + 4].broadcast_to([64, 3]), ALU.mult, ["pv"], ["fb"])
    fq = tmp_s[:, 4:5]
    ts("dve", fq[0:64], pv[0:64, hs + 3:hs + 4], 1.0 / TWO_PI, None, ALU.mult, None, ["pv"], ["fq"])
    ts("dve", fb[0:64], fb[0:64], 1.0 / TWO_PI, None, ALU.mult, None, ["fb"], ["fb"])
    for tt_ in range(NT):
        sl = slice(tt_ * 512, (tt_ + 1) * 512)
        zp = p0_zpos[tt_ % 2]
        zk = f"zpos{tt_ % 2}"
        dma("sp", zp, t_zpos[:, sl], [], [zk], zk)
        srcs = [zp, p0_hd[0], p0_hd[1]]
        srck = [zk, "hd0", "hd1"]
        dsts = [p0_hd[0], p0_hd[1], hdn3[:, sl]]
        dstk = ["hd0", "hd1", "hdn3"]
        for layer in range(3):
            kdim = 33 if layer == 0 else 64
            pi, pk = nextps()
            mm(lambda e, layer=layer, kdim=kdim, pi=pi, srcs=srcs: e.matmul(
                psum[0:64, pi, :], lhsT=p0_w[0:kdim, layer, :], rhs=srcs[layer][0:kdim, :], start=True, stop=True),
               [srck[layer], f"hw{layer + 1}"], [pk])
            ts("dve", p0_t1[0:64], psum[0:64, pi, :], fq[0:64], fb[0:64, layer:layer + 1], ALU.mult, ALU.add,
               [pk, "fq", "fb"], ["t1"])
            ts("dve", p0_junk[0:64, 0:512], p0_t1[0:64], 12582912.0, 12582912.0, ALU.add, ALU.subtract, ["t1"], ["junk"])
            tt("dve", p0_t1[0:64], p0_t1[0:64], p0_junk[0:64, 0:512], ALU.subtract, ["t1", "junk"], ["t1"])
            act(dsts[layer][0:64], p0_t1[0:64], AF.Sin, ["t1"], [dstk[layer]], scale=TWO_PI)

    def cast_load(i):
        nm, kc, cols, parts = WBLK[i]
        cf_ = p0_cf[i % 2]
        n = kc * cols
        cfv = cf_[:, 0:n].rearrange("p (a b) -> p a b", b=cols)
        c_at = 0
        for pi_, (src, col0, ncols) in enumerate(parts):
            sv = src.rearrange("(kc p) n -> p kc n", p=128)[:, :, col0:col0 + ncols]
            dma("sp", cfv[:, :, c_at:c_at + ncols], sv, [], [f"cf{i % 2}_{pi_}"], f"cf{i % 2}")
            c_at += ncols

    def cast_finish(i):
        nm, kc, cols, parts = WBLK[i]
        cf_ = p0_cf[i % 2]
        cb_ = p0_cb[i % 2]
        kb = f"cb{i % 2}"
        kfs = [f"cf{i % 2}_{pi_}" for pi_ in range(4)]
        n = kc * cols
        if i % 2 == 0:
            act(cb_[:, 0:n], cf_[:, 0:n], AF.Copy, kfs, [kb] + kfs[len(parts):])
        else:
            fw.op("dve", lambda e, cb_=cb_, cf_=cf_, n=n: e.tensor_copy(out=cb_[:, 0:n], in_=cf_[:, 0:n]), reads=kfs,
                  writes=[kb] + kfs[len(parts):])
        dma("pool", wblk_d[i, :, 0:n], cb_[:, 0:n], [kb], [f"wblk{i}"], f"wblkst{i % 2}")

    def norm_tiles(src_d, ntile, which, store_fn):
        for ti in range(ntile):
            if which == 0 and ti < NWB:
                cast_load(ti)
            xt = p0_xt[ti % 2]
            xs = p0_xs[ti % 2]
            kx, ks = f"xt{ti % 2}", f"xs{ti % 2}"
            dma("sp", xt, src_d[ti * 128:(ti + 1) * 128, :], [], [kx], kx)
            act(p0_junk, xt, AF.Square, [kx], ["junk", "ss"], accum=p0_st[:, 0:1])
            act(p0_st[:, 1:2], p0_st[:, 0:1], AF.Sqrt, ["ss"], ["rs"], scale=1.0 / D, bias=EPS)
            fw.op("dve", lambda e: e.reciprocal(out=p0_st[:, 2:3], in_=p0_st[:, 1:2]), reads=["rs"], writes=["rstd"])
            ts("dve", xs, xt, p0_st[:, 2:3], None, ALU.mult, None, [kx, "rstd"], [ks])
            for half in range(2):
                pi, pk = nextps()

                def f(e, xs=xs, pi=pi, half=half):
                    ins = None
                    for gg in range(4):
                        g = half * 4 + gg
                        ins = e.transpose(psum[:, pi, gg * 128:(gg + 1) * 128], xs[:, g * 128:(g + 1) * 128], ident)
                    return ins
                mm(f, [ks, "ident"], [pk])
                for gg in range(4):
                    g = half * 4 + gg
                    store_fn(ti, g, psum[:, pi, gg * 128:(gg + 1) * 128], pk)
            if which == 0 and ti < NWB:
                cast_finish(ti)

    def ctx_store(ti, g, src, pk):
        act(hcT[:, g, ti * 128:(ti + 1) * 128], src, AF.Identity, [pk, "A1", "modsb"], [f"hcT{g}"],
            scale=A1[:, g, 1:2], bias=modsb[:, g, 1:2])
    norm_tiles(ctx_d, 2, 1, ctx_store)

    def x_store(ti, g, src, pk):
        st = p0_hst[(ti // 4) % 2]
        kst = f"hst{(ti // 4) % 2}"
        act(st[:, g, (ti % 4) * 128:(ti % 4 + 1) * 128], src, AF.Identity, [pk, "A1", "modsb"], [kst + f"_{g}_{ti % 4}"],
            scale=A1[:, g, 0:1], bias=modsb[:, g, 0:1])
        if g == 7 and ti % 4 == 3:
            j = ti // 4
            dma("sp", hT_d[:, :, j * 512:(j + 1) * 512], st,
                [kst + f"_{gg}_{t4}" for gg in range(8) for t4 in range(4)], ["hT_d"], "hT_d")
    norm_tiles(x_d, 32, 0, x_store)

    fw.barrier()

    cur[0] = GLOBAL_END
    hTt = [alloc([8, 512], BF16) for _ in range(2)]
    wg = alloc([5, 8, 128], BF16)
    bd = alloc([4, 128])
    w4g = alloc([4, 128], BF16)
    sm = alloc([64])
    kbt = [alloc([512], BF16) for _ in range(2)]
    tq = [alloc([512]) for _ in range(6)]
    ucx = alloc([256])
    B1 = alloc([L])
    mark = cur[0]
    B2 = alloc([L])
    B3 = alloc([L])
    B4 = alloc([L])
    B5 = alloc([L])
    B6 = alloc([L])
    endB = cur[0]
    cur[0] = mark
    ctile = tq[0:2]
    gtile = tq[2:4]
    zA4 = alloc([16, 128], BF16)
    kA = hTt[0].rearrange("p a b -> p (a b)").rearrange("p (c b) -> p c b", b=128)
    AZ = alloc([8448], BF16)
    Kf = alloc([NF1, 2, 64])
    Y3 = alloc([3 * NF1, 64], BF16)
    xk = [alloc([4, 2, 64]) for _ in range(2)]
    cur[0] = max(cur[0], endB)
    A_sb = AZ.rearrange("p (c pl f) -> p c pl f", pl=4, f=NF1)
    Z_sb = AZ[:, 0:8192].rearrange("p (c b) -> p c b", b=128)
    BDK = [f"bd{m4}{hh}" for m4 in range(4) for hh in range(2)]
    fw.op("dve", lambda e: e.memset(bd.rearrange("p a b -> p (a b)"), 0.0), writes=BDK)

    w_in_v = w_in_d.rearrange("(kc p) n -> p kc n", p=128)
    hTloads = [0]

    def load_hT(j):
        i = hTloads[0] % 2
        hTloads[0] += 1
        dma("sp", hTt[i], hT_d[:, :, j * 512:(j + 1) * 512], ["hT_d"], [f"hTt{i}"], f"hTt{i}")
        return hTt[i], f"hTt{i}"

    def conv_ops(out, okey, P, pkey, taps, bias, rl):
        o3 = out.rearrange("p (r c) -> p r c", c=rl)
        p3 = P.rearrange("p (r c) -> p r c", c=rl)
        wc = [w for (o, w) in taps if o == 0][0]
        act(out, P, AF.Identity, [pkey, "pv"], [okey], scale=wc, bias=bias)
        for (o, w) in taps:
            if o == 0:
                continue
            if o < 0:
                stt(o3[:, :, -o:], p3[:, :, :rl + o], w, o3[:, :, -o:], ALU.mult, ALU.add, [pkey, okey, "pv"], [okey])
            else:
                stt(o3[:, :, :rl - o], p3[:, :, o:], w, o3[:, :, :rl - o], ALU.mult, ALU.add, [pkey, okey, "pv"], [okey])

    def rglru(g, u, ukey, T, h0, h0k, hf, hfk, hb, hbk, t1, t1k, t3, t3k):
        nt = (T + 511) // 512
        w = min(T, 512)
        for d in range(2):
            hout, hk = (hf, hfk) if d == 0 else (hb, hbk)
            for m, dst, dk in ((0, t1, t1k), (1, t3, t3k)):
                bias = pvc("rg_ba" if m == 0 else "rg_bx", d * 8 + g)
                for j in range(nt):
                    pi, pk = nextps()
                    sl = slice(j * w, (j + 1) * w)
                    mm(lambda e, pi=pi, m=m, d=d, sl=sl: e.matmul(psum[:, pi, 0:w], lhsT=bd[:, 2 * m + d, :], rhs=u[:, sl],
                                                                  start=True, stop=True),
                       [f"bd{2 * m + d}0", f"bd{2 * m + d}1", ukey], [pk])
                    act(dst[:, sl], psum[:, pi, 0:w], AF.Sigmoid, [pk, "pv"], [dk], bias=bias)
            tt("pool", t3[:, 0:T], t3[:, 0:T], u[:, 0:T], ALU.mult, [t3k, ukey], [t3k])
            act(t1[:, 0:T], t1[:, 0:T], AF.Exp, [t1k, "cdec"], [t1k], scale=cdec[:, d, g:g + 1])
            tt("pool", hout[:, 0:T], t1[:, 0:T], t1[:, 0:T], ALU.mult, [t1k], [hk])
            act(hout[:, 0:T], hout[:, 0:T], AF.Sqrt, [hk], [hk], scale=-1.0, bias=1.0)
            tt("dve", t3[:, 0:T], t3[:, 0:T], hout[:, 0:T], ALU.mult, [t3k, hk], [t3k])
            init = 0.0 if h0 is None else h0[d]
            rk = [t1k, t3k] + ([] if h0 is None else [h0k[d]])
            if d == 0:
                fw.op("dve", lambda e, hout=hout, init=init: e.tensor_tensor_scan(
                    out=hout[:, 0:T], data0=t1[:, 0:T], data1=t3[:, 0:T], initial=init, op0=ALU.mult, op1=ALU.add),
                    reads=rk, writes=[hk])
            else:
                fw.op("dve", lambda e, hout=hout, init=init: e.tensor_tensor_scan(
                    out=hout[:, 0:T][:, ::-1], data0=t1[:, 0:T][:, ::-1], data1=t3[:, 0:T][:, ::-1], initial=init,
                    op0=ALU.mult, op1=ALU.add), reads=rk, writes=[hk])

    AKEYS = [f"A_{ch}" for ch in range(64)]

    def fwd_transform(src4, skeys, consume):
        stage1(src4, skeys)
        stage2(consume)

    def stage1(src4, skeys):
        for c in range(16):
            for pair in range(2):
                pi, pk = nextps()
                mm(lambda e, c=c, pair=pair, pi=pi: e.matmul(psum[:, pi, 0:264], lhsT=src4[:, c, :], rhs=D1m[:, pair, :],
                                                            start=True, stop=True), skeys + ["D1m"], [pk])
                dstv = AZ[:, c * 528 + pair * 264: c * 528 + (pair + 1) * 264]
                wk_ = [f"A_{4 * c + 2 * pair}", f"A_{4 * c + 2 * pair + 1}"]
                if (2 * c + pair) % 2 == 0:
                    act(dstv, psum[:, pi, 0:264], AF.Copy, [pk], wk_)
                else:
                    fw.op("dve", lambda e, dstv=dstv, pi=pi: e.tensor_copy(out=dstv, in_=psum[:, pi, 0:264]), reads=[pk], writes=wk_)

    def stage2(consume):
        for bi in range(9):
            f1lo = bi * 4
            nf = min(4, NF1 - f1lo)
            pi, pk = nextps()

            def f(e, f1lo=f1lo, nf=nf, pi=pi):
                ins = None
                for k in range(nf):
                    f1 = f1lo + k
                    e.matmul(psum[:, pi, k * 128:(k + 1) * 128], lhsT=Mtab[:, 0, f1, :],
                             rhs=A_sb[:, :, 0:2, f1].rearrange("p c pl -> p pl c"), start=True, stop=False)
                    ins = e.matmul(psum[:, pi, k * 128:(k + 1) * 128], lhsT=Mtab[:, 1, f1, :],
                                   rhs=A_sb[:, :, 2:4, f1].rearrange("p c pl -> p pl c"), start=False, stop=True)
                return ins
            mm(f, AKEYS + ["Mtab"], [pk])
            consume(bi, pi, pk, f1lo, nf)

    tqc = [0]

    def tqn():
        i = tqc[0] % 6
        tqc[0] += 1
        return tq[i], f"tq{i}"

    def load_wg(g):
        for seg in range(5):
            dma("pool", wg[:, seg, :, :], w_in_v[:, :, seg * D + g * 128: seg * D + (g + 1) * 128], [], [f"wg{seg}"], f"wg{seg}")

    def load_bd(g):
        for m4 in range(4):
            for hh in range(2):
                dma("sp", bd[64 * hh:64 * hh + 64, m4, 64 * hh:64 * hh + 64], rg_w_d[m4, 2 * g + hh], [], [f"bd{m4}{hh}"], f"bdl{m4}{hh}")

    def load_w4(g):
        for cmb in range(4):
            o, dr = cmb // 2, cmb % 2
            c0 = o * 2048 + dr * 1024 + g * 128
            dma("pool", w4g[0:64, cmb, :], hy_w4_d[:, c0:c0 + 128], [], [f"w4g{cmb}"], f"w4g{cmb}")

    load_wg(0)
    load_bd(0)
    load_w4(0)
    for g in range(NG):
        pi, pk = nextps()

        def f(e, pi=pi):
            ins = None
            for kc in range(8):
                ins = e.matmul(psum[:, pi, 0:256], lhsT=wg[:, 0, kc, :], rhs=hcT[:, kc, :], start=(kc == 0), stop=(kc == 7))
            return ins
        mm(f, ["wg0"] + [f"hcT{k}" for k in range(8)], [pk])
        pt, ptk = tqn()
        act(pt[:, 0:256], psum[:, pi, 0:256], AF.Identity, [pk, "pv"], [ptk], bias=pvc("b_in", 0 * 8 + g))
        rtaps = [(k - 2, pvc("rnn_conv_w", k * 8 + g)) for k in range(4)]
        conv_ops(ucx, "uc", pt[:, 0:256], ptk, rtaps, pvc("rnn_conv_b", g), 256)
        rglru(g, ucx, "uc", 256, None, None, B3, "B3", B4, "B4", B5, "B5", B6, "B6")
        fw.op("dve", lambda e: e.tensor_copy(out=sm[:, 0:1], in_=B3[:, 255:256]), reads=["B3"], writes=["h0f"])
        fw.op("dve", lambda e: e.tensor_copy(out=sm[:, 1:2], in_=B4[:, 0:1]), reads=["B4"], writes=["h0b"])

        nxt = load_hT(0)
        for j in range(NT):
            ht, hk = nxt
            if j + 1 < NT:
                nxt = load_hT(j + 1)
            sl = slice(j * 512, (j + 1) * 512)
            for seg in (0, 2, 3, 4, 1):
                pi, pk = seg, f"ps{seg}"

                def f(e, pi=pi, seg=seg, ht=ht):
                    ins = None
                    for kc in range(8):
                        ins = e.matmul(psum[:, pi, :], lhsT=wg[:, seg, kc, :], rhs=ht[:, kc, :], start=(kc == 0), stop=(kc == 7))
                    return ins
                mm(f, [f"wg{seg}", hk], [pk])
                pt, ptk = tqn()
                act(pt, psum[:, pi, :], AF.Identity, [pk, "pv"], [ptk], bias=pvc("b_in", seg * 8 + g))
                if seg == 0:
                    conv_ops(B2[:, sl], "B2", pt, ptk, rtaps, pvc("rnn_conv_b", g), 64)
                elif seg == 1:
                    gq, gqk = tqn()
                    act(gq, pt, AF.Square, [ptk], [gqk])
                    ts("pool", gq, gq, 0.044715, 1.0, ALU.mult, ALU.add, [gqk], [gqk])
                    tt("pool", gq, gq, pt, ALU.mult, [gqk, ptk], [gqk])
                    act(gq, gq, AF.Tanh, [gqk], [gqk], scale=0.7978845608028654)
                    stt(B6[:, sl], gq, 1.0, pt, ALU.add, ALU.mult, [gqk, ptk], ["B6"])
                else:
                    si = seg - 2
                    htaps = [(k - 1, pvc("hy_conv_w", k * 24 + si * 8 + g)) for k in range(3)]
                    if si == 0:
                        conv_ops(B1[:, sl], "B1", pt, ptk, htaps, pvc("hy_conv_b", si * 8 + g), 64)
                    else:
                        ct, ctk = tqn()
                        conv_ops(ct, ctk, pt, ptk, htaps, pvc("hy_conv_b", si * 8 + g), 64)
                        dma("sp", gate_d[si - 1, :, sl], ct, [ctk], [f"gate_d{si - 1}"], f"gate_st{si - 1}")
            dt_, dk_ = tqn()
            ts("dve", sm[:, 8 + j % 2:9 + j % 2], negdl[:, g:g + 1], float(j * 512) / float(L - 1), None, ALU.mult, None,
               ["negdl"], [f"decb{j % 2}"])
            act(dt_, tbase, AF.Exp, ["tbase", "negdl", f"decb{j % 2}"], [dk_], scale=negdl[:, g:g + 1],
                bias=sm[:, 8 + j % 2:9 + j % 2])
            for cmb in range(4):
                o_, dr = cmb // 2, cmb % 2
                pi = 5 + (j * 4 + cmb) % 3
                pk = f"ps{pi}"
                mm(lambda e, pi=pi, cmb=cmb, sl=sl: e.matmul(psum[:, pi, :], lhsT=w4g[0:64, cmb, :], rhs=hdn3[0:64, sl],
                                                             start=True, stop=True), [f"w4g{cmb}", "hdn3"], [pk])
                kt = kbt[cmb % 2]
                ktk = f"kbt{cmb % 2}"
                if dr == 0:
                    tt("dve", kt, psum[:, pi, :], dt_, ALU.mult, [pk, dk_], [ktk])
                else:
                    tt("dve", kt[:, ::-1], psum[:, pi, :], dt_, ALU.mult, [pk, dk_], [ktk])
                    if j == 0:
                        fw.op("dve", lambda e, kt=kt: e.memset(kt[:, 511:512], 0.0), reads=[ktk], writes=[ktk])
                jq, jqk = tqn()
                ncol = 16 + o_ * 24 + dr * 8 + j
                act(jq, kt, AF.Abs, [ktk], [jqk, f"nrm{cmb}_{j}"], accum=sm[:, ncol:ncol + 1])
                if dr == 0:
                    dma("sp", kb_d[o_, :, sl], kt, [ktk], [f"kb_d{o_}_{dr}_{j}"], f"kb_st{o_}")
                else:
                    s0 = L + 3585 - 512 * j
                    n_ = 511 if j == 0 else 512
                    dma("sp", kb_d[o_, :, s0:s0 + n_], kt[:, 0:n_], [ktk], ([f"kb_d{o_}_z"] if j == 7 else []) + [f"kb_d{o_}_{dr}_{j}"], f"kb_st{o_}")
        for o_ in range(2):
            base = 16 + o_ * 24
            fw.op("dve", lambda e, base=base: e.reduce_sum(out=sm[:, base + 16:base + 17], in_=sm[:, base:base + 16],
                                                          axis=mybir.AxisListType.X),
                  reads=[f"nrm{o_ * 2 + dr}_{j}" for dr in range(2) for j in range(NT)], writes=[f"nrmsum{o_}"])
            fw.op("dve", lambda e, base=base: e.reciprocal(out=sm[:, base + 17:base + 18], in_=sm[:, base + 16:base + 17]),
                  reads=[f"nrmsum{o_}"], writes=[f"invn{o_}"])
        if g + 1 < NG:
            load_wg(g + 1)
        KA = [f"kA_{q}_{c8}" for q in range(2) for c8 in range(2)]

        def FT_load(o, h):
            b_ = fw.bufs.get("hTt0")
            ex = ([] if b_ is None else ([b_[0]] if b_[0] is not None else []) + list(b_[1]))
            for q2 in range(2):
                for c8 in range(2):
                    c0 = 64 * h + q2 + 32 * c8
                    fw.op("sp", lambda e, q2=q2, c8=c8, c0=c0, o=o: e.dma_start(
                        out=kA[64 * q2:64 * q2 + 64, 16 * c8:16 * c8 + 16, :],
                        in_=kb_d[o, c0:c0 + 31:2, :].rearrange("c (a b) -> a c b", b=128)),
                        reads=[f"kb_d{o}_{dr_}_{j_}" for dr_ in range(2) for j_ in range(NT)], writes=[f"kA_{q2}_{c8}"],
                        dma="kA", extra=ex)

        dma("pool", zb_d, B1, ["B1"], ["zb_d"], "zb_st")
        FT_load(0, 0)
        rglru(g, B2, "B2", L, [sm[:, 0:1], sm[:, 1:2]], ["h0f", "h0b"], B3, "B3", B2, "B2", B5, "B5", B4, "B4")
        if g + 1 < NG:
            load_bd(g + 1)
        tt("dve", B3, B3, B2, ALU.add, ["B3", "B2"], ["B3"])
        stt(B3, B3, 0.5, B6, ALU.mult, ALU.mult, ["B3", "B6"], ["B3"])
        dma("pool", yr_d[:, g, :], B3, ["B3"], ["yr_d"], "yr_st")
        fw.barrier(dma_keys=["yr_st"] + [f"gate_st{i}" for i in range(2)])

        def FT(o, h, load=True):
            if load:
                FT_load(o, h)
            for c in range(32):
                pi, pk = nextps()
                mm(lambda e, c=c, pi=pi: e.matmul(psum[:, pi, 0:264], lhsT=kA[:, c, :], rhs=D1f, start=True, stop=True),
                   KA + ["D1f"], [pk])
                dstv = AZ[:, c * 264:(c + 1) * 264]
                wk_ = [f"A_{2 * c}", f"A_{2 * c + 1}"]
                if c % 2 == 0:
                    act(dstv, psum[:, pi, 0:264], AF.Copy, [pk], wk_)
                else:
                    fw.op("dve", lambda e, dstv=dstv, pi=pi: e.tensor_copy(out=dstv, in_=psum[:, pi, 0:264]), reads=[pk], writes=wk_)

            def consume_k(bi, pi, pk, f1lo, nf):
                src = psum[:, pi, 0:nf * 128].rearrange("p (f r c) -> p f r c", r=2, c=64)
                act(Kf[:, f1lo:f1lo + nf, :, :], src, AF.Copy, [pk], [f"Kf{bi}"])
            stage2(consume_k)

        def DT(o, h):
            ZA = [f"zA4_{q}" for q in range(4)]
            for q in range(4):
                c0 = 64 * h + q
                dma("sp", zA4[32 * q:32 * q + 32, :, :], zb_d[c0:64 * (h + 1):4, :].rearrange("c (a b) -> a c b", b=128),
                    ["zb_d"], [ZA[q]], "zA4")

            def consume_x(bi, pi, pk, f1lo, nf):
                X = psum[:, pi, 0:nf * 128].rearrange("p (f r c) -> p f r c", r=2, c=64)
                K = Kf[:, f1lo:f1lo + nf, :, :]
                Ksw = Kf[:, f1lo:f1lo + nf, ::-1, :]
                t1_ = xk[0][:, 0:nf]
                t2_ = xk[1][:, 0:nf]
                tt("dve", t1_, X, K, ALU.mult, [pk, f"Kf{bi}"], ["xk0"])
                tt("dve", t2_, X, Ksw, ALU.mult, [pk, f"Kf{bi}"], ["xk1"])
                yv = [Y3[:, pl * NF1 + f1lo: pl * NF1 + f1lo + nf, :] for pl in range(3)]
                tt("dve", yv[1], t1_[:, :, 0, :], t1_[:, :, 1, :], ALU.subtract, ["xk0"], [f"Y3r_{bi}"])
                tt("dve", yv[2], t2_[:, :, 0, :], t2_[:, :, 1, :], ALU.add, ["xk1"], [f"Y3i_{bi}"])
                act(yv[0], yv[2], AF.Copy, [f"Y3i_{bi}"], [f"Y3n_{bi}"], scale=-1.0)
            fwd_transform(zA4, ZA, consume_x)
            ykeys = [f"Y3{x_}_{bi}" for bi in range(9) for x_ in "rin"]
            for c4 in range(16):
                pi, pk = nextps()

                def f(e, c4=c4, pi=pi):
                    ins = None
                    for cc in range(4):
                        ch = c4 * 4 + cc
                        e.matmul(psum[0:66, pi, cc * 128:(cc + 1) * 128],
                                 lhsT=Y3[:, NF1:3 * NF1, ch], rhs=CS[:, 0:128], start=True, stop=False)
                        ins = e.matmul(psum[0:66, pi, cc * 128:(cc + 1) * 128],
                                       lhsT=Y3[:, 0:2 * NF1, ch], rhs=CS[:, 128:256],
                                       start=False, stop=True)
                    return ins
                mm(f, ykeys + ["CS"], [pk])
                act(Z_sb[0:66, c4 * 4:(c4 + 1) * 4, :],
                    psum[0:66, pi, :].rearrange("p (c b) -> p c b", b=128), AF.Copy, [pk], [f"Z_{c4}"] + AKEYS)
            zkeys = [f"Z_{c4}" for c4 in range(16)]
            icol = 16 + o * 24 + 17
            for bj in range(8):
                pi, pk = nextps()

                def f(e, bj=bj, pi=pi):
                    ins = None
                    for bb in range(16):
                        b_ = bj * 16 + bb
                        ins = e.matmul(psum[0:64, pi, bb * 32:(bb + 1) * 32], lhsT=Z_sb[0:66, :, b_], rhs=Gtab[0:66, b_, :],
                                       start=True, stop=True)
                    return ins
                mm(f, zkeys + AKEYS + ["Gtab"], [pk])
                ct = ctile[bj % 2]
                ctk = f"ct{bj % 2}"
                fw.op("dve", lambda e, ct=ct, pi=pi, h=h: e.tensor_copy(out=ct[64 * h:64 * h + 64, :], in_=psum[0:64, pi, :]),
                      reads=[pk], writes=[ctk])
                zv = B1[64 * h:64 * h + 64, :].rearrange("p (a b) -> p a b", b=128)[:, :, bj * 16:(bj + 1) * 16]
                stt(zv, ct[64 * h:64 * h + 64, :].rearrange("p (b a) -> p a b", a=32), sm[64 * h:64 * h + 64, icol:icol + 1], zv,
                    ALU.mult, ALU.add, [ctk, f"invn{o}", "B1"], ["B1"])

        passes = [(o, h) for o in range(2) for h in range(2)]
        FT(0, 0, load=False)
        ts("dve", B1, B1, pvc("hy_skip", 0 * 8 + g), None, ALU.mult, None, ["B1", "pv"], ["B1"])
        for pidx, (o, h) in enumerate(passes):
            DT(o, h)
            if pidx + 1 < len(passes):
                FT(*passes[pidx + 1])
            if h == 1:
                for j in range(NT):
                    sl = slice(j * 512, (j + 1) * 512)
                    gt = gtile[j % 2]
                    dma("sp", gt, gate_d[o, :, sl], [f"gate_d{o}"], [f"gt{j % 2}"], f"gt{j % 2}")
                    tt("dve", B1[:, sl], B1[:, sl], gt, ALU.mult, ["B1", f"gt{j % 2}"], ["B1"])
                if o == 0:
                    dma("pool", zb_d, B1, ["B1"], ["zb_d"], "zb_st")
                    ts("dve", B1, B1, pvc("hy_skip", 1 * 8 + g), None, ALU.mult, None, ["B1", "pv"], ["B1"])
        dma("pool", yh_d[:, g, :], B1, ["B1"], ["yh_d"], "yh_st")
        if g + 1 < NG:
            load_w4(g + 1)
        fw.barrier(dma_keys=["zA4", "kA"] if g + 1 < NG else None)

    cur[0] = GLOBAL_END
    ring = [alloc([4096], BF16) for _ in range(3)]
    yrt = alloc([8, 512], BF16)
    yht = alloc([8, 512], BF16)
    hTc = alloc([8, 512], BF16)
    xtok = alloc([4, D])
    xT = alloc([8, 512])
    merged = alloc([8, 512], BF16)
    gts = [alloc([512]) for _ in range(4)]
    sq = merged
    rstd = alloc([512])
    h2 = alloc([8, 512], BF16)
    hid = alloc([22, 512], BF16)
    wstate = {"issued": 0, "views": {}}
    TOTAL_W = NT * NWB

    def issue_w(upto):
        while wstate["issued"] < min(upto, TOTAL_W):
            sidx = wstate["issued"]
            i = sidx % NWB
            s_ = sidx % 3
            nm, kc, cols, parts = WBLK[i]
            n = kc * cols
            dma("sp", ring[s_][:, 0:n], wblk_d[i, :, 0:n], [f"wblk{i}"], [f"ring{s_}"], f"ring{s_}")
            wstate["views"][sidx] = (ring[s_][:, 0:n].rearrange("p (a b) -> p a b", b=cols), f"ring{s_}")
            wstate["issued"] += 1

    def get_w(j, i):
        sidx = j * NWB + i
        issue_w(sidx + 3)
        return wstate["views"][sidx]

    ostage = hid.rearrange("p a b -> p (a b)")[:, 0:8192].bitcast(F32).rearrange("p (a n) -> p a n", n=D)

    def load_acts(j):
        sl_ = slice(j * 512, (j + 1) * 512)
        dma("sp", yrt, yr_d[:, :, sl_], ["yr_d"], ["yrt"], "yrt")
        dma("sp", yht, yh_d[:, :, sl_], ["yh_d"], ["yht"], "yht")
        dma("sp", hTc, hT_d[:, :, sl_], ["hT_d"], ["hTc"], "hTc")

    def load_x(j):
        dma("sp", xtok, x_d[j * 512:(j + 1) * 512, :].rearrange("(a p) n -> p a n", p=128), [], ["xtok"], "xtok")

    def rms_rstd():
        act(sq, xT, AF.Square, ["xT"], ["sq"])
        pi, pk = nextps()

        def f(e, pi=pi):
            ins = None
            for kc in range(8):
                ins = e.matmul(psum[:, pi, :], lhsT=ones_bf, rhs=sq[:, kc, :], start=(kc == 0), stop=(kc == 7))
            return ins
        mm(f, ["ones", "sq"], [pk])
        act(rstd, psum[:, pi, :], AF.Sqrt, [pk], ["rstd_"], scale=1.0 / D, bias=EPS)
        fw.op("dve", lambda e: e.reciprocal(out=rstd, in_=rstd), reads=["rstd_"], writes=["rstd_"])

    def st_xT(j):
        for a4 in range(4):
            for half in range(2):
                pi, pk = nextps()

                def f(e, a4=a4, half=half, pi=pi):
                    ins = None
                    for gg in range(4):
                        g = half * 4 + gg
                        ins = e.transpose(psum[:, pi, gg * 128:(gg + 1) * 128], xtok[:, a4, g * 128:(g + 1) * 128], ident)
                    return ins
                mm(f, ["xtok", "ident"], [pk])
                act(xT[:, half * 4:(half + 1) * 4, a4 * 128:(a4 + 1) * 128],
                    psum[:, pi, :].rearrange("p (g t) -> p g t", t=128), AF.Copy, [pk], ["xT"])
        if j + 1 < NT:
            load_x(j + 1)
    def st_merged(j):
        for m in range(8):
            wv, wk = get_w(j, m)
            pis = []
            for li, (mov, mk_) in enumerate(((yrt, "yrt"), (yht, "yht"), (hTc, "hTc"), (hTc, "hTc"))):
                pi, pk = nextps()

                def f(e, wv=wv, mov=mov, li=li, pi=pi):
                    ins = None
                    for kc in range(8):
                        ins = e.matmul(psum[:, pi, :], lhsT=wv[:, kc, li * 128:(li + 1) * 128], rhs=mov[:, kc, :],
                                       start=(kc == 0), stop=(kc == 7))
                    return ins
                mm(f, [wk, mk_], [pk])
                pis.append((pi, pk))
            act(gts[0], psum[:, pis[2][0], :], AF.Sigmoid, [pis[2][1], "pv"], ["gt0"], bias=pvc("b_in", 40 + m))
            act(gts[1], psum[:, pis[3][0], :], AF.Sigmoid, [pis[3][1], "pv"], ["gt1"], bias=pvc("b_in", 48 + m))
            tt("dve", gts[2], psum[:, pis[0][0], :], gts[0], ALU.mult, [pis[0][1], "gt0"], ["gt2"])
            tt("dve", gts[3], psum[:, pis[1][0], :], gts[1], ALU.mult, [pis[1][1], "gt1"], ["gt3"])
            tt("pool", merged[:, m, :], gts[2], gts[3], ALU.add, ["gt2", "gt3"], ["merged"])
        if j + 1 < NT:
            load_acts(j + 1)
    def st_mid(j):
        for m in range(8):
            wv, wk = get_w(j, 8 + m // 4)
            col = m % 4
            pi, pk = nextps()

            def f(e, wv=wv, col=col, pi=pi):
                ins = None
                for kc in range(8):
                    ins = e.matmul(psum[:, pi, :], lhsT=wv[:, kc, col * 128:(col + 1) * 128], rhs=merged[:, kc, :],
                                   start=(kc == 0), stop=(kc == 7))
                return ins
            mm(f, [wk, "merged"], [pk])
            stt(xT[:, m, :], psum[:, pi, :], modsb[:, 16 + m, 0:1], xT[:, m, :], ALU.mult, ALU.add, [pk, "modsb", "xT"], ["xT"])

        rms_rstd()
        for m in range(8):
            tt("dve", gts[m % 2], xT[:, m, :], rstd, ALU.mult, ["xT", "rstd_"], [f"gt{m % 2}"])
            act(h2[:, m, :], gts[m % 2], AF.Identity, [f"gt{m % 2}", "A2", "modsb"], ["h2"], scale=A2[:, m:m + 1],
                bias=modsb[:, 24 + m, 0:1])
        for m in range(22):
            wv, wk = get_w(j, 10 + m // 2)
            pis = []
            for part in range(2):
                col = part * 2 + (m % 2)
                pi, pk = nextps()

                def f(e, wv=wv, col=col, pi=pi):
                    ins = None
                    for kc in range(8):
                        ins = e.matmul(psum[:, pi, :], lhsT=wv[:, kc, col * 128:(col + 1) * 128], rhs=h2[:, kc, :],
                                       start=(kc == 0), stop=(kc == 7))
                    return ins
                mm(f, [wk, "h2"], [pk])
                pis.append((pi, pk))
            act(gts[m % 2], psum[:, pis[0][0], :], AF.Silu, [pis[0][1]], [f"gt{m % 2}"])
            tt("dve", hid[:, m, :], psum[:, pis[1][0], :], gts[m % 2], ALU.mult, [pis[1][1], f"gt{m % 2}"], ["hid"])
        for m in range(8):
            wv, wk = get_w(j, 21 + m)
            pi, pk = nextps()

            def f(e, wv=wv, pi=pi):
                ins = None
                for kc in range(22):
                    ins = e.matmul(psum[:, pi, :], lhsT=wv[:, kc, :], rhs=hid[:, kc, :], start=(kc == 0), stop=(kc == 21))
                return ins
            mm(f, [wk, "hid"], [pk])
            stt(xT[:, m, :], psum[:, pi, :], modsb[:, 40 + m, 0:1], xT[:, m, :], ALU.mult, ALU.add, [pk, "modsb", "xT"], ["xT"])
    def st_tailA(j):
        rms_rstd()
        for m in range(8):
            stt(xT[:, m, :], xT[:, m, :], pvc("final_g", m), rstd, ALU.mult, ALU.mult, ["xT", "rstd_", "pv"], ["xT"])
    def st_tailB(j):
        for a4 in range(4):
            for half in range(2):
                pi, pk = nextps()

                def f(e, a4=a4, half=half, pi=pi):
                    ins = None
                    for gg in range(4):
                        g = half * 4 + gg
                        ins = e.transpose(psum[:, pi, gg * 128:(gg + 1) * 128], xT[:, g, a4 * 128:(a4 + 1) * 128], ident)
                    return ins
                mm(f, ["xT", "ident"], [pk])
                act(ostage[:, a4, half * 512:(half + 1) * 512], psum[:, pi, :], AF.Copy, [pk], ["hid"])
        dma("sp", out_d[j * 512:(j + 1) * 512, :].rearrange("(a p) n -> p a n", p=128), ostage, ["hid"], ["out_d"], "out_st")


    load_x(0)
    load_acts(0)
    st_xT(0)
    st_merged(0)
    for j in range(NT):
        st_mid(j)
        st_tailA(j)
        if j + 1 < NT:
            st_merged(j + 1)
        st_tailB(j)
        if j + 1 < NT:
            st_xT(j + 1)

    fw.op("sp", None, extra=fw.all_tokens())
    assert len(fw.dsem) + 5 <= 100, len(fw.dsem)
    fw.emit()
    return nc


_CACHE = {}


def prep_inputs(inp, tb, b):
    PV, NPV = pv_layout()
    pv = np.zeros((128, NPV), np.float32)

    def put(name, arr):
        a = fm(arr)
        pv[:, PV[name]:PV[name] + a.shape[1]] = a
    put("norm1_g", inp["norm1_g"][0])
    put("norm2_g", inp["norm2_g"][0])
    put("final_g", inp["final_g"])
    put("rnn_conv_b", inp["rnn_conv_b"][0])
    put("deltas", tb["deltas"])
    put("b_mod", inp["b_mod"][0])
    put("b_in", inp["b_in"][0])
    put("rnn_conv_w", inp["rnn_conv_w"][0].reshape(-1))
    put("rg_ba", inp["rg_ba"][0].reshape(-1))
    put("rg_bx", inp["rg_bx"][0].reshape(-1))
    put("rg_lambda", inp["rg_lambda"][0].reshape(-1))
    put("hy_conv_w", inp["hy_conv_w"][0].reshape(-1))
    put("hy_conv_b", inp["hy_conv_b"][0])
    put("hy_skip", inp["hy_skip"][0].reshape(-1))
    hs = PV["hy_small"]
    pv[:64, hs + 0] = inp["hy_b1"][0]
    pv[:64, hs + 1] = inp["hy_b2"][0]
    pv[:64, hs + 2] = inp["hy_b3"][0]
    pv[:64, hs + 3] = inp["hy_freq"][0]
    c2 = np.stack([fm(inp["c"][b]), fm(inp["c_ctx"])], axis=-1).reshape(128, 16)
    m = {
        "x": np.ascontiguousarray(inp["x"][b]),
        "ctx": np.ascontiguousarray(inp["ctx"][b]),
        "c2": np.ascontiguousarray(c2, dtype=np.float32),
        "pv": pv,
        "w_mod": inp["w_mod"][0], "w_in": inp["w_in"][0],
        "rg_w": np.ascontiguousarray(np.concatenate([inp["rg_wa"][0], inp["rg_wx"][0]], axis=0)),
        "hy_w1": inp["hy_w1"][0], "hy_w2": inp["hy_w2"][0], "hy_w3": inp["hy_w3"][0], "hy_w4": inp["hy_w4"][0],
        "w_a_out": inp["w_a_out"][0], "w_b_out": inp["w_b_out"][0], "w_out": inp["w_out"][0],
        "w_ffn_in": inp["w_ffn_in"][0], "w_ffn_out": inp["w_ffn_out"][0],
        "t_D1m": tb["D1m"], "t_D1f": tb["D1f"], "t_M": tb["M"], "t_CS": tb["CS"], "t_G": tb["G"], "t_zpos": tb["zposT"],
        "t_tbase": tb["tbase"], "t_ident": tb["ident"],
    }
    return {k: np.ascontiguousarray(np.asarray(v, dtype=np.float32)) for k, v in m.items()}


def kernel(**inputs):
    inp = {k: np.asarray(v) for k, v in inputs.items()}
    tb = make_tables()
    if "nc" not in _CACHE:
        _CACHE["nc"] = build_program(False)
    nc = _CACHE["nc"]
    in_maps = [prep_inputs(inp, tb, b) for b in range(8)]
    res = run_bass_kernel_spmd(nc, in_maps, core_ids=list(range(8)))
    out = np.stack([np.asarray(r["out"]) for r in res.results], axis=0)
    return out.astype(np.float32)
```
